# Optimizing a Trainium2 kernel written in Bass

```python
import math
import jax
import jax.numpy as jnp
from jax import lax
import numpy as np

D_MODEL = 1024
BATCH = 2
SEQ = 16384
DEPTH = 2

GRID_W = 64
CTX_LEN = 256
F32 = jnp.float32

D_FF = 2816
BR_WIDTH = D_MODEL // 2
A_DK = 128
A_WIDTH = BR_WIDTH
A_HEADS = A_WIDTH // A_DK
A_DV = A_WIDTH // A_HEADS
A_CHUNK = 64
B_WIDTH = BR_WIDTH
B_ORDER = 2
B_EMB = 33
B_BANDS = (B_EMB - 1) // 2
B_FFN = 64
B_FAST_DECAY = 0.3
B_SLOW_DECAY = 1.5
B_TARGET = 1e-2
C_HEAD_DIM = 64
C_HEADS = BR_WIDTH // C_HEAD_DIM
C_KV_HEADS = 2
C_GROUP = C_HEADS // C_KV_HEADS
C_WIDTH = C_HEADS * C_HEAD_DIM
C_KV_WIDTH = C_KV_HEADS * C_HEAD_DIM
C_WINDOW = 128
C_BLOCK = C_WINDOW
ROPE_BASE = 10000.0
N_BRANCH = 3
N_MOD = 9
OFF_B = 5 * A_WIDTH
OFF_Q = OFF_B + (B_ORDER + 1) * B_WIDTH
OFF_K = OFF_Q + C_WIDTH
OFF_V = OFF_K + C_KV_WIDTH
OFF_G = OFF_V + C_KV_WIDTH
IN_WIDTH = OFF_G + N_BRANCH * D_MODEL
IN_SPLITS = (OFF_B, OFF_Q, OFF_K, OFF_V, OFF_G)
DN_ALPHA = (2.0 * DEPTH) ** 0.25
DN_BETA = (8.0 * DEPTH) ** -0.25
LN_EPS = 1e-5
RMS_EPS = 1e-6

kernel_name = 'hybrid_hgrn2_hyena_swa_flow_block'


def layer_norm(x, g, b):
    xf = x.astype(F32)
    mu = jnp.mean(xf, axis=-1, keepdims=True)
    var = jnp.mean(jnp.square(xf - mu), axis=-1, keepdims=True)
    return ((xf - mu) * lax.rsqrt(var + LN_EPS) * g.astype(F32) + b.astype(F32)).astype(x.dtype)


def modulate(x, shift, scale):
    return x * (1.0 + scale) + shift


def swiglu(u, w_in, w_out):
    a, b = jnp.split(u @ w_in, 2, axis=-1)
    return (jax.nn.silu(a) * b) @ w_out


def macaron_ffn(h, shift, scale, gate, w_in, w_out, g, b):
    y = swiglu(modulate(h, shift, scale), w_in, w_out)
    return layer_norm(DN_ALPHA * h + 0.5 * gate * y, g, b)


def _heads(a):
    n, b, l, _ = a.shape
    return a.reshape(n, b, l, A_HEADS, -1).transpose(0, 1, 3, 2, 4)


def hgrn_streams(pa, lb):
    q, i, g, zf, zb = jnp.split(pa.astype(F32), 5, axis=-1)
    lbx = lb.astype(F32)[:, None, None, :]
    z = jnp.stack([zf, zb[:, ::-1]], 0)
    logf = jnp.log(lbx + (1.0 - lbx) * jax.nn.sigmoid(z))
    k = (1.0 - lbx) * jax.nn.sigmoid(-z)
    qd = jnp.stack([q, q[:, ::-1]], 0)
    vd = jnp.stack([i, i[:, ::-1]], 0)
    return _heads(qd), _heads(k), _heads(vd), _heads(logf), g


def hgrn_chunk_scan(q, k, v, logf, s0):
    n, b, h, L, _ = q.shape
    nc = L // A_CHUNK

    def to_chunks(a):
        return jnp.moveaxis(a.reshape(n, b, h, nc, A_CHUNK, a.shape[-1]), 3, 0)

    mask = jnp.tril(jnp.ones((A_CHUNK, A_CHUNK), bool))[:, :, None]

    def step(S, inp):
        qc, kc, vc, gc = inp
        bcum = jnp.cumsum(gc, axis=-2)
        inter = jnp.einsum('nbhtk,nbhkv->nbhtv', qc * jnp.exp(bcum), S)
        rel = jnp.exp(jnp.where(mask, bcum[..., :, None, :] - bcum[..., None, :, :], -jnp.inf))
        att = jnp.einsum('nbhtk,nbhsk,nbhtsk->nbhts', qc, kc, rel)
        out = inter + jnp.einsum('nbhts,nbhsv->nbhtv', att, vc)
        blast = bcum[..., -1:, :]
        S = jnp.exp(blast[..., 0, :])[..., None] * S + jnp.einsum(
            'nbhsk,nbhsv->nbhkv', kc * jnp.exp(blast - bcum), vc)
        return S, out

    s_fin, out = lax.scan(step, s0, (to_chunks(q), to_chunks(k), to_chunks(v), to_chunks(logf)))
    out = jnp.moveaxis(out, 0, 3).reshape(n, b, h, L, -1)
    return out, s_fin


def hgrn_readout(out, g, norm_w):
    o = out[0] + out[1][:, :, ::-1]
    o = o.transpose(0, 2, 1, 3)
    o = o * lax.rsqrt(jnp.mean(jnp.square(o), axis=-1, keepdims=True) + RMS_EPS)
    o = o * norm_w.astype(F32).reshape(A_HEADS, A_DV)
    bsz, L = o.shape[:2]
    return o.reshape(bsz, L, A_WIDTH) * jax.nn.silu(g)


def short_conv3(x, w, b):
    xp = jnp.pad(x, ((0, 0), (1, 1), (0, 0)))
    return xp[:, :-2] * w[0] + xp[:, 1:-1] * w[1] + xp[:, 2:] * w[2] + b


def hyena_decay_rates():
    max_decay = math.log(B_TARGET) / B_FAST_DECAY
    min_decay = math.log(B_TARGET) / B_SLOW_DECAY
    return jnp.abs(jnp.linspace(min_decay, max_decay, B_WIDTH, dtype=F32))


def hyena_filters(L, w1, b1, f1, w2, b2, f2, w3):
    pos = jnp.arange(L, dtype=F32)
    t = pos / (L - 1)
    w = 2.0 * math.pi * pos / L
    bands = jnp.linspace(1e-4, B_BANDS - 1, B_BANDS, dtype=F32)
    feats = jnp.concatenate([t[:, None], jnp.cos(w[:, None] * bands), -jnp.sin(w[:, None] * bands)], axis=-1)
    h = jnp.sin(f1.astype(F32) * (feats @ w1.astype(F32) + b1.astype(F32)))
    h = jnp.sin(f2.astype(F32) * (h @ w2.astype(F32) + b2.astype(F32)))
    h = (h @ w3.astype(F32)).reshape(L, 2, B_ORDER, B_WIDTH)
    h = h * jnp.exp(-t[:, None] * hyena_decay_rates())[:, None, None, :]
    h_fwd = h[:, 0]
    h_bwd = h[:0:-1, 1]
    l1 = jnp.sum(jnp.abs(h_fwd), axis=0) + jnp.sum(jnp.abs(h_bwd), axis=0)
    filt = jnp.concatenate([h_fwd, jnp.zeros((1, B_ORDER, B_WIDTH), F32), h_bwd], axis=0)
    return filt / l1


def long_conv(z, filt):
    L = z.shape[1]
    zf = jnp.fft.rfft(z, n=2 * L, axis=1)
    ff = jnp.fft.rfft(filt, n=2 * L, axis=0)
    return jnp.fft.irfft(zf * ff[None], n=2 * L, axis=1)[:, :L]


def hyena_mixer(p, conv_w, conv_b, w1, b1, f1, w2, b2, f2, w3, bias):
    L = p.shape[1]
    parts = jnp.split(short_conv3(p, conv_w, conv_b).astype(F32), B_ORDER + 1, axis=-1)
    filt = hyena_filters(L, w1, b1, f1, w2, b2, f2, w3)
    z = parts[0]
    for n in range(B_ORDER):
        z = parts[n + 1] * (long_conv(z, filt[:, n]) + bias[n].astype(F32) * z)
    return z


def axial_rope_tables(L):
    n_rows = L // GRID_W
    row = jnp.repeat(jnp.arange(n_rows), GRID_W).astype(F32)
    col = jnp.tile(jnp.arange(GRID_W), n_rows).astype(F32)
    nf = C_HEAD_DIM // 4
    inv = ROPE_BASE ** (-jnp.arange(nf, dtype=F32) * 2.0 / (C_HEAD_DIM // 2))
    ang = jnp.concatenate([row[:, None] * inv, col[:, None] * inv], axis=-1)
    return jnp.cos(ang), jnp.sin(ang)


def apply_axial_rope(x, cos, sin):
    bsz, L, hh, _ = x.shape
    nf = C_HEAD_DIM // 4
    xr = x.astype(F32).reshape(bsz, L, hh, 2, 2, nf)
    x1, x2 = xr[..., 0, :], xr[..., 1, :]
    cs = cos.reshape(1, L, 1, 2, nf)
    sn = sin.reshape(1, L, 1, 2, nf)
    out = jnp.stack([x1 * cs - x2 * sn, x2 * cs + x1 * sn], axis=-2)
    return out.reshape(x.shape).astype(x.dtype)


def _sink_column(sink, shape):
    s = sink.astype(F32).reshape(C_KV_HEADS, C_GROUP)[:, :, None, None]
    return jnp.broadcast_to(s, shape)


def windowed_attention(q, k, v, k_ctx, v_ctx, sink):
    bsz, S = q.shape[:2]
    nb = S // C_BLOCK
    scale = C_HEAD_DIM ** -0.5
    qb = q.reshape(bsz, nb, C_BLOCK, C_KV_HEADS, C_GROUP, C_HEAD_DIM)

    def band(a):
        ap = jnp.pad(a, ((0, 0), (C_BLOCK, C_BLOCK), (0, 0), (0, 0)))
        ap = ap.reshape(bsz, nb + 2, C_BLOCK, C_KV_HEADS, C_HEAD_DIM)
        return jnp.concatenate([ap[:, :-2], ap[:, 1:-1], ap[:, 2:]], axis=2)

    kw, vw = band(k), band(v)
    s_loc = jnp.einsum('bnqhgd,bnkhd->bnhgqk', qb, kw).astype(F32) * scale
    s_ctx = jnp.einsum('bnqhgd,bchd->bnhgqc', qb, k_ctx).astype(F32) * scale
    blk = jnp.arange(nb)[:, None, None] * C_BLOCK
    qpos = blk + jnp.arange(C_BLOCK)[None, :, None]
    kpos = blk - C_BLOCK + jnp.arange(3 * C_BLOCK)[None, None, :]
    valid = (jnp.abs(qpos - kpos) <= C_WINDOW) & (kpos >= 0) & (kpos < S)
    s_loc = jnp.where(valid[None, :, None, None], s_loc, -jnp.inf)
    sink_col = _sink_column(sink, s_loc.shape[:-1] + (1,))
    p = jax.nn.softmax(jnp.concatenate([s_loc, s_ctx, sink_col], axis=-1), axis=-1).astype(v.dtype)
    nw = 3 * C_BLOCK
    lc = k_ctx.shape[1]
    o = jnp.einsum('bnhgqk,bnkhd->bnqhgd', p[..., :nw], vw) + jnp.einsum(
        'bnhgqc,bchd->bnqhgd', p[..., nw:nw + lc], v_ctx)
    return o.reshape(bsz, S, C_WIDTH)


def context_attention(q_ctx, k_ctx, v_ctx, sink):
    bsz, lc = q_ctx.shape[:2]
    qg = q_ctx.reshape(bsz, lc, C_KV_HEADS, C_GROUP, C_HEAD_DIM)
    s = jnp.einsum('bqhgd,bkhd->bhgqk', qg, k_ctx).astype(F32) * (C_HEAD_DIM ** -0.5)
    sink_col = _sink_column(sink, s.shape[:-1] + (1,))
    p = jax.nn.softmax(jnp.concatenate([s, sink_col], axis=-1), axis=-1).astype(v_ctx.dtype)
    o = jnp.einsum('bhgqk,bkhd->bqhgd', p[..., :lc], v_ctx)
    return o.reshape(bsz, lc, C_WIDTH)


def merge_branches(ys, pg, w_branch, w_out):
    gates = jnp.split(pg, N_BRANCH, axis=-1)
    m = jax.nn.sigmoid(gates[0]) * (ys[0] @ w_branch[0])
    for n in range(1, N_BRANCH):
        m = m + jax.nn.sigmoid(gates[n]) * (ys[n] @ w_branch[n])
    return m @ w_out


def token_mixer(u, uc, w_in, lb, a_norm_w, b_conv_w, b_conv_b, b_w1, b_b1, b_f1, b_w2, b_b2, b_f2, b_w3,
                b_bias, sink, w_branch, w_out, with_ctx_out):
    bsz, S, _ = u.shape
    lc = uc.shape[1]
    pa, pb, pq, pk, pv, pg = jnp.split(u @ w_in, IN_SPLITS, axis=-1)
    ca, cb, cq, ck, cv, cg = jnp.split(uc @ w_in, IN_SPLITS, axis=-1)
    filt = (b_w1, b_b1, b_f1, b_w2, b_b2, b_f2, b_w3)

    qa_c, ka_c, va_c, lf_c, ga_c = hgrn_streams(ca, lb)
    s0 = jnp.zeros((2, bsz, A_HEADS, A_DK, A_DV), F32)
    oa_c, s_ctx = hgrn_chunk_scan(qa_c, ka_c, va_c, lf_c, s0)
    qa, ka, va, lf, ga = hgrn_streams(pa, lb)
    oa, _ = hgrn_chunk_scan(qa, ka, va, lf, s_ctx)
    y_a = hgrn_readout(oa, ga, a_norm_w).astype(u.dtype)

    y_b = hyena_mixer(pb, b_conv_w, b_conv_b, *filt, b_bias).astype(u.dtype)

    cos, sin = axial_rope_tables(S)
    q = apply_axial_rope(pq.reshape(bsz, S, C_HEADS, C_HEAD_DIM), cos, sin)
    k = apply_axial_rope(pk.reshape(bsz, S, C_KV_HEADS, C_HEAD_DIM), cos, sin)
    v = pv.reshape(bsz, S, C_KV_HEADS, C_HEAD_DIM)
    k_ctx = ck.reshape(bsz, lc, C_KV_HEADS, C_HEAD_DIM)
    v_ctx = cv.reshape(bsz, lc, C_KV_HEADS, C_HEAD_DIM)
    y_c = windowed_attention(q, k, v, k_ctx, v_ctx, sink).astype(u.dtype)

    y = merge_branches((y_a, y_b, y_c), pg, w_branch, w_out)
    if not with_ctx_out:
        return y, None
    yc_a = hgrn_readout(oa_c, ga_c, a_norm_w).astype(uc.dtype)
    yc_b = hyena_mixer(cb, b_conv_w, b_conv_b, *filt, b_bias).astype(uc.dtype)
    yc_c = context_attention(cq.reshape(bsz, lc, C_HEADS, C_HEAD_DIM), k_ctx, v_ctx, sink).astype(uc.dtype)
    yc = merge_branches((yc_a, yc_b, yc_c), cg, w_branch, w_out)
    return y, yc


def setup_inputs(seed: int = 0) -> dict:
    key = jax.random.key(seed)
    ks = jax.random.split(key, 26)

    def nrm(k, shape, scale):
        return jax.random.normal(k, shape, F32) * scale

    gate_offset = jnp.repeat(jnp.array([0.0, 0.0, 1.0] * 3, F32), D_MODEL)
    return {
        'x': nrm(ks[0], (BATCH, SEQ, D_MODEL), 1.0),
        'c': nrm(ks[1], (BATCH, D_MODEL), 1.0),
        'ctx': nrm(ks[2], (BATCH, CTX_LEN, D_MODEL), 1.0),
        'c_ctx': nrm(ks[3], (D_MODEL,), 1.0),
        'ada_w': nrm(ks[4], (DEPTH, D_MODEL, N_MOD * D_MODEL), 0.5 * D_MODEL ** -0.5),
        'ada_b': nrm(ks[5], (DEPTH, N_MOD * D_MODEL), 0.02) + gate_offset,
        'ln_g': 1.0 + nrm(ks[6], (DEPTH, 3, D_MODEL), 0.02),
        'ln_b': nrm(ks[7], (DEPTH, 3, D_MODEL), 0.02),
        'ffn_w_in': nrm(ks[8], (DEPTH, 2, D_MODEL, 2 * D_FF), D_MODEL ** -0.5),
        'ffn_w_out': nrm(ks[9], (DEPTH, 2, D_FF, D_MODEL), DN_BETA * D_FF ** -0.5),
        'mix_w_in': nrm(ks[10], (DEPTH, D_MODEL, IN_WIDTH), D_MODEL ** -0.5),
        'hgrn_lb': nrm(ks[11], (DEPTH, 2, A_WIDTH), 0.5),
        'hgrn_norm_w': 1.0 + nrm(ks[12], (DEPTH, A_WIDTH), 0.02),
        'hyena_conv_w': nrm(ks[13], (DEPTH, 3, (B_ORDER + 1) * B_WIDTH), 3 ** -0.5),
        'hyena_conv_b': nrm(ks[14], (DEPTH, (B_ORDER + 1) * B_WIDTH), 0.02),
        'hyena_w1': nrm(ks[15], (DEPTH, B_EMB, B_FFN), B_EMB ** -0.5),
        'hyena_b1': nrm(ks[16], (DEPTH, B_FFN), 0.1),
        'hyena_f1': 1.0 + nrm(ks[17], (DEPTH, B_FFN), 0.1),
        'hyena_w2': nrm(ks[18], (DEPTH, B_FFN, B_FFN), B_FFN ** -0.5),
        'hyena_b2': nrm(ks[19], (DEPTH, B_FFN), 0.1),
        'hyena_f2': 1.0 + nrm(ks[20], (DEPTH, B_FFN), 0.1),
        'hyena_w3': nrm(ks[21], (DEPTH, B_FFN, 2 * B_ORDER * B_WIDTH), B_FFN ** -0.5),
        'hyena_bias': nrm(ks[22], (DEPTH, B_ORDER, B_WIDTH), 0.5),
        'attn_sink': nrm(ks[23], (DEPTH, C_HEADS), 0.5),
        'branch_w': nrm(ks[24], (DEPTH, N_BRANCH, BR_WIDTH, D_MODEL), BR_WIDTH ** -0.5),
        'out_w': nrm(ks[25], (DEPTH, D_MODEL, D_MODEL), DN_BETA * D_MODEL ** -0.5),
    }


def reference(x, c, ctx, c_ctx, ada_w, ada_b, ln_g, ln_b, ffn_w_in, ffn_w_out, mix_w_in, hgrn_lb, hgrn_norm_w,
              hyena_conv_w, hyena_conv_b, hyena_w1, hyena_b1, hyena_f1, hyena_w2, hyena_b2, hyena_f2, hyena_w3,
              hyena_bias, attn_sink, branch_w, out_w):
    s = jax.nn.softmax(hgrn_lb.astype(F32), axis=0)
    lower_bounds = jnp.cumsum(s, axis=0) - s[0:1]
    h, hc = x, ctx
    for l in range(DEPTH):
        last = l == DEPTH - 1
        mod = jnp.split((jax.nn.silu(c) @ ada_w[l] + ada_b[l])[:, None, :], N_MOD, axis=-1)
        modc = jnp.split(jax.nn.silu(c_ctx) @ ada_w[l] + ada_b[l], N_MOD, axis=-1)
        h = macaron_ffn(h, mod[0], mod[1], mod[2], ffn_w_in[l, 0], ffn_w_out[l, 0], ln_g[l, 0], ln_b[l, 0])
        hc = macaron_ffn(hc, modc[0], modc[1], modc[2], ffn_w_in[l, 0], ffn_w_out[l, 0], ln_g[l, 0], ln_b[l, 0])
        y, yc = token_mixer(modulate(h, mod[3], mod[4]), modulate(hc, modc[3], modc[4]), mix_w_in[l],
                            lower_bounds[l], hgrn_norm_w[l], hyena_conv_w[l], hyena_conv_b[l], hyena_w1[l],
                            hyena_b1[l], hyena_f1[l], hyena_w2[l], hyena_b2[l], hyena_f2[l], hyena_w3[l],
                            hyena_bias[l], attn_sink[l], branch_w[l], out_w[l], not last)
        h = layer_norm(DN_ALPHA * h + mod[5] * y, ln_g[l, 1], ln_b[l, 1])
        h = macaron_ffn(h, mod[6], mod[7], mod[8], ffn_w_in[l, 1], ffn_w_out[l, 1], ln_g[l, 2], ln_b[l, 2])
        if not last:
            hc = layer_norm(DN_ALPHA * hc + modc[5] * yc, ln_g[l, 1], ln_b[l, 1])
            hc = macaron_ffn(hc, modc[6], modc[7], modc[8], ffn_w_in[l, 1], ffn_w_out[l, 1],
                             ln_g[l, 2], ln_b[l, 2])
    return h
```

```python
import numpy as np
from contextlib import ExitStack
import concourse.bass as bass
import concourse.mybir as mybir
from concourse.bass_utils import run_bass_kernel_spmd

F32 = mybir.dt.float32
BF16 = mybir.dt.bfloat16
AF = mybir.ActivationFunctionType
ALU = mybir.AluOpType
AX = mybir.AxisListType

D = 1024
DFF = 2816
SEQ = 16384
CTX = 256
NCORE = 8
TLAT = 4096
TCTX = 64
TT = TLAT + TCTX
DN_ALPHA = 4.0 ** 0.25
LN_EPS = 1e-5
RMS_EPS = 1e-6
NQK = 8576

EPOCH = 12000
NDMASEM = 8


class Prog:
    def __init__(self, nc, stack):
        self.nc = nc
        self.stack = stack
        self.eng = {"pe": nc.tensor, "act": nc.scalar, "dve": nc.vector,
                    "pool": nc.gpsimd, "sp": nc.sync}
        self.cnt = {e: 0 for e in self.eng}
        self.sems = {}
        self.lastw = {}
        self.readers = {}
        self.seen = {e: {} for e in self.eng}
        self.dcnt = {e: 0 for e in self.eng}
        self.dpend = {}

    def _sem(self, name):
        if name not in self.sems:
            self.sems[name] = self.stack.enter_context(self.nc.semaphore(name))
        return self.sems[name]

    def _wait(self, e, tok):
        name, val = tok
        if self.seen[e].get(name, 0) >= val:
            return
        self.eng[e].wait_ge(self._sem(name), val)
        self.seen[e][name] = val

    def _deps(self, reads, writes):
        deps = []
        for k in reads:
            if k in self.lastw:
                deps.append(self.lastw[k])
        for k in writes:
            if k in self.lastw:
                deps.append(self.lastw[k])
            deps.extend(self.readers.get(k, []))
        return deps

    def _commit(self, tok, reads, writes):
        for k in reads:
            self.readers.setdefault(k, []).append(tok)
        for k in writes:
            self.lastw[k] = tok
            self.readers[k] = []

    def op(self, e, fn, reads=(), writes=()):
        for tok in self._deps(reads, writes):
            self._wait(e, tok)
        ins = fn(self.eng[e])
        self.cnt[e] += 1
        ep, v = divmod(self.cnt[e] - 1, EPOCH)
        tok = ("c_%s_%d" % (e, ep), v + 1)
        ins.then_inc(self._sem(tok[0]), 1)
        self._commit(tok, reads, writes)
        return tok

    def dma(self, e, out, in_, reads=(), writes=(), **kw):
        for tok in self._deps(reads, writes):
            self._wait(e, tok)
        i = self.dcnt[e]
        self.dcnt[e] += 1
        name = "d_%s_%d" % (e, i % NDMASEM)
        prev = self.dpend.get(name)
        if prev is not None:
            self._wait(e, prev)
        ins = self.eng[e].dma_start(out=out, in_=in_, **kw)
        ins.then_inc(self._sem(name), 16)
        tok = (name, 16 * (i // NDMASEM + 1))
        self.dpend[name] = tok
        self._commit(tok, reads, writes)
        return tok

    def finish(self, e="sp"):
        for tok in set(self.lastw.values()):
            self._wait(e, tok)


def sb(nc, st, name, shape, dt):
    return st.enter_context(nc.sbuf_tensor(name, shape, dt))


def ps(nc, st, name, shape, dt=F32):
    return st.enter_context(nc.psum_tensor(name, shape, dt))


def load_w(P, nc, st, name, dram, K, N, eng="pool"):
    kc = K // 128
    t = sb(nc, st, name, [128, kc, N], BF16)
    v = dram.rearrange("(k p) n -> p k n", p=128)
    for k in range(kc):
        P.dma(eng, t[:, k, :], v[:, k, :], writes=[(name, k)])
    return t


def wkeys(name, n):
    return [(name, k) for k in range(n)]


def token_tiles(tsz):
    tiles = []
    t = 0
    while t < TLAT:
        n = min(tsz, TLAT - t)
        tiles.append((t, n, 0))
        t += n
    tiles.append((TLAT, TCTX, 1))
    return tiles


class Common:
    def __init__(self, nc, st, P, tsz):
        self.nc, self.st, self.P, self.tsz = nc, st, P, tsz
        self.ones = sb(nc, st, "ones", [128, 128], F32)
        P.op("pool", lambda e: e.memset(self.ones[:], 1.0), writes=["ones"])
        self.r = sb(nc, st, "r", [128, 8, tsz], F32)
        self.sq = sb(nc, st, "sq", [128, 8, tsz], F32)
        self.t1 = sb(nc, st, "t1", [128, 2, tsz], F32)
        self.t2 = sb(nc, st, "t2", [128, 2, tsz], F32)
        self.stt = sb(nc, st, "stt", [128, 4, tsz], F32)
        self.ps1 = ps(nc, st, "ps1", [128, 512])
        self.ps2 = ps(nc, st, "ps2", [128, 512])

    def layer_norm(self, tn, gam, bet, out, outkey):
        P, r, sq = self.P, self.r, self.sq
        for d in range(8):
            P.op("act", lambda e, d=d: e.activation(out=sq[:, d, :tn], in_=r[:, d, :tn], func=AF.Square),
                 reads=[("r", d)], writes=[("sq", d)])
        for d in range(8):
            P.op("pe", lambda e, d=d: e.matmul(self.ps1[:, :tn], self.ones[:], r[:, d, :tn], start=(d == 0), stop=(d == 7)),
                 reads=[("r", d), "ones"], writes=["ps1"])
        for d in range(8):
            P.op("pe", lambda e, d=d: e.matmul(self.ps2[:, :tn], self.ones[:], sq[:, d, :tn], start=(d == 0), stop=(d == 7)),
                 reads=[("sq", d), "ones"], writes=["ps2"])
        mean, msq, var, rstd = (self.stt[:, i, :tn] for i in range(4))
        P.op("dve", lambda e: e.tensor_single_scalar(mean, self.ps1[:, :tn], 1.0 / D, ALU.mult), reads=["ps1"], writes=["mean"])
        P.op("dve", lambda e: e.tensor_tensor(msq, mean, mean, ALU.mult), reads=["mean"], writes=["msq"])
        P.op("dve", lambda e: e.scalar_tensor_tensor(var, self.ps2[:, :tn], 1.0 / D, msq, ALU.mult, ALU.subtract),
             reads=["ps2", "msq"], writes=["var"])
        P.op("dve", lambda e: e.tensor_single_scalar(var, var, LN_EPS, ALU.add), reads=["var"], writes=["var"])
        P.op("act", lambda e: e.sqrt(out=msq, in_=var), reads=["var"], writes=["msq"])
        P.op("dve", lambda e: e.reciprocal(rstd, msq), reads=["msq"], writes=["rstd"])
        for d in range(8):
            b = d % 2
            P.op("dve", lambda e, d=d, b=b: e.tensor_tensor(self.t1[:, b, :tn], r[:, d, :tn], mean, ALU.subtract),
                 reads=[("r", d), "mean"], writes=[("t1", b)])
            P.op("pool", lambda e, d=d, b=b: e.tensor_tensor(self.t2[:, b, :tn], self.t1[:, b, :tn], rstd, ALU.mult),
                 reads=[("t1", b), "rstd"], writes=[("t2", b)])
            P.op("act", lambda e, d=d, b=b: e.activation(out=out[:, d, :tn], in_=self.t2[:, b, :tn], func=AF.Identity,
                                                         scale=gam[:, d:d + 1], bias=bet[:, d:d + 1]),
                 reads=[("t2", b), "tab"], writes=[(outkey, d)])


def emit_modulate(P, tn, src, srckey, dst, dstkey, sc1p, sh):
    for c in range(8):
        P.op("pool", lambda e, c=c: e.tensor_scalar(dst[:, c, :tn], src[:, c, :tn], sc1p[:, c:c + 1], sh[:, c:c + 1],
                                                    ALU.mult, ALU.add),
             reads=[(srckey, c), "tab"], writes=[(dstkey, c)])


NC0 = 2 * 9216 // NCORE


def build_k0():
    nc = bass.Bass("TRN2", target_bir_lowering=False)
    cT = nc.dram_tensor("cT", [D, 3], F32, kind="ExternalInput").ap()
    aw = nc.dram_tensor("aw", [D, NC0], F32, kind="ExternalInput").ap()
    ab = nc.dram_tensor("ab", [128, NC0 // 128], F32, kind="ExternalInput").ap()
    out = nc.dram_tensor("out", [NC0, 3], F32, kind="ExternalOutput").ap()
    nj = NC0 // 128
    with ExitStack() as st:
        P = Prog(nc, st)
        ct = sb(nc, st, "ct", [128, 8, 3], F32)
        stt = sb(nc, st, "st", [128, 8, 3], F32)
        abt = sb(nc, st, "abt", [128, nj], F32)
        ot = sb(nc, st, "ot", [128, nj, 3], F32)
        pp = ps(nc, st, "pp", [128, nj, 4])
        P.dma("sp", ct[:], cT.rearrange("(k p) n -> p k n", p=128), writes=["ct"])
        P.dma("sp", abt[:], ab, writes=["abt"])
        P.op("act", lambda e: e.activation(out=stt[:], in_=ct[:], func=AF.Silu), reads=["ct"], writes=["st"])
        awv = aw.rearrange("(k p) n -> p k n", p=128)
        npc = 3
        cw = NC0 // npc
        wt = [sb(nc, st, "wt%d" % i, [128, 8, cw], F32) for i in range(npc)]
        for i in range(npc):
            for k in range(8):
                P.dma("sp", wt[i][:, k, :], awv[:, k, i * cw:(i + 1) * cw], writes=[("wt", i, k)])
        for j in range(nj):
            i, jj = divmod(j * 128, cw)
            for k in range(8):
                P.op("pe", lambda e, i=i, jj=jj, k=k, j=j: e.matmul(pp[:, j, 0:3], wt[i][:, k, jj:jj + 128], stt[:, k, :],
                                                                  start=(k == 0), stop=(k == 7)),
                     reads=[("wt", i, k), "st"], writes=[("pp", j)])
            P.op("dve", lambda e, j=j: e.tensor_scalar(ot[:, j, :], pp[:, j, 0:3], abt[:, j:j + 1], None, ALU.add),
                 reads=[("pp", j), "abt"], writes=["ot"])
        P.dma("sp", out.rearrange("(j p) n -> p j n", p=128), ot[:], reads=["ot"], writes=["out"])
        P.finish("sp")
    return nc


def prep_tab(P, nc, st, tab_d, ncols, gmul):
    tab = sb(nc, st, "tabs", [128, ncols * 8], F32)
    P.dma("sp", tab[:], tab_d, writes=["tab0"])
    sc1p, sh, gt = [], [], []
    for ms in range(2):
        o = ms * 24
        P.op("dve", lambda e, o=o: e.tensor_single_scalar(tab[:, o + 8:o + 16], tab[:, o + 8:o + 16], 1.0, ALU.add),
             reads=["tab0"], writes=["tab"])
        P.op("dve", lambda e, o=o: e.tensor_single_scalar(tab[:, o + 16:o + 24], tab[:, o + 16:o + 24], gmul, ALU.mult),
             reads=["tab0"], writes=["tab"])
        sh.append(tab[:, o:o + 8])
        sc1p.append(tab[:, o + 8:o + 16])
        gt.append(tab[:, o + 16:o + 24])
    return tab, sh, sc1p, gt


def build_ffn(tsz=256):
    nc = bass.Bass("TRN2", target_bir_lowering=False)
    hin = nc.dram_tensor("hin", [D, TT], F32, kind="ExternalInput").ap()
    w1d = nc.dram_tensor("w1", [D, 2 * DFF], F32, kind="ExternalInput").ap()
    w2d = nc.dram_tensor("w2", [DFF, D], F32, kind="ExternalInput").ap()
    tabd = nc.dram_tensor("tab", [128, 64], F32, kind="ExternalInput").ap()
    hout = nc.dram_tensor("hout", [D, TT], F32, kind="ExternalOutput").ap()
    hv = hin.rearrange("(c p) t -> p c t", p=128)
    ov = hout.rearrange("(c p) t -> p c t", p=128)
    NJ = DFF // 128
    with ExitStack() as st:
        P = Prog(nc, st)
        tab, sh, sc1p, gt = prep_tab(P, nc, st, tabd, 8, 0.5)
        gam, bet = tab[:, 48:56], tab[:, 56:64]
        w1 = load_w(P, nc, st, "w1s", w1d, D, 2 * DFF)
        w2 = load_w(P, nc, st, "w2s", w2d, DFF, D)
        C = Common(nc, st, P, tsz)
        hT = sb(nc, st, "hT", [128, 8, tsz], F32)
        hA = sb(nc, st, "hA", [128, 8, tsz], F32)
        uT = sb(nc, st, "uT", [128, 8, tsz], BF16)
        gT = sb(nc, st, "gT", [128, NJ, tsz], BF16)
        sa = sb(nc, st, "sa", [128, 2, tsz], F32)
        psA = [ps(nc, st, "psA%d" % i, [128, 512]) for i in range(2)]
        psB = [ps(nc, st, "psB%d" % i, [128, 512]) for i in range(2)]
        psY = [ps(nc, st, "psY%d" % i, [128, 512]) for i in range(2)]
        for (t0, tn, ms) in token_tiles(tsz):
            for c in range(8):
                P.dma("sp", hT[:, c, :tn], hv[:, c, t0:t0 + tn], writes=[("hT", c)])
            emit_modulate(P, tn, hT, "hT", uT, "uT", sc1p[ms], sh[ms])
            for c in range(8):
                P.op("pool", lambda e, c=c: e.tensor_single_scalar(hA[:, c, :tn], hT[:, c, :tn], DN_ALPHA, ALU.mult),
                     reads=[("hT", c)], writes=[("hA", c)])
            for j in range(NJ):
                b = j % 2
                for k in range(8):
                    P.op("pe", lambda e, j=j, k=k, b=b: e.matmul(psA[b][:, :tn], w1[:, k, j * 128:(j + 1) * 128], uT[:, k, :tn],
                                                                start=(k == 0), stop=(k == 7)),
                         reads=[("w1s", k), ("uT", k)], writes=[("psA", b)])
                for k in range(8):
                    P.op("pe", lambda e, j=j, k=k, b=b: e.matmul(psB[b][:, :tn], w1[:, k, DFF + j * 128:DFF + (j + 1) * 128],
                                                                uT[:, k, :tn], start=(k == 0), stop=(k == 7)),
                         reads=[("w1s", k), ("uT", k)], writes=[("psB", b)])
                P.op("act", lambda e, b=b: e.activation(out=sa[:, b, :tn], in_=psA[b][:, :tn], func=AF.Silu),
                     reads=[("psA", b)], writes=[("sa", b)])
                P.op("dve", lambda e, j=j, b=b: e.tensor_tensor(gT[:, j, :tn], sa[:, b, :tn], psB[b][:, :tn], ALU.mult),
                     reads=[("sa", b), ("psB", b)], writes=[("gT", j)])
            for d in range(8):
                b = d % 2
                for j in range(NJ):
                    P.op("pe", lambda e, j=j, d=d, b=b: e.matmul(psY[b][:, :tn], w2[:, j, d * 128:(d + 1) * 128], gT[:, j, :tn],
                                                                start=(j == 0), stop=(j == NJ - 1)),
                         reads=[("w2s", j), ("gT", j)], writes=[("psY", b)])
                P.op("dve", lambda e, d=d, b=b: e.scalar_tensor_tensor(C.r[:, d, :tn], psY[b][:, :tn], gt[ms][:, d:d + 1],
                                                                      hA[:, d, :tn], ALU.mult, ALU.add),
                     reads=[("psY", b), ("hA", d), "tab"], writes=[("r", d)])
            C.layer_norm(tn, gam, bet, hT, "hT")
            for c in range(8):
                P.dma("sp", ov[:, c, t0:t0 + tn], hT[:, c, :tn], reads=[("hT", c)], writes=["hout"])
        P.finish("sp")
    return nc


def build_inproj(tsz=256):
    nc = bass.Bass("TRN2", target_bir_lowering=False)
    hin = nc.dram_tensor("hin", [D, TT], F32, kind="ExternalInput").ap()
    wd = nc.dram_tensor("w", [D, NQK], F32, kind="ExternalInput").ap()
    tabd = nc.dram_tensor("tab", [128, 48], F32, kind="ExternalInput").ap()
    pout = nc.dram_tensor("pout", [NQK, TT], F32, kind="ExternalOutput").ap()
    hv = hin.rearrange("(c p) t -> p c t", p=128)
    pv = pout.rearrange("(c p) t -> p c t", p=128)
    NO = NQK // 128
    G = 4
    with ExitStack() as st:
        P = Prog(nc, st)
        tab, sh, sc1p, gt = prep_tab(P, nc, st, tabd, 6, 1.0)
        w = load_w(P, nc, st, "ws", wd, D, NQK)
        hT = sb(nc, st, "hT", [128, 8, tsz], F32)
        uT = sb(nc, st, "uT", [128, 8, tsz], BF16)
        og = [sb(nc, st, "og%d" % i, [128, G, tsz], F32) for i in range(2)]
        pp = [ps(nc, st, "pp%d" % i, [128, 512]) for i in range(4)]
        gi = 0
        for (t0, tn, ms) in token_tiles(tsz):
            for c in range(8):
                P.dma("sp", hT[:, c, :tn], hv[:, c, t0:t0 + tn], writes=[("hT", c)])
            emit_modulate(P, tn, hT, "hT", uT, "uT", sc1p[ms], sh[ms])
            for o0 in range(0, NO, G):
                gn = min(G, NO - o0)
                ob = gi % 2
                gi += 1
                for g in range(gn):
                    o = o0 + g
                    b = o % 4
                    for k in range(8):
                        P.op("pe", lambda e, o=o, k=k, b=b: e.matmul(pp[b][:, :tn], w[:, k, o * 128:(o + 1) * 128], uT[:, k, :tn],
                                                                    start=(k == 0), stop=(k == 7)),
                             reads=[("ws", k), ("uT", k)], writes=[("pp", b)])
                    if o % 2 == 0:
                        P.op("act", lambda e, g=g, b=b, ob=ob: e.copy(out=og[ob][:, g, :tn], in_=pp[b][:, :tn]),
                             reads=[("pp", b)], writes=[("og", ob)])
                    else:
                        P.op("dve", lambda e, g=g, b=b, ob=ob: e.tensor_copy(og[ob][:, g, :tn], pp[b][:, :tn]),
                             reads=[("pp", b)], writes=[("og", ob)])
                P.dma("sp", pv[:, o0:o0 + gn, t0:t0 + tn], og[ob][:, :gn, :tn], reads=[("og", ob)], writes=["pout"])
        P.finish("sp")
    return nc


def build_merge(tsz=256):
    nc = bass.Bass("TRN2", target_bir_lowering=False)
    hin = nc.dram_tensor("hin", [D, TT], F32, kind="ExternalInput").ap()
    yin = nc.dram_tensor("yin", [1536, TT], F32, kind="ExternalInput").ap()
    gin = nc.dram_tensor("gin", [3072, TT], F32, kind="ExternalInput").ap()
    bwd = nc.dram_tensor("bw", [1536, D], F32, kind="ExternalInput").ap()
    owd = nc.dram_tensor("ow", [D, D], F32, kind="ExternalInput").ap()
    tabd = nc.dram_tensor("tab", [128, 64], F32, kind="ExternalInput").ap()
    hout = nc.dram_tensor("hout", [D, TT], F32, kind="ExternalOutput").ap()
    hv = hin.rearrange("(c p) t -> p c t", p=128)
    yv = yin.rearrange("(c p) t -> p c t", p=128)
    gv = gin.rearrange("(c p) t -> p c t", p=128)
    ov = hout.rearrange("(c p) t -> p c t", p=128)
    with ExitStack() as st:
        P = Prog(nc, st)
        tab, sh, sc1p, gt = prep_tab(P, nc, st, tabd, 8, 1.0)
        gam, bet = tab[:, 48:56], tab[:, 56:64]
        bw = load_w(P, nc, st, "bws", bwd, 1536, D)
        ow = load_w(P, nc, st, "ows", owd, D, D)
        C = Common(nc, st, P, tsz)
        hT = sb(nc, st, "hT", [128, 8, tsz], F32)
        hA = sb(nc, st, "hA", [128, 8, tsz], F32)
        yb = sb(nc, st, "yb", [128, 12, tsz], BF16)
        sg = sb(nc, st, "sg", [128, 24, tsz], F32)
        macc = sb(nc, st, "macc", [128, 2, tsz], F32)
        mtmp = sb(nc, st, "mtmp", [128, 2, tsz], F32)
        mT = sb(nc, st, "mT", [128, 8, tsz], BF16)
        psM = [ps(nc, st, "psM%d" % i, [128, 512]) for i in range(3)]
        psY = [ps(nc, st, "psY%d" % i, [128, 512]) for i in range(2)]
        for (t0, tn, ms) in token_tiles(tsz):
            for c in range(8):
                P.dma("sp", hT[:, c, :tn], hv[:, c, t0:t0 + tn], writes=[("hT", c)])
            for c in range(12):
                P.dma("pool", yb[:, c, :tn], yv[:, c, t0:t0 + tn], writes=[("yb", c)])
            for c in range(24):
                P.dma("sp", sg[:, c, :tn], gv[:, c, t0:t0 + tn], writes=[("sg", c)])
                P.op("act", lambda e, c=c: e.activation(out=sg[:, c, :tn], in_=sg[:, c, :tn], func=AF.Sigmoid),
                     reads=[("sg", c)], writes=[("sg", c)])
            for c in range(8):
                P.op("pool", lambda e, c=c: e.tensor_single_scalar(hA[:, c, :tn], hT[:, c, :tn], DN_ALPHA, ALU.mult),
                     reads=[("hT", c)], writes=[("hA", c)])
            for d in range(8):
                b = d % 2
                for n in range(3):
                    for kc in range(4):
                        P.op("pe", lambda e, n=n, kc=kc, d=d: e.matmul(psM[n][:, :tn], bw[:, n * 4 + kc, d * 128:(d + 1) * 128],
                                                                      yb[:, n * 4 + kc, :tn], start=(kc == 0), stop=(kc == 3)),
                             reads=[("bws", n * 4 + kc), ("yb", n * 4 + kc)], writes=[("psM", n)])
                P.op("dve", lambda e, d=d, b=b: e.tensor_tensor(macc[:, b, :tn], sg[:, d, :tn], psM[0][:, :tn], ALU.mult),
                     reads=[("sg", d), ("psM", 0)], writes=[("macc", b)])
                P.op("dve", lambda e, d=d, b=b: e.tensor_tensor(mtmp[:, 0, :tn], sg[:, 8 + d, :tn], psM[1][:, :tn], ALU.mult),
                     reads=[("sg", 8 + d), ("psM", 1)], writes=[("mtmp", 0)])
                P.op("dve", lambda e, d=d, b=b: e.tensor_tensor(mtmp[:, 1, :tn], sg[:, 16 + d, :tn], psM[2][:, :tn], ALU.mult),
                     reads=[("sg", 16 + d), ("psM", 2)], writes=[("mtmp", 1)])
                P.op("pool", lambda e, b=b: e.tensor_tensor(macc[:, b, :tn], macc[:, b, :tn], mtmp[:, 0, :tn], ALU.add),
                     reads=[("macc", b), ("mtmp", 0)], writes=[("macc", b)])
                P.op("pool", lambda e, d=d, b=b: e.tensor_tensor(mT[:, d, :tn], macc[:, b, :tn], mtmp[:, 1, :tn], ALU.add),
                     reads=[("macc", b), ("mtmp", 1)], writes=[("mT", d)])
            for d in range(8):
                b = d % 2
                for k in range(8):
                    P.op("pe", lambda e, k=k, d=d, b=b: e.matmul(psY[b][:, :tn], ow[:, k, d * 128:(d + 1) * 128], mT[:, k, :tn],
                                                                start=(k == 0), stop=(k == 7)),
                         reads=[("ows", k), ("mT", k)], writes=[("psY", b)])
                P.op("dve", lambda e, d=d, b=b: e.scalar_tensor_tensor(C.r[:, d, :tn], psY[b][:, :tn], gt[ms][:, d:d + 1],
                                                                      hA[:, d, :tn], ALU.mult, ALU.add),
                     reads=[("psY", b), ("hA", d), "tab"], writes=[("r", d)])
            C.layer_norm(tn, gam, bet, hT, "hT")
            for c in range(8):
                P.dma("sp", ov[:, c, t0:t0 + tn], hT[:, c, :tn], reads=[("hT", c)], writes=["hout"])
        P.finish("sp")
    return nc


_CACHE = {}


def _get(name, fn, *a):
    key = (name,) + a
    if key not in _CACHE:
        _CACHE[key] = fn(*a)
    return _CACHE[key]


def _run(nc, in_maps):
    res = run_bass_kernel_spmd(nc, in_maps, core_ids=list(range(NCORE)))
    return res.results


def _pc(v):
    return np.ascontiguousarray(v.reshape(-1, 128).T)


def core_bq(i):
    return i // 4, i % 4


def to_cores(lat, cx):
    outs = []
    for i in range(NCORE):
        b, q = core_bq(i)
        a = np.concatenate([lat[b, q * TLAT:(q + 1) * TLAT], cx[b, q * TCTX:(q + 1) * TCTX]], axis=0)
        outs.append(np.ascontiguousarray(a.T))
    return outs


def from_cores(outs):
    C = outs[0].shape[0]
    lat = np.empty((2, SEQ, C), np.float32)
    cx = np.empty((2, CTX, C), np.float32)
    for i in range(NCORE):
        b, q = core_bq(i)
        lat[b, q * TLAT:(q + 1) * TLAT] = outs[i][:, :TLAT].T
        cx[b, q * TCTX:(q + 1) * TCTX] = outs[i][:, TLAT:].T
    return lat, cx


def run_mods(c, c_ctx, ada_w, ada_b):
    cT = np.ascontiguousarray(np.concatenate([c, c_ctx[None]], 0).T)
    aw = np.concatenate([ada_w[0], ada_w[1]], axis=1)
    ab = np.concatenate([ada_b[0], ada_b[1]], axis=0)
    nc = _get("k0", build_k0)
    maps = []
    for i in range(NCORE):
        sl = slice(i * NC0, (i + 1) * NC0)
        maps.append({"cT": cT, "aw": np.ascontiguousarray(aw[:, sl]), "ab": _pc(ab[sl])})
    res = _run(nc, maps)
    allm = np.concatenate([r["out"] for r in res], axis=0)
    return allm.reshape(2, 9, D, 3)


def make_tab(mods_l, b, idx3, extra):
    cols = []
    for v in (b, 2):
        for m in idx3:
            cols.append(_pc(mods_l[m, :, v]) if m is not None else np.zeros((128, 8), np.float32))
    for e in extra:
        cols.append(_pc(e))
    return np.ascontiguousarray(np.concatenate(cols, axis=1).astype(np.float32))


def run_ffn(hc, mods_l, idx3, w1, w2, g, bta):
    nc = _get("ffn", build_ffn)
    maps = []
    for i in range(NCORE):
        b, q = core_bq(i)
        maps.append({"hin": hc[i], "w1": w1, "w2": w2, "tab": make_tab(mods_l, b, idx3, [g, bta])})
    return [r["hout"] for r in _run(nc, maps)]


NQ = TT
NKL = TLAT + 256
NKB = NKL // 128 + 2
NK = NKB * 128


def build_attn():
    nc = bass.Bass("TRN2", target_bir_lowering=False)
    qf = nc.dram_tensor("qf", [64, 8, NQ], F32, kind="ExternalInput").ap()
    qs = nc.dram_tensor("qs", [64, 8, NQ], F32, kind="ExternalInput").ap()
    cq = nc.dram_tensor("cq", [64, NQ], F32, kind="ExternalInput").ap()
    sq = nc.dram_tensor("sq", [64, NQ], F32, kind="ExternalInput").ap()
    kf = nc.dram_tensor("kf", [64, 2, NK], F32, kind="ExternalInput").ap()
    ks = nc.dram_tensor("ks", [64, 2, NK], F32, kind="ExternalInput").ap()
    ck = nc.dram_tensor("ck", [64, NK], F32, kind="ExternalInput").ap()
    sk = nc.dram_tensor("sk", [64, NK], F32, kind="ExternalInput").ap()
    vt = nc.dram_tensor("vt", [128, NKB, 128], F32, kind="ExternalInput").ap()
    mk = nc.dram_tensor("mk", [128, 4, 4, 128], F32, kind="ExternalInput").ap()
    sk8 = nc.dram_tensor("sink", [64, 8], F32, kind="ExternalInput").ap()
    yo = nc.dram_tensor("yo", [64, 8, NQ], F32, kind="ExternalOutput").ap()
    CH = 1152
    with ExitStack() as st:
        P = Prog(nc, st)
        kr = sb(nc, st, "kr", [64, 2, NK], BF16)
        vb = sb(nc, st, "vb", [128, NKB, 128], BF16)
        mb = sb(nc, st, "mb", [128, 4, 4, 128], BF16)
        ones = sb(nc, st, "ones", [128, 64], BF16)
        es = sb(nc, st, "es", [64, 8], F32)
        P.dma("pool", vb[:], vt, writes=["vb"])
        P.dma("pool", mb[:], mk, writes=["mb"])
        P.dma("sp", es[:], sk8, writes=["es"])
        P.op("act", lambda e: e.activation(out=es[:], in_=es[:], func=AF.Exp), reads=["es"], writes=["es"])
        P.op("pool", lambda e: e.memset(ones[:], 1.0), writes=["ones"])
        ta = sb(nc, st, "ta", [64, 2, CH], F32)
        tb = sb(nc, st, "tb", [64, 2, CH], F32)
        tc_ = sb(nc, st, "tc", [64, CH], F32)
        td = sb(nc, st, "td", [64, CH], F32)
        for c0 in range(0, NK, CH):
            P.dma("sp", ta[:], kf[:, :, c0:c0 + CH], writes=["ta"])
            P.dma("sp", tb[:], ks[:, :, c0:c0 + CH], writes=["tb"])
            P.dma("sp", tc_[:], ck[:, c0:c0 + CH], writes=["tc"])
            P.dma("sp", td[:], sk[:, c0:c0 + CH], writes=["td"])
            for h in range(2):
                P.op("dve", lambda e, h=h: e.tensor_tensor(ta[:, h, :], ta[:, h, :], tc_[:], ALU.mult), reads=["ta", "tc"], writes=["ta"])
                P.op("pool", lambda e, h=h: e.tensor_tensor(tb[:, h, :], tb[:, h, :], td[:], ALU.mult), reads=["tb", "td"], writes=["tb"])
                P.op("dve", lambda e, h=h, c0=c0: e.tensor_tensor(kr[:, h, c0:c0 + CH], ta[:, h, :], tb[:, h, :], ALU.add),
                     reads=["ta", "tb"], writes=["kr"])
        qa = sb(nc, st, "qa", [64, 8, 128], F32)
        qb = sb(nc, st, "qb", [64, 8, 128], F32)
        qc = sb(nc, st, "qc", [64, 128], F32)
        qd = sb(nc, st, "qd", [64, 128], F32)
        qr = sb(nc, st, "qr", [64, 8, 128], BF16)
        pT = [sb(nc, st, "pT%d" % i, [128, 4, 128], BF16) for i in range(5)]
        dn = sb(nc, st, "dn", [64, 4, 128], F32)
        ob = sb(nc, st, "ob", [64, 8, 128], F32)
        psS = [ps(nc, st, "psS%d" % i, [128, 512]) for i in range(3)]
        psN = [ps(nc, st, "psN%d" % i, [64, 512]) for i in range(2)]
        psD = [ps(nc, st, "psD%d" % i, [64, 512]) for i in range(2)]
        si = 0
        nblk = TLAT // 128
        for n in range(nblk + 1):
            t0 = n * 128
            nq = 128 if n < nblk else TCTX
            P.dma("sp", qa[:, :, :nq], qf[:, :, t0:t0 + nq], writes=["qa"])
            P.dma("sp", qb[:, :, :nq], qs[:, :, t0:t0 + nq], writes=["qb"])
            P.dma("sp", qc[:, :nq], cq[:, t0:t0 + nq], writes=["qc"])
            P.dma("sp", qd[:, :nq], sq[:, t0:t0 + nq], writes=["qd"])
            for h in range(8):
                P.op("dve", lambda e, h=h: e.tensor_tensor(qa[:, h, :nq], qa[:, h, :nq], qc[:, :nq], ALU.mult), reads=["qa", "qc"], writes=["qa"])
                P.op("pool", lambda e, h=h: e.tensor_tensor(qb[:, h, :nq], qb[:, h, :nq], qd[:, :nq], ALU.mult), reads=["qb", "qd"], writes=["qb"])
                P.op("dve", lambda e, h=h: e.tensor_tensor(qr[:, h, :nq], qa[:, h, :nq], qb[:, h, :nq], ALU.add),
                     reads=["qa", "qb"], writes=[("qr", h // 4)])
            if n < nblk:
                kbs = [(n, 0 if n == 0 else 1), (n + 1, None), (n + 2, 3 if n == nblk - 1 else 2), (NKB - 2, None), (NKB - 1, None)]
            else:
                kbs = [(NKB - 2, None), (NKB - 1, None)]
            for kvh in range(2):
                pb = kvh
                for i, (kb, mi) in enumerate(kbs):
                    sbk = si % 3
                    si += 1
                    for g in range(4):
                        P.op("pe", lambda e, kb=kb, g=g, sbk=sbk, kvh=kvh: e.matmul(psS[sbk][:, g * nq:(g + 1) * nq],
                                                                                   kr[:, kvh, kb * 128:(kb + 1) * 128],
                                                                                   qr[:, kvh * 4 + g, :nq], start=True, stop=True),
                             reads=["kr", ("qr", kvh)], writes=[("psS", sbk)])
                    for g in range(4):
                        P.op("act", lambda e, g=g, i=i, sbk=sbk: e.activation(out=pT[i][:, g, :nq], in_=psS[sbk][:, g * nq:(g + 1) * nq],
                                                                              func=AF.Exp, scale=0.125),
                             reads=[("psS", sbk)], writes=[("pT", i)])
                    if mi is not None:
                        P.op("pool", lambda e, i=i, mi=mi: e.tensor_tensor(pT[i][:, :, :nq], pT[i][:, :, :nq], mb[:, mi, :, :nq], ALU.mult),
                             reads=[("pT", i), "mb"], writes=[("pT", i)])
                for g in range(4):
                    for i, (kb, mi) in enumerate(kbs):
                        P.op("pe", lambda e, i=i, kb=kb, g=g, kvh=kvh, pb=pb: e.matmul(psN[pb][:, g * nq:(g + 1) * nq],
                                                                                      vb[:, kb, kvh * 64:(kvh + 1) * 64], pT[i][:, g, :nq],
                                                                                      start=(i == 0), stop=(i == len(kbs) - 1)),
                             reads=[("pT", i), "vb"], writes=[("psN", pb)])
                for g in range(4):
                    for i, (kb, mi) in enumerate(kbs):
                        P.op("pe", lambda e, i=i, g=g, pb=pb: e.matmul(psD[pb][:, g * nq:(g + 1) * nq], ones[:], pT[i][:, g, :nq],
                                                                      start=(i == 0), stop=(i == len(kbs) - 1)),
                             reads=[("pT", i), "ones"], writes=[("psD", pb)])
                for g in range(4):
                    h = kvh * 4 + g
                    P.op("dve", lambda e, g=g, h=h, pb=pb: e.tensor_scalar(dn[:, g, :nq], psD[pb][:, g * nq:(g + 1) * nq], es[:, h:h + 1], None, ALU.add),
                         reads=[("psD", pb), "es"], writes=["dn"])
                    P.op("dve", lambda e, g=g: e.reciprocal(dn[:, g, :nq], dn[:, g, :nq]), reads=["dn"], writes=["dn"])
                    P.op("dve", lambda e, g=g, h=h, pb=pb: e.tensor_tensor(ob[:, h, :nq], psN[pb][:, g * nq:(g + 1) * nq], dn[:, g, :nq], ALU.mult),
                         reads=[("psN", pb), "dn"], writes=["ob"])
            P.dma("sp", yo[:, :, t0:t0 + nq], ob[:, :, :nq], reads=["ob"], writes=["yo"])
        P.finish("sp")
    return nc


def rope_tables(pos):
    pos = np.asarray(pos)
    valid = pos >= 0
    p = np.where(valid, pos, 0)
    row = (p // 64).astype(np.float32)
    col = (p % 64).astype(np.float32)
    inv = (np.float32(10000.0) ** (-np.arange(16, dtype=np.float32) * np.float32(2.0) / np.float32(32))).astype(np.float32)
    cos = np.ones((64, len(pos)), np.float32)
    sin = np.zeros((64, len(pos)), np.float32)
    for a, base in ((0, row), (1, col)):
        ang = (base[None, :] * inv[:, None]).astype(np.float32)
        for s in range(2):
            sl = slice(a * 32 + s * 16, a * 32 + s * 16 + 16)
            cos[sl] = np.cos(ang)
            sin[sl] = np.sin(ang) * (-1.0 if s == 0 else 1.0)
    cos[:, ~valid] = 1.0
    sin[:, ~valid] = 0.0
    return cos, sin


OFF_B, OFF_Q, OFF_K, OFF_V, OFF_G = 2560, 4096, 4608, 4736, 4864
OFF_QS, OFF_KS = 7936, 8448


def swap_cols(w):
    sh = w.shape
    return w.reshape(sh[:-1] + (-1, 2, 2, 16))[..., ::-1, :].reshape(sh)


def run_attn(Pl, Pc, sink):
    nc = _get("attn", build_attn)
    qi = np.arange(128)
    m_prev = (qi[:, None] >= qi[None, :]).astype(np.float32)
    m_next = (qi[:, None] <= qi[None, :]).astype(np.float32)
    maps = []
    for i in range(NCORE):
        b, q = core_bq(i)
        t0 = q * TLAT
        lat = Pl[b, t0:t0 + TLAT]
        cx = Pc[b, q * TCTX:(q + 1) * TCTX]

        def heads(a, n):
            return np.ascontiguousarray(a.reshape(a.shape[0], n, 64).transpose(2, 1, 0))
        qall = np.concatenate([lat[:, OFF_Q:OFF_K], cx[:, OFF_Q:OFF_K]], 0)
        qsall = np.concatenate([lat[:, OFF_QS:OFF_KS], cx[:, OFF_QS:OFF_KS]], 0)
        qpos = np.concatenate([np.arange(t0, t0 + TLAT), -np.ones(TCTX, np.int64)])
        cqt, sqt = rope_tables(qpos)
        kpos = np.arange(t0 - 128, t0 + TLAT + 128)
        kval = (kpos >= 0) & (kpos < SEQ)
        kidx = np.clip(kpos, 0, SEQ - 1)
        kl = Pl[b, kidx] * kval[:, None]
        kall = np.concatenate([kl[:, OFF_K:OFF_V], Pc[b][:, OFF_K:OFF_V]], 0)
        ksall = np.concatenate([kl[:, OFF_KS:NQK], Pc[b][:, OFF_KS:NQK]], 0)
        vall = np.concatenate([kl[:, OFF_V:OFF_G], Pc[b][:, OFF_V:OFF_G]], 0)
        ckt, skt = rope_tables(np.concatenate([np.where(kval, kpos, -1), -np.ones(CTX, np.int64)]))
        m0 = m_prev if q > 0 else np.zeros_like(m_prev)
        m3 = m_next if q < 3 else np.zeros_like(m_next)
        mk = np.stack([m0, m_prev, m_next, m3], 0)
        mk = np.ascontiguousarray(np.broadcast_to(mk.transpose(1, 0, 2)[:, :, None, :], (128, 4, 4, 128))).astype(np.float32)
        maps.append({
            "qf": heads(qall, 8), "qs": heads(qsall, 8), "cq": cqt, "sq": sqt,
            "kf": heads(kall, 2), "ks": heads(ksall, 2), "ck": ckt, "sk": skt,
            "vt": np.ascontiguousarray(vall.reshape(NKB, 128, 128).transpose(1, 0, 2)),
            "mk": mk, "sink": np.ascontiguousarray(np.broadcast_to(sink[None, :], (64, 8))).astype(np.float32),
        })
    res = _run(nc, maps)
    yl = np.empty((2, SEQ, 512), np.float32)
    yc = np.empty((2, CTX, 512), np.float32)
    for i in range(NCORE):
        b, q = core_bq(i)
        y = res[i]["yo"].transpose(2, 1, 0).reshape(NQ, 512)
        yl[b, q * TLAT:(q + 1) * TLAT] = y[:TLAT]
        yc[b, q * TCTX:(q + 1) * TCTX] = y[TLAT:]
    return yl, yc


NTOK = CTX + SEQ
NG = NTOK // 128


def build_hgrn():
    nc = bass.Bass("TRN2", target_bir_lowering=False)
    zd = [nc.dram_tensor("z%d" % d, [128, NG, 128], F32, kind="ExternalInput").ap() for d in range(2)]
    vd = nc.dram_tensor("v", [128, NG, 128], F32, kind="ExternalInput").ap()
    qd = nc.dram_tensor("q", [128, NTOK], F32, kind="ExternalInput").ap()
    gd = nc.dram_tensor("g", [128, NTOK], F32, kind="ExternalInput").ap()
    lbd = nc.dram_tensor("lb", [128, 2, 2, 128], F32, kind="ExternalInput").ap()
    fld = nc.dram_tensor("flag", [128, 1], F32, kind="ExternalInput").ap()
    nwd = nc.dram_tensor("nw", [128, 1], F32, kind="ExternalInput").ap()
    cfd = nc.dram_tensor("cf", [128, 5, 128], F32, kind="ExternalInput").ap()
    cmd = nc.dram_tensor("cm", [128, 4], F32, kind="ExternalInput").ap()
    yo = nc.dram_tensor("yo", [128, NTOK], F32, kind="ExternalOutput").ap()
    with ExitStack() as st:
        P = Prog(nc, st)
        vb = sb(nc, st, "vb", [128, NG, 128], BF16)
        P.dma("pool", vb[:], vd, writes=["vb"])
        cf = sb(nc, st, "cfs", [128, 5, 128], F32)
        P.dma("sp", cf[:], cfd, writes=["cf"])
        bm = sb(nc, st, "bm", [128, 2, 128], BF16)
        P.dma("pool", bm[:], cfd[:, 0:2, :], writes=["bm"])
        cm = sb(nc, st, "cms", [128, 4], F32)
        P.dma("sp", cm[:], cmd, writes=["cm"])
        nw = sb(nc, st, "nws", [128, 1], F32)
        P.dma("sp", nw[:], nwd, writes=["nw"])
        fl = sb(nc, st, "fls", [128, 1], F32)
        P.dma("sp", fl[:], fld, writes=["fl"])
        lbt = sb(nc, st, "lbs", [128, 2, 2, 128], F32)
        P.dma("sp", lbt[:], lbd, writes=["lbt"])
        LBb = sb(nc, st, "LBb", [128, 2, 128], F32)
        LBa = sb(nc, st, "LBa", [128, 2, 128], F32)
        P.op("dve", lambda e: e.tensor_tensor(LBb[:], lbt[:, 1], lbt[:, 0], ALU.subtract), reads=["lbt"], writes=["LBb"])
        P.op("act", lambda e: e.activation(out=LBb[:], in_=LBb[:], func=AF.Sigmoid), reads=["LBb"], writes=["LBb"])
        P.op("dve", lambda e: e.tensor_scalar(LBb[:], LBb[:], fl[:, 0:1], None, ALU.mult), reads=["LBb", "fl"], writes=["LBb"])
        P.op("dve", lambda e: e.tensor_scalar(LBa[:], LBb[:], -1.0, 1.0, ALU.mult, ALU.add), reads=["LBb"], writes=["LBa"])
        ones = sb(nc, st, "ones", [128, 128], F32)
        P.op("pool", lambda e: e.memset(ones[:], 1.0), writes=["ones"])
        osum = sb(nc, st, "osum", [128, NTOK], F32)
        S = sb(nc, st, "S", [128, 128], F32)
        Sb = sb(nc, st, "Sb", [128, 128], BF16)
        zt = [sb(nc, st, "zt%d" % i, [128, 128], F32) for i in range(2)]
        qt32 = [sb(nc, st, "qt32%d" % i, [128, 128], F32) for i in range(2)]
        ft = [sb(nc, st, "ft%d" % i, [128, 128], F32) for i in range(2)]
        lf = [sb(nc, st, "lf%d" % i, [128, 128], F32) for i in range(2)]
        kk = [sb(nc, st, "kk%d" % i, [128, 128], F32) for i in range(2)]
        Eq = [sb(nc, st, "Eq%d" % i, [128, 128], F32) for i in range(2)]
        Ek = [sb(nc, st, "Ek%d" % i, [128, 128], F32) for i in range(2)]
        Er = [sb(nc, st, "Er%d" % i, [128, 128], F32) for i in range(2)]
        qt = [sb(nc, st, "qt%d" % i, [128, 128], BF16) for i in range(2)]
        kt = [sb(nc, st, "kt%d" % i, [128, 128], BF16) for i in range(2)]
        k4 = [sb(nc, st, "k4%d" % i, [128, 4, 128], BF16) for i in range(2)]
        am = [sb(nc, st, "am%d" % i, [128, 128], BF16) for i in range(2)]
        pB = ps(nc, st, "pB", [128, 512])
        pR = ps(nc, st, "pR", [128, 512])
        pK = ps(nc, st, "pK", [128, 512])
        pA = ps(nc, st, "pA", [128, 512])
        pO = [ps(nc, st, "pO%d" % i, [128, 512]) for i in range(2)]
        pU = [ps(nc, st, "pU%d" % i, [128, 512]) for i in range(2)]
        it = 0
        ui = 0
        for d in range(2):
            P.op("pool", lambda e: e.memset(S[:], 0.0), writes=["S"])
            P.op("pool", lambda e: e.memset(Sb[:], 0.0), writes=["Sb"])
            groups = list(range(NG)) if d == 0 else [1, 0] + list(range(NG - 1, 1, -1))
            chunks = [0, 1, 2, 3] if d == 0 else [3, 2, 1, 0]
            for G in groups:
                b = it % 2
                it += 1
                P.dma("sp", zt[b][:], zd[d][:, G, :], writes=[("zt", b)])
                P.dma("sp", qt32[b][:], qd[:, G * 128:(G + 1) * 128], writes=[("qt32", b)])
                P.op("act", lambda e, b=b: e.activation(out=zt[b][:], in_=zt[b][:], func=AF.Sigmoid), reads=[("zt", b)], writes=[("zt", b)])
                P.op("dve", lambda e, b=b, d=d: e.tensor_tensor(ft[b][:], zt[b][:], LBa[:, d, :], ALU.mult), reads=[("zt", b), "LBa"], writes=[("ft", b)])
                P.op("dve", lambda e, b=b, d=d: e.tensor_tensor(ft[b][:], ft[b][:], LBb[:, d, :], ALU.add), reads=[("ft", b), "LBb"], writes=[("ft", b)])
                P.op("act", lambda e, b=b: e.activation(out=lf[b][:], in_=ft[b][:], func=AF.Ln), reads=[("ft", b)], writes=[("lf", b)])
                P.op("pool", lambda e, b=b: e.tensor_scalar(kk[b][:], ft[b][:], -1.0, 1.0, ALU.mult, ALU.add), reads=[("ft", b)], writes=[("kk", b)])
                P.op("pe", lambda e, b=b, d=d: e.matmul(pB[:, :128], lf[b][:], cf[:, d, :], start=True, stop=True), reads=[("lf", b), "cf"], writes=["pB"])
                P.op("pe", lambda e, b=b, d=d: e.matmul(pR[:, :128], cf[:, 2 + d, :], lf[b][:], start=True, stop=True), reads=[("lf", b), "cf"], writes=["pR"])
                P.op("pe", lambda e, b=b: e.matmul(pK[:, :128], kk[b][:], cf[:, 4, :], start=True, stop=True), reads=[("kk", b), "cf"], writes=["pK"])
                P.op("act", lambda e, b=b: e.activation(out=Eq[b][:], in_=pB[:, :128], func=AF.Exp), reads=["pB"], writes=[("Eq", b)])
                P.op("act", lambda e, b=b: e.activation(out=Ek[b][:], in_=pB[:, :128], func=AF.Exp, scale=-1.0), reads=["pB"], writes=[("Ek", b)])
                P.op("act", lambda e, b=b: e.activation(out=Er[b][:], in_=pR[:, :128], func=AF.Exp), reads=["pR"], writes=[("Er", b)])
                P.op("dve", lambda e, b=b: e.tensor_tensor(qt[b][:], qt32[b][:], Eq[b][:], ALU.mult), reads=[("qt32", b), ("Eq", b)], writes=[("qt", b)])
                P.op("dve", lambda e, b=b: e.tensor_tensor(kt[b][:], pK[:, :128], Ek[b][:], ALU.mult), reads=["pK", ("Ek", b)], writes=[("kt", b)])
                for c in range(4):
                    P.op("dve", lambda e, b=b, c=c: e.scalar_tensor_tensor(k4[b][:, c, :], kk[b][:], cm[:, c:c + 1], Er[b][:], ALU.mult, ALU.mult),
                         reads=[("kk", b), ("Er", b), "cm"], writes=[("k4", b)])
                P.op("pe", lambda e, b=b: e.matmul(pA[:, :128], kt[b][:], qt[b][:], start=True, stop=True), reads=[("kt", b), ("qt", b)], writes=["pA"])
                P.op("dve", lambda e, b=b, d=d: e.tensor_tensor(am[b][:], pA[:, :128], bm[:, d, :], ALU.mult), reads=["pA", "bm"], writes=[("am", b)])
                P.op("pe", lambda e, b=b, G=G: e.matmul(pO[b][:, :128], vb[:, G, :], am[b][:], start=True, stop=False),
                     reads=["vb", ("am", b)], writes=[("pO", b)])
                for ci, c in enumerate(chunks):
                    P.op("pe", lambda e, b=b, c=c, ci=ci: e.matmul(pO[b][:, c * 32:(c + 1) * 32], Sb[:], qt[b][:, c * 32:(c + 1) * 32],
                                                                  start=False, stop=(ci == 3)),
                         reads=["Sb", ("qt", b)], writes=[("pO", b)])
                    u = ui % 2
                    ui += 1
                    P.op("pe", lambda e, b=b, c=c, u=u, G=G: e.matmul(pU[u][:, :128], k4[b][:, c, :], vb[:, G, :], start=True, stop=True),
                         reads=[("k4", b), "vb"], writes=[("pU", u)])
                    dcol = c * 32 + (31 if d == 0 else 0)
                    P.op("dve", lambda e, b=b, u=u, dcol=dcol: e.scalar_tensor_tensor(S[:], S[:], Eq[b][:, dcol:dcol + 1], pU[u][:, :128], ALU.mult, ALU.add),
                         reads=["S", ("Eq", b), ("pU", u)], writes=["S"])
                    P.op("act", lambda e: e.copy(out=Sb[:], in_=S[:]), reads=["S"], writes=["Sb"])
                if d == 0:
                    P.op("act", lambda e, b=b, G=G: e.copy(out=osum[:, G * 128:(G + 1) * 128], in_=pO[b][:, :128]), reads=[("pO", b)], writes=[("osum", G)])
                else:
                    P.op("dve", lambda e, b=b, G=G: e.tensor_tensor(osum[:, G * 128:(G + 1) * 128], osum[:, G * 128:(G + 1) * 128], pO[b][:, :128], ALU.add),
                         reads=[("pO", b), ("osum", G)], writes=[("osum", G)])
        sq = sb(nc, st, "sq", [128, 512], F32)
        gt = sb(nc, st, "gt", [128, 512], F32)
        rs = sb(nc, st, "rs", [128, 512], F32)
        yt = [sb(nc, st, "yt%d" % i, [128, 512], F32) for i in range(2)]
        ri = 0
        for t0 in range(0, NTOK, 512):
            tn = min(512, NTOK - t0)
            b = ri % 2
            ri += 1
            rk = [("osum", G) for G in range(t0 // 128, (t0 + tn) // 128)]
            P.dma("sp", gt[:, :tn], gd[:, t0:t0 + tn], writes=["gt"])
            P.op("act", lambda e, t0=t0, tn=tn: e.activation(out=sq[:, :tn], in_=osum[:, t0:t0 + tn], func=AF.Square), reads=rk, writes=["sq"])
            P.op("pe", lambda e, tn=tn: e.matmul(pB[:, :tn], ones[:], sq[:, :tn], start=True, stop=True), reads=["sq", "ones"], writes=["pB"])
            P.op("dve", lambda e, tn=tn: e.tensor_scalar(rs[:, :tn], pB[:, :tn], 1.0 / 128, RMS_EPS, ALU.mult, ALU.add), reads=["pB"], writes=["rs"])
            P.op("act", lambda e, tn=tn: e.sqrt(out=rs[:, :tn], in_=rs[:, :tn]), reads=["rs"], writes=["rs"])
            P.op("dve", lambda e, tn=tn: e.reciprocal(rs[:, :tn], rs[:, :tn]), reads=["rs"], writes=["rs"])
            P.op("act", lambda e, tn=tn: e.activation(out=gt[:, :tn], in_=gt[:, :tn], func=AF.Silu), reads=["gt"], writes=["gt"])
            P.op("dve", lambda e, b=b, t0=t0, tn=tn: e.tensor_tensor(yt[b][:, :tn], osum[:, t0:t0 + tn], rs[:, :tn], ALU.mult), reads=rk + ["rs"], writes=[("yt", b)])
            P.op("pool", lambda e, b=b, tn=tn: e.tensor_tensor(yt[b][:, :tn], yt[b][:, :tn], gt[:, :tn], ALU.mult), reads=[("yt", b), "gt"], writes=[("yt", b)])
            P.op("act", lambda e, b=b, tn=tn: e.activation(out=yt[b][:, :tn], in_=yt[b][:, :tn], func=AF.Copy, scale=nw[:, 0:1]), reads=[("yt", b), "nw"], writes=[("yt", b)])
            P.dma("sp", yo[:, t0:t0 + tn], yt[b][:, :tn], reads=[("yt", b)], writes=["yo"])
        P.finish("sp")
    return nc


def run_hgrn(Pl, Pc, layer, hgrn_lb, norm_w):
    nc = _get("hgrn", build_hgrn)
    p = np.arange(128)
    same = (p[:, None] // 32) == (p[None, :] // 32)
    incl_f = (same & (p[:, None] <= p[None, :])).astype(np.float32)
    incl_b = (same & (p[:, None] >= p[None, :])).astype(np.float32)
    rem_f = (same & (p[:, None] > p[None, :])).astype(np.float32)
    rem_b = (same & (p[:, None] < p[None, :])).astype(np.float32)
    cf = np.ascontiguousarray(np.stack([incl_f, incl_b, rem_f, rem_b, np.eye(128, dtype=np.float32)], 1))
    cm = (p[:, None] // 32 == np.arange(4)[None, :]).astype(np.float32)
    maps = []
    for i in range(NCORE):
        b, hd = core_bq(i)
        seq = np.concatenate([Pc[b], Pl[b]], 0)
        cs = slice(hd * 128, (hd + 1) * 128)

        def tm(a):
            return np.ascontiguousarray(a.reshape(NG, 128, 128).transpose(1, 0, 2))
        lb = np.broadcast_to(hgrn_lb[:, :, cs][None], (128, 2, 2, 128))
        maps.append({
            "q": np.ascontiguousarray(seq[:, 0:512][:, cs].T), "v": tm(seq[:, 512:1024][:, cs]),
            "g": np.ascontiguousarray(seq[:, 1024:1536][:, cs].T),
            "z0": tm(seq[:, 1536:2048][:, cs]), "z1": tm(seq[:, 2048:2560][:, cs]),
            "lb": np.ascontiguousarray(lb).astype(np.float32),
            "flag": np.full((128, 1), float(layer), np.float32),
            "nw": np.ascontiguousarray(norm_w[cs][:, None]).astype(np.float32), "cf": cf, "cm": cm,
        })
    res = _run(nc, maps)
    yl = np.empty((2, SEQ, 512), np.float32)
    yc = np.empty((2, CTX, 512), np.float32)
    for i in range(NCORE):
        b, hd = core_bq(i)
        y = res[i]["yo"].T
        yc[b, :, hd * 128:(hd + 1) * 128] = y[:CTX]
        yl[b, :, hd * 128:(hd + 1) * 128] = y[CTX:]
    return yl, yc


PI = float(np.pi)


def build_hyena(L):
    nc = bass.Bass("TRN2", target_bir_lowering=False)
    PP = min(L, 1024)
    pbd = [nc.dram_tensor("pb%d" % i, [3, 128, L], F32, kind="ExternalInput").ap() for i in range(3)]
    cwd = nc.dram_tensor("cw", [128, 12], F32, kind="ExternalInput").ap()
    bsd = nc.dram_tensor("bias", [128, 2], F32, kind="ExternalInput").ap()
    ftd = nc.dram_tensor("feats", [33, L], F32, kind="ExternalInput").ap()
    dcd = nc.dram_tensor("decay", [128, L], F32, kind="ExternalInput").ap()
    w1d = nc.dram_tensor("w1", [33, 64], F32, kind="ExternalInput").ap()
    w2d = nc.dram_tensor("w2", [64, 64], F32, kind="ExternalInput").ap()
    w3d = nc.dram_tensor("w3", [64, 4, 128], F32, kind="ExternalInput").ap()
    bfd = nc.dram_tensor("bf", [64, 4], F32, kind="ExternalInput").ap()
    yo = nc.dram_tensor("yo", [128, L], F32, kind="ExternalOutput").ap()
    with ExitStack() as st:
        P = Prog(nc, st)
        cw = sb(nc, st, "cws", [128, 12], F32)
        bs = sb(nc, st, "bss", [128, 2], F32)
        w1 = sb(nc, st, "w1s", [33, 64], F32)
        w2 = sb(nc, st, "w2s", [64, 64], F32)
        w3 = sb(nc, st, "w3s", [64, 4, 128], F32)
        bf = sb(nc, st, "bfs", [64, 4], F32)
        for t, dd in ((cw, cwd), (bs, bsd), (w1, w1d), (w2, w2d), (w3, w3d), (bf, bfd)):
            P.dma("sp", t[:], dd, writes=["consts"])
        z = sb(nc, st, "z", [128, L], F32)
        y = sb(nc, st, "y", [128, L], F32)
        ta = sb(nc, st, "ta", [128, PP], F32)
        tb = sb(nc, st, "tb", [128, PP], F32)
        tcx = sb(nc, st, "tcx", [128, PP], F32)
        xs = sb(nc, st, "xs", [128, PP], F32)
        ft = sb(nc, st, "ft", [33, PP], F32)
        h1 = sb(nc, st, "h1", [64, PP], F32)
        h2 = sb(nc, st, "h2", [64, PP], F32)
        dc = sb(nc, st, "dc", [128, PP], F32)
        hp = [sb(nc, st, "hp%d" % i, [128, PP], F32) for i in range(2)]
        l1 = sb(nc, st, "l1", [128, 4], F32)
        ki = sb(nc, st, "ki", [64, PP], mybir.dt.int32)
        kf = sb(nc, st, "kf", [64, PP], F32)
        pm = ps(nc, st, "pm", [64, 512])
        ph = [ps(nc, st, "ph%d" % i, [128, 512]) for i in range(2)]

        def sconv(part, dst, dkey, t0, tn):
            P.dma("sp", ta[:, :tn], pbd[0][part, :, t0:t0 + tn], writes=["ta"])
            P.dma("sp", tb[:, :tn], pbd[1][part, :, t0:t0 + tn], writes=["tb"])
            P.dma("sp", tcx[:, :tn], pbd[2][part, :, t0:t0 + tn], writes=["tcx"])
            c0 = part * 3
            P.op("dve", lambda e: e.tensor_scalar(ta[:, :tn], ta[:, :tn], cw[:, c0:c0 + 1], cw[:, 9 + part:10 + part], ALU.mult, ALU.add),
                 reads=["ta", "consts"], writes=["ta"])
            P.op("dve", lambda e: e.scalar_tensor_tensor(tb[:, :tn], tb[:, :tn], cw[:, c0 + 1:c0 + 2], ta[:, :tn], ALU.mult, ALU.add),
                 reads=["ta", "tb", "consts"], writes=["tb"])
            P.op("dve", lambda e: e.scalar_tensor_tensor(dst, tcx[:, :tn], cw[:, c0 + 2:c0 + 3], tb[:, :tn], ALU.mult, ALU.add),
                 reads=["tb", "tcx", "consts"], writes=[dkey])

        for t0 in range(0, L, PP):
            sconv(0, z[:, t0:t0 + PP], "z", t0, PP)
        for o in range(2):
            P.op("pool", lambda e: e.memset(y[:], 0.0), writes=["y"])
            P.op("pool", lambda e: e.memset(l1[:], 0.0), writes=["l1"])
            for p0 in range(0, L, PP):
                P.dma("sp", ft[:], ftd[:, p0:p0 + PP], writes=["ft"])
                P.dma("sp", dc[:], dcd[:, p0:p0 + PP], writes=["dc"])
                for (src, skey, w, K_, dst, dkey, bi) in ((ft, "ft", w1, 33, h1, "h1", 0), (h1, "h1", w2, 64, h2, "h2", 2)):
                    for q0 in range(0, PP, 512):
                        qn = min(512, PP - q0)
                        P.op("pe", lambda e, q0=q0, qn=qn, src=src, w=w, K_=K_: e.matmul(pm[:, :qn], w[:K_, :], src[:K_, q0:q0 + qn], start=True, stop=True),
                             reads=[skey, "consts"], writes=["pm"])
                        P.op("dve", lambda e, q0=q0, qn=qn, dst=dst, bi=bi: e.tensor_scalar(dst[:, q0:q0 + qn], pm[:, :qn], bf[:, bi:bi + 1], bf[:, bi + 1:bi + 2], ALU.add, ALU.mult),
                             reads=["pm", "consts"], writes=[dkey])
                    P.op("dve", lambda e, dst=dst: e.tensor_scalar(dst[:], dst[:], 1.0 / (2.0 * PI), 8.5, ALU.mult, ALU.add), reads=[dkey], writes=[dkey])
                    P.op("dve", lambda e, dst=dst: e.tensor_copy(ki[:], dst[:]), reads=[dkey], writes=["ki"])
                    P.op("dve", lambda e, dst=dst: e.tensor_copy(kf[:], ki[:]), reads=["ki"], writes=["kf"])
                    P.op("dve", lambda e, dst=dst: e.tensor_tensor(dst[:], dst[:], kf[:], ALU.subtract), reads=[dkey, "kf"], writes=[dkey])
                    P.op("dve", lambda e, dst=dst: e.tensor_single_scalar(kf[:], dst[:], 0.0, ALU.is_lt), reads=[dkey], writes=["kf"])
                    P.op("dve", lambda e, dst=dst: e.tensor_tensor(dst[:], dst[:], kf[:], ALU.add), reads=[dkey, "kf"], writes=[dkey])
                    P.op("dve", lambda e, dst=dst: e.tensor_scalar(dst[:], dst[:], 2.0 * PI, -PI, ALU.mult, ALU.add), reads=[dkey], writes=[dkey])
                    P.op("act", lambda e, dst=dst: e.activation(out=dst[:], in_=dst[:], func=AF.Sin), reads=[dkey], writes=[dkey])
                for dr in range(2):
                    for q0 in range(0, PP, 512):
                        qn = min(512, PP - q0)
                        b = (q0 // 512) % 2
                        P.op("pe", lambda e, q0=q0, qn=qn, dr=dr, b=b: e.matmul(ph[b][:, :qn], w3[:, dr * 2 + o, :], h2[:, q0:q0 + qn], start=True, stop=True),
                             reads=["h2", "consts"], writes=[("ph", b)])
                        P.op("dve", lambda e, q0=q0, qn=qn, dr=dr, b=b: e.tensor_tensor(hp[dr][:, q0:q0 + qn], ph[b][:, :qn], dc[:, q0:q0 + qn], ALU.mult),
                             reads=[("ph", b), "dc"], writes=[("hp", dr)])
                    if dr == 1 and p0 == 0:
                        P.op("dve", lambda e: e.memset(hp[1][:, 0:1], 0.0), reads=[("hp", 1)], writes=[("hp", 1)])
                    P.op("dve", lambda e, dr=dr: e.tensor_reduce(out=l1[:, 2 + dr:3 + dr], in_=hp[dr][:], axis=AX.X, op=ALU.add, apply_absolute_value=True),
                         reads=[("hp", dr)], writes=["l1p"])
                    P.op("dve", lambda e, dr=dr: e.tensor_tensor(l1[:, 0:1], l1[:, 0:1], l1[:, 2 + dr:3 + dr], ALU.add), reads=["l1p", "l1"], writes=["l1"])
                for j in range(PP):
                    d = p0 + j
                    P.op("dve", lambda e, j=j, d=d: e.scalar_tensor_tensor(y[:, d:L], z[:, 0:L - d], hp[0][:, j:j + 1], y[:, d:L], ALU.mult, ALU.add),
                         reads=[("hp", 0), "z", "y"], writes=["y"])
                    if d >= 1:
                        P.op("dve", lambda e, j=j, d=d: e.scalar_tensor_tensor(y[:, 0:L - d], z[:, d:L], hp[1][:, j:j + 1], y[:, 0:L - d], ALU.mult, ALU.add),
                             reads=[("hp", 1), "z", "y"], writes=["y"])
            P.op("dve", lambda e: e.reciprocal(l1[:, 1:2], l1[:, 0:1]), reads=["l1"], writes=["l1"])
            for t0 in range(0, L, PP):
                sconv(1 + o, xs[:], "xs", t0, PP)
                P.op("dve", lambda e, t0=t0: e.tensor_scalar(y[:, t0:t0 + PP], y[:, t0:t0 + PP], l1[:, 1:2], None, ALU.mult), reads=["y", "l1"], writes=["y"])
                P.op("dve", lambda e, t0=t0: e.scalar_tensor_tensor(y[:, t0:t0 + PP], z[:, t0:t0 + PP], bs[:, o:o + 1], y[:, t0:t0 + PP], ALU.mult, ALU.add),
                     reads=["y", "z", "consts"], writes=["y"])
                P.op("dve", lambda e, t0=t0: e.tensor_tensor(z[:, t0:t0 + PP], y[:, t0:t0 + PP], xs[:], ALU.mult), reads=["y", "xs", "z"], writes=["z"])
        for t0 in range(0, L, PP):
            P.dma("sp", yo[:, t0:t0 + PP], z[:, t0:t0 + PP], reads=["z"], writes=["yo"])
        P.finish("sp")
    return nc


def run_hyena(X, L, cw, cb, w1, b1, f1, w2, b2, f2, w3, bias):
    nc = _get("hyena", build_hyena, L)
    pos = np.arange(L, dtype=np.float32)
    t = pos / np.float32(L - 1)
    wv = np.float32(2.0 * np.pi) * pos / np.float32(L)
    bands = np.linspace(1e-4, 15, 16, dtype=np.float32)
    feats = np.concatenate([t[:, None], np.cos(wv[:, None] * bands), -np.sin(wv[:, None] * bands)], -1).astype(np.float32)
    rates = np.abs(np.linspace(np.log(1e-2) / 1.5, np.log(1e-2) / 0.3, 512, dtype=np.float32))
    decay = np.exp(-t[None, :] * rates[:, None]).astype(np.float32)
    Xp = np.pad(X, ((0, 0), (1, 1), (0, 0)))
    maps = []
    for i in range(NCORE):
        cs = np.arange(i * 64, (i + 1) * 64)

        def rows(a):
            return np.ascontiguousarray(np.stack([a[:, :, p * 512 + cs].transpose(0, 2, 1).reshape(128, L) for p in range(3)], 0))
        cwt = np.concatenate([np.stack([cw[tp, p * 512 + cs] for p in range(3) for tp in range(3)], 1),
                              np.stack([cb[p * 512 + cs] for p in range(3)], 1)], 1)
        w3r = w3.reshape(64, 2, 2, 512)[:, :, :, cs]
        w3r = np.concatenate([w3r, w3r], -1).reshape(64, 4, 128)
        maps.append({
            "pb0": rows(Xp[:, 0:L]), "pb1": rows(Xp[:, 1:L + 1]), "pb2": rows(Xp[:, 2:L + 2]),
            "cw": np.ascontiguousarray(np.concatenate([cwt, cwt], 0)).astype(np.float32),
            "bias": np.ascontiguousarray(np.concatenate([bias[:, cs].T, bias[:, cs].T], 0)).astype(np.float32),
            "feats": np.ascontiguousarray(feats.T), "decay": np.ascontiguousarray(np.concatenate([decay[cs], decay[cs]], 0)),
            "w1": np.ascontiguousarray(w1), "w2": np.ascontiguousarray(w2), "w3": np.ascontiguousarray(w3r).astype(np.float32),
            "bf": np.ascontiguousarray(np.stack([b1, f1, b2, f2], 1)).astype(np.float32),
        })
    res = _run(nc, maps)
    out = np.empty((2, L, 512), np.float32)
    for i in range(NCORE):
        out[:, :, i * 64:(i + 1) * 64] = res[i]["yo"].reshape(2, 64, L).transpose(0, 2, 1)
    return out


def run_inproj(hc, mods_l, w_aug):
    nc = _get("inproj", build_inproj)
    maps = []
    for i in range(NCORE):
        b, q = core_bq(i)
        maps.append({"hin": hc[i], "w": w_aug, "tab": make_tab(mods_l, b, (3, 4, None), [])})
    return [r["pout"] for r in _run(nc, maps)]


def run_merge(hc, yin, gin, mods_l, bw, ow, g, bta):
    nc = _get("merge", build_merge)
    maps = []
    for i in range(NCORE):
        b, q = core_bq(i)
        maps.append({"hin": hc[i], "yin": yin[i], "gin": gin[i], "bw": bw, "ow": ow,
                     "tab": make_tab(mods_l, b, (None, None, 5), [g, bta])})
    return [r["hout"] for r in _run(nc, maps)]


def kernel(x, c, ctx, c_ctx, ada_w, ada_b, ln_g, ln_b, ffn_w_in, ffn_w_out, mix_w_in, hgrn_lb, hgrn_norm_w,
           hyena_conv_w, hyena_conv_b, hyena_w1, hyena_b1, hyena_f1, hyena_w2, hyena_b2, hyena_f2, hyena_w3,
           hyena_bias, attn_sink, branch_w, out_w):
    f = lambda a: np.ascontiguousarray(np.asarray(a, dtype=np.float32))
    (x, c, ctx, c_ctx, ada_w, ada_b, ln_g, ln_b, ffn_w_in, ffn_w_out, mix_w_in, hgrn_lb, hgrn_norm_w, hyena_conv_w,
     hyena_conv_b, hyena_w1, hyena_b1, hyena_f1, hyena_w2, hyena_b2, hyena_f2, hyena_w3, hyena_bias, attn_sink,
     branch_w, out_w) = map(f, (x, c, ctx, c_ctx, ada_w, ada_b, ln_g, ln_b, ffn_w_in, ffn_w_out, mix_w_in, hgrn_lb,
                                hgrn_norm_w, hyena_conv_w, hyena_conv_b, hyena_w1, hyena_b1, hyena_f1, hyena_w2,
                                hyena_b2, hyena_f2, hyena_w3, hyena_bias, attn_sink, branch_w, out_w))
    mods = run_mods(c, c_ctx, ada_w, ada_b)
    hl, hx = x, ctx
    for l in range(2):
        hc = to_cores(hl, hx)
        h1c = run_ffn(hc, mods[l], (0, 1, 2), ffn_w_in[l, 0], ffn_w_out[l, 0], ln_g[l, 0], ln_b[l, 0])
        w = mix_w_in[l]
        w_aug = np.ascontiguousarray(np.concatenate([w, swap_cols(w[:, OFF_Q:OFF_K]), swap_cols(w[:, OFF_K:OFF_V])], 1))
        Pl, Pc = from_cores(run_inproj(h1c, mods[l], w_aug))
        ya, yca = run_hgrn(Pl, Pc, l, hgrn_lb, hgrn_norm_w[l])
        hy = (hyena_conv_w[l], hyena_conv_b[l], hyena_w1[l], hyena_b1[l], hyena_f1[l], hyena_w2[l], hyena_b2[l],
              hyena_f2[l], hyena_w3[l], hyena_bias[l])
        yb = run_hyena(Pl[..., OFF_B:OFF_Q], SEQ, *hy)
        ycb = run_hyena(Pc[..., OFF_B:OFF_Q], CTX, *hy) if l == 0 else np.zeros((2, CTX, 512), np.float32)
        yc, ycc = run_attn(Pl, Pc, attn_sink[l])
        yin = to_cores(np.concatenate([ya, yb, yc], -1), np.concatenate([yca, ycb, ycc], -1))
        gin = to_cores(Pl[..., OFF_G:OFF_QS], Pc[..., OFF_G:OFF_QS])
        h2c = run_merge(h1c, yin, gin, mods[l], np.ascontiguousarray(branch_w[l].reshape(1536, D)), out_w[l], ln_g[l, 1], ln_b[l, 1])
        h3c = run_ffn(h2c, mods[l], (6, 7, 8), ffn_w_in[l, 1], ffn_w_out[l, 1], ln_g[l, 2], ln_b[l, 2])
        hl, hx = from_cores(h3c)
    return hl.astype(np.float32)
```

```python
import numpy as np
from contextlib import ExitStack
import concourse.bass as bass
import concourse.mybir as mybir
from concourse.bass_utils import run_bass_kernel_spmd

F32 = mybir.dt.float32
BF16 = mybir.dt.bfloat16
AF = mybir.ActivationFunctionType
ALU = mybir.AluOpType
AX = mybir.AxisListType

D = 1024
DFF = 2816
SEQ = 16384
CTX = 256
NCORE = 8
TLAT = 4096
TCTX = 64
TT = TLAT + TCTX
DN_ALPHA = 4.0 ** 0.25
LN_EPS = 1e-5
RMS_EPS = 1e-6
NQK = 8576

EPOCH = 12000
NDMASEM = 8


class Prog:
    def __init__(self, nc, stack):
        self.nc = nc
        self.stack = stack
        self.eng = {"pe": nc.tensor, "act": nc.scalar, "dve": nc.vector,
                    "pool": nc.gpsimd, "sp": nc.sync}
        self.cnt = {e: 0 for e in self.eng}
        self.sems = {}
        self.lastw = {}
        self.readers = {}
        self.seen = {e: {} for e in self.eng}
        self.dcnt = {e: 0 for e in self.eng}
        self.dpend = {}
        self.lastw_dma = {}

    def _sem(self, name):
        if name not in self.sems:
            self.sems[name] = self.stack.enter_context(self.nc.semaphore(name))
        return self.sems[name]

    def _wait(self, e, tok):
        name, val = tok
        if self.seen[e].get(name, 0) >= val:
            return
        self.eng[e].wait_ge(self._sem(name), val)
        self.seen[e][name] = val

    def _deps(self, reads, writes):
        deps = []
        for k in list(reads) + list(writes):
            if k in self.lastw:
                deps.append(self.lastw[k])
            if k in self.lastw_dma:
                deps.extend(self.lastw_dma[k].values())
        for k in writes:
            deps.extend(self.readers.get(k, []))
        return deps

    def _commit(self, tok, reads, writes, is_dma=False):
        for k in reads:
            self.readers.setdefault(k, []).append(tok)
        for k in writes:
            if is_dma and k in self.lastw_dma:
                self.lastw_dma[k][tok[0]] = tok
            elif is_dma:
                self.lastw_dma[k] = {tok[0]: tok}
            else:
                self.lastw_dma.pop(k, None)
            self.lastw[k] = tok
            self.readers[k] = []

    def op(self, e, fn, reads=(), writes=()):
        for tok in self._deps(reads, writes):
            self._wait(e, tok)
        ins = fn(self.eng[e])
        self.cnt[e] += 1
        ep, v = divmod(self.cnt[e] - 1, EPOCH)
        tok = ("c_%s_%d" % (e, ep), v + 1)
        ins.then_inc(self._sem(tok[0]), 1)
        self._commit(tok, reads, writes)
        return tok

    def dma(self, e, out, in_, reads=(), writes=(), **kw):
        for tok in self._deps(reads, writes):
            self._wait(e, tok)
        i = self.dcnt[e]
        self.dcnt[e] += 1
        name = "d_%s_%d" % (e, i % NDMASEM)
        prev = self.dpend.get(name)
        if prev is not None:
            self._wait(e, prev)
        ins = self.eng[e].dma_start(out=out, in_=in_, **kw)
        ins.then_inc(self._sem(name), 16)
        tok = (name, 16 * (i // NDMASEM + 1))
        self.dpend[name] = tok
        self._commit(tok, reads, writes, is_dma=True)
        return tok

    def finish(self, e="sp"):
        for tok in set(self.lastw.values()):
            self._wait(e, tok)
        for tok in self.dpend.values():
            self._wait(e, tok)


def sb(nc, st, name, shape, dt):
    return st.enter_context(nc.sbuf_tensor(name, shape, dt))


def ps(nc, st, name, shape, dt=F32):
    return st.enter_context(nc.psum_tensor(name, shape, dt))


def load_w(P, nc, st, name, dram, K, N, eng="pool"):
    kc = K // 128
    t = sb(nc, st, name, [128, kc, N], BF16)
    v = dram.rearrange("(k p) n -> p k n", p=128)
    for k in range(kc):
        P.dma(eng, t[:, k, :], v[:, k, :], writes=[(name, k)])
    return t


def wkeys(name, n):
    return [(name, k) for k in range(n)]


def token_tiles(tsz):
    tiles = []
    t = 0
    while t < TLAT:
        n = min(tsz, TLAT - t)
        tiles.append((t, n, 0))
        t += n
    tiles.append((TLAT, TCTX, 1))
    return tiles


class Common:
    def __init__(self, nc, st, P, tsz):
        self.nc, self.st, self.P, self.tsz = nc, st, P, tsz
        self.ones = sb(nc, st, "ones", [128, 128], F32)
        P.op("pool", lambda e: e.memset(self.ones[:], 1.0), writes=["ones"])
        self.r = sb(nc, st, "r", [128, 8, tsz], F32)
        self.sq = sb(nc, st, "sq", [128, 8, tsz], F32)
        self.t1 = sb(nc, st, "t1", [128, 2, tsz], F32)
        self.t2 = sb(nc, st, "t2", [128, 2, tsz], F32)
        self.stt = sb(nc, st, "stt", [128, 4, tsz], F32)
        self.ps1 = ps(nc, st, "ps1", [128, 512])
        self.ps2 = ps(nc, st, "ps2", [128, 512])

    def layer_norm(self, tn, gam, bet, out, outkey):
        P, r, sq = self.P, self.r, self.sq
        for d in range(8):
            P.op("act", lambda e, d=d: e.activation(out=sq[:, d, :tn], in_=r[:, d, :tn], func=AF.Square),
                 reads=[("r", d)], writes=[("sq", d)])
        for d in range(8):
            P.op("pe", lambda e, d=d: e.matmul(self.ps1[:, :tn], self.ones[:], r[:, d, :tn], start=(d == 0), stop=(d == 7)),
                 reads=[("r", d), "ones"], writes=["ps1"])
        for d in range(8):
            P.op("pe", lambda e, d=d: e.matmul(self.ps2[:, :tn], self.ones[:], sq[:, d, :tn], start=(d == 0), stop=(d == 7)),
                 reads=[("sq", d), "ones"], writes=["ps2"])
        mean, msq, var, rstd = (self.stt[:, i, :tn] for i in range(4))
        P.op("dve", lambda e: e.tensor_single_scalar(mean, self.ps1[:, :tn], 1.0 / D, ALU.mult), reads=["ps1"], writes=["mean"])
        P.op("dve", lambda e: e.tensor_tensor(msq, mean, mean, ALU.mult), reads=["mean"], writes=["msq"])
        P.op("dve", lambda e: e.scalar_tensor_tensor(var, self.ps2[:, :tn], 1.0 / D, msq, ALU.mult, ALU.subtract),
             reads=["ps2", "msq"], writes=["var"])
        P.op("dve", lambda e: e.tensor_single_scalar(var, var, LN_EPS, ALU.add), reads=["var"], writes=["var"])
        P.op("act", lambda e: e.sqrt(out=msq, in_=var), reads=["var"], writes=["msq"])
        P.op("dve", lambda e: e.reciprocal(rstd, msq), reads=["msq"], writes=["rstd"])
        for d in range(8):
            b = d % 2
            P.op("dve", lambda e, d=d, b=b: e.tensor_tensor(self.t1[:, b, :tn], r[:, d, :tn], mean, ALU.subtract),
                 reads=[("r", d), "mean"], writes=[("t1", b)])
            P.op("pool", lambda e, d=d, b=b: e.tensor_tensor(self.t2[:, b, :tn], self.t1[:, b, :tn], rstd, ALU.mult),
                 reads=[("t1", b), "rstd"], writes=[("t2", b)])
            P.op("act", lambda e, d=d, b=b: e.activation(out=out[:, d, :tn], in_=self.t2[:, b, :tn], func=AF.Identity,
                                                         scale=gam[:, d:d + 1], bias=bet[:, d:d + 1]),
                 reads=[("t2", b), "tab"], writes=[(outkey, d)])


def emit_modulate(P, tn, src, srckey, dst, dstkey, sc1p, sh):
    for c in range(8):
        P.op("pool", lambda e, c=c: e.tensor_scalar(dst[:, c, :tn], src[:, c, :tn], sc1p[:, c:c + 1], sh[:, c:c + 1],
                                                    ALU.mult, ALU.add),
             reads=[(srckey, c), "tab"], writes=[(dstkey, c)])


NC0 = 2 * 9216 // NCORE


def build_k0():
    nc = bass.Bass("TRN2", target_bir_lowering=False)
    cT = nc.dram_tensor("cT", [D, 3], F32, kind="ExternalInput").ap()
    aw = nc.dram_tensor("aw", [D, NC0], F32, kind="ExternalInput").ap()
    ab = nc.dram_tensor("ab", [128, NC0 // 128], F32, kind="ExternalInput").ap()
    out = nc.dram_tensor("out", [NC0, 3], F32, kind="ExternalOutput").ap()
    nj = NC0 // 128
    with ExitStack() as st:
        P = Prog(nc, st)
        ct = sb(nc, st, "ct", [128, 8, 3], F32)
        stt = sb(nc, st, "st", [128, 8, 3], F32)
        abt = sb(nc, st, "abt", [128, nj], F32)
        ot = sb(nc, st, "ot", [128, nj, 3], F32)
        pp = ps(nc, st, "pp", [128, nj, 4])
        P.dma("sp", ct[:], cT.rearrange("(k p) n -> p k n", p=128), writes=["ct"])
        P.dma("sp", abt[:], ab, writes=["abt"])
        P.op("act", lambda e: e.activation(out=stt[:], in_=ct[:], func=AF.Silu), reads=["ct"], writes=["st"])
        awv = aw.rearrange("(k p) n -> p k n", p=128)
        npc = 3
        cw = NC0 // npc
        wt = [sb(nc, st, "wt%d" % i, [128, 8, cw], F32) for i in range(npc)]
        for i in range(npc):
            for k in range(8):
                P.dma("sp", wt[i][:, k, :], awv[:, k, i * cw:(i + 1) * cw], writes=[("wt", i, k)])
        for j in range(nj):
            i, jj = divmod(j * 128, cw)
            for k in range(8):
                P.op("pe", lambda e, i=i, jj=jj, k=k, j=j: e.matmul(pp[:, j, 0:3], wt[i][:, k, jj:jj + 128], stt[:, k, :],
                                                                  start=(k == 0), stop=(k == 7)),
                     reads=[("wt", i, k), "st"], writes=[("pp", j)])
            P.op("dve", lambda e, j=j: e.tensor_scalar(ot[:, j, :], pp[:, j, 0:3], abt[:, j:j + 1], None, ALU.add),
                 reads=[("pp", j), "abt"], writes=["ot"])
        P.dma("sp", out.rearrange("(j p) n -> p j n", p=128), ot[:], reads=["ot"], writes=["out"])
        P.finish("sp")
    return nc


def prep_tab(P, nc, st, tab_d, ncols, gmul):
    tab = sb(nc, st, "tabs", [128, ncols * 8], F32)
    P.dma("sp", tab[:], tab_d, writes=["tab0"])
    sc1p, sh, gt = [], [], []
    for ms in range(2):
        o = ms * 24
        P.op("dve", lambda e, o=o: e.tensor_single_scalar(tab[:, o + 8:o + 16], tab[:, o + 8:o + 16], 1.0, ALU.add),
             reads=["tab0"], writes=["tab"])
        P.op("dve", lambda e, o=o: e.tensor_single_scalar(tab[:, o + 16:o + 24], tab[:, o + 16:o + 24], gmul, ALU.mult),
             reads=["tab0"], writes=["tab"])
        sh.append(tab[:, o:o + 8])
        sc1p.append(tab[:, o + 8:o + 16])
        gt.append(tab[:, o + 16:o + 24])
    return tab, sh, sc1p, gt


def build_ffn(tsz=256):
    nc = bass.Bass("TRN2", target_bir_lowering=False)
    hin = nc.dram_tensor("hin", [D, TT], F32, kind="ExternalInput").ap()
    w1d = nc.dram_tensor("w1", [D, 2 * DFF], F32, kind="ExternalInput").ap()
    w2d = nc.dram_tensor("w2", [DFF, D], F32, kind="ExternalInput").ap()
    tabd = nc.dram_tensor("tab", [128, 64], F32, kind="ExternalInput").ap()
    hout = nc.dram_tensor("hout", [D, TT], F32, kind="ExternalOutput").ap()
    hv = hin.rearrange("(c p) t -> p c t", p=128)
    ov = hout.rearrange("(c p) t -> p c t", p=128)
    NJ = DFF // 128
    with ExitStack() as st:
        P = Prog(nc, st)
        tab, sh, sc1p, gt = prep_tab(P, nc, st, tabd, 8, 0.5)
        gam, bet = tab[:, 48:56], tab[:, 56:64]
        w1 = load_w(P, nc, st, "w1s", w1d, D, 2 * DFF)
        w2 = load_w(P, nc, st, "w2s", w2d, DFF, D)
        C = Common(nc, st, P, tsz)
        hT = sb(nc, st, "hT", [128, 8, tsz], F32)
        hA = sb(nc, st, "hA", [128, 8, tsz], F32)
        uT = sb(nc, st, "uT", [128, 8, tsz], BF16)
        gT = sb(nc, st, "gT", [128, NJ, tsz], BF16)
        sa = sb(nc, st, "sa", [128, 2, tsz], F32)
        psA = [ps(nc, st, "psA%d" % i, [128, 512]) for i in range(2)]
        psB = [ps(nc, st, "psB%d" % i, [128, 512]) for i in range(2)]
        psY = [ps(nc, st, "psY%d" % i, [128, 512]) for i in range(2)]
        for (t0, tn, ms) in token_tiles(tsz):
            for c in range(8):
                P.dma("sp", hT[:, c, :tn], hv[:, c, t0:t0 + tn], writes=[("hT", c)])
            emit_modulate(P, tn, hT, "hT", uT, "uT", sc1p[ms], sh[ms])
            for c in range(8):
                P.op("pool", lambda e, c=c: e.tensor_single_scalar(hA[:, c, :tn], hT[:, c, :tn], DN_ALPHA, ALU.mult),
                     reads=[("hT", c)], writes=[("hA", c)])
            for j in range(NJ):
                b = j % 2
                for k in range(8):
                    P.op("pe", lambda e, j=j, k=k, b=b: e.matmul(psA[b][:, :tn], w1[:, k, j * 128:(j + 1) * 128], uT[:, k, :tn],
                                                                start=(k == 0), stop=(k == 7)),
                         reads=[("w1s", k), ("uT", k)], writes=[("psA", b)])
                for k in range(8):
                    P.op("pe", lambda e, j=j, k=k, b=b: e.matmul(psB[b][:, :tn], w1[:, k, DFF + j * 128:DFF + (j + 1) * 128],
                                                                uT[:, k, :tn], start=(k == 0), stop=(k == 7)),
                         reads=[("w1s", k), ("uT", k)], writes=[("psB", b)])
                P.op("act", lambda e, b=b: e.activation(out=sa[:, b, :tn], in_=psA[b][:, :tn], func=AF.Silu),
                     reads=[("psA", b)], writes=[("sa", b)])
                P.op("dve", lambda e, j=j, b=b: e.tensor_tensor(gT[:, j, :tn], sa[:, b, :tn], psB[b][:, :tn], ALU.mult),
                     reads=[("sa", b), ("psB", b)], writes=[("gT", j)])
            for d in range(8):
                b = d % 2
                for j in range(NJ):
                    P.op("pe", lambda e, j=j, d=d, b=b: e.matmul(psY[b][:, :tn], w2[:, j, d * 128:(d + 1) * 128], gT[:, j, :tn],
                                                                start=(j == 0), stop=(j == NJ - 1)),
                         reads=[("w2s", j), ("gT", j)], writes=[("psY", b)])
                P.op("dve", lambda e, d=d, b=b: e.scalar_tensor_tensor(C.r[:, d, :tn], psY[b][:, :tn], gt[ms][:, d:d + 1],
                                                                      hA[:, d, :tn], ALU.mult, ALU.add),
                     reads=[("psY", b), ("hA", d), "tab"], writes=[("r", d)])
            C.layer_norm(tn, gam, bet, hT, "hT")
            for c in range(8):
                P.dma("sp", ov[:, c, t0:t0 + tn], hT[:, c, :tn], reads=[("hT", c)], writes=["hout"])
        P.finish("sp")
    return nc


def build_inproj(tsz=256):
    nc = bass.Bass("TRN2", target_bir_lowering=False)
    hin = nc.dram_tensor("hin", [D, TT], F32, kind="ExternalInput").ap()
    wd = nc.dram_tensor("w", [D, NQK], F32, kind="ExternalInput").ap()
    tabd = nc.dram_tensor("tab", [128, 48], F32, kind="ExternalInput").ap()
    pout = nc.dram_tensor("pout", [NQK, TT], F32, kind="ExternalOutput").ap()
    hv = hin.rearrange("(c p) t -> p c t", p=128)
    pv = pout.rearrange("(c p) t -> p c t", p=128)
    NO = NQK // 128
    G = 4
    with ExitStack() as st:
        P = Prog(nc, st)
        tab, sh, sc1p, gt = prep_tab(P, nc, st, tabd, 6, 1.0)
        w = load_w(P, nc, st, "ws", wd, D, NQK)
        hT = sb(nc, st, "hT", [128, 8, tsz], F32)
        uT = sb(nc, st, "uT", [128, 8, tsz], BF16)
        og = [sb(nc, st, "og%d" % i, [128, G, tsz], F32) for i in range(2)]
        pp = [ps(nc, st, "pp%d" % i, [128, 512]) for i in range(4)]
        gi = 0
        for (t0, tn, ms) in token_tiles(tsz):
            for c in range(8):
                P.dma("sp", hT[:, c, :tn], hv[:, c, t0:t0 + tn], writes=[("hT", c)])
            emit_modulate(P, tn, hT, "hT", uT, "uT", sc1p[ms], sh[ms])
            for o0 in range(0, NO, G):
                gn = min(G, NO - o0)
                ob = gi % 2
                gi += 1
                for g in range(gn):
                    o = o0 + g
                    b = o % 4
                    for k in range(8):
                        P.op("pe", lambda e, o=o, k=k, b=b: e.matmul(pp[b][:, :tn], w[:, k, o * 128:(o + 1) * 128], uT[:, k, :tn],
                                                                    start=(k == 0), stop=(k == 7)),
                             reads=[("ws", k), ("uT", k)], writes=[("pp", b)])
                    if o % 2 == 0:
                        P.op("act", lambda e, g=g, b=b, ob=ob: e.copy(out=og[ob][:, g, :tn], in_=pp[b][:, :tn]),
                             reads=[("pp", b)], writes=[("og", ob)])
                    else:
                        P.op("dve", lambda e, g=g, b=b, ob=ob: e.tensor_copy(og[ob][:, g, :tn], pp[b][:, :tn]),
                             reads=[("pp", b)], writes=[("og", ob)])
                P.dma("sp", pv[:, o0:o0 + gn, t0:t0 + tn], og[ob][:, :gn, :tn], reads=[("og", ob)], writes=["pout"])
        P.finish("sp")
    return nc


def build_merge(tsz=256):
    nc = bass.Bass("TRN2", target_bir_lowering=False)
    hin = nc.dram_tensor("hin", [D, TT], F32, kind="ExternalInput").ap()
    yin = nc.dram_tensor("yin", [1536, TT], F32, kind="ExternalInput").ap()
    gin = nc.dram_tensor("gin", [3072, TT], F32, kind="ExternalInput").ap()
    bwd = nc.dram_tensor("bw", [1536, D], F32, kind="ExternalInput").ap()
    owd = nc.dram_tensor("ow", [D, D], F32, kind="ExternalInput").ap()
    tabd = nc.dram_tensor("tab", [128, 64], F32, kind="ExternalInput").ap()
    hout = nc.dram_tensor("hout", [D, TT], F32, kind="ExternalOutput").ap()
    hv = hin.rearrange("(c p) t -> p c t", p=128)
    yv = yin.rearrange("(c p) t -> p c t", p=128)
    gv = gin.rearrange("(c p) t -> p c t", p=128)
    ov = hout.rearrange("(c p) t -> p c t", p=128)
    with ExitStack() as st:
        P = Prog(nc, st)
        tab, sh, sc1p, gt = prep_tab(P, nc, st, tabd, 8, 1.0)
        gam, bet = tab[:, 48:56], tab[:, 56:64]
        bw = load_w(P, nc, st, "bws", bwd, 1536, D)
        ow = load_w(P, nc, st, "ows", owd, D, D)
        C = Common(nc, st, P, tsz)
        hT = sb(nc, st, "hT", [128, 8, tsz], F32)
        hA = sb(nc, st, "hA", [128, 8, tsz], F32)
        yb = sb(nc, st, "yb", [128, 12, tsz], BF16)
        sg = sb(nc, st, "sg", [128, 24, tsz], F32)
        macc = sb(nc, st, "macc", [128, 2, tsz], F32)
        mtmp = sb(nc, st, "mtmp", [128, 2, tsz], F32)
        mT = sb(nc, st, "mT", [128, 8, tsz], BF16)
        psM = [ps(nc, st, "psM%d" % i, [128, 512]) for i in range(3)]
        psY = [ps(nc, st, "psY%d" % i, [128, 512]) for i in range(2)]
        for (t0, tn, ms) in token_tiles(tsz):
            for c in range(8):
                P.dma("sp", hT[:, c, :tn], hv[:, c, t0:t0 + tn], writes=[("hT", c)])
            for c in range(12):
                P.dma("pool", yb[:, c, :tn], yv[:, c, t0:t0 + tn], writes=[("yb", c)])
            for c in range(24):
                P.dma("sp", sg[:, c, :tn], gv[:, c, t0:t0 + tn], writes=[("sg", c)])
                P.op("act", lambda e, c=c: e.activation(out=sg[:, c, :tn], in_=sg[:, c, :tn], func=AF.Sigmoid),
                     reads=[("sg", c)], writes=[("sg", c)])
            for c in range(8):
                P.op("pool", lambda e, c=c: e.tensor_single_scalar(hA[:, c, :tn], hT[:, c, :tn], DN_ALPHA, ALU.mult),
                     reads=[("hT", c)], writes=[("hA", c)])
            for d in range(8):
                b = d % 2
                for n in range(3):
                    for kc in range(4):
                        P.op("pe", lambda e, n=n, kc=kc, d=d: e.matmul(psM[n][:, :tn], bw[:, n * 4 + kc, d * 128:(d + 1) * 128],
                                                                      yb[:, n * 4 + kc, :tn], start=(kc == 0), stop=(kc == 3)),
                             reads=[("bws", n * 4 + kc), ("yb", n * 4 + kc)], writes=[("psM", n)])
                P.op("dve", lambda e, d=d, b=b: e.tensor_tensor(macc[:, b, :tn], sg[:, d, :tn], psM[0][:, :tn], ALU.mult),
                     reads=[("sg", d), ("psM", 0)], writes=[("macc", b)])
                P.op("dve", lambda e, d=d, b=b: e.tensor_tensor(mtmp[:, 0, :tn], sg[:, 8 + d, :tn], psM[1][:, :tn], ALU.mult),
                     reads=[("sg", 8 + d), ("psM", 1)], writes=[("mtmp", 0)])
                P.op("dve", lambda e, d=d, b=b: e.tensor_tensor(mtmp[:, 1, :tn], sg[:, 16 + d, :tn], psM[2][:, :tn], ALU.mult),
                     reads=[("sg", 16 + d), ("psM", 2)], writes=[("mtmp", 1)])
                P.op("pool", lambda e, b=b: e.tensor_tensor(macc[:, b, :tn], macc[:, b, :tn], mtmp[:, 0, :tn], ALU.add),
                     reads=[("macc", b), ("mtmp", 0)], writes=[("macc", b)])
                P.op("pool", lambda e, d=d, b=b: e.tensor_tensor(mT[:, d, :tn], macc[:, b, :tn], mtmp[:, 1, :tn], ALU.add),
                     reads=[("macc", b), ("mtmp", 1)], writes=[("mT", d)])
            for d in range(8):
                b = d % 2
                for k in range(8):
                    P.op("pe", lambda e, k=k, d=d, b=b: e.matmul(psY[b][:, :tn], ow[:, k, d * 128:(d + 1) * 128], mT[:, k, :tn],
                                                                start=(k == 0), stop=(k == 7)),
                         reads=[("ows", k), ("mT", k)], writes=[("psY", b)])
                P.op("dve", lambda e, d=d, b=b: e.scalar_tensor_tensor(C.r[:, d, :tn], psY[b][:, :tn], gt[ms][:, d:d + 1],
                                                                      hA[:, d, :tn], ALU.mult, ALU.add),
                     reads=[("psY", b), ("hA", d), "tab"], writes=[("r", d)])
            C.layer_norm(tn, gam, bet, hT, "hT")
            for c in range(8):
                P.dma("sp", ov[:, c, t0:t0 + tn], hT[:, c, :tn], reads=[("hT", c)], writes=["hout"])
        P.finish("sp")
    return nc


_CACHE = {}


def _get(name, fn, *a):
    key = (name,) + a
    if key not in _CACHE:
        _CACHE[key] = fn(*a)
    return _CACHE[key]


def _run(nc, in_maps):
    res = run_bass_kernel_spmd(nc, in_maps, core_ids=list(range(NCORE)))
    return res.results


def _pc(v):
    return np.ascontiguousarray(v.reshape(-1, 128).T)


def core_bq(i):
    return i // 4, i % 4


def to_cores(lat, cx):
    outs = []
    for i in range(NCORE):
        b, q = core_bq(i)
        a = np.concatenate([lat[b, q * TLAT:(q + 1) * TLAT], cx[b, q * TCTX:(q + 1) * TCTX]], axis=0)
        outs.append(np.ascontiguousarray(a.T))
    return outs


def from_cores(outs):
    C = outs[0].shape[0]
    lat = np.empty((2, SEQ, C), np.float32)
    cx = np.empty((2, CTX, C), np.float32)
    for i in range(NCORE):
        b, q = core_bq(i)
        lat[b, q * TLAT:(q + 1) * TLAT] = outs[i][:, :TLAT].T
        cx[b, q * TCTX:(q + 1) * TCTX] = outs[i][:, TLAT:].T
    return lat, cx


def run_mods(c, c_ctx, ada_w, ada_b):
    cT = np.ascontiguousarray(np.concatenate([c, c_ctx[None]], 0).T)
    aw = np.concatenate([ada_w[0], ada_w[1]], axis=1)
    ab = np.concatenate([ada_b[0], ada_b[1]], axis=0)
    nc = _get("k0", build_k0)
    maps = []
    for i in range(NCORE):
        sl = slice(i * NC0, (i + 1) * NC0)
        maps.append({"cT": cT, "aw": np.ascontiguousarray(aw[:, sl]), "ab": _pc(ab[sl])})
    res = _run(nc, maps)
    allm = np.concatenate([r["out"] for r in res], axis=0)
    return allm.reshape(2, 9, D, 3)


def make_tab(mods_l, b, idx3, extra):
    cols = []
    for v in (b, 2):
        for m in idx3:
            cols.append(_pc(mods_l[m, :, v]) if m is not None else np.zeros((128, 8), np.float32))
    for e in extra:
        cols.append(_pc(e))
    return np.ascontiguousarray(np.concatenate(cols, axis=1).astype(np.float32))


def run_ffn(hc, mods_l, idx3, w1, w2, g, bta):
    nc = _get("ffn", build_ffn)
    maps = []
    for i in range(NCORE):
        b, q = core_bq(i)
        maps.append({"hin": hc[i], "w1": w1, "w2": w2, "tab": make_tab(mods_l, b, idx3, [g, bta])})
    return [r["hout"] for r in _run(nc, maps)]


NQ = TT
NKL = TLAT + 256
NKB = NKL // 128 + 2
NK = NKB * 128


def build_attn():
    nc = bass.Bass("TRN2", target_bir_lowering=False)
    qf = nc.dram_tensor("qf", [64, 8, NQ], F32, kind="ExternalInput").ap()
    qs = nc.dram_tensor("qs", [64, 8, NQ], F32, kind="ExternalInput").ap()
    cq = nc.dram_tensor("cq", [64, NQ], F32, kind="ExternalInput").ap()
    sq = nc.dram_tensor("sq", [64, NQ], F32, kind="ExternalInput").ap()
    kf = nc.dram_tensor("kf", [64, 2, NK], F32, kind="ExternalInput").ap()
    ks = nc.dram_tensor("ks", [64, 2, NK], F32, kind="ExternalInput").ap()
    ck = nc.dram_tensor("ck", [64, NK], F32, kind="ExternalInput").ap()
    sk = nc.dram_tensor("sk", [64, NK], F32, kind="ExternalInput").ap()
    vt = nc.dram_tensor("vt", [128, NKB, 128], F32, kind="ExternalInput").ap()
    mk = nc.dram_tensor("mk", [128, 4, 4, 128], F32, kind="ExternalInput").ap()
    sk8 = nc.dram_tensor("sink", [64, 8], F32, kind="ExternalInput").ap()
    yo = nc.dram_tensor("yo", [64, 8, NQ], F32, kind="ExternalOutput").ap()
    CH = 1152
    with ExitStack() as st:
        P = Prog(nc, st)
        kr = sb(nc, st, "kr", [64, 2, NK], BF16)
        vb = sb(nc, st, "vb", [128, NKB, 128], BF16)
        mb = sb(nc, st, "mb", [128, 4, 4, 128], BF16)
        ones = sb(nc, st, "ones", [128, 64], BF16)
        es = sb(nc, st, "es", [64, 8], F32)
        P.dma("pool", vb[:], vt, writes=["vb"])
        P.dma("pool", mb[:], mk, writes=["mb"])
        P.dma("sp", es[:], sk8, writes=["es"])
        P.op("act", lambda e: e.activation(out=es[:], in_=es[:], func=AF.Exp), reads=["es"], writes=["es"])
        P.op("pool", lambda e: e.memset(ones[:], 1.0), writes=["ones"])
        ta = sb(nc, st, "ta", [64, 2, CH], F32)
        tb = sb(nc, st, "tb", [64, 2, CH], F32)
        tc_ = sb(nc, st, "tc", [64, CH], F32)
        td = sb(nc, st, "td", [64, CH], F32)
        for c0 in range(0, NK, CH):
            P.dma("sp", ta[:], kf[:, :, c0:c0 + CH], writes=["ta"])
            P.dma("sp", tb[:], ks[:, :, c0:c0 + CH], writes=["tb"])
            P.dma("sp", tc_[:], ck[:, c0:c0 + CH], writes=["tc"])
            P.dma("sp", td[:], sk[:, c0:c0 + CH], writes=["td"])
            for h in range(2):
                P.op("dve", lambda e, h=h: e.tensor_tensor(ta[:, h, :], ta[:, h, :], tc_[:], ALU.mult), reads=["ta", "tc"], writes=["ta"])
                P.op("pool", lambda e, h=h: e.tensor_tensor(tb[:, h, :], tb[:, h, :], td[:], ALU.mult), reads=["tb", "td"], writes=["tb"])
                P.op("dve", lambda e, h=h, c0=c0: e.tensor_tensor(kr[:, h, c0:c0 + CH], ta[:, h, :], tb[:, h, :], ALU.add),
                     reads=["ta", "tb"], writes=["kr"])
        qa = sb(nc, st, "qa", [64, 8, 128], F32)
        qb = sb(nc, st, "qb", [64, 8, 128], F32)
        qc = sb(nc, st, "qc", [64, 128], F32)
        qd = sb(nc, st, "qd", [64, 128], F32)
        qr = sb(nc, st, "qr", [64, 8, 128], BF16)
        pT = [sb(nc, st, "pT%d" % i, [128, 4, 128], BF16) for i in range(5)]
        dn = sb(nc, st, "dn", [64, 4, 128], F32)
        ob = sb(nc, st, "ob", [64, 8, 128], F32)
        psS = [ps(nc, st, "psS%d" % i, [128, 512]) for i in range(3)]
        psN = [ps(nc, st, "psN%d" % i, [64, 512]) for i in range(2)]
        psD = [ps(nc, st, "psD%d" % i, [64, 512]) for i in range(2)]
        si = 0
        nblk = TLAT // 128
        for n in range(nblk + 1):
            t0 = n * 128
            nq = 128 if n < nblk else TCTX
            P.dma("sp", qa[:, :, :nq], qf[:, :, t0:t0 + nq], writes=["qa"])
            P.dma("sp", qb[:, :, :nq], qs[:, :, t0:t0 + nq], writes=["qb"])
            P.dma("sp", qc[:, :nq], cq[:, t0:t0 + nq], writes=["qc"])
            P.dma("sp", qd[:, :nq], sq[:, t0:t0 + nq], writes=["qd"])
            for h in range(8):
                P.op("dve", lambda e, h=h: e.tensor_tensor(qa[:, h, :nq], qa[:, h, :nq], qc[:, :nq], ALU.mult), reads=["qa", "qc"], writes=["qa"])
                P.op("pool", lambda e, h=h: e.tensor_tensor(qb[:, h, :nq], qb[:, h, :nq], qd[:, :nq], ALU.mult), reads=["qb", "qd"], writes=["qb"])
                P.op("dve", lambda e, h=h: e.tensor_tensor(qr[:, h, :nq], qa[:, h, :nq], qb[:, h, :nq], ALU.add),
                     reads=["qa", "qb"], writes=[("qr", h // 4)])
            if n < nblk:
                kbs = [(n, 0 if n == 0 else 1), (n + 1, None), (n + 2, 3 if n == nblk - 1 else 2), (NKB - 2, None), (NKB - 1, None)]
            else:
                kbs = [(NKB - 2, None), (NKB - 1, None)]
            for kvh in range(2):
                pb = kvh
                for i, (kb, mi) in enumerate(kbs):
                    sbk = si % 3
                    si += 1
                    for g in range(4):
                        P.op("pe", lambda e, kb=kb, g=g, sbk=sbk, kvh=kvh: e.matmul(psS[sbk][:, g * nq:(g + 1) * nq],
                                                                                   kr[:, kvh, kb * 128:(kb + 1) * 128],
                                                                                   qr[:, kvh * 4 + g, :nq], start=True, stop=True),
                             reads=["kr", ("qr", kvh)], writes=[("psS", sbk)])
                    for g in range(4):
                        P.op("act", lambda e, g=g, i=i, sbk=sbk: e.activation(out=pT[i][:, g, :nq], in_=psS[sbk][:, g * nq:(g + 1) * nq],
                                                                              func=AF.Exp, scale=0.125),
                             reads=[("psS", sbk)], writes=[("pT", i)])
                    if mi is not None:
                        P.op("pool", lambda e, i=i, mi=mi: e.tensor_tensor(pT[i][:, :, :nq], pT[i][:, :, :nq], mb[:, mi, :, :nq], ALU.mult),
                             reads=[("pT", i), "mb"], writes=[("pT", i)])
                for g in range(4):
                    for i, (kb, mi) in enumerate(kbs):
                        P.op("pe", lambda e, i=i, kb=kb, g=g, kvh=kvh, pb=pb: e.matmul(psN[pb][:, g * nq:(g + 1) * nq],
                                                                                      vb[:, kb, kvh * 64:(kvh + 1) * 64], pT[i][:, g, :nq],
                                                                                      start=(i == 0), stop=(i == len(kbs) - 1)),
                             reads=[("pT", i), "vb"], writes=[("psN", pb)])
                for g in range(4):
                    for i, (kb, mi) in enumerate(kbs):
                        P.op("pe", lambda e, i=i, g=g, pb=pb: e.matmul(psD[pb][:, g * nq:(g + 1) * nq], ones[:], pT[i][:, g, :nq],
                                                                      start=(i == 0), stop=(i == len(kbs) - 1)),
                             reads=[("pT", i), "ones"], writes=[("psD", pb)])
                for g in range(4):
                    h = kvh * 4 + g
                    P.op("dve", lambda e, g=g, h=h, pb=pb: e.tensor_scalar(dn[:, g, :nq], psD[pb][:, g * nq:(g + 1) * nq], es[:, h:h + 1], None, ALU.add),
                         reads=[("psD", pb), "es"], writes=["dn"])
                    P.op("dve", lambda e, g=g: e.reciprocal(dn[:, g, :nq], dn[:, g, :nq]), reads=["dn"], writes=["dn"])
                    P.op("dve", lambda e, g=g, h=h, pb=pb: e.tensor_tensor(ob[:, h, :nq], psN[pb][:, g * nq:(g + 1) * nq], dn[:, g, :nq], ALU.mult),
                         reads=[("psN", pb), "dn"], writes=["ob"])
            P.dma("sp", yo[:, :, t0:t0 + nq], ob[:, :, :nq], reads=["ob"], writes=["yo"])
        P.finish("sp")
    return nc


def rope_tables(pos):
    pos = np.asarray(pos)
    valid = pos >= 0
    p = np.where(valid, pos, 0)
    row = (p // 64).astype(np.float32)
    col = (p % 64).astype(np.float32)
    inv = (np.float32(10000.0) ** (-np.arange(16, dtype=np.float32) * np.float32(2.0) / np.float32(32))).astype(np.float32)
    cos = np.ones((64, len(pos)), np.float32)
    sin = np.zeros((64, len(pos)), np.float32)
    for a, base in ((0, row), (1, col)):
        ang = (base[None, :] * inv[:, None]).astype(np.float32)
        for s in range(2):
            sl = slice(a * 32 + s * 16, a * 32 + s * 16 + 16)
            cos[sl] = np.cos(ang)
            sin[sl] = np.sin(ang) * (-1.0 if s == 0 else 1.0)
    cos[:, ~valid] = 1.0
    sin[:, ~valid] = 0.0
    return cos, sin


OFF_B, OFF_Q, OFF_K, OFF_V, OFF_G = 2560, 4096, 4608, 4736, 4864
OFF_QS, OFF_KS = 7936, 8448


def swap_cols(w):
    sh = w.shape
    return w.reshape(sh[:-1] + (-1, 2, 2, 16))[..., ::-1, :].reshape(sh)


def run_attn(Pl, Pc, sink):
    nc = _get("attn", build_attn)
    qi = np.arange(128)
    m_prev = (qi[:, None] >= qi[None, :]).astype(np.float32)
    m_next = (qi[:, None] <= qi[None, :]).astype(np.float32)
    maps = []
    for i in range(NCORE):
        b, q = core_bq(i)
        t0 = q * TLAT
        lat = Pl[b, t0:t0 + TLAT]
        cx = Pc[b, q * TCTX:(q + 1) * TCTX]

        def heads(a, n):
            return np.ascontiguousarray(a.reshape(a.shape[0], n, 64).transpose(2, 1, 0))
        qall = np.concatenate([lat[:, OFF_Q:OFF_K], cx[:, OFF_Q:OFF_K]], 0)
        qsall = np.concatenate([lat[:, OFF_QS:OFF_KS], cx[:, OFF_QS:OFF_KS]], 0)
        qpos = np.concatenate([np.arange(t0, t0 + TLAT), -np.ones(TCTX, np.int64)])
        cqt, sqt = rope_tables(qpos)
        kpos = np.arange(t0 - 128, t0 + TLAT + 128)
        kval = (kpos >= 0) & (kpos < SEQ)
        kidx = np.clip(kpos, 0, SEQ - 1)
        kl = Pl[b, kidx] * kval[:, None]
        kall = np.concatenate([kl[:, OFF_K:OFF_V], Pc[b][:, OFF_K:OFF_V]], 0)
        ksall = np.concatenate([kl[:, OFF_KS:NQK], Pc[b][:, OFF_KS:NQK]], 0)
        vall = np.concatenate([kl[:, OFF_V:OFF_G], Pc[b][:, OFF_V:OFF_G]], 0)
        ckt, skt = rope_tables(np.concatenate([np.where(kval, kpos, -1), -np.ones(CTX, np.int64)]))
        m0 = m_prev if q > 0 else np.zeros_like(m_prev)
        m3 = m_next if q < 3 else np.zeros_like(m_next)
        mk = np.stack([m0, m_prev, m_next, m3], 0)
        mk = np.ascontiguousarray(np.broadcast_to(mk.transpose(1, 0, 2)[:, :, None, :], (128, 4, 4, 128))).astype(np.float32)
        maps.append({
            "qf": heads(qall, 8), "qs": heads(qsall, 8), "cq": cqt, "sq": sqt,
            "kf": heads(kall, 2), "ks": heads(ksall, 2), "ck": ckt, "sk": skt,
            "vt": np.ascontiguousarray(vall.reshape(NKB, 128, 128).transpose(1, 0, 2)),
            "mk": mk, "sink": np.ascontiguousarray(np.broadcast_to(sink[None, :], (64, 8))).astype(np.float32),
        })
    res = _run(nc, maps)
    yl = np.empty((2, SEQ, 512), np.float32)
    yc = np.empty((2, CTX, 512), np.float32)
    for i in range(NCORE):
        b, q = core_bq(i)
        y = res[i]["yo"].transpose(2, 1, 0).reshape(NQ, 512)
        yl[b, q * TLAT:(q + 1) * TLAT] = y[:TLAT]
        yc[b, q * TCTX:(q + 1) * TCTX] = y[TLAT:]
    return yl, yc


NTOK = CTX + SEQ
NG = NTOK // 128


def build_hgrn():
    nc = bass.Bass("TRN2", target_bir_lowering=False)
    zd = [nc.dram_tensor("z%d" % d, [128, NG, 128], F32, kind="ExternalInput").ap() for d in range(2)]
    vd = nc.dram_tensor("v", [128, NG, 128], F32, kind="ExternalInput").ap()
    qd = nc.dram_tensor("q", [128, NTOK], F32, kind="ExternalInput").ap()
    gd = nc.dram_tensor("g", [128, NTOK], F32, kind="ExternalInput").ap()
    lbd = nc.dram_tensor("lb", [128, 2, 2, 128], F32, kind="ExternalInput").ap()
    fld = nc.dram_tensor("flag", [128, 1], F32, kind="ExternalInput").ap()
    nwd = nc.dram_tensor("nw", [128, 1], F32, kind="ExternalInput").ap()
    cfd = nc.dram_tensor("cf", [128, 5, 128], F32, kind="ExternalInput").ap()
    cmd = nc.dram_tensor("cm", [128, 4], F32, kind="ExternalInput").ap()
    yo = nc.dram_tensor("yo", [128, NTOK], F32, kind="ExternalOutput").ap()
    with ExitStack() as st:
        P = Prog(nc, st)
        vb = sb(nc, st, "vb", [128, NG, 128], BF16)
        P.dma("pool", vb[:], vd, writes=["vb"])
        cf = sb(nc, st, "cfs", [128, 5, 128], F32)
        P.dma("sp", cf[:], cfd, writes=["cf"])
        bm = sb(nc, st, "bm", [128, 2, 128], BF16)
        P.dma("pool", bm[:], cfd[:, 0:2, :], writes=["bm"])
        cm = sb(nc, st, "cms", [128, 4], F32)
        P.dma("sp", cm[:], cmd, writes=["cm"])
        nw = sb(nc, st, "nws", [128, 1], F32)
        P.dma("sp", nw[:], nwd, writes=["nw"])
        fl = sb(nc, st, "fls", [128, 1], F32)
        P.dma("sp", fl[:], fld, writes=["fl"])
        lbt = sb(nc, st, "lbs", [128, 2, 2, 128], F32)
        P.dma("sp", lbt[:], lbd, writes=["lbt"])
        LBb = sb(nc, st, "LBb", [128, 2, 128], F32)
        LBa = sb(nc, st, "LBa", [128, 2, 128], F32)
        P.op("dve", lambda e: e.tensor_tensor(LBb[:], lbt[:, 1], lbt[:, 0], ALU.subtract), reads=["lbt"], writes=["LBb"])
        P.op("act", lambda e: e.activation(out=LBb[:], in_=LBb[:], func=AF.Sigmoid), reads=["LBb"], writes=["LBb"])
        P.op("dve", lambda e: e.tensor_scalar(LBb[:], LBb[:], fl[:, 0:1], None, ALU.mult), reads=["LBb", "fl"], writes=["LBb"])
        P.op("dve", lambda e: e.tensor_scalar(LBa[:], LBb[:], -1.0, 1.0, ALU.mult, ALU.add), reads=["LBb"], writes=["LBa"])
        ones = sb(nc, st, "ones", [128, 128], F32)
        P.op("pool", lambda e: e.memset(ones[:], 1.0), writes=["ones"])
        osum = sb(nc, st, "osum", [128, NTOK], F32)
        S = sb(nc, st, "S", [128, 128], F32)
        Sb = sb(nc, st, "Sb", [128, 128], BF16)
        zt = [sb(nc, st, "zt%d" % i, [128, 128], F32) for i in range(2)]
        qt32 = [sb(nc, st, "qt32%d" % i, [128, 128], F32) for i in range(2)]
        ft = [sb(nc, st, "ft%d" % i, [128, 128], F32) for i in range(2)]
        lf = [sb(nc, st, "lf%d" % i, [128, 128], F32) for i in range(2)]
        kk = [sb(nc, st, "kk%d" % i, [128, 128], F32) for i in range(2)]
        Eq = [sb(nc, st, "Eq%d" % i, [128, 128], F32) for i in range(2)]
        Ek = [sb(nc, st, "Ek%d" % i, [128, 128], F32) for i in range(2)]
        Er = [sb(nc, st, "Er%d" % i, [128, 128], F32) for i in range(2)]
        qt = [sb(nc, st, "qt%d" % i, [128, 128], BF16) for i in range(2)]
        kt = [sb(nc, st, "kt%d" % i, [128, 128], BF16) for i in range(2)]
        k4 = [sb(nc, st, "k4%d" % i, [128, 4, 128], BF16) for i in range(2)]
        am = [sb(nc, st, "am%d" % i, [128, 128], BF16) for i in range(2)]
        pB = ps(nc, st, "pB", [128, 512])
        pR = ps(nc, st, "pR", [128, 512])
        pK = ps(nc, st, "pK", [128, 512])
        pA = ps(nc, st, "pA", [128, 512])
        pO = [ps(nc, st, "pO%d" % i, [128, 512]) for i in range(2)]
        pU = [ps(nc, st, "pU%d" % i, [128, 512]) for i in range(2)]
        it = 0
        ui = 0
        for d in range(2):
            P.op("pool", lambda e: e.memset(S[:], 0.0), writes=["S"])
            P.op("pool", lambda e: e.memset(Sb[:], 0.0), writes=["Sb"])
            groups = list(range(NG)) if d == 0 else [1, 0] + list(range(NG - 1, 1, -1))
            chunks = [0, 1, 2, 3] if d == 0 else [3, 2, 1, 0]
            for G in groups:
                b = it % 2
                it += 1
                P.dma("sp", zt[b][:], zd[d][:, G, :], writes=[("zt", b)])
                P.dma("sp", qt32[b][:], qd[:, G * 128:(G + 1) * 128], writes=[("qt32", b)])
                P.op("act", lambda e, b=b: e.activation(out=zt[b][:], in_=zt[b][:], func=AF.Sigmoid), reads=[("zt", b)], writes=[("zt", b)])
                P.op("dve", lambda e, b=b, d=d: e.tensor_tensor(ft[b][:], zt[b][:], LBa[:, d, :], ALU.mult), reads=[("zt", b), "LBa"], writes=[("ft", b)])
                P.op("dve", lambda e, b=b, d=d: e.tensor_tensor(ft[b][:], ft[b][:], LBb[:, d, :], ALU.add), reads=[("ft", b), "LBb"], writes=[("ft", b)])
                P.op("act", lambda e, b=b: e.activation(out=lf[b][:], in_=ft[b][:], func=AF.Ln), reads=[("ft", b)], writes=[("lf", b)])
                P.op("pool", lambda e, b=b: e.tensor_scalar(kk[b][:], ft[b][:], -1.0, 1.0, ALU.mult, ALU.add), reads=[("ft", b)], writes=[("kk", b)])
                P.op("pe", lambda e, b=b, d=d: e.matmul(pB[:, :128], lf[b][:], cf[:, d, :], start=True, stop=True), reads=[("lf", b), "cf"], writes=["pB"])
                P.op("pe", lambda e, b=b, d=d: e.matmul(pR[:, :128], cf[:, 2 + d, :], lf[b][:], start=True, stop=True), reads=[("lf", b), "cf"], writes=["pR"])
                P.op("pe", lambda e, b=b: e.matmul(pK[:, :128], kk[b][:], cf[:, 4, :], start=True, stop=True), reads=[("kk", b), "cf"], writes=["pK"])
                P.op("act", lambda e, b=b: e.activation(out=Eq[b][:], in_=pB[:, :128], func=AF.Exp), reads=["pB"], writes=[("Eq", b)])
                P.op("act", lambda e, b=b: e.activation(out=Ek[b][:], in_=pB[:, :128], func=AF.Exp, scale=-1.0), reads=["pB"], writes=[("Ek", b)])
                P.op("act", lambda e, b=b: e.activation(out=Er[b][:], in_=pR[:, :128], func=AF.Exp), reads=["pR"], writes=[("Er", b)])
                P.op("dve", lambda e, b=b: e.tensor_tensor(qt[b][:], qt32[b][:], Eq[b][:], ALU.mult), reads=[("qt32", b), ("Eq", b)], writes=[("qt", b)])
                P.op("dve", lambda e, b=b: e.tensor_tensor(kt[b][:], pK[:, :128], Ek[b][:], ALU.mult), reads=["pK", ("Ek", b)], writes=[("kt", b)])
                for c in range(4):
                    P.op("dve", lambda e, b=b, c=c: e.scalar_tensor_tensor(k4[b][:, c, :], kk[b][:], cm[:, c:c + 1], Er[b][:], ALU.mult, ALU.mult),
                         reads=[("kk", b), ("Er", b), "cm"], writes=[("k4", b)])
                P.op("pe", lambda e, b=b: e.matmul(pA[:, :128], kt[b][:], qt[b][:], start=True, stop=True), reads=[("kt", b), ("qt", b)], writes=["pA"])
                P.op("dve", lambda e, b=b, d=d: e.tensor_tensor(am[b][:], pA[:, :128], bm[:, d, :], ALU.mult), reads=["pA", "bm"], writes=[("am", b)])
                P.op("pe", lambda e, b=b, G=G: e.matmul(pO[b][:, :128], vb[:, G, :], am[b][:], start=True, stop=False),
                     reads=["vb", ("am", b)], writes=[("pO", b)])
                for ci, c in enumerate(chunks):
                    P.op("pe", lambda e, b=b, c=c, ci=ci: e.matmul(pO[b][:, c * 32:(c + 1) * 32], Sb[:], qt[b][:, c * 32:(c + 1) * 32],
                                                                  start=False, stop=(ci == 3)),
                         reads=["Sb", ("qt", b)], writes=[("pO", b)])
                    u = ui % 2
                    ui += 1
                    P.op("pe", lambda e, b=b, c=c, u=u, G=G: e.matmul(pU[u][:, :128], k4[b][:, c, :], vb[:, G, :], start=True, stop=True),
                         reads=[("k4", b), "vb"], writes=[("pU", u)])
                    dcol = c * 32 + (31 if d == 0 else 0)
                    P.op("dve", lambda e, b=b, u=u, dcol=dcol: e.scalar_tensor_tensor(S[:], S[:], Eq[b][:, dcol:dcol + 1], pU[u][:, :128], ALU.mult, ALU.add),
                         reads=["S", ("Eq", b), ("pU", u)], writes=["S"])
                    P.op("act", lambda e: e.copy(out=Sb[:], in_=S[:]), reads=["S"], writes=["Sb"])
                if d == 0:
                    P.op("act", lambda e, b=b, G=G: e.copy(out=osum[:, G * 128:(G + 1) * 128], in_=pO[b][:, :128]), reads=[("pO", b)], writes=[("osum", G)])
                else:
                    P.op("dve", lambda e, b=b, G=G: e.tensor_tensor(osum[:, G * 128:(G + 1) * 128], osum[:, G * 128:(G + 1) * 128], pO[b][:, :128], ALU.add),
                         reads=[("pO", b), ("osum", G)], writes=[("osum", G)])
        sq = sb(nc, st, "sq", [128, 512], F32)
        gt = sb(nc, st, "gt", [128, 512], F32)
        rs = sb(nc, st, "rs", [128, 512], F32)
        yt = [sb(nc, st, "yt%d" % i, [128, 512], F32) for i in range(2)]
        ri = 0
        for t0 in range(0, NTOK, 512):
            tn = min(512, NTOK - t0)
            b = ri % 2
            ri += 1
            rk = [("osum", G) for G in range(t0 // 128, (t0 + tn) // 128)]
            P.dma("sp", gt[:, :tn], gd[:, t0:t0 + tn], writes=["gt"])
            P.op("act", lambda e, t0=t0, tn=tn: e.activation(out=sq[:, :tn], in_=osum[:, t0:t0 + tn], func=AF.Square), reads=rk, writes=["sq"])
            P.op("pe", lambda e, tn=tn: e.matmul(pB[:, :tn], ones[:], sq[:, :tn], start=True, stop=True), reads=["sq", "ones"], writes=["pB"])
            P.op("dve", lambda e, tn=tn: e.tensor_scalar(rs[:, :tn], pB[:, :tn], 1.0 / 128, RMS_EPS, ALU.mult, ALU.add), reads=["pB"], writes=["rs"])
            P.op("act", lambda e, tn=tn: e.sqrt(out=rs[:, :tn], in_=rs[:, :tn]), reads=["rs"], writes=["rs"])
            P.op("dve", lambda e, tn=tn: e.reciprocal(rs[:, :tn], rs[:, :tn]), reads=["rs"], writes=["rs"])
            P.op("act", lambda e, tn=tn: e.activation(out=gt[:, :tn], in_=gt[:, :tn], func=AF.Silu), reads=["gt"], writes=["gt"])
            P.op("dve", lambda e, b=b, t0=t0, tn=tn: e.tensor_tensor(yt[b][:, :tn], osum[:, t0:t0 + tn], rs[:, :tn], ALU.mult), reads=rk + ["rs"], writes=[("yt", b)])
            P.op("pool", lambda e, b=b, tn=tn: e.tensor_tensor(yt[b][:, :tn], yt[b][:, :tn], gt[:, :tn], ALU.mult), reads=[("yt", b), "gt"], writes=[("yt", b)])
            P.op("act", lambda e, b=b, tn=tn: e.activation(out=yt[b][:, :tn], in_=yt[b][:, :tn], func=AF.Copy, scale=nw[:, 0:1]), reads=[("yt", b), "nw"], writes=[("yt", b)])
            P.dma("sp", yo[:, t0:t0 + tn], yt[b][:, :tn], reads=[("yt", b)], writes=["yo"])
        P.finish("sp")
    return nc


def run_hgrn(Pl, Pc, layer, hgrn_lb, norm_w):
    nc = _get("hgrn", build_hgrn)
    p = np.arange(128)
    same = (p[:, None] // 32) == (p[None, :] // 32)
    incl_f = (same & (p[:, None] <= p[None, :])).astype(np.float32)
    incl_b = (same & (p[:, None] >= p[None, :])).astype(np.float32)
    rem_f = (same & (p[:, None] > p[None, :])).astype(np.float32)
    rem_b = (same & (p[:, None] < p[None, :])).astype(np.float32)
    cf = np.ascontiguousarray(np.stack([incl_f, incl_b, rem_f, rem_b, np.eye(128, dtype=np.float32)], 1))
    cm = (p[:, None] // 32 == np.arange(4)[None, :]).astype(np.float32)
    maps = []
    for i in range(NCORE):
        b, hd = core_bq(i)
        seq = np.concatenate([Pc[b], Pl[b]], 0)
        cs = slice(hd * 128, (hd + 1) * 128)

        def tm(a):
            return np.ascontiguousarray(a.reshape(NG, 128, 128).transpose(1, 0, 2))
        lb = np.broadcast_to(hgrn_lb[:, :, cs][None], (128, 2, 2, 128))
        maps.append({
            "q": np.ascontiguousarray(seq[:, 0:512][:, cs].T), "v": tm(seq[:, 512:1024][:, cs]),
            "g": np.ascontiguousarray(seq[:, 1024:1536][:, cs].T),
            "z0": tm(seq[:, 1536:2048][:, cs]), "z1": tm(seq[:, 2048:2560][:, cs]),
            "lb": np.ascontiguousarray(lb).astype(np.float32),
            "flag": np.full((128, 1), float(layer), np.float32),
            "nw": np.ascontiguousarray(norm_w[cs][:, None]).astype(np.float32), "cf": cf, "cm": cm,
        })
    res = _run(nc, maps)
    yl = np.empty((2, SEQ, 512), np.float32)
    yc = np.empty((2, CTX, 512), np.float32)
    for i in range(NCORE):
        b, hd = core_bq(i)
        y = res[i]["yo"].T
        yc[b, :, hd * 128:(hd + 1) * 128] = y[:CTX]
        yl[b, :, hd * 128:(hd + 1) * 128] = y[CTX:]
    return yl, yc


PI = float(np.pi)


def build_hyena(L):
    nc = bass.Bass("TRN2", target_bir_lowering=False)
    PP = min(L, 1024)
    pbd = [nc.dram_tensor("pb%d" % i, [3, 128, L], F32, kind="ExternalInput").ap() for i in range(3)]
    cwd = nc.dram_tensor("cw", [128, 12], F32, kind="ExternalInput").ap()
    bsd = nc.dram_tensor("bias", [128, 2], F32, kind="ExternalInput").ap()
    ftd = nc.dram_tensor("feats", [33, L], F32, kind="ExternalInput").ap()
    dcd = nc.dram_tensor("decay", [128, L], F32, kind="ExternalInput").ap()
    w1d = nc.dram_tensor("w1", [33, 64], F32, kind="ExternalInput").ap()
    w2d = nc.dram_tensor("w2", [64, 64], F32, kind="ExternalInput").ap()
    w3d = nc.dram_tensor("w3", [64, 4, 128], F32, kind="ExternalInput").ap()
    bfd = nc.dram_tensor("bf", [64, 4], F32, kind="ExternalInput").ap()
    yo = nc.dram_tensor("yo", [128, L], F32, kind="ExternalOutput").ap()
    with ExitStack() as st:
        P = Prog(nc, st)
        cw = sb(nc, st, "cws", [128, 12], F32)
        bs = sb(nc, st, "bss", [128, 2], F32)
        w1 = sb(nc, st, "w1s", [33, 64], F32)
        w2 = sb(nc, st, "w2s", [64, 64], F32)
        w3 = sb(nc, st, "w3s", [64, 4, 128], F32)
        bf = sb(nc, st, "bfs", [64, 4], F32)
        for t, dd in ((cw, cwd), (bs, bsd), (w1, w1d), (w2, w2d), (w3, w3d), (bf, bfd)):
            P.dma("sp", t[:], dd, writes=["consts"])
        z = sb(nc, st, "z", [128, L], F32)
        y = sb(nc, st, "y", [128, L], F32)
        ta = sb(nc, st, "ta", [128, PP], F32)
        tb = sb(nc, st, "tb", [128, PP], F32)
        tcx = sb(nc, st, "tcx", [128, PP], F32)
        xs = sb(nc, st, "xs", [128, PP], F32)
        ft = sb(nc, st, "ft", [33, PP], F32)
        h1 = sb(nc, st, "h1", [64, PP], F32)
        h2 = sb(nc, st, "h2", [64, PP], F32)
        dc = sb(nc, st, "dc", [128, PP], F32)
        hp = [sb(nc, st, "hp%d" % i, [128, PP], F32) for i in range(2)]
        l1 = sb(nc, st, "l1", [128, 4], F32)
        ki = sb(nc, st, "ki", [64, PP], mybir.dt.int32)
        kf = sb(nc, st, "kf", [64, PP], F32)
        pm = ps(nc, st, "pm", [64, 512])
        ph = [ps(nc, st, "ph%d" % i, [128, 512]) for i in range(2)]

        def sconv(part, dst, dkey, t0, tn):
            P.dma("sp", ta[:, :tn], pbd[0][part, :, t0:t0 + tn], writes=["ta"])
            P.dma("sp", tb[:, :tn], pbd[1][part, :, t0:t0 + tn], writes=["tb"])
            P.dma("sp", tcx[:, :tn], pbd[2][part, :, t0:t0 + tn], writes=["tcx"])
            c0 = part * 3
            P.op("dve", lambda e: e.tensor_scalar(ta[:, :tn], ta[:, :tn], cw[:, c0:c0 + 1], cw[:, 9 + part:10 + part], ALU.mult, ALU.add),
                 reads=["ta", "consts"], writes=["ta"])
            P.op("dve", lambda e: e.scalar_tensor_tensor(tb[:, :tn], tb[:, :tn], cw[:, c0 + 1:c0 + 2], ta[:, :tn], ALU.mult, ALU.add),
                 reads=["ta", "tb", "consts"], writes=["tb"])
            P.op("dve", lambda e: e.scalar_tensor_tensor(dst, tcx[:, :tn], cw[:, c0 + 2:c0 + 3], tb[:, :tn], ALU.mult, ALU.add),
                 reads=["tb", "tcx", "consts"], writes=[dkey])

        for t0 in range(0, L, PP):
            sconv(0, z[:, t0:t0 + PP], "z", t0, PP)
        for o in range(2):
            P.op("pool", lambda e: e.memset(y[:], 0.0), writes=["y"])
            P.op("pool", lambda e: e.memset(l1[:], 0.0), writes=["l1"])
            for p0 in range(0, L, PP):
                P.dma("sp", ft[:], ftd[:, p0:p0 + PP], writes=["ft"])
                P.dma("sp", dc[:], dcd[:, p0:p0 + PP], writes=["dc"])
                for (src, skey, w, K_, dst, dkey, bi) in ((ft, "ft", w1, 33, h1, "h1", 0), (h1, "h1", w2, 64, h2, "h2", 2)):
                    for q0 in range(0, PP, 512):
                        qn = min(512, PP - q0)
                        P.op("pe", lambda e, q0=q0, qn=qn, src=src, w=w, K_=K_: e.matmul(pm[:, :qn], w[:K_, :], src[:K_, q0:q0 + qn], start=True, stop=True),
                             reads=[skey, "consts"], writes=["pm"])
                        P.op("dve", lambda e, q0=q0, qn=qn, dst=dst, bi=bi: e.tensor_scalar(dst[:, q0:q0 + qn], pm[:, :qn], bf[:, bi:bi + 1], bf[:, bi + 1:bi + 2], ALU.add, ALU.mult),
                             reads=["pm", "consts"], writes=[dkey])
                    P.op("dve", lambda e, dst=dst: e.tensor_scalar(dst[:], dst[:], 1.0 / (2.0 * PI), 8.5, ALU.mult, ALU.add), reads=[dkey], writes=[dkey])
                    P.op("dve", lambda e, dst=dst: e.tensor_copy(ki[:], dst[:]), reads=[dkey], writes=["ki"])
                    P.op("dve", lambda e, dst=dst: e.tensor_copy(kf[:], ki[:]), reads=["ki"], writes=["kf"])
                    P.op("dve", lambda e, dst=dst: e.tensor_tensor(dst[:], dst[:], kf[:], ALU.subtract), reads=[dkey, "kf"], writes=[dkey])
                    P.op("dve", lambda e, dst=dst: e.tensor_single_scalar(kf[:], dst[:], 0.0, ALU.is_lt), reads=[dkey], writes=["kf"])
                    P.op("dve", lambda e, dst=dst: e.tensor_tensor(dst[:], dst[:], kf[:], ALU.add), reads=[dkey, "kf"], writes=[dkey])
                    P.op("dve", lambda e, dst=dst: e.tensor_scalar(dst[:], dst[:], 2.0 * PI, -PI, ALU.mult, ALU.add), reads=[dkey], writes=[dkey])
                    P.op("act", lambda e, dst=dst: e.activation(out=dst[:], in_=dst[:], func=AF.Sin), reads=[dkey], writes=[dkey])
                for dr in range(2):
                    for q0 in range(0, PP, 512):
                        qn = min(512, PP - q0)
                        b = (q0 // 512) % 2
                        P.op("pe", lambda e, q0=q0, qn=qn, dr=dr, b=b: e.matmul(ph[b][:, :qn], w3[:, dr * 2 + o, :], h2[:, q0:q0 + qn], start=True, stop=True),
                             reads=["h2", "consts"], writes=[("ph", b)])
                        P.op("dve", lambda e, q0=q0, qn=qn, dr=dr, b=b: e.tensor_tensor(hp[dr][:, q0:q0 + qn], ph[b][:, :qn], dc[:, q0:q0 + qn], ALU.mult),
                             reads=[("ph", b), "dc"], writes=[("hp", dr)])
                    if dr == 1 and p0 == 0:
                        P.op("dve", lambda e: e.memset(hp[1][:, 0:1], 0.0), reads=[("hp", 1)], writes=[("hp", 1)])
                    P.op("dve", lambda e, dr=dr: e.tensor_reduce(out=l1[:, 2 + dr:3 + dr], in_=hp[dr][:], axis=AX.X, op=ALU.add, apply_absolute_value=True),
                         reads=[("hp", dr)], writes=["l1p"])
                    P.op("dve", lambda e, dr=dr: e.tensor_tensor(l1[:, 0:1], l1[:, 0:1], l1[:, 2 + dr:3 + dr], ALU.add), reads=["l1p", "l1"], writes=["l1"])
                for j in range(PP):
                    d = p0 + j
                    P.op("dve", lambda e, j=j, d=d: e.scalar_tensor_tensor(y[:, d:L], z[:, 0:L - d], hp[0][:, j:j + 1], y[:, d:L], ALU.mult, ALU.add),
                         reads=[("hp", 0), "z", "y"], writes=["y"])
                    if d >= 1:
                        P.op("dve", lambda e, j=j, d=d: e.scalar_tensor_tensor(y[:, 0:L - d], z[:, d:L], hp[1][:, j:j + 1], y[:, 0:L - d], ALU.mult, ALU.add),
                             reads=[("hp", 1), "z", "y"], writes=["y"])
            P.op("dve", lambda e: e.reciprocal(l1[:, 1:2], l1[:, 0:1]), reads=["l1"], writes=["l1"])
            for t0 in range(0, L, PP):
                sconv(1 + o, xs[:], "xs", t0, PP)
                P.op("dve", lambda e, t0=t0: e.tensor_scalar(y[:, t0:t0 + PP], y[:, t0:t0 + PP], l1[:, 1:2], None, ALU.mult), reads=["y", "l1"], writes=["y"])
                P.op("dve", lambda e, t0=t0: e.scalar_tensor_tensor(y[:, t0:t0 + PP], z[:, t0:t0 + PP], bs[:, o:o + 1], y[:, t0:t0 + PP], ALU.mult, ALU.add),
                     reads=["y", "z", "consts"], writes=["y"])
                P.op("dve", lambda e, t0=t0: e.tensor_tensor(z[:, t0:t0 + PP], y[:, t0:t0 + PP], xs[:], ALU.mult), reads=["y", "xs", "z"], writes=["z"])
        for t0 in range(0, L, PP):
            P.dma("sp", yo[:, t0:t0 + PP], z[:, t0:t0 + PP], reads=["z"], writes=["yo"])
        P.finish("sp")
    return nc


def run_hyena(X, L, cw, cb, w1, b1, f1, w2, b2, f2, w3, bias):
    nc = _get("hyena", build_hyena, L)
    pos = np.arange(L, dtype=np.float32)
    t = pos / np.float32(L - 1)
    wv = np.float32(2.0 * np.pi) * pos / np.float32(L)
    bands = np.linspace(1e-4, 15, 16, dtype=np.float32)
    feats = np.concatenate([t[:, None], np.cos(wv[:, None] * bands), -np.sin(wv[:, None] * bands)], -1).astype(np.float32)
    rates = np.abs(np.linspace(np.log(1e-2) / 1.5, np.log(1e-2) / 0.3, 512, dtype=np.float32))
    decay = np.exp(-t[None, :] * rates[:, None]).astype(np.float32)
    Xp = np.pad(X, ((0, 0), (1, 1), (0, 0)))
    maps = []
    for i in range(NCORE):
        cs = np.arange(i * 64, (i + 1) * 64)

        def rows(a):
            return np.ascontiguousarray(np.stack([a[:, :, p * 512 + cs].transpose(0, 2, 1).reshape(128, L) for p in range(3)], 0))
        cwt = np.concatenate([np.stack([cw[tp, p * 512 + cs] for p in range(3) for tp in range(3)], 1),
                              np.stack([cb[p * 512 + cs] for p in range(3)], 1)], 1)
        w3r = w3.reshape(64, 2, 2, 512)[:, :, :, cs]
        w3r = np.concatenate([w3r, w3r], -1).reshape(64, 4, 128)
        maps.append({
            "pb0": rows(Xp[:, 0:L]), "pb1": rows(Xp[:, 1:L + 1]), "pb2": rows(Xp[:, 2:L + 2]),
            "cw": np.ascontiguousarray(np.concatenate([cwt, cwt], 0)).astype(np.float32),
            "bias": np.ascontiguousarray(np.concatenate([bias[:, cs].T, bias[:, cs].T], 0)).astype(np.float32),
            "feats": np.ascontiguousarray(feats.T), "decay": np.ascontiguousarray(np.concatenate([decay[cs], decay[cs]], 0)),
            "w1": np.ascontiguousarray(w1), "w2": np.ascontiguousarray(w2), "w3": np.ascontiguousarray(w3r).astype(np.float32),
            "bf": np.ascontiguousarray(np.stack([b1, f1, b2, f2], 1)).astype(np.float32),
        })
    res = _run(nc, maps)
    out = np.empty((2, L, 512), np.float32)
    for i in range(NCORE):
        out[:, :, i * 64:(i + 1) * 64] = res[i]["yo"].reshape(2, 64, L).transpose(0, 2, 1)
    return out


def run_inproj(hc, mods_l, w_aug):
    nc = _get("inproj", build_inproj)
    maps = []
    for i in range(NCORE):
        b, q = core_bq(i)
        maps.append({"hin": hc[i], "w": w_aug, "tab": make_tab(mods_l, b, (3, 4, None), [])})
    return [r["pout"] for r in _run(nc, maps)]


def run_merge(hc, yin, gin, mods_l, bw, ow, g, bta):
    nc = _get("merge", build_merge)
    maps = []
    for i in range(NCORE):
        b, q = core_bq(i)
        maps.append({"hin": hc[i], "yin": yin[i], "gin": gin[i], "bw": bw, "ow": ow,
                     "tab": make_tab(mods_l, b, (None, None, 5), [g, bta])})
    return [r["hout"] for r in _run(nc, maps)]


def kernel(x, c, ctx, c_ctx, ada_w, ada_b, ln_g, ln_b, ffn_w_in, ffn_w_out, mix_w_in, hgrn_lb, hgrn_norm_w,
           hyena_conv_w, hyena_conv_b, hyena_w1, hyena_b1, hyena_f1, hyena_w2, hyena_b2, hyena_f2, hyena_w3,
           hyena_bias, attn_sink, branch_w, out_w):
    f = lambda a: np.ascontiguousarray(np.asarray(a, dtype=np.float32))
    (x, c, ctx, c_ctx, ada_w, ada_b, ln_g, ln_b, ffn_w_in, ffn_w_out, mix_w_in, hgrn_lb, hgrn_norm_w, hyena_conv_w,
     hyena_conv_b, hyena_w1, hyena_b1, hyena_f1, hyena_w2, hyena_b2, hyena_f2, hyena_w3, hyena_bias, attn_sink,
     branch_w, out_w) = map(f, (x, c, ctx, c_ctx, ada_w, ada_b, ln_g, ln_b, ffn_w_in, ffn_w_out, mix_w_in, hgrn_lb,
                                hgrn_norm_w, hyena_conv_w, hyena_conv_b, hyena_w1, hyena_b1, hyena_f1, hyena_w2,
                                hyena_b2, hyena_f2, hyena_w3, hyena_bias, attn_sink, branch_w, out_w))
    mods = run_mods(c, c_ctx, ada_w, ada_b)
    hl, hx = x, ctx
    for l in range(2):
        hc = to_cores(hl, hx)
        h1c = run_ffn(hc, mods[l], (0, 1, 2), ffn_w_in[l, 0], ffn_w_out[l, 0], ln_g[l, 0], ln_b[l, 0])
        w = mix_w_in[l]
        w_aug = np.ascontiguousarray(np.concatenate([w, swap_cols(w[:, OFF_Q:OFF_K]), swap_cols(w[:, OFF_K:OFF_V])], 1))
        Pl, Pc = from_cores(run_inproj(h1c, mods[l], w_aug))
        ya, yca = run_hgrn(Pl, Pc, l, hgrn_lb, hgrn_norm_w[l])
        hy = (hyena_conv_w[l], hyena_conv_b[l], hyena_w1[l], hyena_b1[l], hyena_f1[l], hyena_w2[l], hyena_b2[l],
              hyena_f2[l], hyena_w3[l], hyena_bias[l])
        yb = run_hyfft(Pl[..., OFF_B:OFF_Q], *hy)
        ycb = run_hyena(Pc[..., OFF_B:OFF_Q], CTX, *hy) if l == 0 else np.zeros((2, CTX, 512), np.float32)
        yc, ycc = run_attn(Pl, Pc, attn_sink[l])
        yin = to_cores(np.concatenate([ya, yb, yc], -1), np.concatenate([yca, ycb, ycc], -1))
        gin = to_cores(Pl[..., OFF_G:OFF_QS], Pc[..., OFF_G:OFF_QS])
        h2c = run_merge(h1c, yin, gin, mods[l], np.ascontiguousarray(branch_w[l].reshape(1536, D)), out_w[l], ln_g[l, 1], ln_b[l, 1])
        h3c = run_ffn(h2c, mods[l], (6, 7, 8), ffn_w_in[l, 1], ffn_w_out[l, 1], ln_g[l, 2], ln_b[l, 2])
        hl, hx = from_cores(h3c)
    return hl.astype(np.float32)


LH = SEQ
NF = 2 * LH


def build_hyfft(stage=3):
    L = LH
    nc = bass.Bass("TRN2", target_bir_lowering=False)
    PP = 1024
    pbd = [nc.dram_tensor("pb%d" % i, [3, 128, L], F32, kind="ExternalInput").ap() for i in range(3)]
    cwd = nc.dram_tensor("cw", [128, 12], F32, kind="ExternalInput").ap()
    bsd = nc.dram_tensor("bias", [128, 1], F32, kind="ExternalInput").ap()
    ftd = nc.dram_tensor("feats", [2, 33, L], F32, kind="ExternalInput").ap()
    dcd = nc.dram_tensor("decay", [2, 128, L], F32, kind="ExternalInput").ap()
    w1d = nc.dram_tensor("w1", [33, 64], F32, kind="ExternalInput").ap()
    w2d = nc.dram_tensor("w2", [64, 64], F32, kind="ExternalInput").ap()
    w3d = nc.dram_tensor("w3", [64, 2, 128], F32, kind="ExternalInput").ap()
    bfd = nc.dram_tensor("bf", [64, 4], F32, kind="ExternalInput").ap()
    fad = nc.dram_tensor("fa", [128, 2, 2, 512], F32, kind="ExternalInput").ap()
    twd = nc.dram_tensor("tw", [128, 2, 512], F32, kind="ExternalInput").ap()
    ggd = nc.dram_tensor("gg", [128, 3, 128], F32, kind="ExternalInput").ap()
    ryd = nc.dram_tensor("ry", [128, 2, 256], F32, kind="ExternalInput").ap()
    itd = nc.dram_tensor("it", [128, 2, 2, 256], F32, kind="ExternalInput").ap()
    fid = nc.dram_tensor("fi", [128, 2, 3, 128], F32, kind="ExternalInput").ap()
    yo = nc.dram_tensor("yo", [128, L], F32, kind="ExternalOutput").ap()
    scr = nc.dram_tensor("scr", [3, 128, L], F32).ap()
    z1d = nc.dram_tensor("z1d", [128, L], F32).ap()
    circ = nc.dram_tensor("circ", [128, NF], F32).ap()
    Hs = nc.dram_tensor("Hs", [2, 128, 128, 256], F32).ap()
    CG = 16
    with ExitStack() as st:
        P = Prog(nc, st)
        cw = sb(nc, st, "cws", [128, 12], F32)
        bs = sb(nc, st, "bss", [128, 1], F32)
        w1 = sb(nc, st, "w1s", [33, 64], F32)
        w2 = sb(nc, st, "w2s", [64, 64], F32)
        w3 = sb(nc, st, "w3s", [64, 2, 128], F32)
        bf = sb(nc, st, "bfs", [64, 4], F32)
        fa = sb(nc, st, "fas", [128, 2, 2, 512], F32)
        tw = sb(nc, st, "tws", [128, 2, 512], F32)
        gg = sb(nc, st, "ggs", [128, 3, 128], F32)
        ry = sb(nc, st, "rys", [128, 2, 256], F32)
        itw = sb(nc, st, "its", [128, 2, 2, 256], F32)
        fi = sb(nc, st, "fis", [128, 2, 3, 128], F32)
        for t, dd in ((cw, cwd), (bs, bsd), (w1, w1d), (w2, w2d), (w3, w3d), (bf, bfd), (fa, fad), (tw, twd), (gg, ggd),
                      (ry, ryd), (itw, itd), (fi, fid)):
            P.dma("sp", t[:], dd, writes=["consts"])

        with ExitStack() as s1:
            ta = sb(nc, s1, "ta", [128, PP], F32)
            tb = sb(nc, s1, "tb", [128, PP], F32)
            tcx = sb(nc, s1, "tcx", [128, PP], F32)
            xs = [sb(nc, s1, "xs%d" % i, [128, PP], F32) for i in range(2)]
            ft = sb(nc, s1, "ft", [33, PP], F32)
            h1 = sb(nc, s1, "h1", [64, PP], F32)
            h2 = sb(nc, s1, "h2", [64, PP], F32)
            ki = sb(nc, s1, "ki", [64, PP], mybir.dt.int32)
            kf = sb(nc, s1, "kf", [64, PP], F32)
            dc = sb(nc, s1, "dc", [128, PP], F32)
            hp = [sb(nc, s1, "hp%d" % i, [128, PP], F32) for i in range(2)]
            l1 = sb(nc, s1, "l1", [128, 4], F32)
            zc = sb(nc, s1, "zc", [128, 1], F32)
            pm = ps(nc, s1, "pm", [64, 512])
            ph = [ps(nc, s1, "ph%d" % i, [128, 512]) for i in range(2)]
            xi = 0
            for part in range(3):
                for t0 in range(0, L, PP):
                    xb = xi % 2
                    xi += 1
                    P.dma("sp", ta[:], pbd[0][part, :, t0:t0 + PP], writes=["ta"])
                    P.dma("sp", tb[:], pbd[1][part, :, t0:t0 + PP], writes=["tb"])
                    P.dma("sp", tcx[:], pbd[2][part, :, t0:t0 + PP], writes=["tcx"])
                    c0 = part * 3
                    P.op("dve", lambda e, c0=c0, part=part: e.tensor_scalar(ta[:], ta[:], cw[:, c0:c0 + 1], cw[:, 9 + part:10 + part], ALU.mult, ALU.add),
                         reads=["ta", "consts"], writes=["ta"])
                    P.op("dve", lambda e, c0=c0: e.scalar_tensor_tensor(tb[:], tb[:], cw[:, c0 + 1:c0 + 2], ta[:], ALU.mult, ALU.add),
                         reads=["ta", "tb", "consts"], writes=["tb"])
                    P.op("dve", lambda e, c0=c0, xb=xb: e.scalar_tensor_tensor(xs[xb][:], tcx[:], cw[:, c0 + 2:c0 + 3], tb[:], ALU.mult, ALU.add),
                         reads=["tb", "tcx", "consts"], writes=[("xs", xb)])
                    P.dma("sp", scr[part, :, t0:t0 + PP], xs[xb][:], reads=[("xs", xb)], writes=["scr"])
            P.op("pool", lambda e: e.memset(l1[:], 0.0), writes=["l1"])
            P.op("pool", lambda e: e.memset(zc[:], 0.0), writes=["zc"])
            P.dma("sp", circ[:, L:L + 1], zc[:], reads=["zc"], writes=["circ"], allow_slow_non_contiguous=True)
            for pas in range(2):
                for dr in range(2):
                    for p0 in range(0, L, PP):
                        P.dma("sp", ft[:], ftd[dr, :, p0:p0 + PP], writes=["ft"])
                        P.dma("sp", dc[:], dcd[dr, :, p0:p0 + PP], writes=["dc"])
                        for (src, skey, w, K_, dst, dkey, bi) in ((ft, "ft", w1, 33, h1, "h1", 0), (h1, "h1", w2, 64, h2, "h2", 2)):
                            for q0 in range(0, PP, 512):
                                P.op("pe", lambda e, q0=q0, src=src, w=w, K_=K_: e.matmul(pm[:, :512], w[:K_, :], src[:K_, q0:q0 + 512], start=True, stop=True),
                                     reads=[skey, "consts"], writes=["pm"])
                                P.op("dve", lambda e, q0=q0, dst=dst, bi=bi: e.tensor_scalar(dst[:, q0:q0 + 512], pm[:, :512], bf[:, bi:bi + 1], bf[:, bi + 1:bi + 2], ALU.add, ALU.mult),
                                     reads=["pm", "consts"], writes=[dkey])
                            P.op("dve", lambda e, dst=dst: e.tensor_scalar(dst[:], dst[:], 1.0 / (2.0 * PI), 8.5, ALU.mult, ALU.add), reads=[dkey], writes=[dkey])
                            P.op("dve", lambda e, dst=dst: e.tensor_copy(ki[:], dst[:]), reads=[dkey], writes=["ki"])
                            P.op("dve", lambda e, dst=dst: e.tensor_copy(kf[:], ki[:]), reads=["ki"], writes=["kf"])
                            P.op("dve", lambda e, dst=dst: e.tensor_tensor(dst[:], dst[:], kf[:], ALU.subtract), reads=[dkey, "kf"], writes=[dkey])
                            P.op("dve", lambda e, dst=dst: e.tensor_single_scalar(kf[:], dst[:], 0.0, ALU.is_lt), reads=[dkey], writes=["kf"])
                            P.op("dve", lambda e, dst=dst: e.tensor_tensor(dst[:], dst[:], kf[:], ALU.add), reads=[dkey, "kf"], writes=[dkey])
                            P.op("dve", lambda e, dst=dst: e.tensor_scalar(dst[:], dst[:], 2.0 * PI, -PI, ALU.mult, ALU.add), reads=[dkey], writes=[dkey])
                            P.op("act", lambda e, dst=dst: e.activation(out=dst[:], in_=dst[:], func=AF.Sin), reads=[dkey], writes=[dkey])
                        hb = (p0 // PP) % 2
                        for q0 in range(0, PP, 512):
                            b = (q0 // 512) % 2
                            P.op("pe", lambda e, q0=q0, dr=dr, b=b: e.matmul(ph[b][:, :512], w3[:, dr, :], h2[:, q0:q0 + 512], start=True, stop=True),
                                 reads=["h2", "consts"], writes=[("ph", b)])
                            P.op("dve", lambda e, q0=q0, b=b, hb=hb: e.tensor_tensor(hp[hb][:, q0:q0 + 512], ph[b][:, :512], dc[:, q0:q0 + 512], ALU.mult),
                                 reads=[("ph", b), "dc"], writes=[("hp", hb)])
                        last = (dr == 1 and p0 + PP == L)
                        if pas == 0:
                            if last:
                                P.op("dve", lambda e, hb=hb: e.memset(hp[hb][:, PP - 1:PP], 0.0), reads=[("hp", hb)], writes=[("hp", hb)])
                            P.op("dve", lambda e, hb=hb: e.tensor_reduce(out=l1[:, 2:3], in_=hp[hb][:], axis=AX.X, op=ALU.add, apply_absolute_value=True),
                                 reads=[("hp", hb)], writes=["l1p"])
                            P.op("dve", lambda e: e.tensor_tensor(l1[:, 0:1], l1[:, 0:1], l1[:, 2:3], ALU.add), reads=["l1p", "l1"], writes=["l1"])
                        else:
                            P.op("dve", lambda e, hb=hb: e.tensor_scalar(hp[hb][:], hp[hb][:], l1[:, 1:2], None, ALU.mult), reads=[("hp", hb), "l1"], writes=[("hp", hb)])
                            if dr == 0 and p0 == 0:
                                P.op("dve", lambda e, hb=hb: e.tensor_tensor(hp[hb][:, 0:1], hp[hb][:, 0:1], bs[:, 0:1], ALU.add),
                                     reads=[("hp", hb), "consts"], writes=[("hp", hb)])
                            if dr == 0:
                                P.dma("sp", circ[:, p0:p0 + PP], hp[hb][:], reads=[("hp", hb)], writes=["circ"])
                            else:
                                n = PP - 1 if last else PP
                                P.dma("sp", circ[:, L + 1 + p0:L + 1 + p0 + n], hp[hb][:, :n], reads=[("hp", hb)], writes=["circ"])
                if pas == 0:
                    P.op("dve", lambda e: e.reciprocal(l1[:, 1:2], l1[:, 0:1]), reads=["l1"], writes=["l1"])

        with ExitStack() as s2:
            if stage < 2:
                P.finish("sp")
                return nc
            Xr = sb(nc, s2, "Xr", [128, 2, CG, 128], F32)
            Ar = sb(nc, s2, "Ar", [128, CG, 256], F32)
            Ai = sb(nc, s2, "Ai", [128, CG, 256], F32)
            Hr = sb(nc, s2, "Hr", [128, CG, 256], F32)
            Hi = sb(nc, s2, "Hi", [128, CG, 256], F32)
            Br = sb(nc, s2, "Br", [128, 2, CG, 128], F32)
            Bi = sb(nc, s2, "Bi", [128, 2, CG, 128], F32)
            U = [sb(nc, s2, "U%d" % i, [128, 512], F32) for i in range(2)]
            V = [sb(nc, s2, "V%d" % i, [128, 512], F32) for i in range(2)]
            xg = [sb(nc, s2, "xg%d" % i, [128, 2, 4, 128], F32) for i in range(2)]
            og = [sb(nc, s2, "og%d" % i, [128, 2, 4, 128], F32) for i in range(2)]
            pA = [ps(nc, s2, "pA%d" % i, [128, 512]) for i in range(2)]
            pCr = ps(nc, s2, "pCr", [128, 512])
            pCi = ps(nc, s2, "pCi", [128, 512])
            pI = [ps(nc, s2, "pI%d" % i, [128, 512]) for i in range(2)]
            pOr = ps(nc, s2, "pOr", [128, 512])
            pOi = ps(nc, s2, "pOi", [128, 512])
            cnt = [0]

            def fwd_A_and_twiddle(c, real_only):
                b = cnt[0] % 2
                cnt[0] += 1
                if real_only:
                    P.op("pe", lambda e: e.matmul(pA[b][:], Xr[:, 0, c, :], fa[:, 0, 0, :], start=True, stop=False), reads=["X", "consts"], writes=[("pA", b)])
                    P.op("pe", lambda e: e.matmul(pA[b][:], Xr[:, 1, c, :], fa[:, 1, 0, :], start=False, stop=True), reads=["X", "consts"], writes=[("pA", b)])
                else:
                    P.op("pe", lambda e: e.matmul(pA[b][:], Xr[:, 0, c, :], fa[:, 0, 0, :], start=True, stop=False), reads=["X", "consts"], writes=[("pA", b)])
                    P.op("pe", lambda e: e.matmul(pA[b][:], Xr[:, 1, c, :], fa[:, 0, 1, :], start=False, stop=True), reads=["X", "consts"], writes=[("pA", b)])
                P.op("dve", lambda e: e.tensor_tensor(U[b][:], pA[b][:], tw[:, 0, :], ALU.mult), reads=[("pA", b), "consts"], writes=[("U", b)])
                P.op("dve", lambda e: e.tensor_tensor(V[b][:, 0:256], pA[b][:, 256:512], tw[:, 1, 0:256], ALU.mult), reads=[("pA", b), "consts"], writes=[("V", b)])
                P.op("dve", lambda e: e.tensor_tensor(V[b][:, 256:512], pA[b][:, 0:256], tw[:, 1, 256:512], ALU.mult), reads=[("pA", b), "consts"], writes=[("V", b)])
                P.op("pool", lambda e: e.tensor_tensor(Ar[:, c, :], U[b][:, 0:256], V[b][:, 0:256], ALU.add), reads=[("U", b), ("V", b)], writes=[("A", c // 2)])
                P.op("pool", lambda e: e.tensor_tensor(Ai[:, c, :], U[b][:, 256:512], V[b][:, 256:512], ALU.add), reads=[("U", b), ("V", b)], writes=[("A", c // 2)])

            def fwd_C(j):
                ar = Ar[:, 2 * j:2 * j + 2, :]
                ai = Ai[:, 2 * j:2 * j + 2, :]
                P.op("pe", lambda e: e.matmul(pCr[:], gg[:, 0, :], ar, start=True, stop=False), reads=[("A", j), "consts"], writes=["pCr"])
                P.op("pe", lambda e: e.matmul(pCr[:], gg[:, 2, :], ai, start=False, stop=True), reads=[("A", j), "consts"], writes=["pCr"])
                P.op("pe", lambda e: e.matmul(pCi[:], gg[:, 1, :], ar, start=True, stop=False), reads=[("A", j), "consts"], writes=["pCi"])
                P.op("pe", lambda e: e.matmul(pCi[:], gg[:, 0, :], ai, start=False, stop=True), reads=[("A", j), "consts"], writes=["pCi"])

            for g0 in range(0, 128, CG):
                for blk in range(2):
                    src = circ[g0:g0 + CG, blk * L:(blk + 1) * L].rearrange("c (p n) -> p c n", n=128)
                    P.dma("sp", Xr[:, blk, :, :], src, reads=["circ"], writes=["X"])
                for c in range(CG):
                    fwd_A_and_twiddle(c, True)
                for j in range(CG // 2):
                    fwd_C(j)
                    P.op("act", lambda e, j=j: e.copy(out=Hr[:, 2 * j:2 * j + 2, :], in_=pCr[:]), reads=["pCr"], writes=[("H", j)])
                    P.op("act", lambda e, j=j: e.copy(out=Hi[:, 2 * j:2 * j + 2, :], in_=pCi[:]), reads=["pCi"], writes=[("H", j)])
                hk = [("H", j) for j in range(CG // 2)]
                P.dma("sp", Hs[0, :, g0:g0 + CG, :], Hr[:], reads=hk, writes=["Hs"])
                P.dma("sp", Hs[1, :, g0:g0 + CG, :], Hi[:], reads=hk, writes=["Hs"])

            for o in range(2 if stage >= 3 else 0):
                zsrc = scr[0] if o == 0 else z1d
                xsrc = scr[1 + o]
                zdst = z1d if o == 0 else yo
                for g0 in range(0, 64, CG):
                    for b in range(2):
                        src = zsrc[b * 64 + g0:b * 64 + g0 + CG, :].rearrange("c (p n) -> p c n", n=128)
                        P.dma("sp", Xr[:, b, :, :], src, reads=["scr", "z1d"], writes=["X"])
                    for ri in range(2):
                        P.dma("sp", (Hr if ri == 0 else Hi)[:], Hs[ri, :, o * 64 + g0:o * 64 + g0 + CG, :], reads=["Hs"],
                              writes=[("H", j) for j in range(CG // 2)])
                    for c in range(CG):
                        fwd_A_and_twiddle(c, False)
                    for j in range(CG // 2):
                        fwd_C(j)
                        b = j % 2
                        hr = Hr[:, 2 * j:2 * j + 2, :]
                        hi = Hi[:, 2 * j:2 * j + 2, :]
                        yr = Ar[:, 2 * j:2 * j + 2, :]
                        yi = Ai[:, 2 * j:2 * j + 2, :]
                        P.op("dve", lambda e, b=b, hr=hr: e.tensor_tensor(U[b][:], pCr[:], hr, ALU.mult), reads=["pCr", ("H", j)], writes=[("U", b)])
                        P.op("dve", lambda e, b=b, hi=hi: e.tensor_tensor(V[b][:], pCi[:], hi, ALU.mult), reads=["pCi", ("H", j)], writes=[("V", b)])
                        P.op("pool", lambda e, b=b, yr=yr: e.tensor_tensor(yr, U[b][:], V[b][:], ALU.subtract), reads=[("U", b), ("V", b)], writes=[("A", j)])
                        P.op("dve", lambda e, b=b, hi=hi: e.tensor_tensor(U[b][:], pCr[:], hi, ALU.mult), reads=["pCr", ("H", j)], writes=[("U", b)])
                        P.op("dve", lambda e, b=b, hr=hr: e.tensor_tensor(V[b][:], pCi[:], hr, ALU.mult), reads=["pCi", ("H", j)], writes=[("V", b)])
                        P.op("pool", lambda e, b=b, yi=yi: e.tensor_tensor(yi, U[b][:], V[b][:], ALU.add), reads=[("U", b), ("V", b)], writes=[("A", j)])
                    for c in range(CG):
                        b = c % 2
                        for blk in range(2):
                            P.op("pe", lambda e, c=c, blk=blk, b=b: e.matmul(pI[b][:, blk * 256:(blk + 1) * 256], Ar[:, c, blk * 128:(blk + 1) * 128], ry[:, 0, :],
                                                                             start=True, stop=False), reads=[("A", c // 2), "consts"], writes=[("pI", b)])
                            P.op("pe", lambda e, c=c, blk=blk, b=b: e.matmul(pI[b][:, blk * 256:(blk + 1) * 256], Ai[:, c, blk * 128:(blk + 1) * 128], ry[:, 1, :],
                                                                             start=False, stop=True), reads=[("A", c // 2), "consts"], writes=[("pI", b)])
                        for blk in range(2):
                            lo, mid, hi_ = blk * 256, blk * 256 + 128, blk * 256 + 256
                            P.op("dve", lambda e, b=b, blk=blk, lo=lo, hi_=hi_: e.tensor_tensor(U[b][:, lo:hi_], pI[b][:, lo:hi_], itw[:, blk, 0, :], ALU.mult),
                                 reads=[("pI", b), "consts"], writes=[("U", b)])
                            P.op("dve", lambda e, b=b, blk=blk, lo=lo, mid=mid, hi_=hi_: e.tensor_tensor(V[b][:, lo:mid], pI[b][:, mid:hi_], itw[:, blk, 1, 0:128], ALU.mult),
                                 reads=[("pI", b), "consts"], writes=[("V", b)])
                            P.op("dve", lambda e, b=b, blk=blk, lo=lo, mid=mid, hi_=hi_: e.tensor_tensor(V[b][:, mid:hi_], pI[b][:, lo:mid], itw[:, blk, 1, 128:256], ALU.mult),
                                 reads=[("pI", b), "consts"], writes=[("V", b)])
                            P.op("pool", lambda e, b=b, blk=blk, c=c, lo=lo, mid=mid: e.tensor_tensor(Br[:, blk, c, :], U[b][:, lo:mid], V[b][:, lo:mid], ALU.add),
                                 reads=[("U", b), ("V", b)], writes=[("B", c // 4)])
                            P.op("pool", lambda e, b=b, blk=blk, c=c, mid=mid, hi_=hi_: e.tensor_tensor(Bi[:, blk, c, :], U[b][:, mid:hi_], V[b][:, mid:hi_], ALU.add),
                                 reads=[("U", b), ("V", b)], writes=[("B", c // 4)])
                    for q in range(CG // 4):
                        ob = q % 2
                        cs = slice(4 * q, 4 * q + 4)
                        for b in range(2):
                            src = xsrc[b * 64 + g0 + 4 * q:b * 64 + g0 + 4 * q + 4, :].rearrange("c (p n) -> p c n", n=128)
                            P.dma("sp", xg[ob][:, b, :, :], src, reads=["scr"], writes=[("xg", ob)])
                        for blk in range(2):
                            P.op("pe", lambda e, blk=blk, cs=cs: e.matmul(pOr[:], fi[:, blk, 0, :], Br[:, blk, cs, :], start=(blk == 0), stop=False),
                                 reads=[("B", q), "consts"], writes=["pOr"])
                            P.op("pe", lambda e, blk=blk, cs=cs: e.matmul(pOr[:], fi[:, blk, 2, :], Bi[:, blk, cs, :], start=False, stop=(blk == 1)),
                                 reads=[("B", q), "consts"], writes=["pOr"])
                        for blk in range(2):
                            P.op("pe", lambda e, blk=blk, cs=cs: e.matmul(pOi[:], fi[:, blk, 1, :], Br[:, blk, cs, :], start=(blk == 0), stop=False),
                                 reads=[("B", q), "consts"], writes=["pOi"])
                            P.op("pe", lambda e, blk=blk, cs=cs: e.matmul(pOi[:], fi[:, blk, 0, :], Bi[:, blk, cs, :], start=False, stop=(blk == 1)),
                                 reads=[("B", q), "consts"], writes=["pOi"])
                        P.op("dve", lambda e, ob=ob: e.tensor_tensor(og[ob][:, 0, :, :], pOr[:], xg[ob][:, 0, :, :], ALU.mult), reads=["pOr", ("xg", ob)], writes=[("og", ob)])
                        P.op("dve", lambda e, ob=ob: e.tensor_tensor(og[ob][:, 1, :, :], pOi[:], xg[ob][:, 1, :, :], ALU.mult), reads=["pOi", ("xg", ob)], writes=[("og", ob)])
                        for b in range(2):
                            dst = zdst[b * 64 + g0 + 4 * q:b * 64 + g0 + 4 * q + 4, :].rearrange("c (p n) -> p c n", n=128)
                            P.dma("sp", dst, og[ob][:, b, :, :], reads=[("og", ob)], writes=["z1d" if o == 0 else "yo"])
        P.finish("sp")
    return nc


def hyfft_consts():
    N = NF
    n1 = np.arange(256, dtype=np.float64)
    k1 = np.arange(256, dtype=np.float64)
    n2 = np.arange(128, dtype=np.float64)
    k2 = np.arange(128, dtype=np.float64)
    a = 2 * np.pi * np.outer(n1, k1) / 256
    Fc, Fs = np.cos(a), np.sin(a)
    fa = np.zeros((256, 2, 512))
    fa[:, 0, :256], fa[:, 0, 256:] = Fc, -Fs
    fa[:, 1, :256], fa[:, 1, 256:] = Fs, Fc
    fa = fa.reshape(2, 128, 2, 512).transpose(1, 0, 2, 3)
    t = 2 * np.pi * np.outer(n2, k1) / N
    Tr, Ti = np.cos(t), -np.sin(t)
    tw = np.stack([np.concatenate([Tr, Tr], 1), np.concatenate([-Ti, Ti], 1)], 1)
    g = 2 * np.pi * np.outer(n2, k2) / 128
    Gr, Gi = np.cos(g), -np.sin(g)
    gg = np.stack([Gr, Gi, -Gi], 1)
    ry = np.stack([np.concatenate([Gr, -Gi], 1), np.concatenate([Gi, Gr], 1)], 1)
    tt = 2 * np.pi * np.outer(k1, n2) / N
    cTr, cTi = np.cos(tt), np.sin(tt)
    it = np.stack([np.concatenate([cTr, cTr], 1), np.concatenate([-cTi, cTi], 1)], 1)
    it = it.reshape(2, 128, 2, 256).transpose(1, 0, 2, 3)
    ai = 2 * np.pi * np.outer(k1, n1[:128]) / 256
    fi = np.stack([np.cos(ai), np.sin(ai), -np.sin(ai)], 1) / N
    fi = fi.reshape(2, 128, 3, 128).transpose(1, 0, 2, 3)
    f32 = lambda x: np.ascontiguousarray(x.astype(np.float32))
    return {"fa": f32(fa), "tw": f32(tw), "gg": f32(gg), "ry": f32(ry), "it": f32(it), "fi": f32(fi)}


def run_hyfft(X, cw, cb, w1, b1, f1, w2, b2, f2, w3, bias):
    L = LH
    import os
    nc = _get("hyfft", build_hyfft, int(os.environ.get("HYFFT_STAGE", "3")))
    consts = _get("hyfft_consts", hyfft_consts)
    pos = np.arange(L, dtype=np.float32)
    t = pos / np.float32(L - 1)
    wv = np.float32(2.0 * np.pi) * pos / np.float32(L)
    bands = np.linspace(1e-4, 15, 16, dtype=np.float32)
    feats = np.concatenate([t[:, None], np.cos(wv[:, None] * bands), -np.sin(wv[:, None] * bands)], -1).astype(np.float32).T
    rates = np.abs(np.linspace(np.log(1e-2) / 1.5, np.log(1e-2) / 0.3, 512, dtype=np.float32))
    decay = np.exp(-t[None, :] * rates[:, None]).astype(np.float32)
    Xp = np.pad(X, ((0, 0), (1, 1), (0, 0)))
    feats2 = np.ascontiguousarray(np.stack([feats, feats[:, ::-1]], 0))
    maps = []
    for i in range(NCORE):
        cs = np.arange(i * 64, (i + 1) * 64)

        def rows(a):
            return np.ascontiguousarray(np.stack([a[:, :, p * 512 + cs].transpose(0, 2, 1).reshape(128, L) for p in range(3)], 0))
        cwt = np.concatenate([np.stack([cw[tp, p * 512 + cs] for p in range(3) for tp in range(3)], 1),
                              np.stack([cb[p * 512 + cs] for p in range(3)], 1)], 1)
        w3r = w3.reshape(64, 2, 2, 512)[:, :, :, cs].reshape(64, 2, 128)
        dco = np.concatenate([decay[cs], decay[cs]], 0)
        m = {
            "pb0": rows(Xp[:, 0:L]), "pb1": rows(Xp[:, 1:L + 1]), "pb2": rows(Xp[:, 2:L + 2]),
            "cw": np.ascontiguousarray(np.concatenate([cwt, cwt], 0)).astype(np.float32),
            "bias": np.ascontiguousarray(bias[:, cs].reshape(128, 1)).astype(np.float32),
            "feats": feats2, "decay": np.ascontiguousarray(np.stack([dco, dco[:, ::-1]], 0)),
            "w1": np.ascontiguousarray(w1), "w2": np.ascontiguousarray(w2), "w3": np.ascontiguousarray(w3r).astype(np.float32),
            "bf": np.ascontiguousarray(np.stack([b1, f1, b2, f2], 1)).astype(np.float32),
        }
        m.update(consts)
        maps.append(m)
    res = _run(nc, maps)
    out = np.empty((2, L, 512), np.float32)
    for i in range(NCORE):
        out[:, :, i * 64:(i + 1) * 64] = res[i]["yo"].reshape(2, 64, L).transpose(0, 2, 1)
    return out
```

```python
import numpy as np
from contextlib import ExitStack
import concourse.bass as bass
import concourse.mybir as mybir
from concourse.bass_utils import run_bass_kernel_spmd

F32 = mybir.dt.float32
BF16 = mybir.dt.bfloat16
AF = mybir.ActivationFunctionType
ALU = mybir.AluOpType
AX = mybir.AxisListType

D = 1024
DFF = 2816
SEQ = 16384
CTX = 256
NCORE = 8
TLAT = 4096
TCTX = 64
TT = TLAT + TCTX
DN_ALPHA = 4.0 ** 0.25
LN_EPS = 1e-5
RMS_EPS = 1e-6
NQK = 8576

EPOCH = 12000
NDMASEM = 8


class Prog:
    def __init__(self, nc, stack):
        self.nc = nc
        self.stack = stack
        self.eng = {"pe": nc.tensor, "act": nc.scalar, "dve": nc.vector,
                    "pool": nc.gpsimd, "sp": nc.sync}
        self.cnt = {e: 0 for e in self.eng}
        self.sems = {}
        self.lastw = {}
        self.readers = {}
        self.seen = {e: {} for e in self.eng}
        self.dcnt = {e: 0 for e in self.eng}
        self.dpend = {}
        self.lastw_dma = {}

    def _sem(self, name):
        if name not in self.sems:
            self.sems[name] = self.stack.enter_context(self.nc.semaphore(name))
        return self.sems[name]

    def _wait(self, e, tok):
        name, val = tok
        if self.seen[e].get(name, 0) >= val:
            return
        self.eng[e].wait_ge(self._sem(name), val)
        self.seen[e][name] = val

    def _deps(self, reads, writes):
        deps = []
        for k in list(reads) + list(writes):
            if k in self.lastw:
                deps.append(self.lastw[k])
            if k in self.lastw_dma:
                deps.extend(self.lastw_dma[k].values())
        for k in writes:
            deps.extend(self.readers.get(k, []))
        return deps

    def _commit(self, tok, reads, writes, is_dma=False):
        for k in reads:
            self.readers.setdefault(k, []).append(tok)
        for k in writes:
            if is_dma and k in self.lastw_dma:
                self.lastw_dma[k][tok[0]] = tok
            elif is_dma:
                self.lastw_dma[k] = {tok[0]: tok}
            else:
                self.lastw_dma.pop(k, None)
            self.lastw[k] = tok
            self.readers[k] = []

    def op(self, e, fn, reads=(), writes=(), sig=True):
        ep, v = divmod(self.cnt[e], EPOCH)
        own = "c_%s_%d" % (e, ep)
        for tok in self._deps(reads, writes):
            if tok[0] == own and tok[1] > v:
                continue
            self._wait(e, tok)
        ins = fn(self.eng[e])
        tok = ("c_%s_%d" % (e, ep), v + 1)
        if sig:
            self.cnt[e] += 1
            ins.then_inc(self._sem(tok[0]), 1)
        self._commit(tok, reads, writes)
        return tok

    def dma(self, e, out, in_, reads=(), writes=(), **kw):
        for tok in self._deps(reads, writes):
            self._wait(e, tok)
        i = self.dcnt[e]
        self.dcnt[e] += 1
        name = "d_%s_%d" % (e, i % NDMASEM)
        prev = self.dpend.get(name)
        if prev is not None:
            self._wait(e, prev)
        ins = self.eng[e].dma_start(out=out, in_=in_, **kw)
        ins.then_inc(self._sem(name), 16)
        tok = (name, 16 * (i // NDMASEM + 1))
        self.dpend[name] = tok
        self._commit(tok, reads, writes, is_dma=True)
        return tok

    def finish(self, e="sp"):
        for tok in set(self.lastw.values()):
            self._wait(e, tok)
        for tok in self.dpend.values():
            self._wait(e, tok)


def sb(nc, st, name, shape, dt):
    return st.enter_context(nc.sbuf_tensor(name, shape, dt))


def ps(nc, st, name, shape, dt=F32):
    return st.enter_context(nc.psum_tensor(name, shape, dt))


def load_w(P, nc, st, name, dram, K, N, eng="pool"):
    kc = K // 128
    t = sb(nc, st, name, [128, kc, N], BF16)
    v = dram.rearrange("(k p) n -> p k n", p=128)
    for k in range(kc):
        P.dma(eng, t[:, k, :], v[:, k, :], writes=[(name, k)])
    return t


def wkeys(name, n):
    return [(name, k) for k in range(n)]


def token_tiles(tsz):
    tiles = []
    t = 0
    while t < TLAT:
        n = min(tsz, TLAT - t)
        tiles.append((t, n, 0))
        t += n
    tiles.append((TLAT, TCTX, 1))
    return tiles


class Common:
    def __init__(self, nc, st, P, tsz):
        self.nc, self.st, self.P, self.tsz = nc, st, P, tsz
        self.ones = sb(nc, st, "ones", [128, 128], F32)
        P.op("pool", lambda e: e.memset(self.ones[:], 1.0), writes=["ones"])
        self.r = sb(nc, st, "r", [128, 8, tsz], F32)
        self.sq = sb(nc, st, "sq", [128, 8, tsz], F32)
        self.t1 = sb(nc, st, "t1", [128, 2, tsz], F32)
        self.t2 = sb(nc, st, "t2", [128, 2, tsz], F32)
        self.stt = sb(nc, st, "stt", [128, 4, tsz], F32)
        self.ps1 = ps(nc, st, "ps1", [128, 512])
        self.ps2 = ps(nc, st, "ps2", [128, 512])

    def layer_norm(self, tn, gam, bet, out, outkey):
        P, r, sq = self.P, self.r, self.sq
        for d in range(8):
            P.op("act", lambda e, d=d: e.activation(out=sq[:, d, :tn], in_=r[:, d, :tn], func=AF.Square),
                 reads=[("r", d)], writes=[("sq", d)])
        for d in range(8):
            P.op("pe", lambda e, d=d: e.matmul(self.ps1[:, :tn], self.ones[:], r[:, d, :tn], start=(d == 0), stop=(d == 7)),
                 reads=[("r", d), "ones"], writes=["ps1"], sig=(d == 7))
        for d in range(8):
            P.op("pe", lambda e, d=d: e.matmul(self.ps2[:, :tn], self.ones[:], sq[:, d, :tn], start=(d == 0), stop=(d == 7)),
                 reads=[("sq", d), "ones"], writes=["ps2"], sig=(d == 7))
        mean, msq, var, rstd = (self.stt[:, i, :tn] for i in range(4))
        P.op("dve", lambda e: e.tensor_single_scalar(mean, self.ps1[:, :tn], 1.0 / D, ALU.mult), reads=["ps1"], writes=["mean"])
        P.op("dve", lambda e: e.tensor_tensor(msq, mean, mean, ALU.mult), reads=["mean"], writes=["msq"])
        P.op("dve", lambda e: e.scalar_tensor_tensor(var, self.ps2[:, :tn], 1.0 / D, msq, ALU.mult, ALU.subtract),
             reads=["ps2", "msq"], writes=["var"])
        P.op("dve", lambda e: e.tensor_single_scalar(var, var, LN_EPS, ALU.add), reads=["var"], writes=["var"])
        P.op("act", lambda e: e.sqrt(out=msq, in_=var), reads=["var"], writes=["msq"])
        P.op("dve", lambda e: e.reciprocal(rstd, msq), reads=["msq"], writes=["rstd"])
        for d in range(8):
            b = d % 2
            P.op("dve", lambda e, d=d, b=b: e.tensor_tensor(self.t1[:, b, :tn], r[:, d, :tn], mean, ALU.subtract),
                 reads=[("r", d), "mean"], writes=[("t1", b)])
            P.op("pool", lambda e, d=d, b=b: e.tensor_tensor(self.t2[:, b, :tn], self.t1[:, b, :tn], rstd, ALU.mult),
                 reads=[("t1", b), "rstd"], writes=[("t2", b)])
            P.op("act", lambda e, d=d, b=b: e.activation(out=out[:, d, :tn], in_=self.t2[:, b, :tn], func=AF.Identity,
                                                         scale=gam[:, d:d + 1], bias=bet[:, d:d + 1]),
                 reads=[("t2", b), "tab"], writes=[(outkey, d)])


def emit_modulate(P, tn, src, srckey, dst, dstkey, sc1p, sh):
    for c in range(8):
        P.op("pool", lambda e, c=c: e.tensor_scalar(dst[:, c, :tn], src[:, c, :tn], sc1p[:, c:c + 1], sh[:, c:c + 1],
                                                    ALU.mult, ALU.add),
             reads=[(srckey, c), "tab"], writes=[(dstkey, c)])


NC0 = 2 * 9216 // NCORE


def build_k0():
    nc = bass.Bass("TRN2", target_bir_lowering=False)
    cT = nc.dram_tensor("cT", [D, 3], F32, kind="ExternalInput").ap()
    aw = nc.dram_tensor("aw", [D, NC0], F32, kind="ExternalInput").ap()
    ab = nc.dram_tensor("ab", [128, NC0 // 128], F32, kind="ExternalInput").ap()
    out = nc.dram_tensor("out", [NC0, 3], F32, kind="ExternalOutput").ap()
    nj = NC0 // 128
    with ExitStack() as st:
        P = Prog(nc, st)
        ct = sb(nc, st, "ct", [128, 8, 3], F32)
        stt = sb(nc, st, "st", [128, 8, 3], F32)
        abt = sb(nc, st, "abt", [128, nj], F32)
        ot = sb(nc, st, "ot", [128, nj, 3], F32)
        pp = ps(nc, st, "pp", [128, nj, 4])
        P.dma("sp", ct[:], cT.rearrange("(k p) n -> p k n", p=128), writes=["ct"])
        P.dma("sp", abt[:], ab, writes=["abt"])
        P.op("act", lambda e: e.activation(out=stt[:], in_=ct[:], func=AF.Silu), reads=["ct"], writes=["st"])
        awv = aw.rearrange("(k p) n -> p k n", p=128)
        npc = 3
        cw = NC0 // npc
        wt = [sb(nc, st, "wt%d" % i, [128, 8, cw], F32) for i in range(npc)]
        for i in range(npc):
            for k in range(8):
                P.dma("sp", wt[i][:, k, :], awv[:, k, i * cw:(i + 1) * cw], writes=[("wt", i, k)])
        for j in range(nj):
            i, jj = divmod(j * 128, cw)
            for k in range(8):
                P.op("pe", lambda e, i=i, jj=jj, k=k, j=j: e.matmul(pp[:, j, 0:3], wt[i][:, k, jj:jj + 128], stt[:, k, :],
                                                                  start=(k == 0), stop=(k == 7)),
                     reads=[("wt", i, k), "st"], writes=[("pp", j)])
            P.op("dve", lambda e, j=j: e.tensor_scalar(ot[:, j, :], pp[:, j, 0:3], abt[:, j:j + 1], None, ALU.add),
                 reads=[("pp", j), "abt"], writes=["ot"])
        P.dma("sp", out.rearrange("(j p) n -> p j n", p=128), ot[:], reads=["ot"], writes=["out"])
        P.finish("sp")
    return nc


def prep_tab(P, nc, st, tab_d, ncols, gmul):
    tab = sb(nc, st, "tabs", [128, ncols * 8], F32)
    P.dma("sp", tab[:], tab_d, writes=["tab0"])
    sc1p, sh, gt = [], [], []
    for ms in range(2):
        o = ms * 24
        P.op("dve", lambda e, o=o: e.tensor_single_scalar(tab[:, o + 8:o + 16], tab[:, o + 8:o + 16], 1.0, ALU.add),
             reads=["tab0"], writes=["tab"])
        P.op("dve", lambda e, o=o: e.tensor_single_scalar(tab[:, o + 16:o + 24], tab[:, o + 16:o + 24], gmul, ALU.mult),
             reads=["tab0"], writes=["tab"])
        sh.append(tab[:, o:o + 8])
        sc1p.append(tab[:, o + 8:o + 16])
        gt.append(tab[:, o + 16:o + 24])
    return tab, sh, sc1p, gt


def build_ffn(tsz=256):
    nc = bass.Bass("TRN2", target_bir_lowering=False)
    hin = nc.dram_tensor("hin", [D, TT], F32, kind="ExternalInput").ap()
    w1d = nc.dram_tensor("w1", [D, 2 * DFF], F32, kind="ExternalInput").ap()
    w2d = nc.dram_tensor("w2", [DFF, D], F32, kind="ExternalInput").ap()
    tabd = nc.dram_tensor("tab", [128, 64], F32, kind="ExternalInput").ap()
    hout = nc.dram_tensor("hout", [D, TT], F32, kind="ExternalOutput").ap()
    hv = hin.rearrange("(c p) t -> p c t", p=128)
    ov = hout.rearrange("(c p) t -> p c t", p=128)
    NJ = DFF // 128
    with ExitStack() as st:
        P = Prog(nc, st)
        tab, sh, sc1p, gt = prep_tab(P, nc, st, tabd, 8, 0.5)
        gam, bet = tab[:, 48:56], tab[:, 56:64]
        w1 = load_w(P, nc, st, "w1s", w1d, D, 2 * DFF)
        w2 = load_w(P, nc, st, "w2s", w2d, DFF, D)
        C = Common(nc, st, P, tsz)
        hT = sb(nc, st, "hT", [128, 8, tsz], F32)
        hA = sb(nc, st, "hA", [128, 8, tsz], F32)
        uT = sb(nc, st, "uT", [128, 8, tsz], BF16)
        gT = sb(nc, st, "gT", [128, NJ, tsz], BF16)
        sa = sb(nc, st, "sa", [128, 2, tsz], F32)
        psA = [ps(nc, st, "psA%d" % i, [128, 512]) for i in range(2)]
        psB = [ps(nc, st, "psB%d" % i, [128, 512]) for i in range(2)]
        psY = [ps(nc, st, "psY%d" % i, [128, 512]) for i in range(2)]
        for (t0, tn, ms) in token_tiles(tsz):
            for c in range(8):
                P.dma("sp", hT[:, c, :tn], hv[:, c, t0:t0 + tn], writes=[("hT", c)])
            emit_modulate(P, tn, hT, "hT", uT, "uT", sc1p[ms], sh[ms])
            for c in range(8):
                P.op("pool", lambda e, c=c: e.tensor_single_scalar(hA[:, c, :tn], hT[:, c, :tn], DN_ALPHA, ALU.mult),
                     reads=[("hT", c)], writes=[("hA", c)])
            for j in range(NJ):
                b = j % 2
                for k in range(8):
                    P.op("pe", lambda e, j=j, k=k, b=b: e.matmul(psA[b][:, :tn], w1[:, k, j * 128:(j + 1) * 128], uT[:, k, :tn],
                                                                start=(k == 0), stop=(k == 7)),
                         reads=[("w1s", k), ("uT", k)], writes=[("psA", b)], sig=(k == 7))
                for k in range(8):
                    P.op("pe", lambda e, j=j, k=k, b=b: e.matmul(psB[b][:, :tn], w1[:, k, DFF + j * 128:DFF + (j + 1) * 128],
                                                                uT[:, k, :tn], start=(k == 0), stop=(k == 7)),
                         reads=[("w1s", k), ("uT", k)], writes=[("psB", b)], sig=(k == 7))
                P.op("act", lambda e, b=b: e.activation(out=sa[:, b, :tn], in_=psA[b][:, :tn], func=AF.Silu),
                     reads=[("psA", b)], writes=[("sa", b)])
                P.op("dve", lambda e, j=j, b=b: e.tensor_tensor(gT[:, j, :tn], sa[:, b, :tn], psB[b][:, :tn], ALU.mult),
                     reads=[("sa", b), ("psB", b)], writes=[("gT", j)])
            for d in range(8):
                b = d % 2
                for j in range(NJ):
                    P.op("pe", lambda e, j=j, d=d, b=b: e.matmul(psY[b][:, :tn], w2[:, j, d * 128:(d + 1) * 128], gT[:, j, :tn],
                                                                start=(j == 0), stop=(j == NJ - 1)),
                         reads=[("w2s", j), ("gT", j)], writes=[("psY", b)], sig=(j == NJ - 1))
                P.op("dve", lambda e, d=d, b=b: e.scalar_tensor_tensor(C.r[:, d, :tn], psY[b][:, :tn], gt[ms][:, d:d + 1],
                                                                      hA[:, d, :tn], ALU.mult, ALU.add),
                     reads=[("psY", b), ("hA", d), "tab"], writes=[("r", d)])
            C.layer_norm(tn, gam, bet, hT, "hT")
            for c in range(8):
                P.dma("sp", ov[:, c, t0:t0 + tn], hT[:, c, :tn], reads=[("hT", c)], writes=["hout"])
        P.finish("sp")
    return nc


def build_inproj(tsz=256):
    nc = bass.Bass("TRN2", target_bir_lowering=False)
    hin = nc.dram_tensor("hin", [D, TT], F32, kind="ExternalInput").ap()
    wd = nc.dram_tensor("w", [D, NQK], F32, kind="ExternalInput").ap()
    tabd = nc.dram_tensor("tab", [128, 48], F32, kind="ExternalInput").ap()
    pout = nc.dram_tensor("pout", [NQK, TT], F32, kind="ExternalOutput").ap()
    hv = hin.rearrange("(c p) t -> p c t", p=128)
    pv = pout.rearrange("(c p) t -> p c t", p=128)
    NO = NQK // 128
    G = 4
    with ExitStack() as st:
        P = Prog(nc, st)
        tab, sh, sc1p, gt = prep_tab(P, nc, st, tabd, 6, 1.0)
        w = load_w(P, nc, st, "ws", wd, D, NQK)
        hT = sb(nc, st, "hT", [128, 8, tsz], F32)
        uT = sb(nc, st, "uT", [128, 8, tsz], BF16)
        og = [sb(nc, st, "og%d" % i, [128, G, tsz], F32) for i in range(2)]
        pp = [ps(nc, st, "pp%d" % i, [128, 512]) for i in range(4)]
        gi = 0
        for (t0, tn, ms) in token_tiles(tsz):
            for c in range(8):
                P.dma("sp", hT[:, c, :tn], hv[:, c, t0:t0 + tn], writes=[("hT", c)])
            emit_modulate(P, tn, hT, "hT", uT, "uT", sc1p[ms], sh[ms])
            for o0 in range(0, NO, G):
                gn = min(G, NO - o0)
                ob = gi % 2
                gi += 1
                for g in range(gn):
                    o = o0 + g
                    b = o % 4
                    for k in range(8):
                        P.op("pe", lambda e, o=o, k=k, b=b: e.matmul(pp[b][:, :tn], w[:, k, o * 128:(o + 1) * 128], uT[:, k, :tn],
                                                                    start=(k == 0), stop=(k == 7)),
                             reads=[("ws", k), ("uT", k)], writes=[("pp", b)], sig=(k == 7))
                    if o % 2 == 0:
                        P.op("act", lambda e, g=g, b=b, ob=ob: e.copy(out=og[ob][:, g, :tn], in_=pp[b][:, :tn]),
                             reads=[("pp", b)], writes=[("og", ob)])
                    else:
                        P.op("dve", lambda e, g=g, b=b, ob=ob: e.tensor_copy(og[ob][:, g, :tn], pp[b][:, :tn]),
                             reads=[("pp", b)], writes=[("og", ob)])
                P.dma("sp", pv[:, o0:o0 + gn, t0:t0 + tn], og[ob][:, :gn, :tn], reads=[("og", ob)], writes=["pout"])
        P.finish("sp")
    return nc


def build_merge(tsz=256):
    nc = bass.Bass("TRN2", target_bir_lowering=False)
    hin = nc.dram_tensor("hin", [D, TT], F32, kind="ExternalInput").ap()
    yin = nc.dram_tensor("yin", [1536, TT], F32, kind="ExternalInput").ap()
    gin = nc.dram_tensor("gin", [3072, TT], F32, kind="ExternalInput").ap()
    bwd = nc.dram_tensor("bw", [1536, D], F32, kind="ExternalInput").ap()
    owd = nc.dram_tensor("ow", [D, D], F32, kind="ExternalInput").ap()
    tabd = nc.dram_tensor("tab", [128, 64], F32, kind="ExternalInput").ap()
    hout = nc.dram_tensor("hout", [D, TT], F32, kind="ExternalOutput").ap()
    hv = hin.rearrange("(c p) t -> p c t", p=128)
    yv = yin.rearrange("(c p) t -> p c t", p=128)
    gv = gin.rearrange("(c p) t -> p c t", p=128)
    ov = hout.rearrange("(c p) t -> p c t", p=128)
    with ExitStack() as st:
        P = Prog(nc, st)
        tab, sh, sc1p, gt = prep_tab(P, nc, st, tabd, 8, 1.0)
        gam, bet = tab[:, 48:56], tab[:, 56:64]
        bw = load_w(P, nc, st, "bws", bwd, 1536, D)
        ow = load_w(P, nc, st, "ows", owd, D, D)
        C = Common(nc, st, P, tsz)
        hT = sb(nc, st, "hT", [128, 8, tsz], F32)
        hA = sb(nc, st, "hA", [128, 8, tsz], F32)
        yb = sb(nc, st, "yb", [128, 12, tsz], BF16)
        sg = sb(nc, st, "sg", [128, 24, tsz], F32)
        macc = sb(nc, st, "macc", [128, 2, tsz], F32)
        mtmp = sb(nc, st, "mtmp", [128, 2, tsz], F32)
        mT = sb(nc, st, "mT", [128, 8, tsz], BF16)
        psM = [ps(nc, st, "psM%d" % i, [128, 512]) for i in range(3)]
        psY = [ps(nc, st, "psY%d" % i, [128, 512]) for i in range(2)]
        for (t0, tn, ms) in token_tiles(tsz):
            for c in range(8):
                P.dma("sp", hT[:, c, :tn], hv[:, c, t0:t0 + tn], writes=[("hT", c)])
            for c in range(12):
                P.dma("pool", yb[:, c, :tn], yv[:, c, t0:t0 + tn], writes=[("yb", c)])
            for c in range(24):
                P.dma("sp", sg[:, c, :tn], gv[:, c, t0:t0 + tn], writes=[("sg", c)])
                P.op("act", lambda e, c=c: e.activation(out=sg[:, c, :tn], in_=sg[:, c, :tn], func=AF.Sigmoid),
                     reads=[("sg", c)], writes=[("sg", c)])
            for c in range(8):
                P.op("pool", lambda e, c=c: e.tensor_single_scalar(hA[:, c, :tn], hT[:, c, :tn], DN_ALPHA, ALU.mult),
                     reads=[("hT", c)], writes=[("hA", c)])
            for d in range(8):
                b = d % 2
                for n in range(3):
                    for kc in range(4):
                        P.op("pe", lambda e, n=n, kc=kc, d=d: e.matmul(psM[n][:, :tn], bw[:, n * 4 + kc, d * 128:(d + 1) * 128],
                                                                      yb[:, n * 4 + kc, :tn], start=(kc == 0), stop=(kc == 3)),
                             reads=[("bws", n * 4 + kc), ("yb", n * 4 + kc)], writes=[("psM", n)], sig=(kc == 3))
                P.op("dve", lambda e, d=d, b=b: e.tensor_tensor(macc[:, b, :tn], sg[:, d, :tn], psM[0][:, :tn], ALU.mult),
                     reads=[("sg", d), ("psM", 0)], writes=[("macc", b)])
                P.op("dve", lambda e, d=d, b=b: e.tensor_tensor(mtmp[:, 0, :tn], sg[:, 8 + d, :tn], psM[1][:, :tn], ALU.mult),
                     reads=[("sg", 8 + d), ("psM", 1)], writes=[("mtmp", 0)])
                P.op("dve", lambda e, d=d, b=b: e.tensor_tensor(mtmp[:, 1, :tn], sg[:, 16 + d, :tn], psM[2][:, :tn], ALU.mult),
                     reads=[("sg", 16 + d), ("psM", 2)], writes=[("mtmp", 1)])
                P.op("pool", lambda e, b=b: e.tensor_tensor(macc[:, b, :tn], macc[:, b, :tn], mtmp[:, 0, :tn], ALU.add),
                     reads=[("macc", b), ("mtmp", 0)], writes=[("macc", b)])
                P.op("pool", lambda e, d=d, b=b: e.tensor_tensor(mT[:, d, :tn], macc[:, b, :tn], mtmp[:, 1, :tn], ALU.add),
                     reads=[("macc", b), ("mtmp", 1)], writes=[("mT", d)])
            for d in range(8):
                b = d % 2
                for k in range(8):
                    P.op("pe", lambda e, k=k, d=d, b=b: e.matmul(psY[b][:, :tn], ow[:, k, d * 128:(d + 1) * 128], mT[:, k, :tn],
                                                                start=(k == 0), stop=(k == 7)),
                         reads=[("ows", k), ("mT", k)], writes=[("psY", b)], sig=(k == 7))
                P.op("dve", lambda e, d=d, b=b: e.scalar_tensor_tensor(C.r[:, d, :tn], psY[b][:, :tn], gt[ms][:, d:d + 1],
                                                                      hA[:, d, :tn], ALU.mult, ALU.add),
                     reads=[("psY", b), ("hA", d), "tab"], writes=[("r", d)])
            C.layer_norm(tn, gam, bet, hT, "hT")
            for c in range(8):
                P.dma("sp", ov[:, c, t0:t0 + tn], hT[:, c, :tn], reads=[("hT", c)], writes=["hout"])
        P.finish("sp")
    return nc


_CACHE = {}


def _get(name, fn, *a):
    key = (name,) + a
    if key not in _CACHE:
        _CACHE[key] = fn(*a)
    return _CACHE[key]


def _run(nc, in_maps):
    res = run_bass_kernel_spmd(nc, in_maps, core_ids=list(range(NCORE)))
    return res.results


def _pc(v):
    return np.ascontiguousarray(v.reshape(-1, 128).T)


def core_bq(i):
    return i // 4, i % 4


def to_cores(lat, cx):
    outs = []
    for i in range(NCORE):
        b, q = core_bq(i)
        a = np.concatenate([lat[b, q * TLAT:(q + 1) * TLAT], cx[b, q * TCTX:(q + 1) * TCTX]], axis=0)
        outs.append(np.ascontiguousarray(a.T))
    return outs


def from_cores(outs):
    C = outs[0].shape[0]
    lat = np.empty((2, SEQ, C), np.float32)
    cx = np.empty((2, CTX, C), np.float32)
    for i in range(NCORE):
        b, q = core_bq(i)
        lat[b, q * TLAT:(q + 1) * TLAT] = outs[i][:, :TLAT].T
        cx[b, q * TCTX:(q + 1) * TCTX] = outs[i][:, TLAT:].T
    return lat, cx


def run_mods(c, c_ctx, ada_w, ada_b):
    cT = np.ascontiguousarray(np.concatenate([c, c_ctx[None]], 0).T)
    aw = np.concatenate([ada_w[0], ada_w[1]], axis=1)
    ab = np.concatenate([ada_b[0], ada_b[1]], axis=0)
    nc = _get("k0", build_k0)
    maps = []
    for i in range(NCORE):
        sl = slice(i * NC0, (i + 1) * NC0)
        maps.append({"cT": cT, "aw": np.ascontiguousarray(aw[:, sl]), "ab": _pc(ab[sl])})
    res = _run(nc, maps)
    allm = np.concatenate([r["out"] for r in res], axis=0)
    return allm.reshape(2, 9, D, 3)


def make_tab(mods_l, b, idx3, extra):
    cols = []
    for v in (b, 2):
        for m in idx3:
            cols.append(_pc(mods_l[m, :, v]) if m is not None else np.zeros((128, 8), np.float32))
    for e in extra:
        cols.append(_pc(e))
    return np.ascontiguousarray(np.concatenate(cols, axis=1).astype(np.float32))


def run_ffn(hc, mods_l, idx3, w1, w2, g, bta):
    nc = _get("ffn", build_ffn)
    maps = []
    for i in range(NCORE):
        b, q = core_bq(i)
        maps.append({"hin": hc[i], "w1": w1, "w2": w2, "tab": make_tab(mods_l, b, idx3, [g, bta])})
    return [r["hout"] for r in _run(nc, maps)]


NQ = TT
NKL = TLAT + 256
NKB = NKL // 128 + 2
NK = NKB * 128


def build_attn():
    nc = bass.Bass("TRN2", target_bir_lowering=False)
    qf = nc.dram_tensor("qf", [64, 8, NQ], F32, kind="ExternalInput").ap()
    qs = nc.dram_tensor("qs", [64, 8, NQ], F32, kind="ExternalInput").ap()
    cq = nc.dram_tensor("cq", [64, NQ], F32, kind="ExternalInput").ap()
    sq = nc.dram_tensor("sq", [64, NQ], F32, kind="ExternalInput").ap()
    kf = nc.dram_tensor("kf", [64, 2, NK], F32, kind="ExternalInput").ap()
    ks = nc.dram_tensor("ks", [64, 2, NK], F32, kind="ExternalInput").ap()
    ck = nc.dram_tensor("ck", [64, NK], F32, kind="ExternalInput").ap()
    sk = nc.dram_tensor("sk", [64, NK], F32, kind="ExternalInput").ap()
    vt = nc.dram_tensor("vt", [128, NKB, 128], F32, kind="ExternalInput").ap()
    mk = nc.dram_tensor("mk", [128, 4, 4, 128], F32, kind="ExternalInput").ap()
    sk8 = nc.dram_tensor("sink", [64, 8], F32, kind="ExternalInput").ap()
    yo = nc.dram_tensor("yo", [64, 8, NQ], F32, kind="ExternalOutput").ap()
    CH = 1152
    with ExitStack() as st:
        P = Prog(nc, st)
        kr = sb(nc, st, "kr", [64, 2, NK], BF16)
        vb = sb(nc, st, "vb", [128, NKB, 128], BF16)
        mb = sb(nc, st, "mb", [128, 4, 4, 128], BF16)
        ones = sb(nc, st, "ones", [128, 64], BF16)
        es = sb(nc, st, "es", [64, 8], F32)
        P.dma("pool", vb[:], vt, writes=["vb"])
        P.dma("pool", mb[:], mk, writes=["mb"])
        P.dma("sp", es[:], sk8, writes=["es"])
        P.op("act", lambda e: e.activation(out=es[:], in_=es[:], func=AF.Exp), reads=["es"], writes=["es"])
        P.op("pool", lambda e: e.memset(ones[:], 1.0), writes=["ones"])
        ta = sb(nc, st, "ta", [64, 2, CH], F32)
        tb = sb(nc, st, "tb", [64, 2, CH], F32)
        tc_ = sb(nc, st, "tc", [64, CH], F32)
        td = sb(nc, st, "td", [64, CH], F32)
        for c0 in range(0, NK, CH):
            P.dma("sp", ta[:], kf[:, :, c0:c0 + CH], writes=["ta"])
            P.dma("sp", tb[:], ks[:, :, c0:c0 + CH], writes=["tb"])
            P.dma("sp", tc_[:], ck[:, c0:c0 + CH], writes=["tc"])
            P.dma("sp", td[:], sk[:, c0:c0 + CH], writes=["td"])
            for h in range(2):
                P.op("dve", lambda e, h=h: e.tensor_tensor(ta[:, h, :], ta[:, h, :], tc_[:], ALU.mult), reads=["ta", "tc"], writes=["ta"])
                P.op("pool", lambda e, h=h: e.tensor_tensor(tb[:, h, :], tb[:, h, :], td[:], ALU.mult), reads=["tb", "td"], writes=["tb"])
                P.op("dve", lambda e, h=h, c0=c0: e.tensor_tensor(kr[:, h, c0:c0 + CH], ta[:, h, :], tb[:, h, :], ALU.add),
                     reads=["ta", "tb"], writes=["kr"])
        qa = sb(nc, st, "qa", [64, 8, 128], F32)
        qb = sb(nc, st, "qb", [64, 8, 128], F32)
        qc = sb(nc, st, "qc", [64, 128], F32)
        qd = sb(nc, st, "qd", [64, 128], F32)
        qr = sb(nc, st, "qr", [64, 8, 128], BF16)
        pT = [sb(nc, st, "pT%d" % i, [128, 4, 128], BF16) for i in range(5)]
        dn = sb(nc, st, "dn", [64, 4, 128], F32)
        ob = sb(nc, st, "ob", [64, 8, 128], F32)
        psS = [ps(nc, st, "psS%d" % i, [128, 512]) for i in range(3)]
        psN = [ps(nc, st, "psN%d" % i, [64, 512]) for i in range(2)]
        psD = [ps(nc, st, "psD%d" % i, [64, 512]) for i in range(2)]
        si = 0
        nblk = TLAT // 128
        for n in range(nblk + 1):
            t0 = n * 128
            nq = 128 if n < nblk else TCTX
            P.dma("sp", qa[:, :, :nq], qf[:, :, t0:t0 + nq], writes=["qa"])
            P.dma("sp", qb[:, :, :nq], qs[:, :, t0:t0 + nq], writes=["qb"])
            P.dma("sp", qc[:, :nq], cq[:, t0:t0 + nq], writes=["qc"])
            P.dma("sp", qd[:, :nq], sq[:, t0:t0 + nq], writes=["qd"])
            for h in range(8):
                P.op("dve", lambda e, h=h: e.tensor_tensor(qa[:, h, :nq], qa[:, h, :nq], qc[:, :nq], ALU.mult), reads=["qa", "qc"], writes=["qa"])
                P.op("pool", lambda e, h=h: e.tensor_tensor(qb[:, h, :nq], qb[:, h, :nq], qd[:, :nq], ALU.mult), reads=["qb", "qd"], writes=["qb"])
                P.op("dve", lambda e, h=h: e.tensor_tensor(qr[:, h, :nq], qa[:, h, :nq], qb[:, h, :nq], ALU.add),
                     reads=["qa", "qb"], writes=[("qr", h // 4)])
            if n < nblk:
                kbs = [(n, 0 if n == 0 else 1), (n + 1, None), (n + 2, 3 if n == nblk - 1 else 2), (NKB - 2, None), (NKB - 1, None)]
            else:
                kbs = [(NKB - 2, None), (NKB - 1, None)]
            for kvh in range(2):
                pb = kvh
                for i, (kb, mi) in enumerate(kbs):
                    sbk = si % 3
                    si += 1
                    for g in range(4):
                        P.op("pe", lambda e, kb=kb, g=g, sbk=sbk, kvh=kvh: e.matmul(psS[sbk][:, g * nq:(g + 1) * nq],
                                                                                   kr[:, kvh, kb * 128:(kb + 1) * 128],
                                                                                   qr[:, kvh * 4 + g, :nq], start=True, stop=True),
                             reads=["kr", ("qr", kvh)], writes=[("psS", sbk)], sig=(g == 3))
                    for g in range(4):
                        P.op("act", lambda e, g=g, i=i, sbk=sbk: e.activation(out=pT[i][:, g, :nq], in_=psS[sbk][:, g * nq:(g + 1) * nq],
                                                                              func=AF.Exp, scale=0.125),
                             reads=[("psS", sbk)], writes=[("pT", i)])
                    if mi is not None:
                        P.op("pool", lambda e, i=i, mi=mi: e.tensor_tensor(pT[i][:, :, :nq], pT[i][:, :, :nq], mb[:, mi, :, :nq], ALU.mult),
                             reads=[("pT", i), "mb"], writes=[("pT", i)])
                for g in range(4):
                    for i, (kb, mi) in enumerate(kbs):
                        P.op("pe", lambda e, i=i, kb=kb, g=g, kvh=kvh, pb=pb: e.matmul(psN[pb][:, g * nq:(g + 1) * nq],
                                                                                      vb[:, kb, kvh * 64:(kvh + 1) * 64], pT[i][:, g, :nq],
                                                                                      start=(i == 0), stop=(i == len(kbs) - 1)),
                             reads=[("pT", i), "vb"], writes=[("psN", pb)], sig=(g == 3 and i == len(kbs) - 1))
                for g in range(4):
                    for i, (kb, mi) in enumerate(kbs):
                        P.op("pe", lambda e, i=i, g=g, pb=pb: e.matmul(psD[pb][:, g * nq:(g + 1) * nq], ones[:], pT[i][:, g, :nq],
                                                                      start=(i == 0), stop=(i == len(kbs) - 1)),
                             reads=[("pT", i), "ones"], writes=[("psD", pb)], sig=(g == 3 and i == len(kbs) - 1))
                for g in range(4):
                    h = kvh * 4 + g
                    P.op("dve", lambda e, g=g, h=h, pb=pb: e.tensor_scalar(dn[:, g, :nq], psD[pb][:, g * nq:(g + 1) * nq], es[:, h:h + 1], None, ALU.add),
                         reads=[("psD", pb), "es"], writes=["dn"])
                    P.op("dve", lambda e, g=g: e.reciprocal(dn[:, g, :nq], dn[:, g, :nq]), reads=["dn"], writes=["dn"])
                    P.op("dve", lambda e, g=g, h=h, pb=pb: e.tensor_tensor(ob[:, h, :nq], psN[pb][:, g * nq:(g + 1) * nq], dn[:, g, :nq], ALU.mult),
                         reads=[("psN", pb), "dn"], writes=["ob"])
            P.dma("sp", yo[:, :, t0:t0 + nq], ob[:, :, :nq], reads=["ob"], writes=["yo"])
        P.finish("sp")
    return nc


def rope_tables(pos):
    pos = np.asarray(pos)
    valid = pos >= 0
    p = np.where(valid, pos, 0)
    row = (p // 64).astype(np.float32)
    col = (p % 64).astype(np.float32)
    inv = (np.float32(10000.0) ** (-np.arange(16, dtype=np.float32) * np.float32(2.0) / np.float32(32))).astype(np.float32)
    cos = np.ones((64, len(pos)), np.float32)
    sin = np.zeros((64, len(pos)), np.float32)
    for a, base in ((0, row), (1, col)):
        ang = (base[None, :] * inv[:, None]).astype(np.float32)
        for s in range(2):
            sl = slice(a * 32 + s * 16, a * 32 + s * 16 + 16)
            cos[sl] = np.cos(ang)
            sin[sl] = np.sin(ang) * (-1.0 if s == 0 else 1.0)
    cos[:, ~valid] = 1.0
    sin[:, ~valid] = 0.0
    return cos, sin


OFF_B, OFF_Q, OFF_K, OFF_V, OFF_G = 2560, 4096, 4608, 4736, 4864
OFF_QS, OFF_KS = 7936, 8448


def swap_cols(w):
    sh = w.shape
    return w.reshape(sh[:-1] + (-1, 2, 2, 16))[..., ::-1, :].reshape(sh)


def run_attn(Pl, Pc, sink):
    nc = _get("attn", build_attn)
    qi = np.arange(128)
    m_prev = (qi[:, None] >= qi[None, :]).astype(np.float32)
    m_next = (qi[:, None] <= qi[None, :]).astype(np.float32)
    maps = []
    for i in range(NCORE):
        b, q = core_bq(i)
        t0 = q * TLAT
        lat = Pl[b, t0:t0 + TLAT]
        cx = Pc[b, q * TCTX:(q + 1) * TCTX]

        def heads(a, n):
            return np.ascontiguousarray(a.reshape(a.shape[0], n, 64).transpose(2, 1, 0))
        qall = np.concatenate([lat[:, OFF_Q:OFF_K], cx[:, OFF_Q:OFF_K]], 0)
        qsall = np.concatenate([lat[:, OFF_QS:OFF_KS], cx[:, OFF_QS:OFF_KS]], 0)
        qpos = np.concatenate([np.arange(t0, t0 + TLAT), -np.ones(TCTX, np.int64)])
        cqt, sqt = rope_tables(qpos)
        kpos = np.arange(t0 - 128, t0 + TLAT + 128)
        kval = (kpos >= 0) & (kpos < SEQ)
        kidx = np.clip(kpos, 0, SEQ - 1)
        kl = Pl[b, kidx] * kval[:, None]
        kall = np.concatenate([kl[:, OFF_K:OFF_V], Pc[b][:, OFF_K:OFF_V]], 0)
        ksall = np.concatenate([kl[:, OFF_KS:NQK], Pc[b][:, OFF_KS:NQK]], 0)
        vall = np.concatenate([kl[:, OFF_V:OFF_G], Pc[b][:, OFF_V:OFF_G]], 0)
        ckt, skt = rope_tables(np.concatenate([np.where(kval, kpos, -1), -np.ones(CTX, np.int64)]))
        m0 = m_prev if q > 0 else np.zeros_like(m_prev)
        m3 = m_next if q < 3 else np.zeros_like(m_next)
        mk = np.stack([m0, m_prev, m_next, m3], 0)
        mk = np.ascontiguousarray(np.broadcast_to(mk.transpose(1, 0, 2)[:, :, None, :], (128, 4, 4, 128))).astype(np.float32)
        maps.append({
            "qf": heads(qall, 8), "qs": heads(qsall, 8), "cq": cqt, "sq": sqt,
            "kf": heads(kall, 2), "ks": heads(ksall, 2), "ck": ckt, "sk": skt,
            "vt": np.ascontiguousarray(vall.reshape(NKB, 128, 128).transpose(1, 0, 2)),
            "mk": mk, "sink": np.ascontiguousarray(np.broadcast_to(sink[None, :], (64, 8))).astype(np.float32),
        })
    res = _run(nc, maps)
    yl = np.empty((2, SEQ, 512), np.float32)
    yc = np.empty((2, CTX, 512), np.float32)
    for i in range(NCORE):
        b, q = core_bq(i)
        y = res[i]["yo"].transpose(2, 1, 0).reshape(NQ, 512)
        yl[b, q * TLAT:(q + 1) * TLAT] = y[:TLAT]
        yc[b, q * TCTX:(q + 1) * TCTX] = y[TLAT:]
    return yl, yc


NTOK = CTX + SEQ
NG = NTOK // 128


def build_hgrn():
    nc = bass.Bass("TRN2", target_bir_lowering=False)
    zd = [nc.dram_tensor("z%d" % d, [128, NG, 128], F32, kind="ExternalInput").ap() for d in range(2)]
    vd = nc.dram_tensor("v", [128, NG, 128], F32, kind="ExternalInput").ap()
    qd = nc.dram_tensor("q", [128, NTOK], F32, kind="ExternalInput").ap()
    gd = nc.dram_tensor("g", [128, NTOK], F32, kind="ExternalInput").ap()
    lbd = nc.dram_tensor("lb", [128, 2, 2, 128], F32, kind="ExternalInput").ap()
    fld = nc.dram_tensor("flag", [128, 1], F32, kind="ExternalInput").ap()
    nwd = nc.dram_tensor("nw", [128, 1], F32, kind="ExternalInput").ap()
    cfd = nc.dram_tensor("cf", [128, 5, 128], F32, kind="ExternalInput").ap()
    cmd = nc.dram_tensor("cm", [128, 4], F32, kind="ExternalInput").ap()
    yo = nc.dram_tensor("yo", [128, NTOK], F32, kind="ExternalOutput").ap()
    with ExitStack() as st:
        P = Prog(nc, st)
        vb = sb(nc, st, "vb", [128, NG, 128], BF16)
        P.dma("pool", vb[:], vd, writes=["vb"])
        cf = sb(nc, st, "cfs", [128, 5, 128], F32)
        P.dma("sp", cf[:], cfd, writes=["cf"])
        bm = sb(nc, st, "bm", [128, 2, 128], BF16)
        P.dma("pool", bm[:], cfd[:, 0:2, :], writes=["bm"])
        cm = sb(nc, st, "cms", [128, 4], F32)
        P.dma("sp", cm[:], cmd, writes=["cm"])
        nw = sb(nc, st, "nws", [128, 1], F32)
        P.dma("sp", nw[:], nwd, writes=["nw"])
        fl = sb(nc, st, "fls", [128, 1], F32)
        P.dma("sp", fl[:], fld, writes=["fl"])
        lbt = sb(nc, st, "lbs", [128, 2, 2, 128], F32)
        P.dma("sp", lbt[:], lbd, writes=["lbt"])
        LBb = sb(nc, st, "LBb", [128, 2, 128], F32)
        LBa = sb(nc, st, "LBa", [128, 2, 128], F32)
        P.op("dve", lambda e: e.tensor_tensor(LBb[:], lbt[:, 1], lbt[:, 0], ALU.subtract), reads=["lbt"], writes=["LBb"])
        P.op("act", lambda e: e.activation(out=LBb[:], in_=LBb[:], func=AF.Sigmoid), reads=["LBb"], writes=["LBb"])
        P.op("dve", lambda e: e.tensor_scalar(LBb[:], LBb[:], fl[:, 0:1], None, ALU.mult), reads=["LBb", "fl"], writes=["LBb"])
        P.op("dve", lambda e: e.tensor_scalar(LBa[:], LBb[:], -1.0, 1.0, ALU.mult, ALU.add), reads=["LBb"], writes=["LBa"])
        ones = sb(nc, st, "ones", [128, 128], F32)
        P.op("pool", lambda e: e.memset(ones[:], 1.0), writes=["ones"])
        osum = sb(nc, st, "osum", [128, NTOK], F32)
        S = sb(nc, st, "S", [128, 128], F32)
        NSLOT = 8
        Sbr = sb(nc, st, "Sbr", [128, NSLOT, 128], BF16)
        prev_slot = [0]
        zt = [sb(nc, st, "zt%d" % i, [128, 128], F32) for i in range(2)]
        qt32 = [sb(nc, st, "qt32%d" % i, [128, 128], F32) for i in range(2)]
        ft = [sb(nc, st, "ft%d" % i, [128, 128], F32) for i in range(2)]
        lf = [sb(nc, st, "lf%d" % i, [128, 128], F32) for i in range(2)]
        kk = [sb(nc, st, "kk%d" % i, [128, 128], F32) for i in range(2)]
        Eq = [sb(nc, st, "Eq%d" % i, [128, 128], F32) for i in range(2)]
        Ek = [sb(nc, st, "Ek%d" % i, [128, 128], F32) for i in range(2)]
        Er = [sb(nc, st, "Er%d" % i, [128, 128], F32) for i in range(2)]
        qt = [sb(nc, st, "qt%d" % i, [128, 128], BF16) for i in range(2)]
        kt = [sb(nc, st, "kt%d" % i, [128, 128], BF16) for i in range(2)]
        k4 = [sb(nc, st, "k4%d" % i, [128, 4, 128], BF16) for i in range(2)]
        am = [sb(nc, st, "am%d" % i, [128, 128], BF16) for i in range(2)]
        pB = ps(nc, st, "pB", [128, 512])
        pR = ps(nc, st, "pR", [128, 512])
        pK = ps(nc, st, "pK", [128, 512])
        pA = ps(nc, st, "pA", [128, 512])
        pO = [ps(nc, st, "pO%d" % i, [128, 512]) for i in range(2)]
        pU = [ps(nc, st, "pU%d" % i, [128, 512]) for i in range(2)]
        it = 0
        ui = 0
        for d in range(2):
            P.op("pool", lambda e: e.memset(S[:], 0.0), writes=["S"])
            prev_slot[0] = (prev_slot[0] + 1) % NSLOT
            P.op("pool", lambda e, ps_=prev_slot[0]: e.memset(Sbr[:, ps_, :], 0.0), writes=[("Sb", prev_slot[0])])
            groups = list(range(NG)) if d == 0 else [1, 0] + list(range(NG - 1, 1, -1))
            chunks = [0, 1, 2, 3] if d == 0 else [3, 2, 1, 0]
            for G in groups:
                b = it % 2
                it += 1
                P.dma("sp", zt[b][:], zd[d][:, G, :], writes=[("zt", b)])
                P.dma("sp", qt32[b][:], qd[:, G * 128:(G + 1) * 128], writes=[("qt32", b)])
                P.op("act", lambda e, b=b: e.activation(out=zt[b][:], in_=zt[b][:], func=AF.Sigmoid), reads=[("zt", b)], writes=[("zt", b)])
                P.op("pool", lambda e, b=b, d=d: e.tensor_tensor(ft[b][:], zt[b][:], LBa[:, d, :], ALU.mult), reads=[("zt", b), "LBa"], writes=[("ft", b)])
                P.op("pool", lambda e, b=b, d=d: e.tensor_tensor(ft[b][:], ft[b][:], LBb[:, d, :], ALU.add), reads=[("ft", b), "LBb"], writes=[("ft", b)])
                P.op("act", lambda e, b=b: e.activation(out=lf[b][:], in_=ft[b][:], func=AF.Ln), reads=[("ft", b)], writes=[("lf", b)])
                P.op("pool", lambda e, b=b: e.tensor_scalar(kk[b][:], ft[b][:], -1.0, 1.0, ALU.mult, ALU.add), reads=[("ft", b)], writes=[("kk", b)])
                P.op("pe", lambda e, b=b, d=d: e.matmul(pB[:, :128], lf[b][:], cf[:, d, :], start=True, stop=True), reads=[("lf", b), "cf"], writes=["pB"])
                P.op("pe", lambda e, b=b, d=d: e.matmul(pR[:, :128], cf[:, 2 + d, :], lf[b][:], start=True, stop=True), reads=[("lf", b), "cf"], writes=["pR"])
                P.op("pe", lambda e, b=b: e.matmul(pK[:, :128], kk[b][:], cf[:, 4, :], start=True, stop=True), reads=[("kk", b), "cf"], writes=["pK"])
                P.op("act", lambda e, b=b: e.activation(out=Eq[b][:], in_=pB[:, :128], func=AF.Exp), reads=["pB"], writes=[("Eq", b)])
                P.op("act", lambda e, b=b: e.activation(out=Ek[b][:], in_=pB[:, :128], func=AF.Exp, scale=-1.0), reads=["pB"], writes=[("Ek", b)])
                P.op("act", lambda e, b=b: e.activation(out=Er[b][:], in_=pR[:, :128], func=AF.Exp), reads=["pR"], writes=[("Er", b)])
                P.op("dve", lambda e, b=b: e.tensor_tensor(qt[b][:], qt32[b][:], Eq[b][:], ALU.mult), reads=[("qt32", b), ("Eq", b)], writes=[("qt", b)])
                P.op("dve", lambda e, b=b: e.tensor_tensor(kt[b][:], pK[:, :128], Ek[b][:], ALU.mult), reads=["pK", ("Ek", b)], writes=[("kt", b)])
                P.op("pool", lambda e, b=b: e.tensor_tensor(Er[b][:], kk[b][:], Er[b][:], ALU.mult), reads=[("kk", b), ("Er", b)], writes=[("Er", b)])
                for c in range(4):
                    P.op("pool", lambda e, b=b, c=c: e.tensor_scalar(k4[b][:, c, :], Er[b][:], cm[:, c:c + 1], None, ALU.mult),
                         reads=[("Er", b), "cm"], writes=[("k4", b)])
                P.op("pe", lambda e, b=b: e.matmul(pA[:, :128], kt[b][:], qt[b][:], start=True, stop=True), reads=[("kt", b), ("qt", b)], writes=["pA"])
                P.op("dve", lambda e, b=b, d=d: e.tensor_tensor(am[b][:], pA[:, :128], bm[:, d, :], ALU.mult), reads=["pA", "bm"], writes=[("am", b)])
                for ci, c in enumerate(chunks):
                    P.op("pe", lambda e, b=b, c=c, ci=ci, G=G: e.matmul(pU[b][:, ci * 128:(ci + 1) * 128], k4[b][:, c, :], vb[:, G, :], start=True, stop=True),
                         reads=[("k4", b), "vb"], writes=[("pU", b)], sig=(ci == 3))
                P.op("pe", lambda e, b=b, G=G: e.matmul(pO[b][:, :128], vb[:, G, :], am[b][:], start=True, stop=False),
                     reads=["vb", ("am", b)], writes=[("pO", b)], sig=False)
                for ci, c in enumerate(chunks):
                    P.op("pe", lambda e, b=b, c=c, ci=ci, ps_=prev_slot[0]: e.matmul(pO[b][:, c * 32:(c + 1) * 32], Sbr[:, ps_, :], qt[b][:, c * 32:(c + 1) * 32],
                                                                                start=False, stop=(ci == 3)),
                         reads=[("Sb", prev_slot[0]), ("qt", b)], writes=[("pO", b)], sig=(ci == 3))
                    ns = (prev_slot[0] + 1) % NSLOT
                    dcol = c * 32 + (31 if d == 0 else 0)
                    P.op("dve", lambda e, b=b, ci=ci, ns=ns, dcol=dcol: e.scalar_tensor_tensor(Sbr[:, ns, :], S[:], Eq[b][:, dcol:dcol + 1], pU[b][:, ci * 128:(ci + 1) * 128], ALU.mult, ALU.add),
                         reads=["S", ("Eq", b), ("pU", b)], writes=[("Sb", ns)])
                    P.op("dve", lambda e, b=b, ci=ci, dcol=dcol: e.scalar_tensor_tensor(S[:], S[:], Eq[b][:, dcol:dcol + 1], pU[b][:, ci * 128:(ci + 1) * 128], ALU.mult, ALU.add),
                         reads=["S", ("Eq", b), ("pU", b)], writes=["S"])
                    prev_slot[0] = ns
                if d == 0:
                    P.op("act", lambda e, b=b, G=G: e.copy(out=osum[:, G * 128:(G + 1) * 128], in_=pO[b][:, :128]), reads=[("pO", b)], writes=[("osum", G)])
                else:
                    P.op("dve", lambda e, b=b, G=G: e.tensor_tensor(osum[:, G * 128:(G + 1) * 128], osum[:, G * 128:(G + 1) * 128], pO[b][:, :128], ALU.add),
                         reads=[("pO", b), ("osum", G)], writes=[("osum", G)])
        sq = sb(nc, st, "sq", [128, 512], F32)
        gt = sb(nc, st, "gt", [128, 512], F32)
        rs = sb(nc, st, "rs", [128, 512], F32)
        yt = [sb(nc, st, "yt%d" % i, [128, 512], F32) for i in range(2)]
        ri = 0
        for t0 in range(0, NTOK, 512):
            tn = min(512, NTOK - t0)
            b = ri % 2
            ri += 1
            rk = [("osum", G) for G in range(t0 // 128, (t0 + tn) // 128)]
            P.dma("sp", gt[:, :tn], gd[:, t0:t0 + tn], writes=["gt"])
            P.op("act", lambda e, t0=t0, tn=tn: e.activation(out=sq[:, :tn], in_=osum[:, t0:t0 + tn], func=AF.Square), reads=rk, writes=["sq"])
            P.op("pe", lambda e, tn=tn: e.matmul(pB[:, :tn], ones[:], sq[:, :tn], start=True, stop=True), reads=["sq", "ones"], writes=["pB"])
            P.op("dve", lambda e, tn=tn: e.tensor_scalar(rs[:, :tn], pB[:, :tn], 1.0 / 128, RMS_EPS, ALU.mult, ALU.add), reads=["pB"], writes=["rs"])
            P.op("act", lambda e, tn=tn: e.sqrt(out=rs[:, :tn], in_=rs[:, :tn]), reads=["rs"], writes=["rs"])
            P.op("dve", lambda e, tn=tn: e.reciprocal(rs[:, :tn], rs[:, :tn]), reads=["rs"], writes=["rs"])
            P.op("act", lambda e, tn=tn: e.activation(out=gt[:, :tn], in_=gt[:, :tn], func=AF.Silu), reads=["gt"], writes=["gt"])
            P.op("dve", lambda e, b=b, t0=t0, tn=tn: e.tensor_tensor(yt[b][:, :tn], osum[:, t0:t0 + tn], rs[:, :tn], ALU.mult), reads=rk + ["rs"], writes=[("yt", b)])
            P.op("pool", lambda e, b=b, tn=tn: e.tensor_tensor(yt[b][:, :tn], yt[b][:, :tn], gt[:, :tn], ALU.mult), reads=[("yt", b), "gt"], writes=[("yt", b)])
            P.op("act", lambda e, b=b, tn=tn: e.activation(out=yt[b][:, :tn], in_=yt[b][:, :tn], func=AF.Copy, scale=nw[:, 0:1]), reads=[("yt", b), "nw"], writes=[("yt", b)])
            P.dma("sp", yo[:, t0:t0 + tn], yt[b][:, :tn], reads=[("yt", b)], writes=["yo"])
        P.finish("sp")
    return nc


def run_hgrn(Pl, Pc, layer, hgrn_lb, norm_w):
    nc = _get("hgrn", build_hgrn)
    p = np.arange(128)
    same = (p[:, None] // 32) == (p[None, :] // 32)
    incl_f = (same & (p[:, None] <= p[None, :])).astype(np.float32)
    incl_b = (same & (p[:, None] >= p[None, :])).astype(np.float32)
    rem_f = (same & (p[:, None] > p[None, :])).astype(np.float32)
    rem_b = (same & (p[:, None] < p[None, :])).astype(np.float32)
    cf = np.ascontiguousarray(np.stack([incl_f, incl_b, rem_f, rem_b, np.eye(128, dtype=np.float32)], 1))
    cm = (p[:, None] // 32 == np.arange(4)[None, :]).astype(np.float32)
    maps = []
    for i in range(NCORE):
        b, hd = core_bq(i)
        seq = np.concatenate([Pc[b], Pl[b]], 0)
        cs = slice(hd * 128, (hd + 1) * 128)

        def tm(a):
            return np.ascontiguousarray(a.reshape(NG, 128, 128).transpose(1, 0, 2))
        lb = np.broadcast_to(hgrn_lb[:, :, cs][None], (128, 2, 2, 128))
        maps.append({
            "q": np.ascontiguousarray(seq[:, 0:512][:, cs].T), "v": tm(seq[:, 512:1024][:, cs]),
            "g": np.ascontiguousarray(seq[:, 1024:1536][:, cs].T),
            "z0": tm(seq[:, 1536:2048][:, cs]), "z1": tm(seq[:, 2048:2560][:, cs]),
            "lb": np.ascontiguousarray(lb).astype(np.float32),
            "flag": np.full((128, 1), float(layer), np.float32),
            "nw": np.ascontiguousarray(norm_w[cs][:, None]).astype(np.float32), "cf": cf, "cm": cm,
        })
    res = _run(nc, maps)
    yl = np.empty((2, SEQ, 512), np.float32)
    yc = np.empty((2, CTX, 512), np.float32)
    for i in range(NCORE):
        b, hd = core_bq(i)
        y = res[i]["yo"].T
        yc[b, :, hd * 128:(hd + 1) * 128] = y[:CTX]
        yl[b, :, hd * 128:(hd + 1) * 128] = y[CTX:]
    return yl, yc


PI = float(np.pi)


def build_hyena(L):
    nc = bass.Bass("TRN2", target_bir_lowering=False)
    PP = min(L, 1024)
    pbd = [nc.dram_tensor("pb%d" % i, [3, 128, L], F32, kind="ExternalInput").ap() for i in range(3)]
    cwd = nc.dram_tensor("cw", [128, 12], F32, kind="ExternalInput").ap()
    bsd = nc.dram_tensor("bias", [128, 2], F32, kind="ExternalInput").ap()
    ftd = nc.dram_tensor("feats", [33, L], F32, kind="ExternalInput").ap()
    dcd = nc.dram_tensor("decay", [128, L], F32, kind="ExternalInput").ap()
    w1d = nc.dram_tensor("w1", [33, 64], F32, kind="ExternalInput").ap()
    w2d = nc.dram_tensor("w2", [64, 64], F32, kind="ExternalInput").ap()
    w3d = nc.dram_tensor("w3", [64, 4, 128], F32, kind="ExternalInput").ap()
    bfd = nc.dram_tensor("bf", [64, 4], F32, kind="ExternalInput").ap()
    yo = nc.dram_tensor("yo", [128, L], F32, kind="ExternalOutput").ap()
    with ExitStack() as st:
        P = Prog(nc, st)
        cw = sb(nc, st, "cws", [128, 12], F32)
        bs = sb(nc, st, "bss", [128, 2], F32)
        w1 = sb(nc, st, "w1s", [33, 64], F32)
        w2 = sb(nc, st, "w2s", [64, 64], F32)
        w3 = sb(nc, st, "w3s", [64, 4, 128], F32)
        bf = sb(nc, st, "bfs", [64, 4], F32)
        for t, dd in ((cw, cwd), (bs, bsd), (w1, w1d), (w2, w2d), (w3, w3d), (bf, bfd)):
            P.dma("sp", t[:], dd, writes=["consts"])
        z = sb(nc, st, "z", [128, L], F32)
        y = sb(nc, st, "y", [128, L], F32)
        ta = sb(nc, st, "ta", [128, PP], F32)
        tb = sb(nc, st, "tb", [128, PP], F32)
        tcx = sb(nc, st, "tcx", [128, PP], F32)
        xs = sb(nc, st, "xs", [128, PP], F32)
        ft = sb(nc, st, "ft", [33, PP], F32)
        h1 = sb(nc, st, "h1", [64, PP], F32)
        h2 = sb(nc, st, "h2", [64, PP], F32)
        dc = sb(nc, st, "dc", [128, PP], F32)
        hp = [sb(nc, st, "hp%d" % i, [128, PP], F32) for i in range(2)]
        l1 = sb(nc, st, "l1", [128, 4], F32)
        ki = sb(nc, st, "ki", [64, PP], mybir.dt.int32)
        kf = sb(nc, st, "kf", [64, PP], F32)
        pm = ps(nc, st, "pm", [64, 512])
        ph = [ps(nc, st, "ph%d" % i, [128, 512]) for i in range(2)]

        def sconv(part, dst, dkey, t0, tn):
            P.dma("sp", ta[:, :tn], pbd[0][part, :, t0:t0 + tn], writes=["ta"])
            P.dma("sp", tb[:, :tn], pbd[1][part, :, t0:t0 + tn], writes=["tb"])
            P.dma("sp", tcx[:, :tn], pbd[2][part, :, t0:t0 + tn], writes=["tcx"])
            c0 = part * 3
            P.op("dve", lambda e: e.tensor_scalar(ta[:, :tn], ta[:, :tn], cw[:, c0:c0 + 1], cw[:, 9 + part:10 + part], ALU.mult, ALU.add),
                 reads=["ta", "consts"], writes=["ta"])
            P.op("dve", lambda e: e.scalar_tensor_tensor(tb[:, :tn], tb[:, :tn], cw[:, c0 + 1:c0 + 2], ta[:, :tn], ALU.mult, ALU.add),
                 reads=["ta", "tb", "consts"], writes=["tb"])
            P.op("dve", lambda e: e.scalar_tensor_tensor(dst, tcx[:, :tn], cw[:, c0 + 2:c0 + 3], tb[:, :tn], ALU.mult, ALU.add),
                 reads=["tb", "tcx", "consts"], writes=[dkey])

        for t0 in range(0, L, PP):
            sconv(0, z[:, t0:t0 + PP], "z", t0, PP)
        for o in range(2):
            P.op("pool", lambda e: e.memset(y[:], 0.0), writes=["y"])
            P.op("pool", lambda e: e.memset(l1[:], 0.0), writes=["l1"])
            for p0 in range(0, L, PP):
                P.dma("sp", ft[:], ftd[:, p0:p0 + PP], writes=["ft"])
                P.dma("sp", dc[:], dcd[:, p0:p0 + PP], writes=["dc"])
                for (src, skey, w, K_, dst, dkey, bi) in ((ft, "ft", w1, 33, h1, "h1", 0), (h1, "h1", w2, 64, h2, "h2", 2)):
                    for q0 in range(0, PP, 512):
                        qn = min(512, PP - q0)
                        P.op("pe", lambda e, q0=q0, qn=qn, src=src, w=w, K_=K_: e.matmul(pm[:, :qn], w[:K_, :], src[:K_, q0:q0 + qn], start=True, stop=True),
                             reads=[skey, "consts"], writes=["pm"])
                        P.op("dve", lambda e, q0=q0, qn=qn, dst=dst, bi=bi: e.tensor_scalar(dst[:, q0:q0 + qn], pm[:, :qn], bf[:, bi:bi + 1], bf[:, bi + 1:bi + 2], ALU.add, ALU.mult),
                             reads=["pm", "consts"], writes=[dkey])
                    P.op("dve", lambda e, dst=dst: e.tensor_scalar(dst[:], dst[:], 1.0 / (2.0 * PI), 8.5, ALU.mult, ALU.add), reads=[dkey], writes=[dkey])
                    P.op("dve", lambda e, dst=dst: e.tensor_copy(ki[:], dst[:]), reads=[dkey], writes=["ki"])
                    P.op("dve", lambda e, dst=dst: e.tensor_copy(kf[:], ki[:]), reads=["ki"], writes=["kf"])
                    P.op("dve", lambda e, dst=dst: e.tensor_tensor(dst[:], dst[:], kf[:], ALU.subtract), reads=[dkey, "kf"], writes=[dkey])
                    P.op("dve", lambda e, dst=dst: e.tensor_single_scalar(kf[:], dst[:], 0.0, ALU.is_lt), reads=[dkey], writes=["kf"])
                    P.op("dve", lambda e, dst=dst: e.tensor_tensor(dst[:], dst[:], kf[:], ALU.add), reads=[dkey, "kf"], writes=[dkey])
                    P.op("dve", lambda e, dst=dst: e.tensor_scalar(dst[:], dst[:], 2.0 * PI, -PI, ALU.mult, ALU.add), reads=[dkey], writes=[dkey])
                    P.op("act", lambda e, dst=dst: e.activation(out=dst[:], in_=dst[:], func=AF.Sin), reads=[dkey], writes=[dkey])
                for dr in range(2):
                    for q0 in range(0, PP, 512):
                        qn = min(512, PP - q0)
                        b = (q0 // 512) % 2
                        P.op("pe", lambda e, q0=q0, qn=qn, dr=dr, b=b: e.matmul(ph[b][:, :qn], w3[:, dr * 2 + o, :], h2[:, q0:q0 + qn], start=True, stop=True),
                             reads=["h2", "consts"], writes=[("ph", b)])
                        P.op("dve", lambda e, q0=q0, qn=qn, dr=dr, b=b: e.tensor_tensor(hp[dr][:, q0:q0 + qn], ph[b][:, :qn], dc[:, q0:q0 + qn], ALU.mult),
                             reads=[("ph", b), "dc"], writes=[("hp", dr)])
                    if dr == 1 and p0 == 0:
                        P.op("dve", lambda e: e.memset(hp[1][:, 0:1], 0.0), reads=[("hp", 1)], writes=[("hp", 1)])
                    P.op("dve", lambda e, dr=dr: e.tensor_reduce(out=l1[:, 2 + dr:3 + dr], in_=hp[dr][:], axis=AX.X, op=ALU.add, apply_absolute_value=True),
                         reads=[("hp", dr)], writes=["l1p"])
                    P.op("dve", lambda e, dr=dr: e.tensor_tensor(l1[:, 0:1], l1[:, 0:1], l1[:, 2 + dr:3 + dr], ALU.add), reads=["l1p", "l1"], writes=["l1"])
                for j in range(PP):
                    d = p0 + j
                    P.op("dve", lambda e, j=j, d=d: e.scalar_tensor_tensor(y[:, d:L], z[:, 0:L - d], hp[0][:, j:j + 1], y[:, d:L], ALU.mult, ALU.add),
                         reads=[("hp", 0), "z", "y"], writes=["y"])
                    if d >= 1:
                        P.op("dve", lambda e, j=j, d=d: e.scalar_tensor_tensor(y[:, 0:L - d], z[:, d:L], hp[1][:, j:j + 1], y[:, 0:L - d], ALU.mult, ALU.add),
                             reads=[("hp", 1), "z", "y"], writes=["y"])
            P.op("dve", lambda e: e.reciprocal(l1[:, 1:2], l1[:, 0:1]), reads=["l1"], writes=["l1"])
            for t0 in range(0, L, PP):
                sconv(1 + o, xs[:], "xs", t0, PP)
                P.op("dve", lambda e, t0=t0: e.tensor_scalar(y[:, t0:t0 + PP], y[:, t0:t0 + PP], l1[:, 1:2], None, ALU.mult), reads=["y", "l1"], writes=["y"])
                P.op("dve", lambda e, t0=t0: e.scalar_tensor_tensor(y[:, t0:t0 + PP], z[:, t0:t0 + PP], bs[:, o:o + 1], y[:, t0:t0 + PP], ALU.mult, ALU.add),
                     reads=["y", "z", "consts"], writes=["y"])
                P.op("dve", lambda e, t0=t0: e.tensor_tensor(z[:, t0:t0 + PP], y[:, t0:t0 + PP], xs[:], ALU.mult), reads=["y", "xs", "z"], writes=["z"])
        for t0 in range(0, L, PP):
            P.dma("sp", yo[:, t0:t0 + PP], z[:, t0:t0 + PP], reads=["z"], writes=["yo"])
        P.finish("sp")
    return nc


def run_hyena(X, L, cw, cb, w1, b1, f1, w2, b2, f2, w3, bias):
    nc = _get("hyena", build_hyena, L)
    pos = np.arange(L, dtype=np.float32)
    t = pos / np.float32(L - 1)
    wv = np.float32(2.0 * np.pi) * pos / np.float32(L)
    bands = np.linspace(1e-4, 15, 16, dtype=np.float32)
    feats = np.concatenate([t[:, None], np.cos(wv[:, None] * bands), -np.sin(wv[:, None] * bands)], -1).astype(np.float32)
    rates = np.abs(np.linspace(np.log(1e-2) / 1.5, np.log(1e-2) / 0.3, 512, dtype=np.float32))
    decay = np.exp(-t[None, :] * rates[:, None]).astype(np.float32)
    Xp = np.pad(X, ((0, 0), (1, 1), (0, 0)))
    maps = []
    for i in range(NCORE):
        cs = np.arange(i * 64, (i + 1) * 64)

        def rows(a):
            return np.ascontiguousarray(np.stack([a[:, :, p * 512 + cs].transpose(0, 2, 1).reshape(128, L) for p in range(3)], 0))
        cwt = np.concatenate([np.stack([cw[tp, p * 512 + cs] for p in range(3) for tp in range(3)], 1),
                              np.stack([cb[p * 512 + cs] for p in range(3)], 1)], 1)
        w3r = w3.reshape(64, 2, 2, 512)[:, :, :, cs]
        w3r = np.concatenate([w3r, w3r], -1).reshape(64, 4, 128)
        maps.append({
            "pb0": rows(Xp[:, 0:L]), "pb1": rows(Xp[:, 1:L + 1]), "pb2": rows(Xp[:, 2:L + 2]),
            "cw": np.ascontiguousarray(np.concatenate([cwt, cwt], 0)).astype(np.float32),
            "bias": np.ascontiguousarray(np.concatenate([bias[:, cs].T, bias[:, cs].T], 0)).astype(np.float32),
            "feats": np.ascontiguousarray(feats.T), "decay": np.ascontiguousarray(np.concatenate([decay[cs], decay[cs]], 0)),
            "w1": np.ascontiguousarray(w1), "w2": np.ascontiguousarray(w2), "w3": np.ascontiguousarray(w3r).astype(np.float32),
            "bf": np.ascontiguousarray(np.stack([b1, f1, b2, f2], 1)).astype(np.float32),
        })
    res = _run(nc, maps)
    out = np.empty((2, L, 512), np.float32)
    for i in range(NCORE):
        out[:, :, i * 64:(i + 1) * 64] = res[i]["yo"].reshape(2, 64, L).transpose(0, 2, 1)
    return out


def run_inproj(hc, mods_l, w_aug):
    nc = _get("inproj", build_inproj)
    maps = []
    for i in range(NCORE):
        b, q = core_bq(i)
        maps.append({"hin": hc[i], "w": w_aug, "tab": make_tab(mods_l, b, (3, 4, None), [])})
    return [r["pout"] for r in _run(nc, maps)]


def run_merge(hc, yin, gin, mods_l, bw, ow, g, bta):
    nc = _get("merge", build_merge)
    maps = []
    for i in range(NCORE):
        b, q = core_bq(i)
        maps.append({"hin": hc[i], "yin": yin[i], "gin": gin[i], "bw": bw, "ow": ow,
                     "tab": make_tab(mods_l, b, (None, None, 5), [g, bta])})
    return [r["hout"] for r in _run(nc, maps)]


def kernel(x, c, ctx, c_ctx, ada_w, ada_b, ln_g, ln_b, ffn_w_in, ffn_w_out, mix_w_in, hgrn_lb, hgrn_norm_w,
           hyena_conv_w, hyena_conv_b, hyena_w1, hyena_b1, hyena_f1, hyena_w2, hyena_b2, hyena_f2, hyena_w3,
           hyena_bias, attn_sink, branch_w, out_w):
    f = lambda a: np.ascontiguousarray(np.asarray(a, dtype=np.float32))
    (x, c, ctx, c_ctx, ada_w, ada_b, ln_g, ln_b, ffn_w_in, ffn_w_out, mix_w_in, hgrn_lb, hgrn_norm_w, hyena_conv_w,
     hyena_conv_b, hyena_w1, hyena_b1, hyena_f1, hyena_w2, hyena_b2, hyena_f2, hyena_w3, hyena_bias, attn_sink,
     branch_w, out_w) = map(f, (x, c, ctx, c_ctx, ada_w, ada_b, ln_g, ln_b, ffn_w_in, ffn_w_out, mix_w_in, hgrn_lb,
                                hgrn_norm_w, hyena_conv_w, hyena_conv_b, hyena_w1, hyena_b1, hyena_f1, hyena_w2,
                                hyena_b2, hyena_f2, hyena_w3, hyena_bias, attn_sink, branch_w, out_w))
    mods = run_mods(c, c_ctx, ada_w, ada_b)
    hl, hx = x, ctx
    for l in range(2):
        hc = to_cores(hl, hx)
        h1c = run_ffn(hc, mods[l], (0, 1, 2), ffn_w_in[l, 0], ffn_w_out[l, 0], ln_g[l, 0], ln_b[l, 0])
        w = mix_w_in[l]
        w_aug = np.ascontiguousarray(np.concatenate([w, swap_cols(w[:, OFF_Q:OFF_K]), swap_cols(w[:, OFF_K:OFF_V])], 1))
        Pl, Pc = from_cores(run_inproj(h1c, mods[l], w_aug))
        ya, yca = run_hgrn(Pl, Pc, l, hgrn_lb, hgrn_norm_w[l])
        hy = (hyena_conv_w[l], hyena_conv_b[l], hyena_w1[l], hyena_b1[l], hyena_f1[l], hyena_w2[l], hyena_b2[l],
              hyena_f2[l], hyena_w3[l], hyena_bias[l])
        yb = run_hyfft(Pl[..., OFF_B:OFF_Q], *hy)
        ycb = run_hyena(Pc[..., OFF_B:OFF_Q], CTX, *hy) if l == 0 else np.zeros((2, CTX, 512), np.float32)
        yc, ycc = run_attn(Pl, Pc, attn_sink[l])
        yin = to_cores(np.concatenate([ya, yb, yc], -1), np.concatenate([yca, ycb, ycc], -1))
        gin = to_cores(Pl[..., OFF_G:OFF_QS], Pc[..., OFF_G:OFF_QS])
        h2c = run_merge(h1c, yin, gin, mods[l], np.ascontiguousarray(branch_w[l].reshape(1536, D)), out_w[l], ln_g[l, 1], ln_b[l, 1])
        h3c = run_ffn(h2c, mods[l], (6, 7, 8), ffn_w_in[l, 1], ffn_w_out[l, 1], ln_g[l, 2], ln_b[l, 2])
        hl, hx = from_cores(h3c)
    return hl.astype(np.float32)


LH = SEQ
NF = 2 * LH


def build_hyfft(stage=3):
    L = LH
    nc = bass.Bass("TRN2", target_bir_lowering=False)
    PP = 1024
    pbd = [nc.dram_tensor("pb%d" % i, [3, 128, L], F32, kind="ExternalInput").ap() for i in range(3)]
    cwd = nc.dram_tensor("cw", [128, 12], F32, kind="ExternalInput").ap()
    bsd = nc.dram_tensor("bias", [128, 1], F32, kind="ExternalInput").ap()
    ftd = nc.dram_tensor("feats", [2, 33, L], F32, kind="ExternalInput").ap()
    dcd = nc.dram_tensor("decay", [2, 128, L], F32, kind="ExternalInput").ap()
    w1d = nc.dram_tensor("w1", [33, 64], F32, kind="ExternalInput").ap()
    w2d = nc.dram_tensor("w2", [64, 64], F32, kind="ExternalInput").ap()
    w3d = nc.dram_tensor("w3", [64, 2, 128], F32, kind="ExternalInput").ap()
    bfd = nc.dram_tensor("bf", [64, 4], F32, kind="ExternalInput").ap()
    fad = nc.dram_tensor("fa", [128, 2, 2, 512], F32, kind="ExternalInput").ap()
    twd = nc.dram_tensor("tw", [128, 2, 512], F32, kind="ExternalInput").ap()
    ggd = nc.dram_tensor("gg", [128, 3, 128], F32, kind="ExternalInput").ap()
    ryd = nc.dram_tensor("ry", [128, 2, 256], F32, kind="ExternalInput").ap()
    itd = nc.dram_tensor("it", [128, 2, 2, 256], F32, kind="ExternalInput").ap()
    fid = nc.dram_tensor("fi", [128, 2, 3, 128], F32, kind="ExternalInput").ap()
    yo = nc.dram_tensor("yo", [128, L], F32, kind="ExternalOutput").ap()
    scr = nc.dram_tensor("scr", [3, 128, L], F32).ap()
    z1d = nc.dram_tensor("z1d", [128, L], F32).ap()
    circ = nc.dram_tensor("circ", [128, NF], F32).ap()
    Hs = nc.dram_tensor("Hs", [2, 128, 128, 256], F32).ap()
    CG = 16
    with ExitStack() as st:
        P = Prog(nc, st)
        cw = sb(nc, st, "cws", [128, 12], F32)
        bs = sb(nc, st, "bss", [128, 1], F32)
        w1 = sb(nc, st, "w1s", [33, 64], F32)
        w2 = sb(nc, st, "w2s", [64, 64], F32)
        w3 = sb(nc, st, "w3s", [64, 2, 128], F32)
        bf = sb(nc, st, "bfs", [64, 4], F32)
        fa = sb(nc, st, "fas", [128, 2, 2, 512], F32)
        tw = sb(nc, st, "tws", [128, 2, 512], F32)
        gg = sb(nc, st, "ggs", [128, 3, 128], F32)
        ry = sb(nc, st, "rys", [128, 2, 256], F32)
        itw = sb(nc, st, "its", [128, 2, 2, 256], F32)
        fi = sb(nc, st, "fis", [128, 2, 3, 128], F32)
        for t, dd in ((cw, cwd), (bs, bsd), (w1, w1d), (w2, w2d), (w3, w3d), (bf, bfd), (fa, fad), (tw, twd), (gg, ggd),
                      (ry, ryd), (itw, itd), (fi, fid)):
            P.dma("sp", t[:], dd, writes=["consts"])

        with ExitStack() as s1:
            ta = sb(nc, s1, "ta", [128, PP], F32)
            tb = sb(nc, s1, "tb", [128, PP], F32)
            tcx = sb(nc, s1, "tcx", [128, PP], F32)
            xs = [sb(nc, s1, "xs%d" % i, [128, PP], F32) for i in range(2)]
            ft = sb(nc, s1, "ft", [33, PP], F32)
            h1 = sb(nc, s1, "h1", [64, PP], F32)
            h2 = sb(nc, s1, "h2", [64, PP], F32)
            ki = sb(nc, s1, "ki", [64, PP], mybir.dt.int32)
            kf = sb(nc, s1, "kf", [64, PP], F32)
            dc = sb(nc, s1, "dc", [128, PP], F32)
            hp = [sb(nc, s1, "hp%d" % i, [128, PP], F32) for i in range(2)]
            l1 = sb(nc, s1, "l1", [128, 4], F32)
            zc = sb(nc, s1, "zc", [128, 1], F32)
            pm = ps(nc, s1, "pm", [64, 512])
            ph = [ps(nc, s1, "ph%d" % i, [128, 512]) for i in range(2)]
            xi = 0
            for part in range(3):
                for t0 in range(0, L, PP):
                    xb = xi % 2
                    xi += 1
                    P.dma("sp", ta[:], pbd[0][part, :, t0:t0 + PP], writes=["ta"])
                    P.dma("sp", tb[:], pbd[1][part, :, t0:t0 + PP], writes=["tb"])
                    P.dma("sp", tcx[:], pbd[2][part, :, t0:t0 + PP], writes=["tcx"])
                    c0 = part * 3
                    P.op("dve", lambda e, c0=c0, part=part: e.tensor_scalar(ta[:], ta[:], cw[:, c0:c0 + 1], cw[:, 9 + part:10 + part], ALU.mult, ALU.add),
                         reads=["ta", "consts"], writes=["ta"])
                    P.op("dve", lambda e, c0=c0: e.scalar_tensor_tensor(tb[:], tb[:], cw[:, c0 + 1:c0 + 2], ta[:], ALU.mult, ALU.add),
                         reads=["ta", "tb", "consts"], writes=["tb"])
                    P.op("dve", lambda e, c0=c0, xb=xb: e.scalar_tensor_tensor(xs[xb][:], tcx[:], cw[:, c0 + 2:c0 + 3], tb[:], ALU.mult, ALU.add),
                         reads=["tb", "tcx", "consts"], writes=[("xs", xb)])
                    P.dma("sp", scr[part, :, t0:t0 + PP], xs[xb][:], reads=[("xs", xb)], writes=["scr"])
            P.op("pool", lambda e: e.memset(l1[:], 0.0), writes=["l1"])
            P.op("pool", lambda e: e.memset(zc[:], 0.0), writes=["zc"])
            P.dma("sp", circ[:, L:L + 1], zc[:], reads=["zc"], writes=["circ"], allow_slow_non_contiguous=True)
            for pas in range(1):
                for dr in range(2):
                    for p0 in range(0, L, PP):
                        P.dma("sp", ft[:], ftd[dr, :, p0:p0 + PP], writes=["ft"])
                        P.dma("sp", dc[:], dcd[dr, :, p0:p0 + PP], writes=["dc"])
                        for (src, skey, w, K_, dst, dkey, bi) in ((ft, "ft", w1, 33, h1, "h1", 0), (h1, "h1", w2, 64, h2, "h2", 2)):
                            for q0 in range(0, PP, 512):
                                P.op("pe", lambda e, q0=q0, src=src, w=w, K_=K_: e.matmul(pm[:, :512], w[:K_, :], src[:K_, q0:q0 + 512], start=True, stop=True),
                                     reads=[skey, "consts"], writes=["pm"])
                                P.op("dve", lambda e, q0=q0, dst=dst, bi=bi: e.tensor_scalar(dst[:, q0:q0 + 512], pm[:, :512], bf[:, bi:bi + 1], bf[:, bi + 1:bi + 2], ALU.add, ALU.mult),
                                     reads=["pm", "consts"], writes=[dkey])
                            P.op("dve", lambda e, dst=dst: e.tensor_scalar(dst[:], dst[:], 1.0 / (2.0 * PI), 8.5, ALU.mult, ALU.add), reads=[dkey], writes=[dkey])
                            P.op("dve", lambda e, dst=dst: e.tensor_copy(ki[:], dst[:]), reads=[dkey], writes=["ki"])
                            P.op("dve", lambda e, dst=dst: e.tensor_copy(kf[:], ki[:]), reads=["ki"], writes=["kf"])
                            P.op("dve", lambda e, dst=dst: e.tensor_tensor(dst[:], dst[:], kf[:], ALU.subtract), reads=[dkey, "kf"], writes=[dkey])
                            P.op("dve", lambda e, dst=dst: e.tensor_single_scalar(kf[:], dst[:], 0.0, ALU.is_lt), reads=[dkey], writes=["kf"])
                            P.op("dve", lambda e, dst=dst: e.tensor_tensor(dst[:], dst[:], kf[:], ALU.add), reads=[dkey, "kf"], writes=[dkey])
                            P.op("dve", lambda e, dst=dst: e.tensor_scalar(dst[:], dst[:], 2.0 * PI, -PI, ALU.mult, ALU.add), reads=[dkey], writes=[dkey])
                            P.op("act", lambda e, dst=dst: e.activation(out=dst[:], in_=dst[:], func=AF.Sin), reads=[dkey], writes=[dkey])
                        hb = (p0 // PP) % 2
                        for q0 in range(0, PP, 512):
                            b = (q0 // 512) % 2
                            P.op("pe", lambda e, q0=q0, dr=dr, b=b: e.matmul(ph[b][:, :512], w3[:, dr, :], h2[:, q0:q0 + 512], start=True, stop=True),
                                 reads=["h2", "consts"], writes=[("ph", b)])
                            P.op("dve", lambda e, q0=q0, b=b, hb=hb: e.tensor_tensor(hp[hb][:, q0:q0 + 512], ph[b][:, :512], dc[:, q0:q0 + 512], ALU.mult),
                                 reads=[("ph", b), "dc"], writes=[("hp", hb)])
                        last = (dr == 1 and p0 + PP == L)
                        if last:
                            P.op("dve", lambda e, hb=hb: e.memset(hp[hb][:, PP - 1:PP], 0.0), reads=[("hp", hb)], writes=[("hp", hb)])
                        P.op("dve", lambda e, hb=hb: e.tensor_reduce(out=l1[:, 2:3], in_=hp[hb][:], axis=AX.X, op=ALU.add, apply_absolute_value=True),
                             reads=[("hp", hb)], writes=["l1p"])
                        P.op("dve", lambda e: e.tensor_tensor(l1[:, 0:1], l1[:, 0:1], l1[:, 2:3], ALU.add), reads=["l1p", "l1"], writes=["l1"])
                        if dr == 0:
                            P.dma("sp", circ[:, p0:p0 + PP], hp[hb][:], reads=[("hp", hb)], writes=["circ"])
                        else:
                            n = PP - 1 if last else PP
                            P.dma("sp", circ[:, L + 1 + p0:L + 1 + p0 + n], hp[hb][:, :n], reads=[("hp", hb)], writes=["circ"])
            P.op("dve", lambda e: e.reciprocal(l1[:, 1:2], l1[:, 0:1]), reads=["l1"], writes=["l1"])
            for p0 in range(0, NF, PP):
                hb = (p0 // PP) % 2
                P.dma("sp", hp[hb][:], circ[:, p0:p0 + PP], reads=["circ"], writes=[("hp", hb)])
                P.op("dve", lambda e, hb=hb: e.tensor_scalar(hp[hb][:], hp[hb][:], l1[:, 1:2], None, ALU.mult), reads=[("hp", hb), "l1"], writes=[("hp", hb)])
                if p0 == 0:
                    P.op("dve", lambda e, hb=hb: e.tensor_tensor(hp[hb][:, 0:1], hp[hb][:, 0:1], bs[:, 0:1], ALU.add),
                         reads=[("hp", hb), "consts"], writes=[("hp", hb)])
                P.dma("sp", circ[:, p0:p0 + PP], hp[hb][:], reads=[("hp", hb)], writes=["circ"])

        with ExitStack() as s2:
            if stage < 2:
                P.finish("sp")
                return nc
            Xr = sb(nc, s2, "Xr", [128, 2, CG, 128], F32)
            Ar = sb(nc, s2, "Ar", [128, CG, 256], F32)
            Ai = sb(nc, s2, "Ai", [128, CG, 256], F32)
            Hr = sb(nc, s2, "Hr", [128, CG, 256], F32)
            Hi = sb(nc, s2, "Hi", [128, CG, 256], F32)
            Br = sb(nc, s2, "Br", [128, 2, CG, 128], F32)
            Bi = sb(nc, s2, "Bi", [128, 2, CG, 128], F32)
            U = [sb(nc, s2, "U%d" % i, [128, 512], F32) for i in range(2)]
            V = [sb(nc, s2, "V%d" % i, [128, 512], F32) for i in range(2)]
            xg = [sb(nc, s2, "xg%d" % i, [128, 2, 4, 128], F32) for i in range(2)]
            og = [sb(nc, s2, "og%d" % i, [128, 2, 4, 128], F32) for i in range(2)]
            pA = [ps(nc, s2, "pA%d" % i, [128, 512]) for i in range(2)]
            pCr = ps(nc, s2, "pCr", [128, 512])
            pCi = ps(nc, s2, "pCi", [128, 512])
            pI = [ps(nc, s2, "pI%d" % i, [128, 512]) for i in range(2)]
            pOr = ps(nc, s2, "pOr", [128, 512])
            pOi = ps(nc, s2, "pOi", [128, 512])
            cnt = [0]

            def fwd_A_and_twiddle(c, real_only):
                b = cnt[0] % 2
                cnt[0] += 1
                if real_only:
                    P.op("pe", lambda e: e.matmul(pA[b][:], Xr[:, 0, c, :], fa[:, 0, 0, :], start=True, stop=False), reads=["X", "consts"], writes=[("pA", b)], sig=False)
                    P.op("pe", lambda e: e.matmul(pA[b][:], Xr[:, 1, c, :], fa[:, 1, 0, :], start=False, stop=True), reads=["X", "consts"], writes=[("pA", b)])
                else:
                    P.op("pe", lambda e: e.matmul(pA[b][:], Xr[:, 0, c, :], fa[:, 0, 0, :], start=True, stop=False), reads=["X", "consts"], writes=[("pA", b)], sig=False)
                    P.op("pe", lambda e: e.matmul(pA[b][:], Xr[:, 1, c, :], fa[:, 0, 1, :], start=False, stop=True), reads=["X", "consts"], writes=[("pA", b)])
                P.op("dve", lambda e: e.tensor_tensor(U[b][:], pA[b][:], tw[:, 0, :], ALU.mult), reads=[("pA", b), "consts"], writes=[("U", b)])
                P.op("dve", lambda e: e.tensor_tensor(V[b][:, 0:256], pA[b][:, 256:512], tw[:, 1, 0:256], ALU.mult), reads=[("pA", b), "consts"], writes=[("V", b)])
                P.op("dve", lambda e: e.tensor_tensor(V[b][:, 256:512], pA[b][:, 0:256], tw[:, 1, 256:512], ALU.mult), reads=[("pA", b), "consts"], writes=[("V", b)])
                P.op("pool", lambda e: e.tensor_tensor(Ar[:, c, :], U[b][:, 0:256], V[b][:, 0:256], ALU.add), reads=[("U", b), ("V", b)], writes=[("A", c // 2)])
                P.op("pool", lambda e: e.tensor_tensor(Ai[:, c, :], U[b][:, 256:512], V[b][:, 256:512], ALU.add), reads=[("U", b), ("V", b)], writes=[("A", c // 2)])

            def fwd_C(j):
                ar = Ar[:, 2 * j:2 * j + 2, :]
                ai = Ai[:, 2 * j:2 * j + 2, :]
                P.op("pe", lambda e: e.matmul(pCr[:], gg[:, 0, :], ar, start=True, stop=False), reads=[("A", j), "consts"], writes=["pCr"], sig=False)
                P.op("pe", lambda e: e.matmul(pCr[:], gg[:, 2, :], ai, start=False, stop=True), reads=[("A", j), "consts"], writes=["pCr"])
                P.op("pe", lambda e: e.matmul(pCi[:], gg[:, 1, :], ar, start=True, stop=False), reads=[("A", j), "consts"], writes=["pCi"], sig=False)
                P.op("pe", lambda e: e.matmul(pCi[:], gg[:, 0, :], ai, start=False, stop=True), reads=[("A", j), "consts"], writes=["pCi"])

            for g0 in range(0, 128, CG):
                for blk in range(2):
                    src = circ[g0:g0 + CG, blk * L:(blk + 1) * L].rearrange("c (p n) -> p c n", n=128)
                    P.dma("sp", Xr[:, blk, :, :], src, reads=["circ"], writes=["X"])
                for c in range(CG):
                    fwd_A_and_twiddle(c, True)
                for j in range(CG // 2):
                    fwd_C(j)
                    P.op("act", lambda e, j=j: e.copy(out=Hr[:, 2 * j:2 * j + 2, :], in_=pCr[:]), reads=["pCr"], writes=[("H", j)])
                    P.op("act", lambda e, j=j: e.copy(out=Hi[:, 2 * j:2 * j + 2, :], in_=pCi[:]), reads=["pCi"], writes=[("H", j)])
                hk = [("H", j) for j in range(CG // 2)]
                P.dma("sp", Hs[0, :, g0:g0 + CG, :], Hr[:], reads=hk, writes=["Hs"])
                P.dma("sp", Hs[1, :, g0:g0 + CG, :], Hi[:], reads=hk, writes=["Hs"])

            for o in range(2 if stage >= 3 else 0):
                zsrc = scr[0] if o == 0 else z1d
                xsrc = scr[1 + o]
                zdst = z1d if o == 0 else yo
                for g0 in range(0, 64, CG):
                    for b in range(2):
                        src = zsrc[b * 64 + g0:b * 64 + g0 + CG, :].rearrange("c (p n) -> p c n", n=128)
                        P.dma("sp", Xr[:, b, :, :], src, reads=["scr", "z1d"], writes=["X"])
                    for ri in range(2):
                        P.dma("sp", (Hr if ri == 0 else Hi)[:], Hs[ri, :, o * 64 + g0:o * 64 + g0 + CG, :], reads=["Hs"],
                              writes=[("H", j) for j in range(CG // 2)])
                    for c in range(CG):
                        fwd_A_and_twiddle(c, False)
                    for j in range(CG // 2):
                        fwd_C(j)
                        b = j % 2
                        hr = Hr[:, 2 * j:2 * j + 2, :]
                        hi = Hi[:, 2 * j:2 * j + 2, :]
                        yr = Ar[:, 2 * j:2 * j + 2, :]
                        yi = Ai[:, 2 * j:2 * j + 2, :]
                        P.op("dve", lambda e, b=b, hr=hr: e.tensor_tensor(U[b][:], pCr[:], hr, ALU.mult), reads=["pCr", ("H", j)], writes=[("U", b)])
                        P.op("dve", lambda e, b=b, hi=hi: e.tensor_tensor(V[b][:], pCi[:], hi, ALU.mult), reads=["pCi", ("H", j)], writes=[("V", b)])
                        P.op("pool", lambda e, b=b, yr=yr: e.tensor_tensor(yr, U[b][:], V[b][:], ALU.subtract), reads=[("U", b), ("V", b)], writes=[("A", j)])
                        P.op("dve", lambda e, b=b, hi=hi: e.tensor_tensor(U[b][:], pCr[:], hi, ALU.mult), reads=["pCr", ("H", j)], writes=[("U", b)])
                        P.op("dve", lambda e, b=b, hr=hr: e.tensor_tensor(V[b][:], pCi[:], hr, ALU.mult), reads=["pCi", ("H", j)], writes=[("V", b)])
                        P.op("pool", lambda e, b=b, yi=yi: e.tensor_tensor(yi, U[b][:], V[b][:], ALU.add), reads=[("U", b), ("V", b)], writes=[("A", j)])
                    for c in range(CG):
                        b = c % 2
                        for blk in range(2):
                            P.op("pe", lambda e, c=c, blk=blk, b=b: e.matmul(pI[b][:, blk * 256:(blk + 1) * 256], Ar[:, c, blk * 128:(blk + 1) * 128], ry[:, 0, :],
                                                                             start=True, stop=False), reads=[("A", c // 2), "consts"], writes=[("pI", b)], sig=False)
                            P.op("pe", lambda e, c=c, blk=blk, b=b: e.matmul(pI[b][:, blk * 256:(blk + 1) * 256], Ai[:, c, blk * 128:(blk + 1) * 128], ry[:, 1, :],
                                                                             start=False, stop=True), reads=[("A", c // 2), "consts"], writes=[("pI", b)])
                        for blk in range(2):
                            lo, mid, hi_ = blk * 256, blk * 256 + 128, blk * 256 + 256
                            P.op("dve", lambda e, b=b, blk=blk, lo=lo, hi_=hi_: e.tensor_tensor(U[b][:, lo:hi_], pI[b][:, lo:hi_], itw[:, blk, 0, :], ALU.mult),
                                 reads=[("pI", b), "consts"], writes=[("U", b)])
                            P.op("dve", lambda e, b=b, blk=blk, lo=lo, mid=mid, hi_=hi_: e.tensor_tensor(V[b][:, lo:mid], pI[b][:, mid:hi_], itw[:, blk, 1, 0:128], ALU.mult),
                                 reads=[("pI", b), "consts"], writes=[("V", b)])
                            P.op("dve", lambda e, b=b, blk=blk, lo=lo, mid=mid, hi_=hi_: e.tensor_tensor(V[b][:, mid:hi_], pI[b][:, lo:mid], itw[:, blk, 1, 128:256], ALU.mult),
                                 reads=[("pI", b), "consts"], writes=[("V", b)])
                            P.op("pool", lambda e, b=b, blk=blk, c=c, lo=lo, mid=mid: e.tensor_tensor(Br[:, blk, c, :], U[b][:, lo:mid], V[b][:, lo:mid], ALU.add),
                                 reads=[("U", b), ("V", b)], writes=[("B", c // 4)])
                            P.op("pool", lambda e, b=b, blk=blk, c=c, mid=mid, hi_=hi_: e.tensor_tensor(Bi[:, blk, c, :], U[b][:, mid:hi_], V[b][:, mid:hi_], ALU.add),
                                 reads=[("U", b), ("V", b)], writes=[("B", c // 4)])
                    for q in range(CG // 4):
                        ob = q % 2
                        cs = slice(4 * q, 4 * q + 4)
                        for b in range(2):
                            src = xsrc[b * 64 + g0 + 4 * q:b * 64 + g0 + 4 * q + 4, :].rearrange("c (p n) -> p c n", n=128)
                            P.dma("sp", xg[ob][:, b, :, :], src, reads=["scr"], writes=[("xg", ob)])
                        for blk in range(2):
                            P.op("pe", lambda e, blk=blk, cs=cs: e.matmul(pOr[:], fi[:, blk, 0, :], Br[:, blk, cs, :], start=(blk == 0), stop=False),
                                 reads=[("B", q), "consts"], writes=["pOr"], sig=False)
                            P.op("pe", lambda e, blk=blk, cs=cs: e.matmul(pOr[:], fi[:, blk, 2, :], Bi[:, blk, cs, :], start=False, stop=(blk == 1)),
                                 reads=[("B", q), "consts"], writes=["pOr"], sig=(blk == 1))
                        for blk in range(2):
                            P.op("pe", lambda e, blk=blk, cs=cs: e.matmul(pOi[:], fi[:, blk, 1, :], Br[:, blk, cs, :], start=(blk == 0), stop=False),
                                 reads=[("B", q), "consts"], writes=["pOi"], sig=False)
                            P.op("pe", lambda e, blk=blk, cs=cs: e.matmul(pOi[:], fi[:, blk, 0, :], Bi[:, blk, cs, :], start=False, stop=(blk == 1)),
                                 reads=[("B", q), "consts"], writes=["pOi"], sig=(blk == 1))
                        P.op("dve", lambda e, ob=ob: e.tensor_tensor(og[ob][:, 0, :, :], pOr[:], xg[ob][:, 0, :, :], ALU.mult), reads=["pOr", ("xg", ob)], writes=[("og", ob)])
                        P.op("dve", lambda e, ob=ob: e.tensor_tensor(og[ob][:, 1, :, :], pOi[:], xg[ob][:, 1, :, :], ALU.mult), reads=["pOi", ("xg", ob)], writes=[("og", ob)])
                        for b in range(2):
                            dst = zdst[b * 64 + g0 + 4 * q:b * 64 + g0 + 4 * q + 4, :].rearrange("c (p n) -> p c n", n=128)
                            P.dma("sp", dst, og[ob][:, b, :, :], reads=[("og", ob)], writes=["z1d" if o == 0 else "yo"])
        P.finish("sp")
    return nc


def hyfft_consts():
    N = NF
    n1 = np.arange(256, dtype=np.float64)
    k1 = np.arange(256, dtype=np.float64)
    n2 = np.arange(128, dtype=np.float64)
    k2 = np.arange(128, dtype=np.float64)
    a = 2 * np.pi * np.outer(n1, k1) / 256
    Fc, Fs = np.cos(a), np.sin(a)
    fa = np.zeros((256, 2, 512))
    fa[:, 0, :256], fa[:, 0, 256:] = Fc, -Fs
    fa[:, 1, :256], fa[:, 1, 256:] = Fs, Fc
    fa = fa.reshape(2, 128, 2, 512).transpose(1, 0, 2, 3)
    t = 2 * np.pi * np.outer(n2, k1) / N
    Tr, Ti = np.cos(t), -np.sin(t)
    tw = np.stack([np.concatenate([Tr, Tr], 1), np.concatenate([-Ti, Ti], 1)], 1)
    g = 2 * np.pi * np.outer(n2, k2) / 128
    Gr, Gi = np.cos(g), -np.sin(g)
    gg = np.stack([Gr, Gi, -Gi], 1)
    ry = np.stack([np.concatenate([Gr, -Gi], 1), np.concatenate([Gi, Gr], 1)], 1)
    tt = 2 * np.pi * np.outer(k1, n2) / N
    cTr, cTi = np.cos(tt), np.sin(tt)
    it = np.stack([np.concatenate([cTr, cTr], 1), np.concatenate([-cTi, cTi], 1)], 1)
    it = it.reshape(2, 128, 2, 256).transpose(1, 0, 2, 3)
    ai = 2 * np.pi * np.outer(k1, n1[:128]) / 256
    fi = np.stack([np.cos(ai), np.sin(ai), -np.sin(ai)], 1) / N
    fi = fi.reshape(2, 128, 3, 128).transpose(1, 0, 2, 3)
    f32 = lambda x: np.ascontiguousarray(x.astype(np.float32))
    return {"fa": f32(fa), "tw": f32(tw), "gg": f32(gg), "ry": f32(ry), "it": f32(it), "fi": f32(fi)}


def run_hyfft(X, cw, cb, w1, b1, f1, w2, b2, f2, w3, bias):
    L = LH
    import os
    nc = _get("hyfft", build_hyfft, int(os.environ.get("HYFFT_STAGE", "3")))
    consts = _get("hyfft_consts", hyfft_consts)
    pos = np.arange(L, dtype=np.float32)
    t = pos / np.float32(L - 1)
    wv = np.float32(2.0 * np.pi) * pos / np.float32(L)
    bands = np.linspace(1e-4, 15, 16, dtype=np.float32)
    feats = np.concatenate([t[:, None], np.cos(wv[:, None] * bands), -np.sin(wv[:, None] * bands)], -1).astype(np.float32).T
    rates = np.abs(np.linspace(np.log(1e-2) / 1.5, np.log(1e-2) / 0.3, 512, dtype=np.float32))
    decay = np.exp(-t[None, :] * rates[:, None]).astype(np.float32)
    Xp = np.pad(X, ((0, 0), (1, 1), (0, 0)))
    feats2 = np.ascontiguousarray(np.stack([feats, feats[:, ::-1]], 0))
    maps = []
    for i in range(NCORE):
        cs = np.arange(i * 64, (i + 1) * 64)

        def rows(a):
            return np.ascontiguousarray(np.stack([a[:, :, p * 512 + cs].transpose(0, 2, 1).reshape(128, L) for p in range(3)], 0))
        cwt = np.concatenate([np.stack([cw[tp, p * 512 + cs] for p in range(3) for tp in range(3)], 1),
                              np.stack([cb[p * 512 + cs] for p in range(3)], 1)], 1)
        w3r = w3.reshape(64, 2, 2, 512)[:, :, :, cs].reshape(64, 2, 128)
        dco = np.concatenate([decay[cs], decay[cs]], 0)
        m = {
            "pb0": rows(Xp[:, 0:L]), "pb1": rows(Xp[:, 1:L + 1]), "pb2": rows(Xp[:, 2:L + 2]),
            "cw": np.ascontiguousarray(np.concatenate([cwt, cwt], 0)).astype(np.float32),
            "bias": np.ascontiguousarray(bias[:, cs].reshape(128, 1)).astype(np.float32),
            "feats": feats2, "decay": np.ascontiguousarray(np.stack([dco, dco[:, ::-1]], 0)),
            "w1": np.ascontiguousarray(w1), "w2": np.ascontiguousarray(w2), "w3": np.ascontiguousarray(w3r).astype(np.float32),
            "bf": np.ascontiguousarray(np.stack([b1, f1, b2, f2], 1)).astype(np.float32),
        }
        m.update(consts)
        maps.append(m)
    res = _run(nc, maps)
    out = np.empty((2, L, 512), np.float32)
    for i in range(NCORE):
        out[:, :, i * 64:(i + 1) * 64] = res[i]["yo"].reshape(2, 64, L).transpose(0, 2, 1)
    return out
```

```python
import numpy as np
from contextlib import ExitStack
import concourse.bass as bass
import concourse.mybir as mybir
from concourse.bass_utils import run_bass_kernel_spmd

F32 = mybir.dt.float32
BF16 = mybir.dt.bfloat16
AF = mybir.ActivationFunctionType
ALU = mybir.AluOpType
AX = mybir.AxisListType

D = 1024
DFF = 2816
SEQ = 16384
CTX = 256
NCORE = 8
TLAT = 4096
TCTX = 64
TT = TLAT + TCTX
DN_ALPHA = 4.0 ** 0.25
LN_EPS = 1e-5
RMS_EPS = 1e-6
NQK = 8576

EPOCH = 12000
NDMASEM = 8


class Prog:
    def __init__(self, nc, stack):
        self.nc = nc
        self.stack = stack
        self.eng = {"pe": nc.tensor, "act": nc.scalar, "dve": nc.vector,
                    "pool": nc.gpsimd, "sp": nc.sync}
        self.cnt = {e: 0 for e in self.eng}
        self.sems = {}
        self.lastw = {}
        self.readers = {}
        self.seen = {e: {} for e in self.eng}
        self.dcnt = {e: 0 for e in self.eng}
        self.dpend = {}
        self.lastw_dma = {}

    def _sem(self, name):
        if name not in self.sems:
            self.sems[name] = self.stack.enter_context(self.nc.semaphore(name))
        return self.sems[name]

    def _wait(self, e, tok):
        name, val = tok
        if self.seen[e].get(name, 0) >= val:
            return
        self.eng[e].wait_ge(self._sem(name), val)
        self.seen[e][name] = val

    def _deps(self, reads, writes):
        deps = []
        for k in list(reads) + list(writes):
            if k in self.lastw:
                deps.append(self.lastw[k])
            if k in self.lastw_dma:
                deps.extend(self.lastw_dma[k].values())
        for k in writes:
            deps.extend(self.readers.get(k, []))
        return deps

    def _commit(self, tok, reads, writes, is_dma=False):
        for k in reads:
            self.readers.setdefault(k, []).append(tok)
        for k in writes:
            if is_dma and k in self.lastw_dma:
                self.lastw_dma[k][tok[0]] = tok
            elif is_dma:
                self.lastw_dma[k] = {tok[0]: tok}
            else:
                self.lastw_dma.pop(k, None)
            self.lastw[k] = tok
            self.readers[k] = []

    def op(self, e, fn, reads=(), writes=(), sig=True):
        ep, v = divmod(self.cnt[e], EPOCH)
        own = "c_%s_%d" % (e, ep)
        for tok in self._deps(reads, writes):
            if tok[0] == own and tok[1] > v:
                continue
            self._wait(e, tok)
        ins = fn(self.eng[e])
        tok = ("c_%s_%d" % (e, ep), v + 1)
        if sig:
            self.cnt[e] += 1
            ins.then_inc(self._sem(tok[0]), 1)
        self._commit(tok, reads, writes)
        return tok

    def dma(self, e, out, in_, reads=(), writes=(), **kw):
        for tok in self._deps(reads, writes):
            self._wait(e, tok)
        i = self.dcnt[e]
        self.dcnt[e] += 1
        name = "d_%s_%d" % (e, i % NDMASEM)
        prev = self.dpend.get(name)
        if prev is not None:
            self._wait(e, prev)
        ins = self.eng[e].dma_start(out=out, in_=in_, **kw)
        ins.then_inc(self._sem(name), 16)
        tok = (name, 16 * (i // NDMASEM + 1))
        self.dpend[name] = tok
        self._commit(tok, reads, writes, is_dma=True)
        return tok

    def finish(self, e="sp"):
        for tok in set(self.lastw.values()):
            self._wait(e, tok)
        for tok in self.dpend.values():
            self._wait(e, tok)


def sb(nc, st, name, shape, dt):
    return st.enter_context(nc.sbuf_tensor(name, shape, dt))


def ps(nc, st, name, shape, dt=F32):
    return st.enter_context(nc.psum_tensor(name, shape, dt))


def load_w(P, nc, st, name, dram, K, N, eng="pool"):
    kc = K // 128
    t = sb(nc, st, name, [128, kc, N], BF16)
    v = dram.rearrange("(k p) n -> p k n", p=128)
    for k in range(kc):
        P.dma(eng, t[:, k, :], v[:, k, :], writes=[(name, k)])
    return t


def wkeys(name, n):
    return [(name, k) for k in range(n)]


def token_tiles(tsz):
    tiles = []
    t = 0
    while t < TLAT:
        n = min(tsz, TLAT - t)
        tiles.append((t, n, 0))
        t += n
    tiles.append((TLAT, TCTX, 1))
    return tiles


class Common:
    def __init__(self, nc, st, P, tsz):
        self.nc, self.st, self.P, self.tsz = nc, st, P, tsz
        self.ones = sb(nc, st, "ones", [128, 128], F32)
        P.op("pool", lambda e: e.memset(self.ones[:], 1.0), writes=["ones"])
        self.r = sb(nc, st, "r", [128, 8, tsz], F32)
        self.sq = sb(nc, st, "sq", [128, 8, tsz], F32)
        self.t1 = sb(nc, st, "t1", [128, 2, tsz], F32)
        self.t2 = sb(nc, st, "t2", [128, 2, tsz], F32)
        self.stt = sb(nc, st, "stt", [128, 4, tsz], F32)
        self.ps1 = ps(nc, st, "ps1", [128, 512])
        self.ps2 = ps(nc, st, "ps2", [128, 512])

    def layer_norm(self, tn, gam, bet, out, outkey):
        P, r, sq = self.P, self.r, self.sq
        for d in range(8):
            P.op("act", lambda e, d=d: e.activation(out=sq[:, d, :tn], in_=r[:, d, :tn], func=AF.Square),
                 reads=[("r", d)], writes=[("sq", d)])
        for d in range(8):
            P.op("pe", lambda e, d=d: e.matmul(self.ps1[:, :tn], self.ones[:], r[:, d, :tn], start=(d == 0), stop=(d == 7)),
                 reads=[("r", d), "ones"], writes=["ps1"], sig=(d == 7))
        for d in range(8):
            P.op("pe", lambda e, d=d: e.matmul(self.ps2[:, :tn], self.ones[:], sq[:, d, :tn], start=(d == 0), stop=(d == 7)),
                 reads=[("sq", d), "ones"], writes=["ps2"], sig=(d == 7))
        mean, msq, var, rstd = (self.stt[:, i, :tn] for i in range(4))
        P.op("dve", lambda e: e.tensor_single_scalar(mean, self.ps1[:, :tn], 1.0 / D, ALU.mult), reads=["ps1"], writes=["mean"])
        P.op("dve", lambda e: e.tensor_tensor(msq, mean, mean, ALU.mult), reads=["mean"], writes=["msq"])
        P.op("dve", lambda e: e.scalar_tensor_tensor(var, self.ps2[:, :tn], 1.0 / D, msq, ALU.mult, ALU.subtract),
             reads=["ps2", "msq"], writes=["var"])
        P.op("dve", lambda e: e.tensor_single_scalar(var, var, LN_EPS, ALU.add), reads=["var"], writes=["var"])
        P.op("act", lambda e: e.sqrt(out=msq, in_=var), reads=["var"], writes=["msq"])
        P.op("dve", lambda e: e.reciprocal(rstd, msq), reads=["msq"], writes=["rstd"])
        for d in range(8):
            b = d % 2
            P.op("dve", lambda e, d=d, b=b: e.tensor_tensor(self.t1[:, b, :tn], r[:, d, :tn], mean, ALU.subtract),
                 reads=[("r", d), "mean"], writes=[("t1", b)])
            P.op("pool", lambda e, d=d, b=b: e.tensor_tensor(self.t2[:, b, :tn], self.t1[:, b, :tn], rstd, ALU.mult),
                 reads=[("t1", b), "rstd"], writes=[("t2", b)])
            P.op("act", lambda e, d=d, b=b: e.activation(out=out[:, d, :tn], in_=self.t2[:, b, :tn], func=AF.Identity,
                                                         scale=gam[:, d:d + 1], bias=bet[:, d:d + 1]),
                 reads=[("t2", b), "tab"], writes=[(outkey, d)])


def emit_modulate(P, tn, src, srckey, dst, dstkey, sc1p, sh):
    for c in range(8):
        P.op("pool", lambda e, c=c: e.tensor_scalar(dst[:, c, :tn], src[:, c, :tn], sc1p[:, c:c + 1], sh[:, c:c + 1],
                                                    ALU.mult, ALU.add),
             reads=[(srckey, c), "tab"], writes=[(dstkey, c)])


NC0 = 2 * 9216 // NCORE


def build_k0():
    nc = bass.Bass("TRN2", target_bir_lowering=False)
    cT = nc.dram_tensor("cT", [D, 3], F32, kind="ExternalInput").ap()
    aw = nc.dram_tensor("aw", [D, NC0], F32, kind="ExternalInput").ap()
    ab = nc.dram_tensor("ab", [128, NC0 // 128], F32, kind="ExternalInput").ap()
    out = nc.dram_tensor("out", [NC0, 3], F32, kind="ExternalOutput").ap()
    nj = NC0 // 128
    with ExitStack() as st:
        P = Prog(nc, st)
        ct = sb(nc, st, "ct", [128, 8, 3], F32)
        stt = sb(nc, st, "st", [128, 8, 3], F32)
        abt = sb(nc, st, "abt", [128, nj], F32)
        ot = sb(nc, st, "ot", [128, nj, 3], F32)
        pp = ps(nc, st, "pp", [128, nj, 4])
        P.dma("sp", ct[:], cT.rearrange("(k p) n -> p k n", p=128), writes=["ct"])
        P.dma("sp", abt[:], ab, writes=["abt"])
        P.op("act", lambda e: e.activation(out=stt[:], in_=ct[:], func=AF.Silu), reads=["ct"], writes=["st"])
        awv = aw.rearrange("(k p) n -> p k n", p=128)
        npc = 3
        cw = NC0 // npc
        wt = [sb(nc, st, "wt%d" % i, [128, 8, cw], F32) for i in range(npc)]
        for i in range(npc):
            for k in range(8):
                P.dma("sp", wt[i][:, k, :], awv[:, k, i * cw:(i + 1) * cw], writes=[("wt", i, k)])
        for j in range(nj):
            i, jj = divmod(j * 128, cw)
            for k in range(8):
                P.op("pe", lambda e, i=i, jj=jj, k=k, j=j: e.matmul(pp[:, j, 0:3], wt[i][:, k, jj:jj + 128], stt[:, k, :],
                                                                  start=(k == 0), stop=(k == 7)),
                     reads=[("wt", i, k), "st"], writes=[("pp", j)])
            P.op("dve", lambda e, j=j: e.tensor_scalar(ot[:, j, :], pp[:, j, 0:3], abt[:, j:j + 1], None, ALU.add),
                 reads=[("pp", j), "abt"], writes=["ot"])
        P.dma("sp", out.rearrange("(j p) n -> p j n", p=128), ot[:], reads=["ot"], writes=["out"])
        P.finish("sp")
    return nc


def prep_tab(P, nc, st, tab_d, ncols, gmul):
    tab = sb(nc, st, "tabs", [128, ncols * 8], F32)
    P.dma("sp", tab[:], tab_d, writes=["tab0"])
    sc1p, sh, gt = [], [], []
    for ms in range(2):
        o = ms * 24
        P.op("dve", lambda e, o=o: e.tensor_single_scalar(tab[:, o + 8:o + 16], tab[:, o + 8:o + 16], 1.0, ALU.add),
             reads=["tab0"], writes=["tab"])
        P.op("dve", lambda e, o=o: e.tensor_single_scalar(tab[:, o + 16:o + 24], tab[:, o + 16:o + 24], gmul, ALU.mult),
             reads=["tab0"], writes=["tab"])
        sh.append(tab[:, o:o + 8])
        sc1p.append(tab[:, o + 8:o + 16])
        gt.append(tab[:, o + 16:o + 24])
    return tab, sh, sc1p, gt


def build_ffn(tsz=256):
    nc = bass.Bass("TRN2", target_bir_lowering=False)
    hin = nc.dram_tensor("hin", [D, TT], F32, kind="ExternalInput").ap()
    w1d = nc.dram_tensor("w1", [D, 2 * DFF], F32, kind="ExternalInput").ap()
    w2d = nc.dram_tensor("w2", [DFF, D], F32, kind="ExternalInput").ap()
    tabd = nc.dram_tensor("tab", [128, 64], F32, kind="ExternalInput").ap()
    hout = nc.dram_tensor("hout", [D, TT], F32, kind="ExternalOutput").ap()
    hv = hin.rearrange("(c p) t -> p c t", p=128)
    ov = hout.rearrange("(c p) t -> p c t", p=128)
    NJ = DFF // 128
    with ExitStack() as st:
        P = Prog(nc, st)
        tab, sh, sc1p, gt = prep_tab(P, nc, st, tabd, 8, 0.5)
        gam, bet = tab[:, 48:56], tab[:, 56:64]
        w1 = load_w(P, nc, st, "w1s", w1d, D, 2 * DFF)
        w2 = load_w(P, nc, st, "w2s", w2d, DFF, D)
        C = Common(nc, st, P, tsz)
        hT = sb(nc, st, "hT", [128, 8, tsz], F32)
        hA = sb(nc, st, "hA", [128, 8, tsz], F32)
        uT = sb(nc, st, "uT", [128, 8, tsz], BF16)
        gT = sb(nc, st, "gT", [128, NJ, tsz], BF16)
        sa = sb(nc, st, "sa", [128, 2, tsz], F32)
        psA = [ps(nc, st, "psA%d" % i, [128, 512]) for i in range(2)]
        psB = [ps(nc, st, "psB%d" % i, [128, 512]) for i in range(2)]
        psY = [ps(nc, st, "psY%d" % i, [128, 512]) for i in range(2)]
        for (t0, tn, ms) in token_tiles(tsz):
            for c in range(8):
                P.dma("sp", hT[:, c, :tn], hv[:, c, t0:t0 + tn], writes=[("hT", c)])
            emit_modulate(P, tn, hT, "hT", uT, "uT", sc1p[ms], sh[ms])
            for c in range(8):
                P.op("pool", lambda e, c=c: e.tensor_single_scalar(hA[:, c, :tn], hT[:, c, :tn], DN_ALPHA, ALU.mult),
                     reads=[("hT", c)], writes=[("hA", c)])
            for j in range(NJ):
                b = j % 2
                for k in range(8):
                    P.op("pe", lambda e, j=j, k=k, b=b: e.matmul(psA[b][:, :tn], w1[:, k, j * 128:(j + 1) * 128], uT[:, k, :tn],
                                                                start=(k == 0), stop=(k == 7)),
                         reads=[("w1s", k), ("uT", k)], writes=[("psA", b)], sig=(k == 7))
                for k in range(8):
                    P.op("pe", lambda e, j=j, k=k, b=b: e.matmul(psB[b][:, :tn], w1[:, k, DFF + j * 128:DFF + (j + 1) * 128],
                                                                uT[:, k, :tn], start=(k == 0), stop=(k == 7)),
                         reads=[("w1s", k), ("uT", k)], writes=[("psB", b)], sig=(k == 7))
                P.op("act", lambda e, b=b: e.activation(out=sa[:, b, :tn], in_=psA[b][:, :tn], func=AF.Silu),
                     reads=[("psA", b)], writes=[("sa", b)])
                P.op("dve", lambda e, j=j, b=b: e.tensor_tensor(gT[:, j, :tn], sa[:, b, :tn], psB[b][:, :tn], ALU.mult),
                     reads=[("sa", b), ("psB", b)], writes=[("gT", j)])
            for d in range(8):
                b = d % 2
                for j in range(NJ):
                    P.op("pe", lambda e, j=j, d=d, b=b: e.matmul(psY[b][:, :tn], w2[:, j, d * 128:(d + 1) * 128], gT[:, j, :tn],
                                                                start=(j == 0), stop=(j == NJ - 1)),
                         reads=[("w2s", j), ("gT", j)], writes=[("psY", b)], sig=(j == NJ - 1))
                P.op("dve", lambda e, d=d, b=b: e.scalar_tensor_tensor(C.r[:, d, :tn], psY[b][:, :tn], gt[ms][:, d:d + 1],
                                                                      hA[:, d, :tn], ALU.mult, ALU.add),
                     reads=[("psY", b), ("hA", d), "tab"], writes=[("r", d)])
            C.layer_norm(tn, gam, bet, hT, "hT")
            for c in range(8):
                P.dma("sp", ov[:, c, t0:t0 + tn], hT[:, c, :tn], reads=[("hT", c)], writes=["hout"])
        P.finish("sp")
    return nc


def build_inproj(tsz=256):
    nc = bass.Bass("TRN2", target_bir_lowering=False)
    hin = nc.dram_tensor("hin", [D, TT], F32, kind="ExternalInput").ap()
    wd = nc.dram_tensor("w", [D, NQK], F32, kind="ExternalInput").ap()
    tabd = nc.dram_tensor("tab", [128, 48], F32, kind="ExternalInput").ap()
    pout = nc.dram_tensor("pout", [NQK, TT], F32, kind="ExternalOutput").ap()
    hv = hin.rearrange("(c p) t -> p c t", p=128)
    pv = pout.rearrange("(c p) t -> p c t", p=128)
    NO = NQK // 128
    G = 4
    with ExitStack() as st:
        P = Prog(nc, st)
        tab, sh, sc1p, gt = prep_tab(P, nc, st, tabd, 6, 1.0)
        w = load_w(P, nc, st, "ws", wd, D, NQK)
        hT = sb(nc, st, "hT", [128, 8, tsz], F32)
        uT = sb(nc, st, "uT", [128, 8, tsz], BF16)
        og = [sb(nc, st, "og%d" % i, [128, G, tsz], F32) for i in range(2)]
        pp = [ps(nc, st, "pp%d" % i, [128, 512]) for i in range(4)]
        gi = 0
        for (t0, tn, ms) in token_tiles(tsz):
            for c in range(8):
                P.dma("sp", hT[:, c, :tn], hv[:, c, t0:t0 + tn], writes=[("hT", c)])
            emit_modulate(P, tn, hT, "hT", uT, "uT", sc1p[ms], sh[ms])
            for o0 in range(0, NO, G):
                gn = min(G, NO - o0)
                ob = gi % 2
                gi += 1
                for g in range(gn):
                    o = o0 + g
                    b = o % 4
                    for k in range(8):
                        P.op("pe", lambda e, o=o, k=k, b=b: e.matmul(pp[b][:, :tn], w[:, k, o * 128:(o + 1) * 128], uT[:, k, :tn],
                                                                    start=(k == 0), stop=(k == 7)),
                             reads=[("ws", k), ("uT", k)], writes=[("pp", b)], sig=(k == 7))
                    if o % 2 == 0:
                        P.op("act", lambda e, g=g, b=b, ob=ob: e.copy(out=og[ob][:, g, :tn], in_=pp[b][:, :tn]),
                             reads=[("pp", b)], writes=[("og", ob)])
                    else:
                        P.op("dve", lambda e, g=g, b=b, ob=ob: e.tensor_copy(og[ob][:, g, :tn], pp[b][:, :tn]),
                             reads=[("pp", b)], writes=[("og", ob)])
                P.dma("sp", pv[:, o0:o0 + gn, t0:t0 + tn], og[ob][:, :gn, :tn], reads=[("og", ob)], writes=["pout"])
        P.finish("sp")
    return nc


def build_merge(tsz=256):
    nc = bass.Bass("TRN2", target_bir_lowering=False)
    hin = nc.dram_tensor("hin", [D, TT], F32, kind="ExternalInput").ap()
    yin = nc.dram_tensor("yin", [1536, TT], F32, kind="ExternalInput").ap()
    gin = nc.dram_tensor("gin", [3072, TT], F32, kind="ExternalInput").ap()
    bwd = nc.dram_tensor("bw", [1536, D], F32, kind="ExternalInput").ap()
    owd = nc.dram_tensor("ow", [D, D], F32, kind="ExternalInput").ap()
    tabd = nc.dram_tensor("tab", [128, 64], F32, kind="ExternalInput").ap()
    hout = nc.dram_tensor("hout", [D, TT], F32, kind="ExternalOutput").ap()
    hv = hin.rearrange("(c p) t -> p c t", p=128)
    yv = yin.rearrange("(c p) t -> p c t", p=128)
    gv = gin.rearrange("(c p) t -> p c t", p=128)
    ov = hout.rearrange("(c p) t -> p c t", p=128)
    with ExitStack() as st:
        P = Prog(nc, st)
        tab, sh, sc1p, gt = prep_tab(P, nc, st, tabd, 8, 1.0)
        gam, bet = tab[:, 48:56], tab[:, 56:64]
        bw = load_w(P, nc, st, "bws", bwd, 1536, D)
        ow = load_w(P, nc, st, "ows", owd, D, D)
        C = Common(nc, st, P, tsz)
        hT = sb(nc, st, "hT", [128, 8, tsz], F32)
        hA = sb(nc, st, "hA", [128, 8, tsz], F32)
        yb = sb(nc, st, "yb", [128, 12, tsz], BF16)
        sg = sb(nc, st, "sg", [128, 24, tsz], F32)
        macc = sb(nc, st, "macc", [128, 2, tsz], F32)
        mtmp = sb(nc, st, "mtmp", [128, 2, tsz], F32)
        mT = sb(nc, st, "mT", [128, 8, tsz], BF16)
        psM = [ps(nc, st, "psM%d" % i, [128, 512]) for i in range(3)]
        psY = [ps(nc, st, "psY%d" % i, [128, 512]) for i in range(2)]
        for (t0, tn, ms) in token_tiles(tsz):
            for c in range(8):
                P.dma("sp", hT[:, c, :tn], hv[:, c, t0:t0 + tn], writes=[("hT", c)])
            for c in range(12):
                P.dma("pool", yb[:, c, :tn], yv[:, c, t0:t0 + tn], writes=[("yb", c)])
            for c in range(24):
                P.dma("sp", sg[:, c, :tn], gv[:, c, t0:t0 + tn], writes=[("sg", c)])
                P.op("act", lambda e, c=c: e.activation(out=sg[:, c, :tn], in_=sg[:, c, :tn], func=AF.Sigmoid),
                     reads=[("sg", c)], writes=[("sg", c)])
            for c in range(8):
                P.op("pool", lambda e, c=c: e.tensor_single_scalar(hA[:, c, :tn], hT[:, c, :tn], DN_ALPHA, ALU.mult),
                     reads=[("hT", c)], writes=[("hA", c)])
            for d in range(8):
                b = d % 2
                for n in range(3):
                    for kc in range(4):
                        P.op("pe", lambda e, n=n, kc=kc, d=d: e.matmul(psM[n][:, :tn], bw[:, n * 4 + kc, d * 128:(d + 1) * 128],
                                                                      yb[:, n * 4 + kc, :tn], start=(kc == 0), stop=(kc == 3)),
                             reads=[("bws", n * 4 + kc), ("yb", n * 4 + kc)], writes=[("psM", n)], sig=(kc == 3))
                P.op("dve", lambda e, d=d, b=b: e.tensor_tensor(macc[:, b, :tn], sg[:, d, :tn], psM[0][:, :tn], ALU.mult),
                     reads=[("sg", d), ("psM", 0)], writes=[("macc", b)])
                P.op("dve", lambda e, d=d, b=b: e.tensor_tensor(mtmp[:, 0, :tn], sg[:, 8 + d, :tn], psM[1][:, :tn], ALU.mult),
                     reads=[("sg", 8 + d), ("psM", 1)], writes=[("mtmp", 0)])
                P.op("dve", lambda e, d=d, b=b: e.tensor_tensor(mtmp[:, 1, :tn], sg[:, 16 + d, :tn], psM[2][:, :tn], ALU.mult),
                     reads=[("sg", 16 + d), ("psM", 2)], writes=[("mtmp", 1)])
                P.op("pool", lambda e, b=b: e.tensor_tensor(macc[:, b, :tn], macc[:, b, :tn], mtmp[:, 0, :tn], ALU.add),
                     reads=[("macc", b), ("mtmp", 0)], writes=[("macc", b)])
                P.op("pool", lambda e, d=d, b=b: e.tensor_tensor(mT[:, d, :tn], macc[:, b, :tn], mtmp[:, 1, :tn], ALU.add),
                     reads=[("macc", b), ("mtmp", 1)], writes=[("mT", d)])
            for d in range(8):
                b = d % 2
                for k in range(8):
                    P.op("pe", lambda e, k=k, d=d, b=b: e.matmul(psY[b][:, :tn], ow[:, k, d * 128:(d + 1) * 128], mT[:, k, :tn],
                                                                start=(k == 0), stop=(k == 7)),
                         reads=[("ows", k), ("mT", k)], writes=[("psY", b)], sig=(k == 7))
                P.op("dve", lambda e, d=d, b=b: e.scalar_tensor_tensor(C.r[:, d, :tn], psY[b][:, :tn], gt[ms][:, d:d + 1],
                                                                      hA[:, d, :tn], ALU.mult, ALU.add),
                     reads=[("psY", b), ("hA", d), "tab"], writes=[("r", d)])
            C.layer_norm(tn, gam, bet, hT, "hT")
            for c in range(8):
                P.dma("sp", ov[:, c, t0:t0 + tn], hT[:, c, :tn], reads=[("hT", c)], writes=["hout"])
        P.finish("sp")
    return nc


_CACHE = {}


def _get(name, fn, *a):
    key = (name,) + a
    if key not in _CACHE:
        _CACHE[key] = fn(*a)
    return _CACHE[key]


def _run(nc, in_maps):
    res = run_bass_kernel_spmd(nc, in_maps, core_ids=list(range(NCORE)))
    return res.results


def _pc(v):
    return np.ascontiguousarray(v.reshape(-1, 128).T)


def core_bq(i):
    return i // 4, i % 4


def to_cores(lat, cx):
    outs = []
    for i in range(NCORE):
        b, q = core_bq(i)
        a = np.concatenate([lat[b, q * TLAT:(q + 1) * TLAT], cx[b, q * TCTX:(q + 1) * TCTX]], axis=0)
        outs.append(np.ascontiguousarray(a.T))
    return outs


def from_cores(outs):
    C = outs[0].shape[0]
    lat = np.empty((2, SEQ, C), np.float32)
    cx = np.empty((2, CTX, C), np.float32)
    for i in range(NCORE):
        b, q = core_bq(i)
        lat[b, q * TLAT:(q + 1) * TLAT] = outs[i][:, :TLAT].T
        cx[b, q * TCTX:(q + 1) * TCTX] = outs[i][:, TLAT:].T
    return lat, cx


def run_mods(c, c_ctx, ada_w, ada_b):
    cT = np.ascontiguousarray(np.concatenate([c, c_ctx[None]], 0).T)
    aw = np.concatenate([ada_w[0], ada_w[1]], axis=1)
    ab = np.concatenate([ada_b[0], ada_b[1]], axis=0)
    nc = _get("k0", build_k0)
    maps = []
    for i in range(NCORE):
        sl = slice(i * NC0, (i + 1) * NC0)
        maps.append({"cT": cT, "aw": np.ascontiguousarray(aw[:, sl]), "ab": _pc(ab[sl])})
    res = _run(nc, maps)
    allm = np.concatenate([r["out"] for r in res], axis=0)
    return allm.reshape(2, 9, D, 3)


def make_tab(mods_l, b, idx3, extra):
    cols = []
    for v in (b, 2):
        for m in idx3:
            cols.append(_pc(mods_l[m, :, v]) if m is not None else np.zeros((128, 8), np.float32))
    for e in extra:
        cols.append(_pc(e))
    return np.ascontiguousarray(np.concatenate(cols, axis=1).astype(np.float32))


def run_ffn(hc, mods_l, idx3, w1, w2, g, bta):
    nc = _get("ffn", build_ffn)
    maps = []
    for i in range(NCORE):
        b, q = core_bq(i)
        maps.append({"hin": hc[i], "w1": w1, "w2": w2, "tab": make_tab(mods_l, b, idx3, [g, bta])})
    return [r["hout"] for r in _run(nc, maps)]


NQ = TT
NKL = TLAT + 256
NKB = NKL // 128 + 2
NK = NKB * 128


def build_attn():
    nc = bass.Bass("TRN2", target_bir_lowering=False)
    qf = nc.dram_tensor("qf", [64, 8, NQ], F32, kind="ExternalInput").ap()
    qs = nc.dram_tensor("qs", [64, 8, NQ], F32, kind="ExternalInput").ap()
    cq = nc.dram_tensor("cq", [64, NQ], F32, kind="ExternalInput").ap()
    sq = nc.dram_tensor("sq", [64, NQ], F32, kind="ExternalInput").ap()
    kf = nc.dram_tensor("kf", [64, 2, NK], F32, kind="ExternalInput").ap()
    ks = nc.dram_tensor("ks", [64, 2, NK], F32, kind="ExternalInput").ap()
    ck = nc.dram_tensor("ck", [64, NK], F32, kind="ExternalInput").ap()
    sk = nc.dram_tensor("sk", [64, NK], F32, kind="ExternalInput").ap()
    vt = nc.dram_tensor("vt", [128, NKB, 128], F32, kind="ExternalInput").ap()
    mk = nc.dram_tensor("mk", [128, 4, 4, 128], F32, kind="ExternalInput").ap()
    sk8 = nc.dram_tensor("sink", [64, 8], F32, kind="ExternalInput").ap()
    yo = nc.dram_tensor("yo", [64, 8, NQ], F32, kind="ExternalOutput").ap()
    CH = 1152
    with ExitStack() as st:
        P = Prog(nc, st)
        kr = sb(nc, st, "kr", [64, 2, NK], BF16)
        vb = sb(nc, st, "vb", [128, NKB, 128], BF16)
        mb = sb(nc, st, "mb", [128, 4, 4, 128], BF16)
        ones = sb(nc, st, "ones", [128, 64], BF16)
        es = sb(nc, st, "es", [64, 8], F32)
        P.dma("pool", vb[:], vt, writes=["vb"])
        P.dma("pool", mb[:], mk, writes=["mb"])
        P.dma("sp", es[:], sk8, writes=["es"])
        P.op("act", lambda e: e.activation(out=es[:], in_=es[:], func=AF.Exp), reads=["es"], writes=["es"])
        P.op("pool", lambda e: e.memset(ones[:], 1.0), writes=["ones"])
        ta = sb(nc, st, "ta", [64, 2, CH], F32)
        tb = sb(nc, st, "tb", [64, 2, CH], F32)
        tc_ = sb(nc, st, "tc", [64, CH], F32)
        td = sb(nc, st, "td", [64, CH], F32)
        for c0 in range(0, NK, CH):
            P.dma("sp", ta[:], kf[:, :, c0:c0 + CH], writes=["ta"])
            P.dma("sp", tb[:], ks[:, :, c0:c0 + CH], writes=["tb"])
            P.dma("sp", tc_[:], ck[:, c0:c0 + CH], writes=["tc"])
            P.dma("sp", td[:], sk[:, c0:c0 + CH], writes=["td"])
            for h in range(2):
                P.op("dve", lambda e, h=h: e.tensor_tensor(ta[:, h, :], ta[:, h, :], tc_[:], ALU.mult), reads=["ta", "tc"], writes=["ta"])
                P.op("pool", lambda e, h=h: e.tensor_tensor(tb[:, h, :], tb[:, h, :], td[:], ALU.mult), reads=["tb", "td"], writes=["tb"])
                P.op("dve", lambda e, h=h, c0=c0: e.tensor_tensor(kr[:, h, c0:c0 + CH], ta[:, h, :], tb[:, h, :], ALU.add),
                     reads=["ta", "tb"], writes=["kr"])
        qa = sb(nc, st, "qa", [64, 8, 128], F32)
        qb = sb(nc, st, "qb", [64, 8, 128], F32)
        qc = sb(nc, st, "qc", [64, 128], F32)
        qd = sb(nc, st, "qd", [64, 128], F32)
        qr = sb(nc, st, "qr", [64, 8, 128], BF16)
        pT = [sb(nc, st, "pT%d" % i, [128, 4, 128], BF16) for i in range(5)]
        dn = sb(nc, st, "dn", [64, 4, 128], F32)
        ob = sb(nc, st, "ob", [64, 8, 128], F32)
        psS = [ps(nc, st, "psS%d" % i, [128, 512]) for i in range(3)]
        psN = [ps(nc, st, "psN%d" % i, [64, 512]) for i in range(2)]
        psD = [ps(nc, st, "psD%d" % i, [64, 512]) for i in range(2)]
        si = 0
        nblk = TLAT // 128
        for n in range(nblk + 1):
            t0 = n * 128
            nq = 128 if n < nblk else TCTX
            P.dma("sp", qa[:, :, :nq], qf[:, :, t0:t0 + nq], writes=["qa"])
            P.dma("sp", qb[:, :, :nq], qs[:, :, t0:t0 + nq], writes=["qb"])
            P.dma("sp", qc[:, :nq], cq[:, t0:t0 + nq], writes=["qc"])
            P.dma("sp", qd[:, :nq], sq[:, t0:t0 + nq], writes=["qd"])
            for h in range(8):
                P.op("dve", lambda e, h=h: e.tensor_tensor(qa[:, h, :nq], qa[:, h, :nq], qc[:, :nq], ALU.mult), reads=["qa", "qc"], writes=["qa"])
                P.op("pool", lambda e, h=h: e.tensor_tensor(qb[:, h, :nq], qb[:, h, :nq], qd[:, :nq], ALU.mult), reads=["qb", "qd"], writes=["qb"])
                P.op("dve", lambda e, h=h: e.tensor_tensor(qr[:, h, :nq], qa[:, h, :nq], qb[:, h, :nq], ALU.add),
                     reads=["qa", "qb"], writes=[("qr", h // 4)])
            if n < nblk:
                kbs = [(n, 0 if n == 0 else 1), (n + 1, None), (n + 2, 3 if n == nblk - 1 else 2), (NKB - 2, None), (NKB - 1, None)]
            else:
                kbs = [(NKB - 2, None), (NKB - 1, None)]
            for kvh in range(2):
                pb = kvh
                for i, (kb, mi) in enumerate(kbs):
                    sbk = si % 3
                    si += 1
                    for g in range(4):
                        P.op("pe", lambda e, kb=kb, g=g, sbk=sbk, kvh=kvh: e.matmul(psS[sbk][:, g * nq:(g + 1) * nq],
                                                                                   kr[:, kvh, kb * 128:(kb + 1) * 128],
                                                                                   qr[:, kvh * 4 + g, :nq], start=True, stop=True),
                             reads=["kr", ("qr", kvh)], writes=[("psS", sbk)], sig=(g == 3))
                    for g in range(4):
                        P.op("act", lambda e, g=g, i=i, sbk=sbk: e.activation(out=pT[i][:, g, :nq], in_=psS[sbk][:, g * nq:(g + 1) * nq],
                                                                              func=AF.Exp, scale=0.125),
                             reads=[("psS", sbk)], writes=[("pT", i)])
                    if mi is not None:
                        P.op("pool", lambda e, i=i, mi=mi: e.tensor_tensor(pT[i][:, :, :nq], pT[i][:, :, :nq], mb[:, mi, :, :nq], ALU.mult),
                             reads=[("pT", i), "mb"], writes=[("pT", i)])
                for g in range(4):
                    for i, (kb, mi) in enumerate(kbs):
                        P.op("pe", lambda e, i=i, kb=kb, g=g, kvh=kvh, pb=pb: e.matmul(psN[pb][:, g * nq:(g + 1) * nq],
                                                                                      vb[:, kb, kvh * 64:(kvh + 1) * 64], pT[i][:, g, :nq],
                                                                                      start=(i == 0), stop=(i == len(kbs) - 1)),
                             reads=[("pT", i), "vb"], writes=[("psN", pb)], sig=(g == 3 and i == len(kbs) - 1))
                for g in range(4):
                    for i, (kb, mi) in enumerate(kbs):
                        P.op("pe", lambda e, i=i, g=g, pb=pb: e.matmul(psD[pb][:, g * nq:(g + 1) * nq], ones[:], pT[i][:, g, :nq],
                                                                      start=(i == 0), stop=(i == len(kbs) - 1)),
                             reads=[("pT", i), "ones"], writes=[("psD", pb)], sig=(g == 3 and i == len(kbs) - 1))
                for g in range(4):
                    h = kvh * 4 + g
                    P.op("dve", lambda e, g=g, h=h, pb=pb: e.tensor_scalar(dn[:, g, :nq], psD[pb][:, g * nq:(g + 1) * nq], es[:, h:h + 1], None, ALU.add),
                         reads=[("psD", pb), "es"], writes=["dn"])
                    P.op("dve", lambda e, g=g: e.reciprocal(dn[:, g, :nq], dn[:, g, :nq]), reads=["dn"], writes=["dn"])
                    P.op("dve", lambda e, g=g, h=h, pb=pb: e.tensor_tensor(ob[:, h, :nq], psN[pb][:, g * nq:(g + 1) * nq], dn[:, g, :nq], ALU.mult),
                         reads=[("psN", pb), "dn"], writes=["ob"])
            P.dma("sp", yo[:, :, t0:t0 + nq], ob[:, :, :nq], reads=["ob"], writes=["yo"])
        P.finish("sp")
    return nc


def rope_tables(pos):
    pos = np.asarray(pos)
    valid = pos >= 0
    p = np.where(valid, pos, 0)
    row = (p // 64).astype(np.float32)
    col = (p % 64).astype(np.float32)
    inv = (np.float32(10000.0) ** (-np.arange(16, dtype=np.float32) * np.float32(2.0) / np.float32(32))).astype(np.float32)
    cos = np.ones((64, len(pos)), np.float32)
    sin = np.zeros((64, len(pos)), np.float32)
    for a, base in ((0, row), (1, col)):
        ang = (base[None, :] * inv[:, None]).astype(np.float32)
        for s in range(2):
            sl = slice(a * 32 + s * 16, a * 32 + s * 16 + 16)
            cos[sl] = np.cos(ang)
            sin[sl] = np.sin(ang) * (-1.0 if s == 0 else 1.0)
    cos[:, ~valid] = 1.0
    sin[:, ~valid] = 0.0
    return cos, sin


OFF_B, OFF_Q, OFF_K, OFF_V, OFF_G = 2560, 4096, 4608, 4736, 4864
OFF_QS, OFF_KS = 7936, 8448


def swap_cols(w):
    sh = w.shape
    return w.reshape(sh[:-1] + (-1, 2, 2, 16))[..., ::-1, :].reshape(sh)


def run_attn(Pl, Pc, sink):
    nc = _get("attn", build_attn)
    qi = np.arange(128)
    m_prev = (qi[:, None] >= qi[None, :]).astype(np.float32)
    m_next = (qi[:, None] <= qi[None, :]).astype(np.float32)
    maps = []
    for i in range(NCORE):
        b, q = core_bq(i)
        t0 = q * TLAT
        lat = Pl[b, t0:t0 + TLAT]
        cx = Pc[b, q * TCTX:(q + 1) * TCTX]

        def heads(a, n):
            return np.ascontiguousarray(a.reshape(a.shape[0], n, 64).transpose(2, 1, 0))
        qall = np.concatenate([lat[:, OFF_Q:OFF_K], cx[:, OFF_Q:OFF_K]], 0)
        qsall = np.concatenate([lat[:, OFF_QS:OFF_KS], cx[:, OFF_QS:OFF_KS]], 0)
        qpos = np.concatenate([np.arange(t0, t0 + TLAT), -np.ones(TCTX, np.int64)])
        cqt, sqt = rope_tables(qpos)
        kpos = np.arange(t0 - 128, t0 + TLAT + 128)
        kval = (kpos >= 0) & (kpos < SEQ)
        kidx = np.clip(kpos, 0, SEQ - 1)
        kl = Pl[b, kidx] * kval[:, None]
        kall = np.concatenate([kl[:, OFF_K:OFF_V], Pc[b][:, OFF_K:OFF_V]], 0)
        ksall = np.concatenate([kl[:, OFF_KS:NQK], Pc[b][:, OFF_KS:NQK]], 0)
        vall = np.concatenate([kl[:, OFF_V:OFF_G], Pc[b][:, OFF_V:OFF_G]], 0)
        ckt, skt = rope_tables(np.concatenate([np.where(kval, kpos, -1), -np.ones(CTX, np.int64)]))
        m0 = m_prev if q > 0 else np.zeros_like(m_prev)
        m3 = m_next if q < 3 else np.zeros_like(m_next)
        mk = np.stack([m0, m_prev, m_next, m3], 0)
        mk = np.ascontiguousarray(np.broadcast_to(mk.transpose(1, 0, 2)[:, :, None, :], (128, 4, 4, 128))).astype(np.float32)
        maps.append({
            "qf": heads(qall, 8), "qs": heads(qsall, 8), "cq": cqt, "sq": sqt,
            "kf": heads(kall, 2), "ks": heads(ksall, 2), "ck": ckt, "sk": skt,
            "vt": np.ascontiguousarray(vall.reshape(NKB, 128, 128).transpose(1, 0, 2)),
            "mk": mk, "sink": np.ascontiguousarray(np.broadcast_to(sink[None, :], (64, 8))).astype(np.float32),
        })
    res = _run(nc, maps)
    yl = np.empty((2, SEQ, 512), np.float32)
    yc = np.empty((2, CTX, 512), np.float32)
    for i in range(NCORE):
        b, q = core_bq(i)
        y = res[i]["yo"].transpose(2, 1, 0).reshape(NQ, 512)
        yl[b, q * TLAT:(q + 1) * TLAT] = y[:TLAT]
        yc[b, q * TCTX:(q + 1) * TCTX] = y[TLAT:]
    return yl, yc


NTOK = CTX + SEQ
NG = NTOK // 128


def build_hgrn():
    nc = bass.Bass("TRN2", target_bir_lowering=False)
    zd = [nc.dram_tensor("z%d" % d, [128, NG, 128], F32, kind="ExternalInput").ap() for d in range(2)]
    vd = nc.dram_tensor("v", [128, NG, 128], F32, kind="ExternalInput").ap()
    qd = nc.dram_tensor("q", [128, NTOK], F32, kind="ExternalInput").ap()
    gd = nc.dram_tensor("g", [128, NTOK], F32, kind="ExternalInput").ap()
    lbd = nc.dram_tensor("lb", [128, 2, 2, 128], F32, kind="ExternalInput").ap()
    fld = nc.dram_tensor("flag", [128, 1], F32, kind="ExternalInput").ap()
    nwd = nc.dram_tensor("nw", [128, 1], F32, kind="ExternalInput").ap()
    cfd = nc.dram_tensor("cf", [128, 5, 128], F32, kind="ExternalInput").ap()
    cmd = nc.dram_tensor("cm", [128, 4], F32, kind="ExternalInput").ap()
    yo = nc.dram_tensor("yo", [128, NTOK], F32, kind="ExternalOutput").ap()
    GB = 4
    with ExitStack() as st:
        P = Prog(nc, st)
        vb = sb(nc, st, "vb", [128, NG, 128], BF16)
        P.dma("pool", vb[:], vd, writes=["vb"])
        cf = sb(nc, st, "cfs", [128, 5, 128], F32)
        P.dma("sp", cf[:], cfd, writes=["cf"])
        bm = sb(nc, st, "bm", [128, 2, GB, 128], BF16)
        for i in range(GB):
            P.dma("pool", bm[:, :, i, :], cfd[:, 0:2, :], writes=["bm"])
        cm = sb(nc, st, "cms", [128, 4], F32)
        P.dma("sp", cm[:], cmd, writes=["cm"])
        nw = sb(nc, st, "nws", [128, 1], F32)
        P.dma("sp", nw[:], nwd, writes=["nw"])
        fl = sb(nc, st, "fls", [128, 1], F32)
        P.dma("sp", fl[:], fld, writes=["fl"])
        lbt = sb(nc, st, "lbs", [128, 2, 2, 128], F32)
        P.dma("sp", lbt[:], lbd, writes=["lbt"])
        LBb = sb(nc, st, "LBb", [128, 2, GB, 128], F32)
        LBa = sb(nc, st, "LBa", [128, 2, GB, 128], F32)
        for i in range(GB):
            P.op("dve", lambda e, i=i: e.tensor_tensor(LBb[:, :, i, :], lbt[:, 1], lbt[:, 0], ALU.subtract), reads=["lbt"], writes=["LBb"])
        P.op("act", lambda e: e.activation(out=LBb[:], in_=LBb[:], func=AF.Sigmoid), reads=["LBb"], writes=["LBb"])
        P.op("dve", lambda e: e.tensor_scalar(LBb[:], LBb[:], fl[:, 0:1], None, ALU.mult), reads=["LBb", "fl"], writes=["LBb"])
        P.op("dve", lambda e: e.tensor_scalar(LBa[:], LBb[:], -1.0, 1.0, ALU.mult, ALU.add), reads=["LBb"], writes=["LBa"])
        ones = sb(nc, st, "ones", [128, 128], F32)
        P.op("pool", lambda e: e.memset(ones[:], 1.0), writes=["ones"])
        osum = sb(nc, st, "osum", [128, NTOK], F32)
        S = sb(nc, st, "S", [128, 128], F32)
        NSLOT = 8
        Sbr = sb(nc, st, "Sbr", [128, NSLOT, 128], BF16)
        prev_slot = [0]
        W = GB * 128
        zt = [sb(nc, st, "zt%d" % i, [128, W], F32) for i in range(2)]
        qt32 = [sb(nc, st, "qt32%d" % i, [128, W], F32) for i in range(2)]
        ft = [sb(nc, st, "ft%d" % i, [128, W], F32) for i in range(2)]
        lf = [sb(nc, st, "lf%d" % i, [128, W], F32) for i in range(2)]
        kk = [sb(nc, st, "kk%d" % i, [128, W], F32) for i in range(2)]
        Eq = [sb(nc, st, "Eq%d" % i, [128, W], F32) for i in range(2)]
        Ek = [sb(nc, st, "Ek%d" % i, [128, W], F32) for i in range(2)]
        Er = [sb(nc, st, "Er%d" % i, [128, W], F32) for i in range(2)]
        qt = [sb(nc, st, "qt%d" % i, [128, W], BF16) for i in range(2)]
        kt = [sb(nc, st, "kt%d" % i, [128, W], BF16) for i in range(2)]
        k4 = [sb(nc, st, "k4%d" % i, [128, GB, 4, 128], BF16) for i in range(2)]
        am = [sb(nc, st, "am%d" % i, [128, W], BF16) for i in range(2)]
        pB = ps(nc, st, "pB", [128, 512])
        pR = ps(nc, st, "pR", [128, 512])
        pKA = ps(nc, st, "pKA", [128, 512])
        pO = ps(nc, st, "pO", [128, 512])
        pU = ps(nc, st, "pU", [128, GB, 4, 128])

        def phase1(d, batch, b):
            g0 = min(batch)
            nb = len(batch)
            w = nb * 128
            P.dma("sp", zt[b][:, :w], zd[d][:, g0:g0 + nb, :], writes=[("zt", b)])
            P.dma("sp", qt32[b][:, :w], qd[:, g0 * 128:(g0 + nb) * 128], writes=[("qt32", b)])
            P.op("act", lambda e: e.activation(out=zt[b][:, :w], in_=zt[b][:, :w], func=AF.Sigmoid), reads=[("zt", b)], writes=[("zt", b)])
            P.op("dve", lambda e: e.tensor_tensor(ft[b][:, :w], zt[b][:, :w], LBa[:, d, :nb, :], ALU.mult), reads=[("zt", b), "LBa"], writes=[("ft", b)])
            P.op("dve", lambda e: e.tensor_tensor(ft[b][:, :w], ft[b][:, :w], LBb[:, d, :nb, :], ALU.add), reads=[("ft", b), "LBb"], writes=[("ft", b)])
            P.op("act", lambda e: e.activation(out=lf[b][:, :w], in_=ft[b][:, :w], func=AF.Ln), reads=[("ft", b)], writes=[("lf", b)])
            P.op("pool", lambda e: e.tensor_scalar(kk[b][:, :w], ft[b][:, :w], -1.0, 1.0, ALU.mult, ALU.add), reads=[("ft", b)], writes=[("kk", b)])
            for i in range(nb):
                sl = slice(i * 128, (i + 1) * 128)
                P.op("pe", lambda e, sl=sl: e.matmul(pB[:, sl], lf[b][:, sl], cf[:, d, :], start=True, stop=True), reads=[("lf", b), "cf"], writes=["pB"], sig=(i == nb - 1))
            for i in range(nb):
                sl = slice(i * 128, (i + 1) * 128)
                P.op("pe", lambda e, sl=sl: e.matmul(pR[:, sl], cf[:, 2 + d, :], lf[b][:, sl], start=True, stop=True), reads=[("lf", b), "cf"], writes=["pR"], sig=(i == nb - 1))
            for i in range(nb):
                sl = slice(i * 128, (i + 1) * 128)
                P.op("pe", lambda e, sl=sl: e.matmul(pKA[:, sl], kk[b][:, sl], cf[:, 4, :], start=True, stop=True), reads=[("kk", b), "cf"], writes=["pKA"], sig=(i == nb - 1))
            P.op("act", lambda e: e.activation(out=Eq[b][:, :w], in_=pB[:, :w], func=AF.Exp), reads=["pB"], writes=[("Eq", b)])
            P.op("act", lambda e: e.activation(out=Ek[b][:, :w], in_=pB[:, :w], func=AF.Exp, scale=-1.0), reads=["pB"], writes=[("Ek", b)])
            P.op("act", lambda e: e.activation(out=Er[b][:, :w], in_=pR[:, :w], func=AF.Exp), reads=["pR"], writes=[("Er", b)])
            P.op("dve", lambda e: e.tensor_tensor(qt[b][:, :w], qt32[b][:, :w], Eq[b][:, :w], ALU.mult), reads=[("qt32", b), ("Eq", b)], writes=[("qt", b)])
            P.op("dve", lambda e: e.tensor_tensor(kt[b][:, :w], pKA[:, :w], Ek[b][:, :w], ALU.mult), reads=["pKA", ("Ek", b)], writes=[("kt", b)])
            P.op("pool", lambda e: e.tensor_tensor(Er[b][:, :w], kk[b][:, :w], Er[b][:, :w], ALU.mult), reads=[("kk", b), ("Er", b)], writes=[("Er", b)])
            for c in range(4):
                P.op("pool", lambda e, c=c: e.tensor_scalar(k4[b][:, :nb, c, :], Er[b][:, :w], cm[:, c:c + 1], None, ALU.mult),
                     reads=[("Er", b), "cm"], writes=[("k4", b)])
            for i in range(nb):
                sl = slice(i * 128, (i + 1) * 128)
                P.op("pe", lambda e, sl=sl: e.matmul(pKA[:, sl], kt[b][:, sl], qt[b][:, sl], start=True, stop=True), reads=[("kt", b), ("qt", b)], writes=["pKA"], sig=(i == nb - 1))
            P.op("dve", lambda e: e.tensor_tensor(am[b][:, :w], pKA[:, :w], bm[:, d, :nb, :], ALU.mult), reads=["pKA", "bm"], writes=[("am", b)])

        def scan(d, batch, b):
            g0 = min(batch)
            nb = len(batch)
            chunks = [0, 1, 2, 3] if d == 0 else [3, 2, 1, 0]
            for G in batch:
                i = G - g0
                for ci, c in enumerate(chunks):
                    P.op("pe", lambda e, i=i, c=c, G=G: e.matmul(pU[:, i, c, :], k4[b][:, i, c, :], vb[:, G, :], start=True, stop=True),
                         reads=[("k4", b), "vb"], writes=["pU"], sig=(G == batch[-1] and ci == 3))
            for G in batch:
                i = G - g0
                P.op("pe", lambda e, i=i, G=G: e.matmul(pO[:, i * 128:(i + 1) * 128], vb[:, G, :], am[b][:, i * 128:(i + 1) * 128], start=True, stop=False),
                     reads=["vb", ("am", b)], writes=["pO"], sig=False)
                for ci, c in enumerate(chunks):
                    col = i * 128 + c * 32
                    P.op("pe", lambda e, col=col, ci=ci, ps_=prev_slot[0]: e.matmul(pO[:, col:col + 32], Sbr[:, ps_, :], qt[b][:, col:col + 32],
                                                                                 start=False, stop=(ci == 3)),
                         reads=[("Sb", prev_slot[0]), ("qt", b)], writes=["pO"], sig=(ci == 3))
                    ns = (prev_slot[0] + 1) % NSLOT
                    dcol = col + (31 if d == 0 else 0)
                    P.op("dve", lambda e, i=i, c=c, ns=ns, dcol=dcol: e.scalar_tensor_tensor(Sbr[:, ns, :], S[:], Eq[b][:, dcol:dcol + 1], pU[:, i, c, :], ALU.mult, ALU.add),
                         reads=["S", ("Eq", b), "pU"], writes=[("Sb", ns)])
                    P.op("dve", lambda e, i=i, c=c, dcol=dcol: e.scalar_tensor_tensor(S[:], S[:], Eq[b][:, dcol:dcol + 1], pU[:, i, c, :], ALU.mult, ALU.add),
                         reads=["S", ("Eq", b), "pU"], writes=["S"])
                    prev_slot[0] = ns
            w = nb * 128
            ok = [("osum", G) for G in batch]
            if d == 0:
                P.op("act", lambda e: e.copy(out=osum[:, g0 * 128:g0 * 128 + w], in_=pO[:, :w]), reads=["pO"], writes=ok)
            else:
                P.op("dve", lambda e: e.tensor_tensor(osum[:, g0 * 128:g0 * 128 + w], osum[:, g0 * 128:g0 * 128 + w], pO[:, :w], ALU.add),
                     reads=["pO"] + ok, writes=ok)

        it = 0
        for d in range(2):
            P.op("pool", lambda e: e.memset(S[:], 0.0), writes=["S"])
            prev_slot[0] = (prev_slot[0] + 1) % NSLOT
            P.op("pool", lambda e, ps_=prev_slot[0]: e.memset(Sbr[:, ps_, :], 0.0), writes=[("Sb", prev_slot[0])])
            if d == 0:
                order = list(range(NG))
                batches = [order[i:i + GB] for i in range(0, NG, GB)]
            else:
                rest = list(range(NG - 1, 1, -1))
                batches = [[1, 0]] + [rest[i:i + GB] for i in range(0, len(rest), GB)]
            phase1(d, batches[0], it % 2)
            for k, batch in enumerate(batches):
                b = it % 2
                it += 1
                if k + 1 < len(batches):
                    phase1(d, batches[k + 1], it % 2)
                scan(d, batch, b)
        sq = sb(nc, st, "sq", [128, 512], F32)
        gt = sb(nc, st, "gt", [128, 512], F32)
        rs = sb(nc, st, "rs", [128, 512], F32)
        yt = [sb(nc, st, "yt%d" % i, [128, 512], F32) for i in range(2)]
        ri = 0
        for t0 in range(0, NTOK, 512):
            tn = min(512, NTOK - t0)
            b = ri % 2
            ri += 1
            rk = [("osum", G) for G in range(t0 // 128, (t0 + tn) // 128)]
            P.dma("sp", gt[:, :tn], gd[:, t0:t0 + tn], writes=["gt"])
            P.op("act", lambda e, t0=t0, tn=tn: e.activation(out=sq[:, :tn], in_=osum[:, t0:t0 + tn], func=AF.Square), reads=rk, writes=["sq"])
            P.op("pe", lambda e, tn=tn: e.matmul(pB[:, :tn], ones[:], sq[:, :tn], start=True, stop=True), reads=["sq", "ones"], writes=["pB"])
            P.op("dve", lambda e, tn=tn: e.tensor_scalar(rs[:, :tn], pB[:, :tn], 1.0 / 128, RMS_EPS, ALU.mult, ALU.add), reads=["pB"], writes=["rs"])
            P.op("act", lambda e, tn=tn: e.sqrt(out=rs[:, :tn], in_=rs[:, :tn]), reads=["rs"], writes=["rs"])
            P.op("dve", lambda e, tn=tn: e.reciprocal(rs[:, :tn], rs[:, :tn]), reads=["rs"], writes=["rs"])
            P.op("act", lambda e, tn=tn: e.activation(out=gt[:, :tn], in_=gt[:, :tn], func=AF.Silu), reads=["gt"], writes=["gt"])
            P.op("dve", lambda e, b=b, t0=t0, tn=tn: e.tensor_tensor(yt[b][:, :tn], osum[:, t0:t0 + tn], rs[:, :tn], ALU.mult), reads=rk + ["rs"], writes=[("yt", b)])
            P.op("pool", lambda e, b=b, tn=tn: e.tensor_tensor(yt[b][:, :tn], yt[b][:, :tn], gt[:, :tn], ALU.mult), reads=[("yt", b), "gt"], writes=[("yt", b)])
            P.op("act", lambda e, b=b, tn=tn: e.activation(out=yt[b][:, :tn], in_=yt[b][:, :tn], func=AF.Copy, scale=nw[:, 0:1]), reads=[("yt", b), "nw"], writes=[("yt", b)])
            P.dma("sp", yo[:, t0:t0 + tn], yt[b][:, :tn], reads=[("yt", b)], writes=["yo"])
        P.finish("sp")
    return nc


def run_hgrn(Pl, Pc, layer, hgrn_lb, norm_w):
    nc = _get("hgrn", build_hgrn)
    p = np.arange(128)
    same = (p[:, None] // 32) == (p[None, :] // 32)
    incl_f = (same & (p[:, None] <= p[None, :])).astype(np.float32)
    incl_b = (same & (p[:, None] >= p[None, :])).astype(np.float32)
    rem_f = (same & (p[:, None] > p[None, :])).astype(np.float32)
    rem_b = (same & (p[:, None] < p[None, :])).astype(np.float32)
    cf = np.ascontiguousarray(np.stack([incl_f, incl_b, rem_f, rem_b, np.eye(128, dtype=np.float32)], 1))
    cm = (p[:, None] // 32 == np.arange(4)[None, :]).astype(np.float32)
    maps = []
    for i in range(NCORE):
        b, hd = core_bq(i)
        seq = np.concatenate([Pc[b], Pl[b]], 0)
        cs = slice(hd * 128, (hd + 1) * 128)

        def tm(a):
            return np.ascontiguousarray(a.reshape(NG, 128, 128).transpose(1, 0, 2))
        lb = np.broadcast_to(hgrn_lb[:, :, cs][None], (128, 2, 2, 128))
        maps.append({
            "q": np.ascontiguousarray(seq[:, 0:512][:, cs].T), "v": tm(seq[:, 512:1024][:, cs]),
            "g": np.ascontiguousarray(seq[:, 1024:1536][:, cs].T),
            "z0": tm(seq[:, 1536:2048][:, cs]), "z1": tm(seq[:, 2048:2560][:, cs]),
            "lb": np.ascontiguousarray(lb).astype(np.float32),
            "flag": np.full((128, 1), float(layer), np.float32),
            "nw": np.ascontiguousarray(norm_w[cs][:, None]).astype(np.float32), "cf": cf, "cm": cm,
        })
    res = _run(nc, maps)
    yl = np.empty((2, SEQ, 512), np.float32)
    yc = np.empty((2, CTX, 512), np.float32)
    for i in range(NCORE):
        b, hd = core_bq(i)
        y = res[i]["yo"].T
        yc[b, :, hd * 128:(hd + 1) * 128] = y[:CTX]
        yl[b, :, hd * 128:(hd + 1) * 128] = y[CTX:]
    return yl, yc


PI = float(np.pi)


def build_hyena(L):
    nc = bass.Bass("TRN2", target_bir_lowering=False)
    PP = min(L, 1024)
    pbd = [nc.dram_tensor("pb%d" % i, [3, 128, L], F32, kind="ExternalInput").ap() for i in range(3)]
    cwd = nc.dram_tensor("cw", [128, 12], F32, kind="ExternalInput").ap()
    bsd = nc.dram_tensor("bias", [128, 2], F32, kind="ExternalInput").ap()
    ftd = nc.dram_tensor("feats", [33, L], F32, kind="ExternalInput").ap()
    dcd = nc.dram_tensor("decay", [128, L], F32, kind="ExternalInput").ap()
    w1d = nc.dram_tensor("w1", [33, 64], F32, kind="ExternalInput").ap()
    w2d = nc.dram_tensor("w2", [64, 64], F32, kind="ExternalInput").ap()
    w3d = nc.dram_tensor("w3", [64, 4, 128], F32, kind="ExternalInput").ap()
    bfd = nc.dram_tensor("bf", [64, 4], F32, kind="ExternalInput").ap()
    yo = nc.dram_tensor("yo", [128, L], F32, kind="ExternalOutput").ap()
    with ExitStack() as st:
        P = Prog(nc, st)
        cw = sb(nc, st, "cws", [128, 12], F32)
        bs = sb(nc, st, "bss", [128, 2], F32)
        w1 = sb(nc, st, "w1s", [33, 64], F32)
        w2 = sb(nc, st, "w2s", [64, 64], F32)
        w3 = sb(nc, st, "w3s", [64, 4, 128], F32)
        bf = sb(nc, st, "bfs", [64, 4], F32)
        for t, dd in ((cw, cwd), (bs, bsd), (w1, w1d), (w2, w2d), (w3, w3d), (bf, bfd)):
            P.dma("sp", t[:], dd, writes=["consts"])
        z = sb(nc, st, "z", [128, L], F32)
        y = sb(nc, st, "y", [128, L], F32)
        ta = sb(nc, st, "ta", [128, PP], F32)
        tb = sb(nc, st, "tb", [128, PP], F32)
        tcx = sb(nc, st, "tcx", [128, PP], F32)
        xs = sb(nc, st, "xs", [128, PP], F32)
        ft = sb(nc, st, "ft", [33, PP], F32)
        h1 = sb(nc, st, "h1", [64, PP], F32)
        h2 = sb(nc, st, "h2", [64, PP], F32)
        dc = sb(nc, st, "dc", [128, PP], F32)
        hp = [sb(nc, st, "hp%d" % i, [128, PP], F32) for i in range(2)]
        l1 = sb(nc, st, "l1", [128, 4], F32)
        ki = sb(nc, st, "ki", [64, PP], mybir.dt.int32)
        kf = sb(nc, st, "kf", [64, PP], F32)
        pm = ps(nc, st, "pm", [64, 512])
        ph = [ps(nc, st, "ph%d" % i, [128, 512]) for i in range(2)]

        def sconv(part, dst, dkey, t0, tn):
            P.dma("sp", ta[:, :tn], pbd[0][part, :, t0:t0 + tn], writes=["ta"])
            P.dma("sp", tb[:, :tn], pbd[1][part, :, t0:t0 + tn], writes=["tb"])
            P.dma("sp", tcx[:, :tn], pbd[2][part, :, t0:t0 + tn], writes=["tcx"])
            c0 = part * 3
            P.op("dve", lambda e: e.tensor_scalar(ta[:, :tn], ta[:, :tn], cw[:, c0:c0 + 1], cw[:, 9 + part:10 + part], ALU.mult, ALU.add),
                 reads=["ta", "consts"], writes=["ta"])
            P.op("dve", lambda e: e.scalar_tensor_tensor(tb[:, :tn], tb[:, :tn], cw[:, c0 + 1:c0 + 2], ta[:, :tn], ALU.mult, ALU.add),
                 reads=["ta", "tb", "consts"], writes=["tb"])
            P.op("dve", lambda e: e.scalar_tensor_tensor(dst, tcx[:, :tn], cw[:, c0 + 2:c0 + 3], tb[:, :tn], ALU.mult, ALU.add),
                 reads=["tb", "tcx", "consts"], writes=[dkey])

        for t0 in range(0, L, PP):
            sconv(0, z[:, t0:t0 + PP], "z", t0, PP)
        for o in range(2):
            P.op("pool", lambda e: e.memset(y[:], 0.0), writes=["y"])
            P.op("pool", lambda e: e.memset(l1[:], 0.0), writes=["l1"])
            for p0 in range(0, L, PP):
                P.dma("sp", ft[:], ftd[:, p0:p0 + PP], writes=["ft"])
                P.dma("sp", dc[:], dcd[:, p0:p0 + PP], writes=["dc"])
                for (src, skey, w, K_, dst, dkey, bi) in ((ft, "ft", w1, 33, h1, "h1", 0), (h1, "h1", w2, 64, h2, "h2", 2)):
                    for q0 in range(0, PP, 512):
                        qn = min(512, PP - q0)
                        P.op("pe", lambda e, q0=q0, qn=qn, src=src, w=w, K_=K_: e.matmul(pm[:, :qn], w[:K_, :], src[:K_, q0:q0 + qn], start=True, stop=True),
                             reads=[skey, "consts"], writes=["pm"])
                        P.op("dve", lambda e, q0=q0, qn=qn, dst=dst, bi=bi: e.tensor_scalar(dst[:, q0:q0 + qn], pm[:, :qn], bf[:, bi:bi + 1], bf[:, bi + 1:bi + 2], ALU.add, ALU.mult),
                             reads=["pm", "consts"], writes=[dkey])
                    P.op("dve", lambda e, dst=dst: e.tensor_scalar(dst[:], dst[:], 1.0 / (2.0 * PI), 8.5, ALU.mult, ALU.add), reads=[dkey], writes=[dkey])
                    P.op("dve", lambda e, dst=dst: e.tensor_copy(ki[:], dst[:]), reads=[dkey], writes=["ki"])
                    P.op("dve", lambda e, dst=dst: e.tensor_copy(kf[:], ki[:]), reads=["ki"], writes=["kf"])
                    P.op("dve", lambda e, dst=dst: e.tensor_tensor(dst[:], dst[:], kf[:], ALU.subtract), reads=[dkey, "kf"], writes=[dkey])
                    P.op("dve", lambda e, dst=dst: e.tensor_single_scalar(kf[:], dst[:], 0.0, ALU.is_lt), reads=[dkey], writes=["kf"])
                    P.op("dve", lambda e, dst=dst: e.tensor_tensor(dst[:], dst[:], kf[:], ALU.add), reads=[dkey, "kf"], writes=[dkey])
                    P.op("dve", lambda e, dst=dst: e.tensor_scalar(dst[:], dst[:], 2.0 * PI, -PI, ALU.mult, ALU.add), reads=[dkey], writes=[dkey])
                    P.op("act", lambda e, dst=dst: e.activation(out=dst[:], in_=dst[:], func=AF.Sin), reads=[dkey], writes=[dkey])
                for dr in range(2):
                    for q0 in range(0, PP, 512):
                        qn = min(512, PP - q0)
                        b = (q0 // 512) % 2
                        P.op("pe", lambda e, q0=q0, qn=qn, dr=dr, b=b: e.matmul(ph[b][:, :qn], w3[:, dr * 2 + o, :], h2[:, q0:q0 + qn], start=True, stop=True),
                             reads=["h2", "consts"], writes=[("ph", b)])
                        P.op("dve", lambda e, q0=q0, qn=qn, dr=dr, b=b: e.tensor_tensor(hp[dr][:, q0:q0 + qn], ph[b][:, :qn], dc[:, q0:q0 + qn], ALU.mult),
                             reads=[("ph", b), "dc"], writes=[("hp", dr)])
                    if dr == 1 and p0 == 0:
                        P.op("dve", lambda e: e.memset(hp[1][:, 0:1], 0.0), reads=[("hp", 1)], writes=[("hp", 1)])
                    P.op("dve", lambda e, dr=dr: e.tensor_reduce(out=l1[:, 2 + dr:3 + dr], in_=hp[dr][:], axis=AX.X, op=ALU.add, apply_absolute_value=True),
                         reads=[("hp", dr)], writes=["l1p"])
                    P.op("dve", lambda e, dr=dr: e.tensor_tensor(l1[:, 0:1], l1[:, 0:1], l1[:, 2 + dr:3 + dr], ALU.add), reads=["l1p", "l1"], writes=["l1"])
                for j in range(PP):
                    d = p0 + j
                    P.op("dve", lambda e, j=j, d=d: e.scalar_tensor_tensor(y[:, d:L], z[:, 0:L - d], hp[0][:, j:j + 1], y[:, d:L], ALU.mult, ALU.add),
                         reads=[("hp", 0), "z", "y"], writes=["y"])
                    if d >= 1:
                        P.op("dve", lambda e, j=j, d=d: e.scalar_tensor_tensor(y[:, 0:L - d], z[:, d:L], hp[1][:, j:j + 1], y[:, 0:L - d], ALU.mult, ALU.add),
                             reads=[("hp", 1), "z", "y"], writes=["y"])
            P.op("dve", lambda e: e.reciprocal(l1[:, 1:2], l1[:, 0:1]), reads=["l1"], writes=["l1"])
            for t0 in range(0, L, PP):
                sconv(1 + o, xs[:], "xs", t0, PP)
                P.op("dve", lambda e, t0=t0: e.tensor_scalar(y[:, t0:t0 + PP], y[:, t0:t0 + PP], l1[:, 1:2], None, ALU.mult), reads=["y", "l1"], writes=["y"])
                P.op("dve", lambda e, t0=t0: e.scalar_tensor_tensor(y[:, t0:t0 + PP], z[:, t0:t0 + PP], bs[:, o:o + 1], y[:, t0:t0 + PP], ALU.mult, ALU.add),
                     reads=["y", "z", "consts"], writes=["y"])
                P.op("dve", lambda e, t0=t0: e.tensor_tensor(z[:, t0:t0 + PP], y[:, t0:t0 + PP], xs[:], ALU.mult), reads=["y", "xs", "z"], writes=["z"])
        for t0 in range(0, L, PP):
            P.dma("sp", yo[:, t0:t0 + PP], z[:, t0:t0 + PP], reads=["z"], writes=["yo"])
        P.finish("sp")
    return nc


def run_hyena(X, L, cw, cb, w1, b1, f1, w2, b2, f2, w3, bias):
    nc = _get("hyena", build_hyena, L)
    pos = np.arange(L, dtype=np.float32)
    t = pos / np.float32(L - 1)
    wv = np.float32(2.0 * np.pi) * pos / np.float32(L)
    bands = np.linspace(1e-4, 15, 16, dtype=np.float32)
    feats = np.concatenate([t[:, None], np.cos(wv[:, None] * bands), -np.sin(wv[:, None] * bands)], -1).astype(np.float32)
    rates = np.abs(np.linspace(np.log(1e-2) / 1.5, np.log(1e-2) / 0.3, 512, dtype=np.float32))
    decay = np.exp(-t[None, :] * rates[:, None]).astype(np.float32)
    Xp = np.pad(X, ((0, 0), (1, 1), (0, 0)))
    maps = []
    for i in range(NCORE):
        cs = np.arange(i * 64, (i + 1) * 64)

        def rows(a):
            return np.ascontiguousarray(np.stack([a[:, :, p * 512 + cs].transpose(0, 2, 1).reshape(128, L) for p in range(3)], 0))
        cwt = np.concatenate([np.stack([cw[tp, p * 512 + cs] for p in range(3) for tp in range(3)], 1),
                              np.stack([cb[p * 512 + cs] for p in range(3)], 1)], 1)
        w3r = w3.reshape(64, 2, 2, 512)[:, :, :, cs]
        w3r = np.concatenate([w3r, w3r], -1).reshape(64, 4, 128)
        maps.append({
            "pb0": rows(Xp[:, 0:L]), "pb1": rows(Xp[:, 1:L + 1]), "pb2": rows(Xp[:, 2:L + 2]),
            "cw": np.ascontiguousarray(np.concatenate([cwt, cwt], 0)).astype(np.float32),
            "bias": np.ascontiguousarray(np.concatenate([bias[:, cs].T, bias[:, cs].T], 0)).astype(np.float32),
            "feats": np.ascontiguousarray(feats.T), "decay": np.ascontiguousarray(np.concatenate([decay[cs], decay[cs]], 0)),
            "w1": np.ascontiguousarray(w1), "w2": np.ascontiguousarray(w2), "w3": np.ascontiguousarray(w3r).astype(np.float32),
            "bf": np.ascontiguousarray(np.stack([b1, f1, b2, f2], 1)).astype(np.float32),
        })
    res = _run(nc, maps)
    out = np.empty((2, L, 512), np.float32)
    for i in range(NCORE):
        out[:, :, i * 64:(i + 1) * 64] = res[i]["yo"].reshape(2, 64, L).transpose(0, 2, 1)
    return out


def run_inproj(hc, mods_l, w_aug):
    nc = _get("inproj", build_inproj)
    maps = []
    for i in range(NCORE):
        b, q = core_bq(i)
        maps.append({"hin": hc[i], "w": w_aug, "tab": make_tab(mods_l, b, (3, 4, None), [])})
    return [r["pout"] for r in _run(nc, maps)]


def run_merge(hc, yin, gin, mods_l, bw, ow, g, bta):
    nc = _get("merge", build_merge)
    maps = []
    for i in range(NCORE):
        b, q = core_bq(i)
        maps.append({"hin": hc[i], "yin": yin[i], "gin": gin[i], "bw": bw, "ow": ow,
                     "tab": make_tab(mods_l, b, (None, None, 5), [g, bta])})
    return [r["hout"] for r in _run(nc, maps)]


def kernel(x, c, ctx, c_ctx, ada_w, ada_b, ln_g, ln_b, ffn_w_in, ffn_w_out, mix_w_in, hgrn_lb, hgrn_norm_w,
           hyena_conv_w, hyena_conv_b, hyena_w1, hyena_b1, hyena_f1, hyena_w2, hyena_b2, hyena_f2, hyena_w3,
           hyena_bias, attn_sink, branch_w, out_w):
    f = lambda a: np.ascontiguousarray(np.asarray(a, dtype=np.float32))
    (x, c, ctx, c_ctx, ada_w, ada_b, ln_g, ln_b, ffn_w_in, ffn_w_out, mix_w_in, hgrn_lb, hgrn_norm_w, hyena_conv_w,
     hyena_conv_b, hyena_w1, hyena_b1, hyena_f1, hyena_w2, hyena_b2, hyena_f2, hyena_w3, hyena_bias, attn_sink,
     branch_w, out_w) = map(f, (x, c, ctx, c_ctx, ada_w, ada_b, ln_g, ln_b, ffn_w_in, ffn_w_out, mix_w_in, hgrn_lb,
                                hgrn_norm_w, hyena_conv_w, hyena_conv_b, hyena_w1, hyena_b1, hyena_f1, hyena_w2,
                                hyena_b2, hyena_f2, hyena_w3, hyena_bias, attn_sink, branch_w, out_w))
    mods = run_mods(c, c_ctx, ada_w, ada_b)
    hl, hx = x, ctx
    for l in range(2):
        hc = to_cores(hl, hx)
        h1c = run_ffn(hc, mods[l], (0, 1, 2), ffn_w_in[l, 0], ffn_w_out[l, 0], ln_g[l, 0], ln_b[l, 0])
        w = mix_w_in[l]
        w_aug = np.ascontiguousarray(np.concatenate([w, swap_cols(w[:, OFF_Q:OFF_K]), swap_cols(w[:, OFF_K:OFF_V])], 1))
        Pl, Pc = from_cores(run_inproj(h1c, mods[l], w_aug))
        ya, yca = run_hgrn(Pl, Pc, l, hgrn_lb, hgrn_norm_w[l])
        hy = (hyena_conv_w[l], hyena_conv_b[l], hyena_w1[l], hyena_b1[l], hyena_f1[l], hyena_w2[l], hyena_b2[l],
              hyena_f2[l], hyena_w3[l], hyena_bias[l])
        yb = run_hyfft(Pl[..., OFF_B:OFF_Q], *hy)
        ycb = run_hyena(Pc[..., OFF_B:OFF_Q], CTX, *hy) if l == 0 else np.zeros((2, CTX, 512), np.float32)
        yc, ycc = run_attn(Pl, Pc, attn_sink[l])
        yin = to_cores(np.concatenate([ya, yb, yc], -1), np.concatenate([yca, ycb, ycc], -1))
        gin = to_cores(Pl[..., OFF_G:OFF_QS], Pc[..., OFF_G:OFF_QS])
        h2c = run_merge(h1c, yin, gin, mods[l], np.ascontiguousarray(branch_w[l].reshape(1536, D)), out_w[l], ln_g[l, 1], ln_b[l, 1])
        h3c = run_ffn(h2c, mods[l], (6, 7, 8), ffn_w_in[l, 1], ffn_w_out[l, 1], ln_g[l, 2], ln_b[l, 2])
        hl, hx = from_cores(h3c)
    return hl.astype(np.float32)


LH = SEQ
NF = 2 * LH


def build_hyfft(stage=3):
    L = LH
    nc = bass.Bass("TRN2", target_bir_lowering=False)
    PP = 1024
    pbd = [nc.dram_tensor("pb%d" % i, [3, 128, L], F32, kind="ExternalInput").ap() for i in range(3)]
    cwd = nc.dram_tensor("cw", [128, 12], F32, kind="ExternalInput").ap()
    bsd = nc.dram_tensor("bias", [128, 1], F32, kind="ExternalInput").ap()
    ftd = nc.dram_tensor("feats", [2, 33, L], F32, kind="ExternalInput").ap()
    dcd = nc.dram_tensor("decay", [2, 128, L], F32, kind="ExternalInput").ap()
    w1d = nc.dram_tensor("w1", [33, 64], F32, kind="ExternalInput").ap()
    w2d = nc.dram_tensor("w2", [64, 64], F32, kind="ExternalInput").ap()
    w3d = nc.dram_tensor("w3", [64, 2, 128], F32, kind="ExternalInput").ap()
    bfd = nc.dram_tensor("bf", [64, 4], F32, kind="ExternalInput").ap()
    fad = nc.dram_tensor("fa", [128, 2, 2, 512], F32, kind="ExternalInput").ap()
    twd = nc.dram_tensor("tw", [128, 2, 512], F32, kind="ExternalInput").ap()
    ggd = nc.dram_tensor("gg", [128, 3, 128], F32, kind="ExternalInput").ap()
    ryd = nc.dram_tensor("ry", [128, 2, 256], F32, kind="ExternalInput").ap()
    itd = nc.dram_tensor("it", [128, 2, 2, 256], F32, kind="ExternalInput").ap()
    fid = nc.dram_tensor("fi", [128, 2, 3, 128], F32, kind="ExternalInput").ap()
    yo = nc.dram_tensor("yo", [128, L], F32, kind="ExternalOutput").ap()
    scr = nc.dram_tensor("scr", [3, 128, L], F32).ap()
    z1d = nc.dram_tensor("z1d", [128, L], F32).ap()
    circ = nc.dram_tensor("circ", [128, NF], F32).ap()
    Hs = nc.dram_tensor("Hs", [2, 128, 128, 256], F32).ap()
    CG = 16
    with ExitStack() as st:
        P = Prog(nc, st)
        cw = sb(nc, st, "cws", [128, 12], F32)
        bs = sb(nc, st, "bss", [128, 1], F32)
        w1 = sb(nc, st, "w1s", [33, 64], F32)
        w2 = sb(nc, st, "w2s", [64, 64], F32)
        w3 = sb(nc, st, "w3s", [64, 2, 128], F32)
        bf = sb(nc, st, "bfs", [64, 4], F32)
        fa = sb(nc, st, "fas", [128, 2, 2, 512], F32)
        tw = sb(nc, st, "tws", [128, 2, 512], F32)
        gg = sb(nc, st, "ggs", [128, 3, 128], F32)
        ry = sb(nc, st, "rys", [128, 2, 256], F32)
        itw = sb(nc, st, "its", [128, 2, 2, 256], F32)
        fi = sb(nc, st, "fis", [128, 2, 3, 128], F32)
        for t, dd in ((cw, cwd), (bs, bsd), (w1, w1d), (w2, w2d), (w3, w3d), (bf, bfd), (fa, fad), (tw, twd), (gg, ggd),
                      (ry, ryd), (itw, itd), (fi, fid)):
            P.dma("sp", t[:], dd, writes=["consts"])

        with ExitStack() as s1:
            ta = sb(nc, s1, "ta", [128, PP], F32)
            tb = sb(nc, s1, "tb", [128, PP], F32)
            tcx = sb(nc, s1, "tcx", [128, PP], F32)
            xs = [sb(nc, s1, "xs%d" % i, [128, PP], F32) for i in range(2)]
            ft = sb(nc, s1, "ft", [33, PP], F32)
            h1 = sb(nc, s1, "h1", [64, PP], F32)
            h2 = sb(nc, s1, "h2", [64, PP], F32)
            ki = sb(nc, s1, "ki", [64, PP], mybir.dt.int32)
            kf = sb(nc, s1, "kf", [64, PP], F32)
            dc = sb(nc, s1, "dc", [128, PP], F32)
            hp = [sb(nc, s1, "hp%d" % i, [128, PP], F32) for i in range(2)]
            l1 = sb(nc, s1, "l1", [128, 4], F32)
            zc = sb(nc, s1, "zc", [128, 1], F32)
            pm = ps(nc, s1, "pm", [64, 512])
            ph = [ps(nc, s1, "ph%d" % i, [128, 512]) for i in range(2)]
            xi = 0
            for part in range(3):
                for t0 in range(0, L, PP):
                    xb = xi % 2
                    xi += 1
                    P.dma("sp", ta[:], pbd[0][part, :, t0:t0 + PP], writes=["ta"])
                    P.dma("sp", tb[:], pbd[1][part, :, t0:t0 + PP], writes=["tb"])
                    P.dma("sp", tcx[:], pbd[2][part, :, t0:t0 + PP], writes=["tcx"])
                    c0 = part * 3
                    P.op("dve", lambda e, c0=c0, part=part: e.tensor_scalar(ta[:], ta[:], cw[:, c0:c0 + 1], cw[:, 9 + part:10 + part], ALU.mult, ALU.add),
                         reads=["ta", "consts"], writes=["ta"])
                    P.op("dve", lambda e, c0=c0: e.scalar_tensor_tensor(tb[:], tb[:], cw[:, c0 + 1:c0 + 2], ta[:], ALU.mult, ALU.add),
                         reads=["ta", "tb", "consts"], writes=["tb"])
                    P.op("dve", lambda e, c0=c0, xb=xb: e.scalar_tensor_tensor(xs[xb][:], tcx[:], cw[:, c0 + 2:c0 + 3], tb[:], ALU.mult, ALU.add),
                         reads=["tb", "tcx", "consts"], writes=[("xs", xb)])
                    P.dma("sp", scr[part, :, t0:t0 + PP], xs[xb][:], reads=[("xs", xb)], writes=["scr"])
            P.op("pool", lambda e: e.memset(l1[:], 0.0), writes=["l1"])
            P.op("pool", lambda e: e.memset(zc[:], 0.0), writes=["zc"])
            P.dma("sp", circ[:, L:L + 1], zc[:], reads=["zc"], writes=["circ"], allow_slow_non_contiguous=True)
            for pas in range(1):
                for dr in range(2):
                    for p0 in range(0, L, PP):
                        P.dma("sp", ft[:], ftd[dr, :, p0:p0 + PP], writes=["ft"])
                        P.dma("sp", dc[:], dcd[dr, :, p0:p0 + PP], writes=["dc"])
                        for (src, skey, w, K_, dst, dkey, bi) in ((ft, "ft", w1, 33, h1, "h1", 0), (h1, "h1", w2, 64, h2, "h2", 2)):
                            for q0 in range(0, PP, 512):
                                P.op("pe", lambda e, q0=q0, src=src, w=w, K_=K_: e.matmul(pm[:, :512], w[:K_, :], src[:K_, q0:q0 + 512], start=True, stop=True),
                                     reads=[skey, "consts"], writes=["pm"])
                                P.op("dve", lambda e, q0=q0, dst=dst, bi=bi: e.tensor_scalar(dst[:, q0:q0 + 512], pm[:, :512], bf[:, bi:bi + 1], bf[:, bi + 1:bi + 2], ALU.add, ALU.mult),
                                     reads=["pm", "consts"], writes=[dkey])
                            P.op("dve", lambda e, dst=dst: e.tensor_scalar(dst[:], dst[:], 1.0 / (2.0 * PI), 8.5, ALU.mult, ALU.add), reads=[dkey], writes=[dkey])
                            P.op("dve", lambda e, dst=dst: e.tensor_copy(ki[:], dst[:]), reads=[dkey], writes=["ki"])
                            P.op("dve", lambda e, dst=dst: e.tensor_copy(kf[:], ki[:]), reads=["ki"], writes=["kf"])
                            P.op("dve", lambda e, dst=dst: e.tensor_tensor(dst[:], dst[:], kf[:], ALU.subtract), reads=[dkey, "kf"], writes=[dkey])
                            P.op("dve", lambda e, dst=dst: e.tensor_single_scalar(kf[:], dst[:], 0.0, ALU.is_lt), reads=[dkey], writes=["kf"])
                            P.op("dve", lambda e, dst=dst: e.tensor_tensor(dst[:], dst[:], kf[:], ALU.add), reads=[dkey, "kf"], writes=[dkey])
                            P.op("dve", lambda e, dst=dst: e.tensor_scalar(dst[:], dst[:], 2.0 * PI, -PI, ALU.mult, ALU.add), reads=[dkey], writes=[dkey])
                            P.op("act", lambda e, dst=dst: e.activation(out=dst[:], in_=dst[:], func=AF.Sin), reads=[dkey], writes=[dkey])
                        hb = (p0 // PP) % 2
                        for q0 in range(0, PP, 512):
                            b = (q0 // 512) % 2
                            P.op("pe", lambda e, q0=q0, dr=dr, b=b: e.matmul(ph[b][:, :512], w3[:, dr, :], h2[:, q0:q0 + 512], start=True, stop=True),
                                 reads=["h2", "consts"], writes=[("ph", b)])
                            P.op("dve", lambda e, q0=q0, b=b, hb=hb: e.tensor_tensor(hp[hb][:, q0:q0 + 512], ph[b][:, :512], dc[:, q0:q0 + 512], ALU.mult),
                                 reads=[("ph", b), "dc"], writes=[("hp", hb)])
                        last = (dr == 1 and p0 + PP == L)
                        if last:
                            P.op("dve", lambda e, hb=hb: e.memset(hp[hb][:, PP - 1:PP], 0.0), reads=[("hp", hb)], writes=[("hp", hb)])
                        P.op("dve", lambda e, hb=hb: e.tensor_reduce(out=l1[:, 2:3], in_=hp[hb][:], axis=AX.X, op=ALU.add, apply_absolute_value=True),
                             reads=[("hp", hb)], writes=["l1p"])
                        P.op("dve", lambda e: e.tensor_tensor(l1[:, 0:1], l1[:, 0:1], l1[:, 2:3], ALU.add), reads=["l1p", "l1"], writes=["l1"])
                        if dr == 0:
                            P.dma("sp", circ[:, p0:p0 + PP], hp[hb][:], reads=[("hp", hb)], writes=["circ"])
                        else:
                            n = PP - 1 if last else PP
                            P.dma("sp", circ[:, L + 1 + p0:L + 1 + p0 + n], hp[hb][:, :n], reads=[("hp", hb)], writes=["circ"])
            P.op("dve", lambda e: e.reciprocal(l1[:, 1:2], l1[:, 0:1]), reads=["l1"], writes=["l1"])
            for p0 in range(0, NF, PP):
                hb = (p0 // PP) % 2
                P.dma("sp", hp[hb][:], circ[:, p0:p0 + PP], reads=["circ"], writes=[("hp", hb)])
                P.op("dve", lambda e, hb=hb: e.tensor_scalar(hp[hb][:], hp[hb][:], l1[:, 1:2], None, ALU.mult), reads=[("hp", hb), "l1"], writes=[("hp", hb)])
                if p0 == 0:
                    P.op("dve", lambda e, hb=hb: e.tensor_tensor(hp[hb][:, 0:1], hp[hb][:, 0:1], bs[:, 0:1], ALU.add),
                         reads=[("hp", hb), "consts"], writes=[("hp", hb)])
                P.dma("sp", circ[:, p0:p0 + PP], hp[hb][:], reads=[("hp", hb)], writes=["circ"])

        with ExitStack() as s2:
            if stage < 2:
                P.finish("sp")
                return nc
            Xr = sb(nc, s2, "Xr", [128, 2, CG, 128], F32)
            Ar = sb(nc, s2, "Ar", [128, CG, 256], F32)
            Ai = sb(nc, s2, "Ai", [128, CG, 256], F32)
            Hr = sb(nc, s2, "Hr", [128, CG, 256], F32)
            Hi = sb(nc, s2, "Hi", [128, CG, 256], F32)
            Br = sb(nc, s2, "Br", [128, 2, CG, 128], F32)
            Bi = sb(nc, s2, "Bi", [128, 2, CG, 128], F32)
            U = [sb(nc, s2, "U%d" % i, [128, 512], F32) for i in range(2)]
            V = [sb(nc, s2, "V%d" % i, [128, 512], F32) for i in range(2)]
            xg = [sb(nc, s2, "xg%d" % i, [128, 2, 4, 128], F32) for i in range(2)]
            og = [sb(nc, s2, "og%d" % i, [128, 2, 4, 128], F32) for i in range(2)]
            pA = [ps(nc, s2, "pA%d" % i, [128, 512]) for i in range(2)]
            pCr = ps(nc, s2, "pCr", [128, 512])
            pCi = ps(nc, s2, "pCi", [128, 512])
            pI = [ps(nc, s2, "pI%d" % i, [128, 512]) for i in range(2)]
            pOr = ps(nc, s2, "pOr", [128, 512])
            pOi = ps(nc, s2, "pOi", [128, 512])
            cnt = [0]

            def fwd_A_and_twiddle(c, real_only):
                b = cnt[0] % 2
                cnt[0] += 1
                if real_only:
                    P.op("pe", lambda e: e.matmul(pA[b][:], Xr[:, 0, c, :], fa[:, 0, 0, :], start=True, stop=False), reads=["X", "consts"], writes=[("pA", b)], sig=False)
                    P.op("pe", lambda e: e.matmul(pA[b][:], Xr[:, 1, c, :], fa[:, 1, 0, :], start=False, stop=True), reads=["X", "consts"], writes=[("pA", b)])
                else:
                    P.op("pe", lambda e: e.matmul(pA[b][:], Xr[:, 0, c, :], fa[:, 0, 0, :], start=True, stop=False), reads=["X", "consts"], writes=[("pA", b)], sig=False)
                    P.op("pe", lambda e: e.matmul(pA[b][:], Xr[:, 1, c, :], fa[:, 0, 1, :], start=False, stop=True), reads=["X", "consts"], writes=[("pA", b)])
                P.op("dve", lambda e: e.tensor_tensor(U[b][:], pA[b][:], tw[:, 0, :], ALU.mult), reads=[("pA", b), "consts"], writes=[("U", b)])
                P.op("dve", lambda e: e.tensor_tensor(V[b][:, 0:256], pA[b][:, 256:512], tw[:, 1, 0:256], ALU.mult), reads=[("pA", b), "consts"], writes=[("V", b)])
                P.op("dve", lambda e: e.tensor_tensor(V[b][:, 256:512], pA[b][:, 0:256], tw[:, 1, 256:512], ALU.mult), reads=[("pA", b), "consts"], writes=[("V", b)])
                P.op("pool", lambda e: e.tensor_tensor(Ar[:, c, :], U[b][:, 0:256], V[b][:, 0:256], ALU.add), reads=[("U", b), ("V", b)], writes=[("A", c // 2)])
                P.op("pool", lambda e: e.tensor_tensor(Ai[:, c, :], U[b][:, 256:512], V[b][:, 256:512], ALU.add), reads=[("U", b), ("V", b)], writes=[("A", c // 2)])

            def fwd_C(j):
                ar = Ar[:, 2 * j:2 * j + 2, :]
                ai = Ai[:, 2 * j:2 * j + 2, :]
                P.op("pe", lambda e: e.matmul(pCr[:], gg[:, 0, :], ar, start=True, stop=False), reads=[("A", j), "consts"], writes=["pCr"], sig=False)
                P.op("pe", lambda e: e.matmul(pCr[:], gg[:, 2, :], ai, start=False, stop=True), reads=[("A", j), "consts"], writes=["pCr"])
                P.op("pe", lambda e: e.matmul(pCi[:], gg[:, 1, :], ar, start=True, stop=False), reads=[("A", j), "consts"], writes=["pCi"], sig=False)
                P.op("pe", lambda e: e.matmul(pCi[:], gg[:, 0, :], ai, start=False, stop=True), reads=[("A", j), "consts"], writes=["pCi"])

            for g0 in range(0, 128, CG):
                for blk in range(2):
                    src = circ[g0:g0 + CG, blk * L:(blk + 1) * L].rearrange("c (p n) -> p c n", n=128)
                    P.dma("sp", Xr[:, blk, :, :], src, reads=["circ"], writes=["X"])
                for c in range(CG):
                    fwd_A_and_twiddle(c, True)
                for j in range(CG // 2):
                    fwd_C(j)
                    P.op("act", lambda e, j=j: e.copy(out=Hr[:, 2 * j:2 * j + 2, :], in_=pCr[:]), reads=["pCr"], writes=[("H", j)])
                    P.op("act", lambda e, j=j: e.copy(out=Hi[:, 2 * j:2 * j + 2, :], in_=pCi[:]), reads=["pCi"], writes=[("H", j)])
                hk = [("H", j) for j in range(CG // 2)]
                P.dma("sp", Hs[0, :, g0:g0 + CG, :], Hr[:], reads=hk, writes=["Hs"])
                P.dma("sp", Hs[1, :, g0:g0 + CG, :], Hi[:], reads=hk, writes=["Hs"])

            for o in range(2 if stage >= 3 else 0):
                zsrc = scr[0] if o == 0 else z1d
                xsrc = scr[1 + o]
                zdst = z1d if o == 0 else yo
                for g0 in range(0, 64, CG):
                    for b in range(2):
                        src = zsrc[b * 64 + g0:b * 64 + g0 + CG, :].rearrange("c (p n) -> p c n", n=128)
                        P.dma("sp", Xr[:, b, :, :], src, reads=["scr", "z1d"], writes=["X"])
                    for ri in range(2):
                        P.dma("sp", (Hr if ri == 0 else Hi)[:], Hs[ri, :, o * 64 + g0:o * 64 + g0 + CG, :], reads=["Hs"],
                              writes=[("H", j) for j in range(CG // 2)])
                    for c in range(CG):
                        fwd_A_and_twiddle(c, False)
                    for j in range(CG // 2):
                        fwd_C(j)
                        b = j % 2
                        hr = Hr[:, 2 * j:2 * j + 2, :]
                        hi = Hi[:, 2 * j:2 * j + 2, :]
                        yr = Ar[:, 2 * j:2 * j + 2, :]
                        yi = Ai[:, 2 * j:2 * j + 2, :]
                        P.op("dve", lambda e, b=b, hr=hr: e.tensor_tensor(U[b][:], pCr[:], hr, ALU.mult), reads=["pCr", ("H", j)], writes=[("U", b)])
                        P.op("dve", lambda e, b=b, hi=hi: e.tensor_tensor(V[b][:], pCi[:], hi, ALU.mult), reads=["pCi", ("H", j)], writes=[("V", b)])
                        P.op("pool", lambda e, b=b, yr=yr: e.tensor_tensor(yr, U[b][:], V[b][:], ALU.subtract), reads=[("U", b), ("V", b)], writes=[("A", j)])
                        P.op("dve", lambda e, b=b, hi=hi: e.tensor_tensor(U[b][:], pCr[:], hi, ALU.mult), reads=["pCr", ("H", j)], writes=[("U", b)])
                        P.op("dve", lambda e, b=b, hr=hr: e.tensor_tensor(V[b][:], pCi[:], hr, ALU.mult), reads=["pCi", ("H", j)], writes=[("V", b)])
                        P.op("pool", lambda e, b=b, yi=yi: e.tensor_tensor(yi, U[b][:], V[b][:], ALU.add), reads=[("U", b), ("V", b)], writes=[("A", j)])
                    for c in range(CG):
                        b = c % 2
                        for blk in range(2):
                            P.op("pe", lambda e, c=c, blk=blk, b=b: e.matmul(pI[b][:, blk * 256:(blk + 1) * 256], Ar[:, c, blk * 128:(blk + 1) * 128], ry[:, 0, :],
                                                                             start=True, stop=False), reads=[("A", c // 2), "consts"], writes=[("pI", b)], sig=False)
                            P.op("pe", lambda e, c=c, blk=blk, b=b: e.matmul(pI[b][:, blk * 256:(blk + 1) * 256], Ai[:, c, blk * 128:(blk + 1) * 128], ry[:, 1, :],
                                                                             start=False, stop=True), reads=[("A", c // 2), "consts"], writes=[("pI", b)])
                        for blk in range(2):
                            lo, mid, hi_ = blk * 256, blk * 256 + 128, blk * 256 + 256
                            P.op("dve", lambda e, b=b, blk=blk, lo=lo, hi_=hi_: e.tensor_tensor(U[b][:, lo:hi_], pI[b][:, lo:hi_], itw[:, blk, 0, :], ALU.mult),
                                 reads=[("pI", b), "consts"], writes=[("U", b)])
                            P.op("dve", lambda e, b=b, blk=blk, lo=lo, mid=mid, hi_=hi_: e.tensor_tensor(V[b][:, lo:mid], pI[b][:, mid:hi_], itw[:, blk, 1, 0:128], ALU.mult),
                                 reads=[("pI", b), "consts"], writes=[("V", b)])
                            P.op("dve", lambda e, b=b, blk=blk, lo=lo, mid=mid, hi_=hi_: e.tensor_tensor(V[b][:, mid:hi_], pI[b][:, lo:mid], itw[:, blk, 1, 128:256], ALU.mult),
                                 reads=[("pI", b), "consts"], writes=[("V", b)])
                            P.op("pool", lambda e, b=b, blk=blk, c=c, lo=lo, mid=mid: e.tensor_tensor(Br[:, blk, c, :], U[b][:, lo:mid], V[b][:, lo:mid], ALU.add),
                                 reads=[("U", b), ("V", b)], writes=[("B", c // 4)])
                            P.op("pool", lambda e, b=b, blk=blk, c=c, mid=mid, hi_=hi_: e.tensor_tensor(Bi[:, blk, c, :], U[b][:, mid:hi_], V[b][:, mid:hi_], ALU.add),
                                 reads=[("U", b), ("V", b)], writes=[("B", c // 4)])
                    for q in range(CG // 4):
                        ob = q % 2
                        cs = slice(4 * q, 4 * q + 4)
                        for b in range(2):
                            src = xsrc[b * 64 + g0 + 4 * q:b * 64 + g0 + 4 * q + 4, :].rearrange("c (p n) -> p c n", n=128)
                            P.dma("sp", xg[ob][:, b, :, :], src, reads=["scr"], writes=[("xg", ob)])
                        for blk in range(2):
                            P.op("pe", lambda e, blk=blk, cs=cs: e.matmul(pOr[:], fi[:, blk, 0, :], Br[:, blk, cs, :], start=(blk == 0), stop=False),
                                 reads=[("B", q), "consts"], writes=["pOr"], sig=False)
                            P.op("pe", lambda e, blk=blk, cs=cs: e.matmul(pOr[:], fi[:, blk, 2, :], Bi[:, blk, cs, :], start=False, stop=(blk == 1)),
                                 reads=[("B", q), "consts"], writes=["pOr"], sig=(blk == 1))
                        for blk in range(2):
                            P.op("pe", lambda e, blk=blk, cs=cs: e.matmul(pOi[:], fi[:, blk, 1, :], Br[:, blk, cs, :], start=(blk == 0), stop=False),
                                 reads=[("B", q), "consts"], writes=["pOi"], sig=False)
                            P.op("pe", lambda e, blk=blk, cs=cs: e.matmul(pOi[:], fi[:, blk, 0, :], Bi[:, blk, cs, :], start=False, stop=(blk == 1)),
                                 reads=[("B", q), "consts"], writes=["pOi"], sig=(blk == 1))
                        P.op("dve", lambda e, ob=ob: e.tensor_tensor(og[ob][:, 0, :, :], pOr[:], xg[ob][:, 0, :, :], ALU.mult), reads=["pOr", ("xg", ob)], writes=[("og", ob)])
                        P.op("dve", lambda e, ob=ob: e.tensor_tensor(og[ob][:, 1, :, :], pOi[:], xg[ob][:, 1, :, :], ALU.mult), reads=["pOi", ("xg", ob)], writes=[("og", ob)])
                        for b in range(2):
                            dst = zdst[b * 64 + g0 + 4 * q:b * 64 + g0 + 4 * q + 4, :].rearrange("c (p n) -> p c n", n=128)
                            P.dma("sp", dst, og[ob][:, b, :, :], reads=[("og", ob)], writes=["z1d" if o == 0 else "yo"])
        P.finish("sp")
    return nc


def hyfft_consts():
    N = NF
    n1 = np.arange(256, dtype=np.float64)
    k1 = np.arange(256, dtype=np.float64)
    n2 = np.arange(128, dtype=np.float64)
    k2 = np.arange(128, dtype=np.float64)
    a = 2 * np.pi * np.outer(n1, k1) / 256
    Fc, Fs = np.cos(a), np.sin(a)
    fa = np.zeros((256, 2, 512))
    fa[:, 0, :256], fa[:, 0, 256:] = Fc, -Fs
    fa[:, 1, :256], fa[:, 1, 256:] = Fs, Fc
    fa = fa.reshape(2, 128, 2, 512).transpose(1, 0, 2, 3)
    t = 2 * np.pi * np.outer(n2, k1) / N
    Tr, Ti = np.cos(t), -np.sin(t)
    tw = np.stack([np.concatenate([Tr, Tr], 1), np.concatenate([-Ti, Ti], 1)], 1)
    g = 2 * np.pi * np.outer(n2, k2) / 128
    Gr, Gi = np.cos(g), -np.sin(g)
    gg = np.stack([Gr, Gi, -Gi], 1)
    ry = np.stack([np.concatenate([Gr, -Gi], 1), np.concatenate([Gi, Gr], 1)], 1)
    tt = 2 * np.pi * np.outer(k1, n2) / N
    cTr, cTi = np.cos(tt), np.sin(tt)
    it = np.stack([np.concatenate([cTr, cTr], 1), np.concatenate([-cTi, cTi], 1)], 1)
    it = it.reshape(2, 128, 2, 256).transpose(1, 0, 2, 3)
    ai = 2 * np.pi * np.outer(k1, n1[:128]) / 256
    fi = np.stack([np.cos(ai), np.sin(ai), -np.sin(ai)], 1) / N
    fi = fi.reshape(2, 128, 3, 128).transpose(1, 0, 2, 3)
    f32 = lambda x: np.ascontiguousarray(x.astype(np.float32))
    return {"fa": f32(fa), "tw": f32(tw), "gg": f32(gg), "ry": f32(ry), "it": f32(it), "fi": f32(fi)}


def run_hyfft(X, cw, cb, w1, b1, f1, w2, b2, f2, w3, bias):
    L = LH
    import os
    nc = _get("hyfft", build_hyfft, int(os.environ.get("HYFFT_STAGE", "3")))
    consts = _get("hyfft_consts", hyfft_consts)
    pos = np.arange(L, dtype=np.float32)
    t = pos / np.float32(L - 1)
    wv = np.float32(2.0 * np.pi) * pos / np.float32(L)
    bands = np.linspace(1e-4, 15, 16, dtype=np.float32)
    feats = np.concatenate([t[:, None], np.cos(wv[:, None] * bands), -np.sin(wv[:, None] * bands)], -1).astype(np.float32).T
    rates = np.abs(np.linspace(np.log(1e-2) / 1.5, np.log(1e-2) / 0.3, 512, dtype=np.float32))
    decay = np.exp(-t[None, :] * rates[:, None]).astype(np.float32)
    Xp = np.pad(X, ((0, 0), (1, 1), (0, 0)))
    feats2 = np.ascontiguousarray(np.stack([feats, feats[:, ::-1]], 0))
    maps = []
    for i in range(NCORE):
        cs = np.arange(i * 64, (i + 1) * 64)

        def rows(a):
            return np.ascontiguousarray(np.stack([a[:, :, p * 512 + cs].transpose(0, 2, 1).reshape(128, L) for p in range(3)], 0))
        cwt = np.concatenate([np.stack([cw[tp, p * 512 + cs] for p in range(3) for tp in range(3)], 1),
                              np.stack([cb[p * 512 + cs] for p in range(3)], 1)], 1)
        w3r = w3.reshape(64, 2, 2, 512)[:, :, :, cs].reshape(64, 2, 128)
        dco = np.concatenate([decay[cs], decay[cs]], 0)
        m = {
            "pb0": rows(Xp[:, 0:L]), "pb1": rows(Xp[:, 1:L + 1]), "pb2": rows(Xp[:, 2:L + 2]),
            "cw": np.ascontiguousarray(np.concatenate([cwt, cwt], 0)).astype(np.float32),
            "bias": np.ascontiguousarray(bias[:, cs].reshape(128, 1)).astype(np.float32),
            "feats": feats2, "decay": np.ascontiguousarray(np.stack([dco, dco[:, ::-1]], 0)),
            "w1": np.ascontiguousarray(w1), "w2": np.ascontiguousarray(w2), "w3": np.ascontiguousarray(w3r).astype(np.float32),
            "bf": np.ascontiguousarray(np.stack([b1, f1, b2, f2], 1)).astype(np.float32),
        }
        m.update(consts)
        maps.append(m)
    res = _run(nc, maps)
    out = np.empty((2, L, 512), np.float32)
    for i in range(NCORE):
        out[:, :, i * 64:(i + 1) * 64] = res[i]["yo"].reshape(2, 64, L).transpose(0, 2, 1)
    return out
```

```python
import numpy as np
from contextlib import ExitStack
import concourse.bass as bass
import concourse.mybir as mybir
from concourse.bass_utils import run_bass_kernel_spmd

F32 = mybir.dt.float32
BF16 = mybir.dt.bfloat16
AF = mybir.ActivationFunctionType
ALU = mybir.AluOpType
AX = mybir.AxisListType

D = 1024
DFF = 2816
SEQ = 16384
CTX = 256
NCORE = 8
TLAT = 4096
TCTX = 64
TT = TLAT + TCTX
DN_ALPHA = 4.0 ** 0.25
LN_EPS = 1e-5
RMS_EPS = 1e-6
NQK = 8576

EPOCH = 12000
NDMASEM = 8


class Prog:
    def __init__(self, nc, stack):
        self.nc = nc
        self.stack = stack
        self.eng = {"pe": nc.tensor, "act": nc.scalar, "dve": nc.vector,
                    "pool": nc.gpsimd, "sp": nc.sync}
        self.cnt = {e: 0 for e in self.eng}
        self.sems = {}
        self.lastw = {}
        self.readers = {}
        self.seen = {e: {} for e in self.eng}
        self.dcnt = {e: 0 for e in self.eng}
        self.dpend = {}
        self.lastw_dma = {}

    def _sem(self, name):
        if name not in self.sems:
            self.sems[name] = self.stack.enter_context(self.nc.semaphore(name))
        return self.sems[name]

    def _wait(self, e, tok):
        name, val = tok
        if self.seen[e].get(name, 0) >= val:
            return
        self.eng[e].wait_ge(self._sem(name), val)
        self.seen[e][name] = val

    def _deps(self, reads, writes):
        deps = []
        for k in list(reads) + list(writes):
            if k in self.lastw:
                deps.append(self.lastw[k])
            if k in self.lastw_dma:
                deps.extend(self.lastw_dma[k].values())
        for k in writes:
            deps.extend(self.readers.get(k, []))
        return deps

    def _commit(self, tok, reads, writes, is_dma=False):
        for k in reads:
            self.readers.setdefault(k, []).append(tok)
        for k in writes:
            if is_dma and k in self.lastw_dma:
                self.lastw_dma[k][tok[0]] = tok
            elif is_dma:
                self.lastw_dma[k] = {tok[0]: tok}
            else:
                self.lastw_dma.pop(k, None)
            self.lastw[k] = tok
            self.readers[k] = []

    def op(self, e, fn, reads=(), writes=(), sig=True):
        ep, v = divmod(self.cnt[e], EPOCH)
        own = "c_%s_%d" % (e, ep)
        for tok in self._deps(reads, writes):
            if tok[0] == own and tok[1] > v:
                continue
            self._wait(e, tok)
        ins = fn(self.eng[e])
        tok = ("c_%s_%d" % (e, ep), v + 1)
        if sig:
            self.cnt[e] += 1
            ins.then_inc(self._sem(tok[0]), 1)
        self._commit(tok, reads, writes)
        return tok

    def dma(self, e, out, in_, reads=(), writes=(), **kw):
        for tok in self._deps(reads, writes):
            self._wait(e, tok)
        i = self.dcnt[e]
        self.dcnt[e] += 1
        name = "d_%s_%d" % (e, i % NDMASEM)
        prev = self.dpend.get(name)
        if prev is not None:
            self._wait(e, prev)
        ins = self.eng[e].dma_start(out=out, in_=in_, **kw)
        ins.then_inc(self._sem(name), 16)
        tok = (name, 16 * (i // NDMASEM + 1))
        self.dpend[name] = tok
        self._commit(tok, reads, writes, is_dma=True)
        return tok

    def finish(self, e="sp"):
        for tok in set(self.lastw.values()):
            self._wait(e, tok)
        for tok in self.dpend.values():
            self._wait(e, tok)


def sb(nc, st, name, shape, dt):
    return st.enter_context(nc.sbuf_tensor(name, shape, dt))


def ps(nc, st, name, shape, dt=F32):
    return st.enter_context(nc.psum_tensor(name, shape, dt))


def load_w(P, nc, st, name, dram, K, N, eng="pool"):
    kc = K // 128
    t = sb(nc, st, name, [128, kc, N], BF16)
    v = dram.rearrange("(k p) n -> p k n", p=128)
    for k in range(kc):
        P.dma(eng, t[:, k, :], v[:, k, :], writes=[(name, k)])
    return t


def wkeys(name, n):
    return [(name, k) for k in range(n)]


def token_tiles(tsz):
    tiles = []
    t = 0
    while t < TLAT:
        n = min(tsz, TLAT - t)
        tiles.append((t, n, 0))
        t += n
    tiles.append((TLAT, TCTX, 1))
    return tiles


class Common:
    def __init__(self, nc, st, P, tsz):
        self.nc, self.st, self.P, self.tsz = nc, st, P, tsz
        self.ones = sb(nc, st, "ones", [128, 128], F32)
        P.op("pool", lambda e: e.memset(self.ones[:], 1.0), writes=["ones"])
        self.r = sb(nc, st, "r", [128, 8, tsz], F32)
        self.sq = sb(nc, st, "sq", [128, 8, tsz], F32)
        self.t1 = sb(nc, st, "t1", [128, 2, tsz], F32)
        self.t2 = sb(nc, st, "t2", [128, 2, tsz], F32)
        self.stt = sb(nc, st, "stt", [128, 4, tsz], F32)
        self.ps1 = ps(nc, st, "ps1", [128, 512])
        self.ps2 = ps(nc, st, "ps2", [128, 512])

    def layer_norm(self, tn, gam, bet, out, outkey):
        P, r, sq = self.P, self.r, self.sq
        for d in range(8):
            P.op("act", lambda e, d=d: e.activation(out=sq[:, d, :tn], in_=r[:, d, :tn], func=AF.Square),
                 reads=[("r", d)], writes=[("sq", d)])
        for d in range(8):
            P.op("pe", lambda e, d=d: e.matmul(self.ps1[:, :tn], self.ones[:], r[:, d, :tn], start=(d == 0), stop=(d == 7)),
                 reads=[("r", d), "ones"], writes=["ps1"], sig=(d == 7))
        for d in range(8):
            P.op("pe", lambda e, d=d: e.matmul(self.ps2[:, :tn], self.ones[:], sq[:, d, :tn], start=(d == 0), stop=(d == 7)),
                 reads=[("sq", d), "ones"], writes=["ps2"], sig=(d == 7))
        mean, msq, var, rstd = (self.stt[:, i, :tn] for i in range(4))
        P.op("dve", lambda e: e.tensor_single_scalar(mean, self.ps1[:, :tn], 1.0 / D, ALU.mult), reads=["ps1"], writes=["mean"])
        P.op("dve", lambda e: e.tensor_tensor(msq, mean, mean, ALU.mult), reads=["mean"], writes=["msq"])
        P.op("dve", lambda e: e.scalar_tensor_tensor(var, self.ps2[:, :tn], 1.0 / D, msq, ALU.mult, ALU.subtract),
             reads=["ps2", "msq"], writes=["var"])
        P.op("dve", lambda e: e.tensor_single_scalar(var, var, LN_EPS, ALU.add), reads=["var"], writes=["var"])
        P.op("act", lambda e: e.sqrt(out=msq, in_=var), reads=["var"], writes=["msq"])
        P.op("dve", lambda e: e.reciprocal(rstd, msq), reads=["msq"], writes=["rstd"])
        for d in range(8):
            b = d % 2
            P.op("dve", lambda e, d=d, b=b: e.tensor_tensor(self.t1[:, b, :tn], r[:, d, :tn], mean, ALU.subtract),
                 reads=[("r", d), "mean"], writes=[("t1", b)])
            P.op("pool", lambda e, d=d, b=b: e.tensor_tensor(self.t2[:, b, :tn], self.t1[:, b, :tn], rstd, ALU.mult),
                 reads=[("t1", b), "rstd"], writes=[("t2", b)])
            P.op("act", lambda e, d=d, b=b: e.activation(out=out[:, d, :tn], in_=self.t2[:, b, :tn], func=AF.Identity,
                                                         scale=gam[:, d:d + 1], bias=bet[:, d:d + 1]),
                 reads=[("t2", b), "tab"], writes=[(outkey, d)])


def emit_modulate(P, tn, src, srckey, dst, dstkey, sc1p, sh):
    for c in range(8):
        P.op("pool", lambda e, c=c: e.tensor_scalar(dst[:, c, :tn], src[:, c, :tn], sc1p[:, c:c + 1], sh[:, c:c + 1],
                                                    ALU.mult, ALU.add),
             reads=[(srckey, c), "tab"], writes=[(dstkey, c)])


NC0 = 2 * 9216 // NCORE


def build_k0():
    nc = bass.Bass("TRN2", target_bir_lowering=False)
    cT = nc.dram_tensor("cT", [D, 3], F32, kind="ExternalInput").ap()
    aw = nc.dram_tensor("aw", [D, NC0], F32, kind="ExternalInput").ap()
    ab = nc.dram_tensor("ab", [128, NC0 // 128], F32, kind="ExternalInput").ap()
    out = nc.dram_tensor("out", [NC0, 3], F32, kind="ExternalOutput").ap()
    nj = NC0 // 128
    with ExitStack() as st:
        P = Prog(nc, st)
        ct = sb(nc, st, "ct", [128, 8, 3], F32)
        stt = sb(nc, st, "st", [128, 8, 3], F32)
        abt = sb(nc, st, "abt", [128, nj], F32)
        ot = sb(nc, st, "ot", [128, nj, 3], F32)
        pp = ps(nc, st, "pp", [128, nj, 4])
        P.dma("sp", ct[:], cT.rearrange("(k p) n -> p k n", p=128), writes=["ct"])
        P.dma("sp", abt[:], ab, writes=["abt"])
        P.op("act", lambda e: e.activation(out=stt[:], in_=ct[:], func=AF.Silu), reads=["ct"], writes=["st"])
        awv = aw.rearrange("(k p) n -> p k n", p=128)
        npc = 3
        cw = NC0 // npc
        wt = [sb(nc, st, "wt%d" % i, [128, 8, cw], F32) for i in range(npc)]
        for i in range(npc):
            for k in range(8):
                P.dma("sp", wt[i][:, k, :], awv[:, k, i * cw:(i + 1) * cw], writes=[("wt", i, k)])
        for j in range(nj):
            i, jj = divmod(j * 128, cw)
            for k in range(8):
                P.op("pe", lambda e, i=i, jj=jj, k=k, j=j: e.matmul(pp[:, j, 0:3], wt[i][:, k, jj:jj + 128], stt[:, k, :],
                                                                  start=(k == 0), stop=(k == 7)),
                     reads=[("wt", i, k), "st"], writes=[("pp", j)])
            P.op("dve", lambda e, j=j: e.tensor_scalar(ot[:, j, :], pp[:, j, 0:3], abt[:, j:j + 1], None, ALU.add),
                 reads=[("pp", j), "abt"], writes=["ot"])
        P.dma("sp", out.rearrange("(j p) n -> p j n", p=128), ot[:], reads=["ot"], writes=["out"])
        P.finish("sp")
    return nc


def prep_tab(P, nc, st, tab_d, ncols, gmul):
    tab = sb(nc, st, "tabs", [128, ncols * 8], F32)
    P.dma("sp", tab[:], tab_d, writes=["tab0"])
    sc1p, sh, gt = [], [], []
    for ms in range(2):
        o = ms * 24
        P.op("dve", lambda e, o=o: e.tensor_single_scalar(tab[:, o + 8:o + 16], tab[:, o + 8:o + 16], 1.0, ALU.add),
             reads=["tab0"], writes=["tab"])
        P.op("dve", lambda e, o=o: e.tensor_single_scalar(tab[:, o + 16:o + 24], tab[:, o + 16:o + 24], gmul, ALU.mult),
             reads=["tab0"], writes=["tab"])
        sh.append(tab[:, o:o + 8])
        sc1p.append(tab[:, o + 8:o + 16])
        gt.append(tab[:, o + 16:o + 24])
    return tab, sh, sc1p, gt


def build_ffn(tsz=256):
    nc = bass.Bass("TRN2", target_bir_lowering=False)
    hin = nc.dram_tensor("hin", [D, TT], F32, kind="ExternalInput").ap()
    w1d = nc.dram_tensor("w1", [D, 2 * DFF], F32, kind="ExternalInput").ap()
    w2d = nc.dram_tensor("w2", [DFF, D], F32, kind="ExternalInput").ap()
    tabd = nc.dram_tensor("tab", [128, 64], F32, kind="ExternalInput").ap()
    hout = nc.dram_tensor("hout", [D, TT], F32, kind="ExternalOutput").ap()
    hv = hin.rearrange("(c p) t -> p c t", p=128)
    ov = hout.rearrange("(c p) t -> p c t", p=128)
    NJ = DFF // 128
    with ExitStack() as st:
        P = Prog(nc, st)
        tab, sh, sc1p, gt = prep_tab(P, nc, st, tabd, 8, 0.5)
        gam, bet = tab[:, 48:56], tab[:, 56:64]
        w1 = load_w(P, nc, st, "w1s", w1d, D, 2 * DFF)
        w2 = load_w(P, nc, st, "w2s", w2d, DFF, D)
        C = Common(nc, st, P, tsz)
        hT = sb(nc, st, "hT", [128, 8, tsz], F32)
        hA = sb(nc, st, "hA", [128, 8, tsz], F32)
        uT = sb(nc, st, "uT", [128, 8, tsz], BF16)
        gT = sb(nc, st, "gT", [128, NJ, tsz], BF16)
        sa = sb(nc, st, "sa", [128, 2, tsz], F32)
        psA = [ps(nc, st, "psA%d" % i, [128, 512]) for i in range(2)]
        psB = [ps(nc, st, "psB%d" % i, [128, 512]) for i in range(2)]
        psY = [ps(nc, st, "psY%d" % i, [128, 512]) for i in range(2)]
        for (t0, tn, ms) in token_tiles(tsz):
            for c in range(8):
                P.dma("sp", hT[:, c, :tn], hv[:, c, t0:t0 + tn], writes=[("hT", c)])
            emit_modulate(P, tn, hT, "hT", uT, "uT", sc1p[ms], sh[ms])
            for c in range(8):
                P.op("pool", lambda e, c=c: e.tensor_single_scalar(hA[:, c, :tn], hT[:, c, :tn], DN_ALPHA, ALU.mult),
                     reads=[("hT", c)], writes=[("hA", c)])
            for j in range(NJ):
                b = j % 2
                for k in range(8):
                    P.op("pe", lambda e, j=j, k=k, b=b: e.matmul(psA[b][:, :tn], w1[:, k, j * 128:(j + 1) * 128], uT[:, k, :tn],
                                                                start=(k == 0), stop=(k == 7)),
                         reads=[("w1s", k), ("uT", k)], writes=[("psA", b)], sig=(k == 7))
                for k in range(8):
                    P.op("pe", lambda e, j=j, k=k, b=b: e.matmul(psB[b][:, :tn], w1[:, k, DFF + j * 128:DFF + (j + 1) * 128],
                                                                uT[:, k, :tn], start=(k == 0), stop=(k == 7)),
                         reads=[("w1s", k), ("uT", k)], writes=[("psB", b)], sig=(k == 7))
                P.op("act", lambda e, b=b: e.activation(out=sa[:, b, :tn], in_=psA[b][:, :tn], func=AF.Silu),
                     reads=[("psA", b)], writes=[("sa", b)])
                P.op("dve", lambda e, j=j, b=b: e.tensor_tensor(gT[:, j, :tn], sa[:, b, :tn], psB[b][:, :tn], ALU.mult),
                     reads=[("sa", b), ("psB", b)], writes=[("gT", j)])
            for d in range(8):
                b = d % 2
                for j in range(NJ):
                    P.op("pe", lambda e, j=j, d=d, b=b: e.matmul(psY[b][:, :tn], w2[:, j, d * 128:(d + 1) * 128], gT[:, j, :tn],
                                                                start=(j == 0), stop=(j == NJ - 1)),
                         reads=[("w2s", j), ("gT", j)], writes=[("psY", b)], sig=(j == NJ - 1))
                P.op("dve", lambda e, d=d, b=b: e.scalar_tensor_tensor(C.r[:, d, :tn], psY[b][:, :tn], gt[ms][:, d:d + 1],
                                                                      hA[:, d, :tn], ALU.mult, ALU.add),
                     reads=[("psY", b), ("hA", d), "tab"], writes=[("r", d)])
            C.layer_norm(tn, gam, bet, hT, "hT")
            for c in range(8):
                P.dma("sp", ov[:, c, t0:t0 + tn], hT[:, c, :tn], reads=[("hT", c)], writes=["hout"])
        P.finish("sp")
    return nc


def build_inproj(tsz=256):
    nc = bass.Bass("TRN2", target_bir_lowering=False)
    hin = nc.dram_tensor("hin", [D, TT], F32, kind="ExternalInput").ap()
    wd = nc.dram_tensor("w", [D, NQK], F32, kind="ExternalInput").ap()
    tabd = nc.dram_tensor("tab", [128, 48], F32, kind="ExternalInput").ap()
    pout = nc.dram_tensor("pout", [NQK, TT], F32, kind="ExternalOutput").ap()
    hv = hin.rearrange("(c p) t -> p c t", p=128)
    pv = pout.rearrange("(c p) t -> p c t", p=128)
    NO = NQK // 128
    G = 4
    with ExitStack() as st:
        P = Prog(nc, st)
        tab, sh, sc1p, gt = prep_tab(P, nc, st, tabd, 6, 1.0)
        w = load_w(P, nc, st, "ws", wd, D, NQK)
        hT = sb(nc, st, "hT", [128, 8, tsz], F32)
        uT = sb(nc, st, "uT", [128, 8, tsz], BF16)
        og = [sb(nc, st, "og%d" % i, [128, G, tsz], F32) for i in range(2)]
        pp = [ps(nc, st, "pp%d" % i, [128, 512]) for i in range(4)]
        gi = 0
        for (t0, tn, ms) in token_tiles(tsz):
            for c in range(8):
                P.dma("sp", hT[:, c, :tn], hv[:, c, t0:t0 + tn], writes=[("hT", c)])
            emit_modulate(P, tn, hT, "hT", uT, "uT", sc1p[ms], sh[ms])
            for o0 in range(0, NO, G):
                gn = min(G, NO - o0)
                ob = gi % 2
                gi += 1
                for g in range(gn):
                    o = o0 + g
                    b = o % 4
                    for k in range(8):
                        P.op("pe", lambda e, o=o, k=k, b=b: e.matmul(pp[b][:, :tn], w[:, k, o * 128:(o + 1) * 128], uT[:, k, :tn],
                                                                    start=(k == 0), stop=(k == 7)),
                             reads=[("ws", k), ("uT", k)], writes=[("pp", b)], sig=(k == 7))
                    if o % 2 == 0:
                        P.op("act", lambda e, g=g, b=b, ob=ob: e.copy(out=og[ob][:, g, :tn], in_=pp[b][:, :tn]),
                             reads=[("pp", b)], writes=[("og", ob)])
                    else:
                        P.op("dve", lambda e, g=g, b=b, ob=ob: e.tensor_copy(og[ob][:, g, :tn], pp[b][:, :tn]),
                             reads=[("pp", b)], writes=[("og", ob)])
                P.dma("sp", pv[:, o0:o0 + gn, t0:t0 + tn], og[ob][:, :gn, :tn], reads=[("og", ob)], writes=["pout"])
        P.finish("sp")
    return nc


def build_merge(tsz=256):
    nc = bass.Bass("TRN2", target_bir_lowering=False)
    hin = nc.dram_tensor("hin", [D, TT], F32, kind="ExternalInput").ap()
    yin = nc.dram_tensor("yin", [1536, TT], F32, kind="ExternalInput").ap()
    gin = nc.dram_tensor("gin", [3072, TT], F32, kind="ExternalInput").ap()
    bwd = nc.dram_tensor("bw", [1536, D], F32, kind="ExternalInput").ap()
    owd = nc.dram_tensor("ow", [D, D], F32, kind="ExternalInput").ap()
    tabd = nc.dram_tensor("tab", [128, 64], F32, kind="ExternalInput").ap()
    hout = nc.dram_tensor("hout", [D, TT], F32, kind="ExternalOutput").ap()
    hv = hin.rearrange("(c p) t -> p c t", p=128)
    yv = yin.rearrange("(c p) t -> p c t", p=128)
    gv = gin.rearrange("(c p) t -> p c t", p=128)
    ov = hout.rearrange("(c p) t -> p c t", p=128)
    with ExitStack() as st:
        P = Prog(nc, st)
        tab, sh, sc1p, gt = prep_tab(P, nc, st, tabd, 8, 1.0)
        gam, bet = tab[:, 48:56], tab[:, 56:64]
        bw = load_w(P, nc, st, "bws", bwd, 1536, D)
        ow = load_w(P, nc, st, "ows", owd, D, D)
        C = Common(nc, st, P, tsz)
        hT = sb(nc, st, "hT", [128, 8, tsz], F32)
        hA = sb(nc, st, "hA", [128, 8, tsz], F32)
        yb = sb(nc, st, "yb", [128, 12, tsz], BF16)
        sg = sb(nc, st, "sg", [128, 24, tsz], F32)
        macc = sb(nc, st, "macc", [128, 2, tsz], F32)
        mtmp = sb(nc, st, "mtmp", [128, 2, tsz], F32)
        mT = sb(nc, st, "mT", [128, 8, tsz], BF16)
        psM = [ps(nc, st, "psM%d" % i, [128, 512]) for i in range(3)]
        psY = [ps(nc, st, "psY%d" % i, [128, 512]) for i in range(2)]
        for (t0, tn, ms) in token_tiles(tsz):
            for c in range(8):
                P.dma("sp", hT[:, c, :tn], hv[:, c, t0:t0 + tn], writes=[("hT", c)])
            for c in range(12):
                P.dma("pool", yb[:, c, :tn], yv[:, c, t0:t0 + tn], writes=[("yb", c)])
            for c in range(24):
                P.dma("sp", sg[:, c, :tn], gv[:, c, t0:t0 + tn], writes=[("sg", c)])
                P.op("act", lambda e, c=c: e.activation(out=sg[:, c, :tn], in_=sg[:, c, :tn], func=AF.Sigmoid),
                     reads=[("sg", c)], writes=[("sg", c)])
            for c in range(8):
                P.op("pool", lambda e, c=c: e.tensor_single_scalar(hA[:, c, :tn], hT[:, c, :tn], DN_ALPHA, ALU.mult),
                     reads=[("hT", c)], writes=[("hA", c)])
            for d in range(8):
                b = d % 2
                for n in range(3):
                    for kc in range(4):
                        P.op("pe", lambda e, n=n, kc=kc, d=d: e.matmul(psM[n][:, :tn], bw[:, n * 4 + kc, d * 128:(d + 1) * 128],
                                                                      yb[:, n * 4 + kc, :tn], start=(kc == 0), stop=(kc == 3)),
                             reads=[("bws", n * 4 + kc), ("yb", n * 4 + kc)], writes=[("psM", n)], sig=(kc == 3))
                P.op("dve", lambda e, d=d, b=b: e.tensor_tensor(macc[:, b, :tn], sg[:, d, :tn], psM[0][:, :tn], ALU.mult),
                     reads=[("sg", d), ("psM", 0)], writes=[("macc", b)])
                P.op("dve", lambda e, d=d, b=b: e.tensor_tensor(mtmp[:, 0, :tn], sg[:, 8 + d, :tn], psM[1][:, :tn], ALU.mult),
                     reads=[("sg", 8 + d), ("psM", 1)], writes=[("mtmp", 0)])
                P.op("dve", lambda e, d=d, b=b: e.tensor_tensor(mtmp[:, 1, :tn], sg[:, 16 + d, :tn], psM[2][:, :tn], ALU.mult),
                     reads=[("sg", 16 + d), ("psM", 2)], writes=[("mtmp", 1)])
                P.op("pool", lambda e, b=b: e.tensor_tensor(macc[:, b, :tn], macc[:, b, :tn], mtmp[:, 0, :tn], ALU.add),
                     reads=[("macc", b), ("mtmp", 0)], writes=[("macc", b)])
                P.op("pool", lambda e, d=d, b=b: e.tensor_tensor(mT[:, d, :tn], macc[:, b, :tn], mtmp[:, 1, :tn], ALU.add),
                     reads=[("macc", b), ("mtmp", 1)], writes=[("mT", d)])
            for d in range(8):
                b = d % 2
                for k in range(8):
                    P.op("pe", lambda e, k=k, d=d, b=b: e.matmul(psY[b][:, :tn], ow[:, k, d * 128:(d + 1) * 128], mT[:, k, :tn],
                                                                start=(k == 0), stop=(k == 7)),
                         reads=[("ows", k), ("mT", k)], writes=[("psY", b)], sig=(k == 7))
                P.op("dve", lambda e, d=d, b=b: e.scalar_tensor_tensor(C.r[:, d, :tn], psY[b][:, :tn], gt[ms][:, d:d + 1],
                                                                      hA[:, d, :tn], ALU.mult, ALU.add),
                     reads=[("psY", b), ("hA", d), "tab"], writes=[("r", d)])
            C.layer_norm(tn, gam, bet, hT, "hT")
            for c in range(8):
                P.dma("sp", ov[:, c, t0:t0 + tn], hT[:, c, :tn], reads=[("hT", c)], writes=["hout"])
        P.finish("sp")
    return nc


_CACHE = {}


def _get(name, fn, *a):
    key = (name,) + a
    if key not in _CACHE:
        _CACHE[key] = fn(*a)
    return _CACHE[key]


def _run(nc, in_maps):
    res = run_bass_kernel_spmd(nc, in_maps, core_ids=list(range(NCORE)))
    return res.results


def _pc(v):
    return np.ascontiguousarray(v.reshape(-1, 128).T)


def core_bq(i):
    return i // 4, i % 4


def to_cores(lat, cx):
    outs = []
    for i in range(NCORE):
        b, q = core_bq(i)
        a = np.concatenate([lat[b, q * TLAT:(q + 1) * TLAT], cx[b, q * TCTX:(q + 1) * TCTX]], axis=0)
        outs.append(np.ascontiguousarray(a.T))
    return outs


def from_cores(outs):
    C = outs[0].shape[0]
    lat = np.empty((2, SEQ, C), np.float32)
    cx = np.empty((2, CTX, C), np.float32)
    for i in range(NCORE):
        b, q = core_bq(i)
        lat[b, q * TLAT:(q + 1) * TLAT] = outs[i][:, :TLAT].T
        cx[b, q * TCTX:(q + 1) * TCTX] = outs[i][:, TLAT:].T
    return lat, cx


def run_mods(c, c_ctx, ada_w, ada_b):
    cT = np.ascontiguousarray(np.concatenate([c, c_ctx[None]], 0).T)
    aw = np.concatenate([ada_w[0], ada_w[1]], axis=1)
    ab = np.concatenate([ada_b[0], ada_b[1]], axis=0)
    nc = _get("k0", build_k0)
    maps = []
    for i in range(NCORE):
        sl = slice(i * NC0, (i + 1) * NC0)
        maps.append({"cT": cT, "aw": np.ascontiguousarray(aw[:, sl]), "ab": _pc(ab[sl])})
    res = _run(nc, maps)
    allm = np.concatenate([r["out"] for r in res], axis=0)
    return allm.reshape(2, 9, D, 3)


def make_tab(mods_l, b, idx3, extra):
    cols = []
    for v in (b, 2):
        for m in idx3:
            cols.append(_pc(mods_l[m, :, v]) if m is not None else np.zeros((128, 8), np.float32))
    for e in extra:
        cols.append(_pc(e))
    return np.ascontiguousarray(np.concatenate(cols, axis=1).astype(np.float32))


def run_ffn(hc, mods_l, idx3, w1, w2, g, bta):
    nc = _get("ffn", build_ffn)
    maps = []
    for i in range(NCORE):
        b, q = core_bq(i)
        maps.append({"hin": hc[i], "w1": w1, "w2": w2, "tab": make_tab(mods_l, b, idx3, [g, bta])})
    return [r["hout"] for r in _run(nc, maps)]


NQ = TT
NKL = TLAT + 256
NKB = NKL // 128 + 2
NK = NKB * 128


def build_attn():
    nc = bass.Bass("TRN2", target_bir_lowering=False)
    qf = nc.dram_tensor("qf", [64, 8, NQ], F32, kind="ExternalInput").ap()
    qs = nc.dram_tensor("qs", [64, 8, NQ], F32, kind="ExternalInput").ap()
    cq = nc.dram_tensor("cq", [64, NQ], F32, kind="ExternalInput").ap()
    sq = nc.dram_tensor("sq", [64, NQ], F32, kind="ExternalInput").ap()
    kf = nc.dram_tensor("kf", [64, 2, NK], F32, kind="ExternalInput").ap()
    ks = nc.dram_tensor("ks", [64, 2, NK], F32, kind="ExternalInput").ap()
    ck = nc.dram_tensor("ck", [64, NK], F32, kind="ExternalInput").ap()
    sk = nc.dram_tensor("sk", [64, NK], F32, kind="ExternalInput").ap()
    vt = nc.dram_tensor("vt", [128, NKB, 128], F32, kind="ExternalInput").ap()
    mk = nc.dram_tensor("mk", [128, 4, 4, 128], F32, kind="ExternalInput").ap()
    sk8 = nc.dram_tensor("sink", [64, 8], F32, kind="ExternalInput").ap()
    yo = nc.dram_tensor("yo", [64, 8, NQ], F32, kind="ExternalOutput").ap()
    CH = 1152
    with ExitStack() as st:
        P = Prog(nc, st)
        kr = sb(nc, st, "kr", [64, 2, NK], BF16)
        vb = sb(nc, st, "vb", [128, NKB, 128], BF16)
        mb = sb(nc, st, "mb", [128, 4, 4, 128], BF16)
        ones = sb(nc, st, "ones", [128, 64], BF16)
        es = sb(nc, st, "es", [64, 8], F32)
        P.dma("pool", vb[:], vt, writes=["vb"])
        P.dma("pool", mb[:], mk, writes=["mb"])
        P.dma("sp", es[:], sk8, writes=["es"])
        P.op("act", lambda e: e.activation(out=es[:], in_=es[:], func=AF.Exp), reads=["es"], writes=["es"])
        P.op("pool", lambda e: e.memset(ones[:], 1.0), writes=["ones"])
        ta = sb(nc, st, "ta", [64, 2, CH], F32)
        tb = sb(nc, st, "tb", [64, 2, CH], F32)
        tc_ = sb(nc, st, "tc", [64, CH], F32)
        td = sb(nc, st, "td", [64, CH], F32)
        for c0 in range(0, NK, CH):
            P.dma("sp", ta[:], kf[:, :, c0:c0 + CH], writes=["ta"])
            P.dma("sp", tb[:], ks[:, :, c0:c0 + CH], writes=["tb"])
            P.dma("sp", tc_[:], ck[:, c0:c0 + CH], writes=["tc"])
            P.dma("sp", td[:], sk[:, c0:c0 + CH], writes=["td"])
            for h in range(2):
                P.op("dve", lambda e, h=h: e.tensor_tensor(ta[:, h, :], ta[:, h, :], tc_[:], ALU.mult), reads=["ta", "tc"], writes=["ta"])
                P.op("pool", lambda e, h=h: e.tensor_tensor(tb[:, h, :], tb[:, h, :], td[:], ALU.mult), reads=["tb", "td"], writes=["tb"])
                P.op("dve", lambda e, h=h, c0=c0: e.tensor_tensor(kr[:, h, c0:c0 + CH], ta[:, h, :], tb[:, h, :], ALU.add),
                     reads=["ta", "tb"], writes=["kr"])
        qa = sb(nc, st, "qa", [64, 8, 128], F32)
        qb = sb(nc, st, "qb", [64, 8, 128], F32)
        qc = sb(nc, st, "qc", [64, 128], F32)
        qd = sb(nc, st, "qd", [64, 128], F32)
        qr = sb(nc, st, "qr", [64, 8, 128], BF16)
        pT = [sb(nc, st, "pT%d" % i, [128, 4, 128], BF16) for i in range(5)]
        dn = sb(nc, st, "dn", [64, 4, 128], F32)
        ob = sb(nc, st, "ob", [64, 8, 128], F32)
        psS = [ps(nc, st, "psS%d" % i, [128, 512]) for i in range(3)]
        psN = [ps(nc, st, "psN%d" % i, [64, 512]) for i in range(2)]
        psD = [ps(nc, st, "psD%d" % i, [64, 512]) for i in range(2)]
        si = 0
        nblk = TLAT // 128
        for n in range(nblk + 1):
            t0 = n * 128
            nq = 128 if n < nblk else TCTX
            P.dma("sp", qa[:, :, :nq], qf[:, :, t0:t0 + nq], writes=["qa"])
            P.dma("sp", qb[:, :, :nq], qs[:, :, t0:t0 + nq], writes=["qb"])
            P.dma("sp", qc[:, :nq], cq[:, t0:t0 + nq], writes=["qc"])
            P.dma("sp", qd[:, :nq], sq[:, t0:t0 + nq], writes=["qd"])
            for h in range(8):
                P.op("dve", lambda e, h=h: e.tensor_tensor(qa[:, h, :nq], qa[:, h, :nq], qc[:, :nq], ALU.mult), reads=["qa", "qc"], writes=["qa"])
                P.op("pool", lambda e, h=h: e.tensor_tensor(qb[:, h, :nq], qb[:, h, :nq], qd[:, :nq], ALU.mult), reads=["qb", "qd"], writes=["qb"])
                P.op("dve", lambda e, h=h: e.tensor_tensor(qr[:, h, :nq], qa[:, h, :nq], qb[:, h, :nq], ALU.add),
                     reads=["qa", "qb"], writes=[("qr", h // 4)])
            if n < nblk:
                kbs = [(n, 0 if n == 0 else 1), (n + 1, None), (n + 2, 3 if n == nblk - 1 else 2), (NKB - 2, None), (NKB - 1, None)]
            else:
                kbs = [(NKB - 2, None), (NKB - 1, None)]
            for kvh in range(2):
                pb = kvh
                wide = (nq == 128)
                for i, (kb, mi) in enumerate(kbs):
                    sbk = si % 3
                    si += 1
                    if wide:
                        P.op("pe", lambda e, kb=kb, sbk=sbk, kvh=kvh: e.matmul(psS[sbk][:, :512], kr[:, kvh, kb * 128:(kb + 1) * 128],
                                                                              qr[:, kvh * 4:(kvh + 1) * 4, :], start=True, stop=True),
                             reads=["kr", ("qr", kvh)], writes=[("psS", sbk)])
                        P.op("act", lambda e, i=i, sbk=sbk: e.activation(out=pT[i][:], in_=psS[sbk][:, :512], func=AF.Exp, scale=0.125),
                             reads=[("psS", sbk)], writes=[("pT", i)])
                    else:
                        for g in range(4):
                            P.op("pe", lambda e, kb=kb, g=g, sbk=sbk, kvh=kvh: e.matmul(psS[sbk][:, g * nq:(g + 1) * nq],
                                                                                       kr[:, kvh, kb * 128:(kb + 1) * 128],
                                                                                       qr[:, kvh * 4 + g, :nq], start=True, stop=True),
                                 reads=["kr", ("qr", kvh)], writes=[("psS", sbk)], sig=(g == 3))
                        for g in range(4):
                            P.op("act", lambda e, g=g, i=i, sbk=sbk: e.activation(out=pT[i][:, g, :nq], in_=psS[sbk][:, g * nq:(g + 1) * nq],
                                                                                  func=AF.Exp, scale=0.125),
                                 reads=[("psS", sbk)], writes=[("pT", i)])
                    if mi is not None:
                        P.op("pool", lambda e, i=i, mi=mi: e.tensor_tensor(pT[i][:, :, :nq], pT[i][:, :, :nq], mb[:, mi, :, :nq], ALU.mult),
                             reads=[("pT", i), "mb"], writes=[("pT", i)])
                if wide:
                    for i, (kb, mi) in enumerate(kbs):
                        P.op("pe", lambda e, i=i, kb=kb, kvh=kvh, pb=pb: e.matmul(psN[pb][:, :512], vb[:, kb, kvh * 64:(kvh + 1) * 64], pT[i][:],
                                                                                 start=(i == 0), stop=(i == len(kbs) - 1)),
                             reads=[("pT", i), "vb"], writes=[("psN", pb)], sig=(i == len(kbs) - 1))
                    for i, (kb, mi) in enumerate(kbs):
                        P.op("pe", lambda e, i=i, pb=pb: e.matmul(psD[pb][:, :512], ones[:], pT[i][:], start=(i == 0), stop=(i == len(kbs) - 1)),
                             reads=[("pT", i), "ones"], writes=[("psD", pb)], sig=(i == len(kbs) - 1))
                else:
                    for g in range(4):
                        for i, (kb, mi) in enumerate(kbs):
                            P.op("pe", lambda e, i=i, kb=kb, g=g, kvh=kvh, pb=pb: e.matmul(psN[pb][:, g * nq:(g + 1) * nq],
                                                                                          vb[:, kb, kvh * 64:(kvh + 1) * 64], pT[i][:, g, :nq],
                                                                                          start=(i == 0), stop=(i == len(kbs) - 1)),
                                 reads=[("pT", i), "vb"], writes=[("psN", pb)], sig=(g == 3 and i == len(kbs) - 1))
                    for g in range(4):
                        for i, (kb, mi) in enumerate(kbs):
                            P.op("pe", lambda e, i=i, g=g, pb=pb: e.matmul(psD[pb][:, g * nq:(g + 1) * nq], ones[:], pT[i][:, g, :nq],
                                                                          start=(i == 0), stop=(i == len(kbs) - 1)),
                                 reads=[("pT", i), "ones"], writes=[("psD", pb)], sig=(g == 3 and i == len(kbs) - 1))
                for g in range(4):
                    h = kvh * 4 + g
                    P.op("dve", lambda e, g=g, h=h, pb=pb: e.tensor_scalar(dn[:, g, :nq], psD[pb][:, g * nq:(g + 1) * nq], es[:, h:h + 1], None, ALU.add),
                         reads=[("psD", pb), "es"], writes=["dn"])
                    P.op("dve", lambda e, g=g: e.reciprocal(dn[:, g, :nq], dn[:, g, :nq]), reads=["dn"], writes=["dn"])
                    P.op("dve", lambda e, g=g, h=h, pb=pb: e.tensor_tensor(ob[:, h, :nq], psN[pb][:, g * nq:(g + 1) * nq], dn[:, g, :nq], ALU.mult),
                         reads=[("psN", pb), "dn"], writes=["ob"])
            P.dma("sp", yo[:, :, t0:t0 + nq], ob[:, :, :nq], reads=["ob"], writes=["yo"])
        P.finish("sp")
    return nc


def rope_tables(pos):
    pos = np.asarray(pos)
    valid = pos >= 0
    p = np.where(valid, pos, 0)
    row = (p // 64).astype(np.float32)
    col = (p % 64).astype(np.float32)
    inv = (np.float32(10000.0) ** (-np.arange(16, dtype=np.float32) * np.float32(2.0) / np.float32(32))).astype(np.float32)
    cos = np.ones((64, len(pos)), np.float32)
    sin = np.zeros((64, len(pos)), np.float32)
    for a, base in ((0, row), (1, col)):
        ang = (base[None, :] * inv[:, None]).astype(np.float32)
        for s in range(2):
            sl = slice(a * 32 + s * 16, a * 32 + s * 16 + 16)
            cos[sl] = np.cos(ang)
            sin[sl] = np.sin(ang) * (-1.0 if s == 0 else 1.0)
    cos[:, ~valid] = 1.0
    sin[:, ~valid] = 0.0
    return cos, sin


OFF_B, OFF_Q, OFF_K, OFF_V, OFF_G = 2560, 4096, 4608, 4736, 4864
OFF_QS, OFF_KS = 7936, 8448


def swap_cols(w):
    sh = w.shape
    return w.reshape(sh[:-1] + (-1, 2, 2, 16))[..., ::-1, :].reshape(sh)


def run_attn(Pl, Pc, sink):
    nc = _get("attn", build_attn)
    qi = np.arange(128)
    m_prev = (qi[:, None] >= qi[None, :]).astype(np.float32)
    m_next = (qi[:, None] <= qi[None, :]).astype(np.float32)
    maps = []
    for i in range(NCORE):
        b, q = core_bq(i)
        t0 = q * TLAT
        lat = Pl[b, t0:t0 + TLAT]
        cx = Pc[b, q * TCTX:(q + 1) * TCTX]

        def heads(a, n):
            return np.ascontiguousarray(a.reshape(a.shape[0], n, 64).transpose(2, 1, 0))
        qall = np.concatenate([lat[:, OFF_Q:OFF_K], cx[:, OFF_Q:OFF_K]], 0)
        qsall = np.concatenate([lat[:, OFF_QS:OFF_KS], cx[:, OFF_QS:OFF_KS]], 0)
        qpos = np.concatenate([np.arange(t0, t0 + TLAT), -np.ones(TCTX, np.int64)])
        cqt, sqt = rope_tables(qpos)
        kpos = np.arange(t0 - 128, t0 + TLAT + 128)
        kval = (kpos >= 0) & (kpos < SEQ)
        kidx = np.clip(kpos, 0, SEQ - 1)
        kl = Pl[b, kidx] * kval[:, None]
        kall = np.concatenate([kl[:, OFF_K:OFF_V], Pc[b][:, OFF_K:OFF_V]], 0)
        ksall = np.concatenate([kl[:, OFF_KS:NQK], Pc[b][:, OFF_KS:NQK]], 0)
        vall = np.concatenate([kl[:, OFF_V:OFF_G], Pc[b][:, OFF_V:OFF_G]], 0)
        ckt, skt = rope_tables(np.concatenate([np.where(kval, kpos, -1), -np.ones(CTX, np.int64)]))
        m0 = m_prev if q > 0 else np.zeros_like(m_prev)
        m3 = m_next if q < 3 else np.zeros_like(m_next)
        mk = np.stack([m0, m_prev, m_next, m3], 0)
        mk = np.ascontiguousarray(np.broadcast_to(mk.transpose(1, 0, 2)[:, :, None, :], (128, 4, 4, 128))).astype(np.float32)
        maps.append({
            "qf": heads(qall, 8), "qs": heads(qsall, 8), "cq": cqt, "sq": sqt,
            "kf": heads(kall, 2), "ks": heads(ksall, 2), "ck": ckt, "sk": skt,
            "vt": np.ascontiguousarray(vall.reshape(NKB, 128, 128).transpose(1, 0, 2)),
            "mk": mk, "sink": np.ascontiguousarray(np.broadcast_to(sink[None, :], (64, 8))).astype(np.float32),
        })
    res = _run(nc, maps)
    yl = np.empty((2, SEQ, 512), np.float32)
    yc = np.empty((2, CTX, 512), np.float32)
    for i in range(NCORE):
        b, q = core_bq(i)
        y = res[i]["yo"].transpose(2, 1, 0).reshape(NQ, 512)
        yl[b, q * TLAT:(q + 1) * TLAT] = y[:TLAT]
        yc[b, q * TCTX:(q + 1) * TCTX] = y[TLAT:]
    return yl, yc


NTOK = CTX + SEQ
NG = NTOK // 128


def build_hgrn():
    nc = bass.Bass("TRN2", target_bir_lowering=False)
    zd = [nc.dram_tensor("z%d" % d, [128, NG, 128], F32, kind="ExternalInput").ap() for d in range(2)]
    vd = nc.dram_tensor("v", [128, NG, 128], F32, kind="ExternalInput").ap()
    qd = nc.dram_tensor("q", [128, NTOK], F32, kind="ExternalInput").ap()
    gd = nc.dram_tensor("g", [128, NTOK], F32, kind="ExternalInput").ap()
    lbd = nc.dram_tensor("lb", [128, 2, 2, 128], F32, kind="ExternalInput").ap()
    fld = nc.dram_tensor("flag", [128, 1], F32, kind="ExternalInput").ap()
    nwd = nc.dram_tensor("nw", [128, 1], F32, kind="ExternalInput").ap()
    cfd = nc.dram_tensor("cf", [128, 5, 128], F32, kind="ExternalInput").ap()
    cmd = nc.dram_tensor("cm", [128, 4], F32, kind="ExternalInput").ap()
    yo = nc.dram_tensor("yo", [128, NTOK], F32, kind="ExternalOutput").ap()
    GB = 4
    with ExitStack() as st:
        P = Prog(nc, st)
        vb = sb(nc, st, "vb", [128, NG, 128], BF16)
        P.dma("pool", vb[:], vd, writes=["vb"])
        cf = sb(nc, st, "cfs", [128, 5, 128], F32)
        P.dma("sp", cf[:], cfd, writes=["cf"])
        bm = sb(nc, st, "bm", [128, 2, GB, 128], BF16)
        for i in range(GB):
            P.dma("pool", bm[:, :, i, :], cfd[:, 0:2, :], writes=["bm"])
        cm = sb(nc, st, "cms", [128, 4], F32)
        P.dma("sp", cm[:], cmd, writes=["cm"])
        nw = sb(nc, st, "nws", [128, 1], F32)
        P.dma("sp", nw[:], nwd, writes=["nw"])
        fl = sb(nc, st, "fls", [128, 1], F32)
        P.dma("sp", fl[:], fld, writes=["fl"])
        lbt = sb(nc, st, "lbs", [128, 2, 2, 128], F32)
        P.dma("sp", lbt[:], lbd, writes=["lbt"])
        LBb = sb(nc, st, "LBb", [128, 2, GB, 128], F32)
        LBa = sb(nc, st, "LBa", [128, 2, GB, 128], F32)
        for i in range(GB):
            P.op("dve", lambda e, i=i: e.tensor_tensor(LBb[:, :, i, :], lbt[:, 1], lbt[:, 0], ALU.subtract), reads=["lbt"], writes=["LBb"])
        P.op("act", lambda e: e.activation(out=LBb[:], in_=LBb[:], func=AF.Sigmoid), reads=["LBb"], writes=["LBb"])
        P.op("dve", lambda e: e.tensor_scalar(LBb[:], LBb[:], fl[:, 0:1], None, ALU.mult), reads=["LBb", "fl"], writes=["LBb"])
        P.op("dve", lambda e: e.tensor_scalar(LBa[:], LBb[:], -1.0, 1.0, ALU.mult, ALU.add), reads=["LBb"], writes=["LBa"])
        ones = sb(nc, st, "ones", [128, 128], F32)
        P.op("pool", lambda e: e.memset(ones[:], 1.0), writes=["ones"])
        osum = sb(nc, st, "osum", [128, NTOK], F32)
        S = sb(nc, st, "S", [128, 128], F32)
        NSLOT = 8
        Sbr = sb(nc, st, "Sbr", [128, NSLOT, 128], BF16)
        prev_slot = [0]
        W = GB * 128
        zt = [sb(nc, st, "zt%d" % i, [128, W], F32) for i in range(2)]
        qt32 = [sb(nc, st, "qt32%d" % i, [128, W], F32) for i in range(2)]
        ft = [sb(nc, st, "ft%d" % i, [128, W], F32) for i in range(2)]
        lf = [sb(nc, st, "lf%d" % i, [128, W], F32) for i in range(2)]
        kk = [sb(nc, st, "kk%d" % i, [128, W], F32) for i in range(2)]
        Eq = [sb(nc, st, "Eq%d" % i, [128, W], F32) for i in range(2)]
        Ek = [sb(nc, st, "Ek%d" % i, [128, W], F32) for i in range(2)]
        Er = [sb(nc, st, "Er%d" % i, [128, W], F32) for i in range(2)]
        qt = [sb(nc, st, "qt%d" % i, [128, W], BF16) for i in range(2)]
        kt = [sb(nc, st, "kt%d" % i, [128, W], BF16) for i in range(2)]
        k4 = [sb(nc, st, "k4%d" % i, [128, GB, 4, 128], BF16) for i in range(2)]
        am = [sb(nc, st, "am%d" % i, [128, W], BF16) for i in range(2)]
        pB = ps(nc, st, "pB", [128, 512])
        pR = ps(nc, st, "pR", [128, 512])
        pKA = ps(nc, st, "pKA", [128, 512])
        pO = ps(nc, st, "pO", [128, 512])
        pU = ps(nc, st, "pU", [128, GB, 4, 128])

        def phase1(d, batch, b):
            g0 = min(batch)
            nb = len(batch)
            w = nb * 128
            P.dma("sp", zt[b][:, :w], zd[d][:, g0:g0 + nb, :], writes=[("zt", b)])
            P.dma("sp", qt32[b][:, :w], qd[:, g0 * 128:(g0 + nb) * 128], writes=[("qt32", b)])
            P.op("act", lambda e: e.activation(out=zt[b][:, :w], in_=zt[b][:, :w], func=AF.Sigmoid), reads=[("zt", b)], writes=[("zt", b)])
            P.op("dve", lambda e: e.tensor_tensor(ft[b][:, :w], zt[b][:, :w], LBa[:, d, :nb, :], ALU.mult), reads=[("zt", b), "LBa"], writes=[("ft", b)])
            P.op("dve", lambda e: e.tensor_tensor(ft[b][:, :w], ft[b][:, :w], LBb[:, d, :nb, :], ALU.add), reads=[("ft", b), "LBb"], writes=[("ft", b)])
            P.op("act", lambda e: e.activation(out=lf[b][:, :w], in_=ft[b][:, :w], func=AF.Ln), reads=[("ft", b)], writes=[("lf", b)])
            P.op("pool", lambda e: e.tensor_scalar(kk[b][:, :w], ft[b][:, :w], -1.0, 1.0, ALU.mult, ALU.add), reads=[("ft", b)], writes=[("kk", b)])
            for i in range(nb):
                sl = slice(i * 128, (i + 1) * 128)
                P.op("pe", lambda e, sl=sl: e.matmul(pB[:, sl], lf[b][:, sl], cf[:, d, :], start=True, stop=True), reads=[("lf", b), "cf"], writes=["pB"], sig=(i == nb - 1))
            for i in range(nb):
                sl = slice(i * 128, (i + 1) * 128)
                P.op("pe", lambda e, sl=sl: e.matmul(pR[:, sl], cf[:, 2 + d, :], lf[b][:, sl], start=True, stop=True), reads=[("lf", b), "cf"], writes=["pR"], sig=(i == nb - 1))
            for i in range(nb):
                sl = slice(i * 128, (i + 1) * 128)
                P.op("pe", lambda e, sl=sl: e.matmul(pKA[:, sl], kk[b][:, sl], cf[:, 4, :], start=True, stop=True), reads=[("kk", b), "cf"], writes=["pKA"], sig=(i == nb - 1))
            P.op("act", lambda e: e.activation(out=Eq[b][:, :w], in_=pB[:, :w], func=AF.Exp), reads=["pB"], writes=[("Eq", b)])
            P.op("act", lambda e: e.activation(out=Ek[b][:, :w], in_=pB[:, :w], func=AF.Exp, scale=-1.0), reads=["pB"], writes=[("Ek", b)])
            P.op("act", lambda e: e.activation(out=Er[b][:, :w], in_=pR[:, :w], func=AF.Exp), reads=["pR"], writes=[("Er", b)])
            P.op("dve", lambda e: e.tensor_tensor(qt[b][:, :w], qt32[b][:, :w], Eq[b][:, :w], ALU.mult), reads=[("qt32", b), ("Eq", b)], writes=[("qt", b)])
            P.op("dve", lambda e: e.tensor_tensor(kt[b][:, :w], pKA[:, :w], Ek[b][:, :w], ALU.mult), reads=["pKA", ("Ek", b)], writes=[("kt", b)])
            P.op("pool", lambda e: e.tensor_tensor(Er[b][:, :w], kk[b][:, :w], Er[b][:, :w], ALU.mult), reads=[("kk", b), ("Er", b)], writes=[("Er", b)])
            for c in range(4):
                P.op("pool", lambda e, c=c: e.tensor_scalar(k4[b][:, :nb, c, :], Er[b][:, :w], cm[:, c:c + 1], None, ALU.mult),
                     reads=[("Er", b), "cm"], writes=[("k4", b)])
            for i in range(nb):
                sl = slice(i * 128, (i + 1) * 128)
                P.op("pe", lambda e, sl=sl: e.matmul(pKA[:, sl], kt[b][:, sl], qt[b][:, sl], start=True, stop=True), reads=[("kt", b), ("qt", b)], writes=["pKA"], sig=(i == nb - 1))
            P.op("dve", lambda e: e.tensor_tensor(am[b][:, :w], pKA[:, :w], bm[:, d, :nb, :], ALU.mult), reads=["pKA", "bm"], writes=[("am", b)])

        def scan(d, batch, b):
            g0 = min(batch)
            nb = len(batch)
            chunks = [0, 1, 2, 3] if d == 0 else [3, 2, 1, 0]
            for G in batch:
                i = G - g0
                for ci, c in enumerate(chunks):
                    P.op("pe", lambda e, i=i, c=c, G=G: e.matmul(pU[:, i, c, :], k4[b][:, i, c, :], vb[:, G, :], start=True, stop=True),
                         reads=[("k4", b), "vb"], writes=["pU"], sig=(G == batch[-1] and ci == 3))
            for G in batch:
                i = G - g0
                P.op("pe", lambda e, i=i, G=G: e.matmul(pO[:, i * 128:(i + 1) * 128], vb[:, G, :], am[b][:, i * 128:(i + 1) * 128], start=True, stop=False),
                     reads=["vb", ("am", b)], writes=["pO"], sig=False)
                for ci, c in enumerate(chunks):
                    col = i * 128 + c * 32
                    P.op("pe", lambda e, col=col, ci=ci, ps_=prev_slot[0]: e.matmul(pO[:, col:col + 32], Sbr[:, ps_, :], qt[b][:, col:col + 32],
                                                                                 start=False, stop=(ci == 3)),
                         reads=[("Sb", prev_slot[0]), ("qt", b)], writes=["pO"], sig=(ci == 3))
                    ns = (prev_slot[0] + 1) % NSLOT
                    dcol = col + (31 if d == 0 else 0)
                    P.op("dve", lambda e, i=i, c=c, ns=ns, dcol=dcol: e.scalar_tensor_tensor(Sbr[:, ns, :], S[:], Eq[b][:, dcol:dcol + 1], pU[:, i, c, :], ALU.mult, ALU.add),
                         reads=["S", ("Eq", b), "pU"], writes=[("Sb", ns)])
                    P.op("dve", lambda e, i=i, c=c, dcol=dcol: e.scalar_tensor_tensor(S[:], S[:], Eq[b][:, dcol:dcol + 1], pU[:, i, c, :], ALU.mult, ALU.add),
                         reads=["S", ("Eq", b), "pU"], writes=["S"])
                    prev_slot[0] = ns
            w = nb * 128
            ok = [("osum", G) for G in batch]
            if d == 0:
                P.op("act", lambda e: e.copy(out=osum[:, g0 * 128:g0 * 128 + w], in_=pO[:, :w]), reads=["pO"], writes=ok)
            else:
                P.op("dve", lambda e: e.tensor_tensor(osum[:, g0 * 128:g0 * 128 + w], osum[:, g0 * 128:g0 * 128 + w], pO[:, :w], ALU.add),
                     reads=["pO"] + ok, writes=ok)

        it = 0
        for d in range(2):
            P.op("pool", lambda e: e.memset(S[:], 0.0), writes=["S"])
            prev_slot[0] = (prev_slot[0] + 1) % NSLOT
            P.op("pool", lambda e, ps_=prev_slot[0]: e.memset(Sbr[:, ps_, :], 0.0), writes=[("Sb", prev_slot[0])])
            if d == 0:
                order = list(range(NG))
                batches = [order[i:i + GB] for i in range(0, NG, GB)]
            else:
                rest = list(range(NG - 1, 1, -1))
                batches = [[1, 0]] + [rest[i:i + GB] for i in range(0, len(rest), GB)]
            phase1(d, batches[0], it % 2)
            for k, batch in enumerate(batches):
                b = it % 2
                it += 1
                if k + 1 < len(batches):
                    phase1(d, batches[k + 1], it % 2)
                scan(d, batch, b)
        sq = sb(nc, st, "sq", [128, 512], F32)
        gt = sb(nc, st, "gt", [128, 512], F32)
        rs = sb(nc, st, "rs", [128, 512], F32)
        yt = [sb(nc, st, "yt%d" % i, [128, 512], F32) for i in range(2)]
        ri = 0
        for t0 in range(0, NTOK, 512):
            tn = min(512, NTOK - t0)
            b = ri % 2
            ri += 1
            rk = [("osum", G) for G in range(t0 // 128, (t0 + tn) // 128)]
            P.dma("sp", gt[:, :tn], gd[:, t0:t0 + tn], writes=["gt"])
            P.op("act", lambda e, t0=t0, tn=tn: e.activation(out=sq[:, :tn], in_=osum[:, t0:t0 + tn], func=AF.Square), reads=rk, writes=["sq"])
            P.op("pe", lambda e, tn=tn: e.matmul(pB[:, :tn], ones[:], sq[:, :tn], start=True, stop=True), reads=["sq", "ones"], writes=["pB"])
            P.op("dve", lambda e, tn=tn: e.tensor_scalar(rs[:, :tn], pB[:, :tn], 1.0 / 128, RMS_EPS, ALU.mult, ALU.add), reads=["pB"], writes=["rs"])
            P.op("act", lambda e, tn=tn: e.sqrt(out=rs[:, :tn], in_=rs[:, :tn]), reads=["rs"], writes=["rs"])
            P.op("dve", lambda e, tn=tn: e.reciprocal(rs[:, :tn], rs[:, :tn]), reads=["rs"], writes=["rs"])
            P.op("act", lambda e, tn=tn: e.activation(out=gt[:, :tn], in_=gt[:, :tn], func=AF.Silu), reads=["gt"], writes=["gt"])
            P.op("dve", lambda e, b=b, t0=t0, tn=tn: e.tensor_tensor(yt[b][:, :tn], osum[:, t0:t0 + tn], rs[:, :tn], ALU.mult), reads=rk + ["rs"], writes=[("yt", b)])
            P.op("pool", lambda e, b=b, tn=tn: e.tensor_tensor(yt[b][:, :tn], yt[b][:, :tn], gt[:, :tn], ALU.mult), reads=[("yt", b), "gt"], writes=[("yt", b)])
            P.op("act", lambda e, b=b, tn=tn: e.activation(out=yt[b][:, :tn], in_=yt[b][:, :tn], func=AF.Copy, scale=nw[:, 0:1]), reads=[("yt", b), "nw"], writes=[("yt", b)])
            P.dma("sp", yo[:, t0:t0 + tn], yt[b][:, :tn], reads=[("yt", b)], writes=["yo"])
        P.finish("sp")
    return nc


def run_hgrn(Pl, Pc, layer, hgrn_lb, norm_w):
    nc = _get("hgrn", build_hgrn)
    p = np.arange(128)
    same = (p[:, None] // 32) == (p[None, :] // 32)
    incl_f = (same & (p[:, None] <= p[None, :])).astype(np.float32)
    incl_b = (same & (p[:, None] >= p[None, :])).astype(np.float32)
    rem_f = (same & (p[:, None] > p[None, :])).astype(np.float32)
    rem_b = (same & (p[:, None] < p[None, :])).astype(np.float32)
    cf = np.ascontiguousarray(np.stack([incl_f, incl_b, rem_f, rem_b, np.eye(128, dtype=np.float32)], 1))
    cm = (p[:, None] // 32 == np.arange(4)[None, :]).astype(np.float32)
    maps = []
    for i in range(NCORE):
        b, hd = core_bq(i)
        seq = np.concatenate([Pc[b], Pl[b]], 0)
        cs = slice(hd * 128, (hd + 1) * 128)

        def tm(a):
            return np.ascontiguousarray(a.reshape(NG, 128, 128).transpose(1, 0, 2))
        lb = np.broadcast_to(hgrn_lb[:, :, cs][None], (128, 2, 2, 128))
        maps.append({
            "q": np.ascontiguousarray(seq[:, 0:512][:, cs].T), "v": tm(seq[:, 512:1024][:, cs]),
            "g": np.ascontiguousarray(seq[:, 1024:1536][:, cs].T),
            "z0": tm(seq[:, 1536:2048][:, cs]), "z1": tm(seq[:, 2048:2560][:, cs]),
            "lb": np.ascontiguousarray(lb).astype(np.float32),
            "flag": np.full((128, 1), float(layer), np.float32),
            "nw": np.ascontiguousarray(norm_w[cs][:, None]).astype(np.float32), "cf": cf, "cm": cm,
        })
    res = _run(nc, maps)
    yl = np.empty((2, SEQ, 512), np.float32)
    yc = np.empty((2, CTX, 512), np.float32)
    for i in range(NCORE):
        b, hd = core_bq(i)
        y = res[i]["yo"].T
        yc[b, :, hd * 128:(hd + 1) * 128] = y[:CTX]
        yl[b, :, hd * 128:(hd + 1) * 128] = y[CTX:]
    return yl, yc


PI = float(np.pi)


def build_hyena(L):
    nc = bass.Bass("TRN2", target_bir_lowering=False)
    PP = min(L, 1024)
    pbd = [nc.dram_tensor("pb%d" % i, [3, 128, L], F32, kind="ExternalInput").ap() for i in range(3)]
    cwd = nc.dram_tensor("cw", [128, 12], F32, kind="ExternalInput").ap()
    bsd = nc.dram_tensor("bias", [128, 2], F32, kind="ExternalInput").ap()
    ftd = nc.dram_tensor("feats", [33, L], F32, kind="ExternalInput").ap()
    dcd = nc.dram_tensor("decay", [128, L], F32, kind="ExternalInput").ap()
    w1d = nc.dram_tensor("w1", [33, 64], F32, kind="ExternalInput").ap()
    w2d = nc.dram_tensor("w2", [64, 64], F32, kind="ExternalInput").ap()
    w3d = nc.dram_tensor("w3", [64, 4, 128], F32, kind="ExternalInput").ap()
    bfd = nc.dram_tensor("bf", [64, 4], F32, kind="ExternalInput").ap()
    yo = nc.dram_tensor("yo", [128, L], F32, kind="ExternalOutput").ap()
    with ExitStack() as st:
        P = Prog(nc, st)
        cw = sb(nc, st, "cws", [128, 12], F32)
        bs = sb(nc, st, "bss", [128, 2], F32)
        w1 = sb(nc, st, "w1s", [33, 64], F32)
        w2 = sb(nc, st, "w2s", [64, 64], F32)
        w3 = sb(nc, st, "w3s", [64, 4, 128], F32)
        bf = sb(nc, st, "bfs", [64, 4], F32)
        for t, dd in ((cw, cwd), (bs, bsd), (w1, w1d), (w2, w2d), (w3, w3d), (bf, bfd)):
            P.dma("sp", t[:], dd, writes=["consts"])
        z = sb(nc, st, "z", [128, L], F32)
        y = sb(nc, st, "y", [128, L], F32)
        ta = sb(nc, st, "ta", [128, PP], F32)
        tb = sb(nc, st, "tb", [128, PP], F32)
        tcx = sb(nc, st, "tcx", [128, PP], F32)
        xs = sb(nc, st, "xs", [128, PP], F32)
        ft = sb(nc, st, "ft", [33, PP], F32)
        h1 = sb(nc, st, "h1", [64, PP], F32)
        h2 = sb(nc, st, "h2", [64, PP], F32)
        dc = sb(nc, st, "dc", [128, PP], F32)
        hp = [sb(nc, st, "hp%d" % i, [128, PP], F32) for i in range(2)]
        l1 = sb(nc, st, "l1", [128, 4], F32)
        ki = sb(nc, st, "ki", [64, PP], mybir.dt.int32)
        kf = sb(nc, st, "kf", [64, PP], F32)
        pm = ps(nc, st, "pm", [64, 512])
        ph = [ps(nc, st, "ph%d" % i, [128, 512]) for i in range(2)]

        def sconv(part, dst, dkey, t0, tn):
            P.dma("sp", ta[:, :tn], pbd[0][part, :, t0:t0 + tn], writes=["ta"])
            P.dma("sp", tb[:, :tn], pbd[1][part, :, t0:t0 + tn], writes=["tb"])
            P.dma("sp", tcx[:, :tn], pbd[2][part, :, t0:t0 + tn], writes=["tcx"])
            c0 = part * 3
            P.op("dve", lambda e: e.tensor_scalar(ta[:, :tn], ta[:, :tn], cw[:, c0:c0 + 1], cw[:, 9 + part:10 + part], ALU.mult, ALU.add),
                 reads=["ta", "consts"], writes=["ta"])
            P.op("dve", lambda e: e.scalar_tensor_tensor(tb[:, :tn], tb[:, :tn], cw[:, c0 + 1:c0 + 2], ta[:, :tn], ALU.mult, ALU.add),
                 reads=["ta", "tb", "consts"], writes=["tb"])
            P.op("dve", lambda e: e.scalar_tensor_tensor(dst, tcx[:, :tn], cw[:, c0 + 2:c0 + 3], tb[:, :tn], ALU.mult, ALU.add),
                 reads=["tb", "tcx", "consts"], writes=[dkey])

        for t0 in range(0, L, PP):
            sconv(0, z[:, t0:t0 + PP], "z", t0, PP)
        for o in range(2):
            P.op("pool", lambda e: e.memset(y[:], 0.0), writes=["y"])
            P.op("pool", lambda e: e.memset(l1[:], 0.0), writes=["l1"])
            for p0 in range(0, L, PP):
                P.dma("sp", ft[:], ftd[:, p0:p0 + PP], writes=["ft"])
                P.dma("sp", dc[:], dcd[:, p0:p0 + PP], writes=["dc"])
                for (src, skey, w, K_, dst, dkey, bi) in ((ft, "ft", w1, 33, h1, "h1", 0), (h1, "h1", w2, 64, h2, "h2", 2)):
                    for q0 in range(0, PP, 512):
                        qn = min(512, PP - q0)
                        P.op("pe", lambda e, q0=q0, qn=qn, src=src, w=w, K_=K_: e.matmul(pm[:, :qn], w[:K_, :], src[:K_, q0:q0 + qn], start=True, stop=True),
                             reads=[skey, "consts"], writes=["pm"])
                        P.op("dve", lambda e, q0=q0, qn=qn, dst=dst, bi=bi: e.tensor_scalar(dst[:, q0:q0 + qn], pm[:, :qn], bf[:, bi:bi + 1], bf[:, bi + 1:bi + 2], ALU.add, ALU.mult),
                             reads=["pm", "consts"], writes=[dkey])
                    P.op("dve", lambda e, dst=dst: e.tensor_scalar(dst[:], dst[:], 1.0 / (2.0 * PI), 8.5, ALU.mult, ALU.add), reads=[dkey], writes=[dkey])
                    P.op("dve", lambda e, dst=dst: e.tensor_copy(ki[:], dst[:]), reads=[dkey], writes=["ki"])
                    P.op("dve", lambda e, dst=dst: e.tensor_copy(kf[:], ki[:]), reads=["ki"], writes=["kf"])
                    P.op("dve", lambda e, dst=dst: e.tensor_tensor(dst[:], dst[:], kf[:], ALU.subtract), reads=[dkey, "kf"], writes=[dkey])
                    P.op("dve", lambda e, dst=dst: e.tensor_single_scalar(kf[:], dst[:], 0.0, ALU.is_lt), reads=[dkey], writes=["kf"])
                    P.op("dve", lambda e, dst=dst: e.tensor_tensor(dst[:], dst[:], kf[:], ALU.add), reads=[dkey, "kf"], writes=[dkey])
                    P.op("dve", lambda e, dst=dst: e.tensor_scalar(dst[:], dst[:], 2.0 * PI, -PI, ALU.mult, ALU.add), reads=[dkey], writes=[dkey])
                    P.op("act", lambda e, dst=dst: e.activation(out=dst[:], in_=dst[:], func=AF.Sin), reads=[dkey], writes=[dkey])
                for dr in range(2):
                    for q0 in range(0, PP, 512):
                        qn = min(512, PP - q0)
                        b = (q0 // 512) % 2
                        P.op("pe", lambda e, q0=q0, qn=qn, dr=dr, b=b: e.matmul(ph[b][:, :qn], w3[:, dr * 2 + o, :], h2[:, q0:q0 + qn], start=True, stop=True),
                             reads=["h2", "consts"], writes=[("ph", b)])
                        P.op("dve", lambda e, q0=q0, qn=qn, dr=dr, b=b: e.tensor_tensor(hp[dr][:, q0:q0 + qn], ph[b][:, :qn], dc[:, q0:q0 + qn], ALU.mult),
                             reads=[("ph", b), "dc"], writes=[("hp", dr)])
                    if dr == 1 and p0 == 0:
                        P.op("dve", lambda e: e.memset(hp[1][:, 0:1], 0.0), reads=[("hp", 1)], writes=[("hp", 1)])
                    P.op("dve", lambda e, dr=dr: e.tensor_reduce(out=l1[:, 2 + dr:3 + dr], in_=hp[dr][:], axis=AX.X, op=ALU.add, apply_absolute_value=True),
                         reads=[("hp", dr)], writes=["l1p"])
                    P.op("dve", lambda e, dr=dr: e.tensor_tensor(l1[:, 0:1], l1[:, 0:1], l1[:, 2 + dr:3 + dr], ALU.add), reads=["l1p", "l1"], writes=["l1"])
                for j in range(PP):
                    d = p0 + j
                    P.op("dve", lambda e, j=j, d=d: e.scalar_tensor_tensor(y[:, d:L], z[:, 0:L - d], hp[0][:, j:j + 1], y[:, d:L], ALU.mult, ALU.add),
                         reads=[("hp", 0), "z", "y"], writes=["y"])
                    if d >= 1:
                        P.op("dve", lambda e, j=j, d=d: e.scalar_tensor_tensor(y[:, 0:L - d], z[:, d:L], hp[1][:, j:j + 1], y[:, 0:L - d], ALU.mult, ALU.add),
                             reads=[("hp", 1), "z", "y"], writes=["y"])
            P.op("dve", lambda e: e.reciprocal(l1[:, 1:2], l1[:, 0:1]), reads=["l1"], writes=["l1"])
            for t0 in range(0, L, PP):
                sconv(1 + o, xs[:], "xs", t0, PP)
                P.op("dve", lambda e, t0=t0: e.tensor_scalar(y[:, t0:t0 + PP], y[:, t0:t0 + PP], l1[:, 1:2], None, ALU.mult), reads=["y", "l1"], writes=["y"])
                P.op("dve", lambda e, t0=t0: e.scalar_tensor_tensor(y[:, t0:t0 + PP], z[:, t0:t0 + PP], bs[:, o:o + 1], y[:, t0:t0 + PP], ALU.mult, ALU.add),
                     reads=["y", "z", "consts"], writes=["y"])
                P.op("dve", lambda e, t0=t0: e.tensor_tensor(z[:, t0:t0 + PP], y[:, t0:t0 + PP], xs[:], ALU.mult), reads=["y", "xs", "z"], writes=["z"])
        for t0 in range(0, L, PP):
            P.dma("sp", yo[:, t0:t0 + PP], z[:, t0:t0 + PP], reads=["z"], writes=["yo"])
        P.finish("sp")
    return nc


def run_hyena(X, L, cw, cb, w1, b1, f1, w2, b2, f2, w3, bias):
    nc = _get("hyena", build_hyena, L)
    pos = np.arange(L, dtype=np.float32)
    t = pos / np.float32(L - 1)
    wv = np.float32(2.0 * np.pi) * pos / np.float32(L)
    bands = np.linspace(1e-4, 15, 16, dtype=np.float32)
    feats = np.concatenate([t[:, None], np.cos(wv[:, None] * bands), -np.sin(wv[:, None] * bands)], -1).astype(np.float32)
    rates = np.abs(np.linspace(np.log(1e-2) / 1.5, np.log(1e-2) / 0.3, 512, dtype=np.float32))
    decay = np.exp(-t[None, :] * rates[:, None]).astype(np.float32)
    Xp = np.pad(X, ((0, 0), (1, 1), (0, 0)))
    maps = []
    for i in range(NCORE):
        cs = np.arange(i * 64, (i + 1) * 64)

        def rows(a):
            return np.ascontiguousarray(np.stack([a[:, :, p * 512 + cs].transpose(0, 2, 1).reshape(128, L) for p in range(3)], 0))
        cwt = np.concatenate([np.stack([cw[tp, p * 512 + cs] for p in range(3) for tp in range(3)], 1),
                              np.stack([cb[p * 512 + cs] for p in range(3)], 1)], 1)
        w3r = w3.reshape(64, 2, 2, 512)[:, :, :, cs]
        w3r = np.concatenate([w3r, w3r], -1).reshape(64, 4, 128)
        maps.append({
            "pb0": rows(Xp[:, 0:L]), "pb1": rows(Xp[:, 1:L + 1]), "pb2": rows(Xp[:, 2:L + 2]),
            "cw": np.ascontiguousarray(np.concatenate([cwt, cwt], 0)).astype(np.float32),
            "bias": np.ascontiguousarray(np.concatenate([bias[:, cs].T, bias[:, cs].T], 0)).astype(np.float32),
            "feats": np.ascontiguousarray(feats.T), "decay": np.ascontiguousarray(np.concatenate([decay[cs], decay[cs]], 0)),
            "w1": np.ascontiguousarray(w1), "w2": np.ascontiguousarray(w2), "w3": np.ascontiguousarray(w3r).astype(np.float32),
            "bf": np.ascontiguousarray(np.stack([b1, f1, b2, f2], 1)).astype(np.float32),
        })
    res = _run(nc, maps)
    out = np.empty((2, L, 512), np.float32)
    for i in range(NCORE):
        out[:, :, i * 64:(i + 1) * 64] = res[i]["yo"].reshape(2, 64, L).transpose(0, 2, 1)
    return out


def run_inproj(hc, mods_l, w_aug):
    nc = _get("inproj", build_inproj)
    maps = []
    for i in range(NCORE):
        b, q = core_bq(i)
        maps.append({"hin": hc[i], "w": w_aug, "tab": make_tab(mods_l, b, (3, 4, None), [])})
    return [r["pout"] for r in _run(nc, maps)]


def run_merge(hc, yin, gin, mods_l, bw, ow, g, bta):
    nc = _get("merge", build_merge)
    maps = []
    for i in range(NCORE):
        b, q = core_bq(i)
        maps.append({"hin": hc[i], "yin": yin[i], "gin": gin[i], "bw": bw, "ow": ow,
                     "tab": make_tab(mods_l, b, (None, None, 5), [g, bta])})
    return [r["hout"] for r in _run(nc, maps)]


def kernel(x, c, ctx, c_ctx, ada_w, ada_b, ln_g, ln_b, ffn_w_in, ffn_w_out, mix_w_in, hgrn_lb, hgrn_norm_w,
           hyena_conv_w, hyena_conv_b, hyena_w1, hyena_b1, hyena_f1, hyena_w2, hyena_b2, hyena_f2, hyena_w3,
           hyena_bias, attn_sink, branch_w, out_w):
    f = lambda a: np.ascontiguousarray(np.asarray(a, dtype=np.float32))
    (x, c, ctx, c_ctx, ada_w, ada_b, ln_g, ln_b, ffn_w_in, ffn_w_out, mix_w_in, hgrn_lb, hgrn_norm_w, hyena_conv_w,
     hyena_conv_b, hyena_w1, hyena_b1, hyena_f1, hyena_w2, hyena_b2, hyena_f2, hyena_w3, hyena_bias, attn_sink,
     branch_w, out_w) = map(f, (x, c, ctx, c_ctx, ada_w, ada_b, ln_g, ln_b, ffn_w_in, ffn_w_out, mix_w_in, hgrn_lb,
                                hgrn_norm_w, hyena_conv_w, hyena_conv_b, hyena_w1, hyena_b1, hyena_f1, hyena_w2,
                                hyena_b2, hyena_f2, hyena_w3, hyena_bias, attn_sink, branch_w, out_w))
    mods = run_mods(c, c_ctx, ada_w, ada_b)
    hl, hx = x, ctx
    for l in range(2):
        hc = to_cores(hl, hx)
        h1c = run_ffn(hc, mods[l], (0, 1, 2), ffn_w_in[l, 0], ffn_w_out[l, 0], ln_g[l, 0], ln_b[l, 0])
        w = mix_w_in[l]
        w_aug = np.ascontiguousarray(np.concatenate([w, swap_cols(w[:, OFF_Q:OFF_K]), swap_cols(w[:, OFF_K:OFF_V])], 1))
        Pl, Pc = from_cores(run_inproj(h1c, mods[l], w_aug))
        ya, yca = run_hgrn(Pl, Pc, l, hgrn_lb, hgrn_norm_w[l])
        hy = (hyena_conv_w[l], hyena_conv_b[l], hyena_w1[l], hyena_b1[l], hyena_f1[l], hyena_w2[l], hyena_b2[l],
              hyena_f2[l], hyena_w3[l], hyena_bias[l])
        yb = run_hyfft(Pl[..., OFF_B:OFF_Q], *hy)
        ycb = run_hyena(Pc[..., OFF_B:OFF_Q], CTX, *hy) if l == 0 else np.zeros((2, CTX, 512), np.float32)
        yc, ycc = run_attn(Pl, Pc, attn_sink[l])
        yin = to_cores(np.concatenate([ya, yb, yc], -1), np.concatenate([yca, ycb, ycc], -1))
        gin = to_cores(Pl[..., OFF_G:OFF_QS], Pc[..., OFF_G:OFF_QS])
        h2c = run_merge(h1c, yin, gin, mods[l], np.ascontiguousarray(branch_w[l].reshape(1536, D)), out_w[l], ln_g[l, 1], ln_b[l, 1])
        h3c = run_ffn(h2c, mods[l], (6, 7, 8), ffn_w_in[l, 1], ffn_w_out[l, 1], ln_g[l, 2], ln_b[l, 2])
        hl, hx = from_cores(h3c)
    return hl.astype(np.float32)


LH = SEQ
NF = 2 * LH


def build_hyfft(stage=3):
    L = LH
    nc = bass.Bass("TRN2", target_bir_lowering=False)
    PP = 1024
    pbd = [nc.dram_tensor("pb%d" % i, [3, 128, L], F32, kind="ExternalInput").ap() for i in range(3)]
    cwd = nc.dram_tensor("cw", [128, 12], F32, kind="ExternalInput").ap()
    bsd = nc.dram_tensor("bias", [128, 1], F32, kind="ExternalInput").ap()
    ftd = nc.dram_tensor("feats", [2, 33, L], F32, kind="ExternalInput").ap()
    dcd = nc.dram_tensor("decay", [2, 128, L], F32, kind="ExternalInput").ap()
    w1d = nc.dram_tensor("w1", [33, 64], F32, kind="ExternalInput").ap()
    w2d = nc.dram_tensor("w2", [64, 64], F32, kind="ExternalInput").ap()
    w3d = nc.dram_tensor("w3", [64, 2, 128], F32, kind="ExternalInput").ap()
    bfd = nc.dram_tensor("bf", [64, 4], F32, kind="ExternalInput").ap()
    fad = nc.dram_tensor("fa", [128, 2, 2, 512], F32, kind="ExternalInput").ap()
    twd = nc.dram_tensor("tw", [128, 2, 512], F32, kind="ExternalInput").ap()
    ggd = nc.dram_tensor("gg", [128, 3, 128], F32, kind="ExternalInput").ap()
    ryd = nc.dram_tensor("ry", [128, 2, 256], F32, kind="ExternalInput").ap()
    itd = nc.dram_tensor("it", [128, 2, 2, 256], F32, kind="ExternalInput").ap()
    fid = nc.dram_tensor("fi", [128, 2, 3, 128], F32, kind="ExternalInput").ap()
    yo = nc.dram_tensor("yo", [128, L], F32, kind="ExternalOutput").ap()
    scr = nc.dram_tensor("scr", [3, 128, L], F32).ap()
    z1d = nc.dram_tensor("z1d", [128, L], F32).ap()
    circ = nc.dram_tensor("circ", [128, NF], F32).ap()
    Hs = nc.dram_tensor("Hs", [2, 128, 128, 256], F32).ap()
    CG = 16
    with ExitStack() as st:
        P = Prog(nc, st)
        cw = sb(nc, st, "cws", [128, 12], F32)
        bs = sb(nc, st, "bss", [128, 1], F32)
        w1 = sb(nc, st, "w1s", [33, 64], F32)
        w2 = sb(nc, st, "w2s", [64, 64], F32)
        w3 = sb(nc, st, "w3s", [64, 2, 128], F32)
        bf = sb(nc, st, "bfs", [64, 4], F32)
        fa = sb(nc, st, "fas", [128, 2, 2, 512], F32)
        tw = sb(nc, st, "tws", [128, 2, 512], F32)
        gg = sb(nc, st, "ggs", [128, 3, 128], F32)
        ry = sb(nc, st, "rys", [128, 2, 256], F32)
        itw = sb(nc, st, "its", [128, 2, 2, 256], F32)
        fi = sb(nc, st, "fis", [128, 2, 3, 128], F32)
        for t, dd in ((cw, cwd), (bs, bsd), (w1, w1d), (w2, w2d), (w3, w3d), (bf, bfd), (fa, fad), (tw, twd), (gg, ggd),
                      (ry, ryd), (itw, itd), (fi, fid)):
            P.dma("sp", t[:], dd, writes=["consts"])

        with ExitStack() as s1:
            ta = sb(nc, s1, "ta", [128, PP], F32)
            tb = sb(nc, s1, "tb", [128, PP], F32)
            tcx = sb(nc, s1, "tcx", [128, PP], F32)
            xs = [sb(nc, s1, "xs%d" % i, [128, PP], F32) for i in range(2)]
            ft = sb(nc, s1, "ft", [33, PP], F32)
            h1 = sb(nc, s1, "h1", [64, PP], F32)
            h2 = sb(nc, s1, "h2", [64, PP], F32)
            ki = sb(nc, s1, "ki", [64, PP], mybir.dt.int32)
            kf = sb(nc, s1, "kf", [64, PP], F32)
            dc = sb(nc, s1, "dc", [128, PP], F32)
            hp = [sb(nc, s1, "hp%d" % i, [128, PP], F32) for i in range(2)]
            l1 = sb(nc, s1, "l1", [128, 4], F32)
            zc = sb(nc, s1, "zc", [128, 1], F32)
            pm = ps(nc, s1, "pm", [64, 512])
            ph = [ps(nc, s1, "ph%d" % i, [128, 512]) for i in range(2)]
            xi = 0
            for part in range(3):
                for t0 in range(0, L, PP):
                    xb = xi % 2
                    xi += 1
                    P.dma("sp", ta[:], pbd[0][part, :, t0:t0 + PP], writes=["ta"])
                    P.dma("sp", tb[:], pbd[1][part, :, t0:t0 + PP], writes=["tb"])
                    P.dma("sp", tcx[:], pbd[2][part, :, t0:t0 + PP], writes=["tcx"])
                    c0 = part * 3
                    P.op("dve", lambda e, c0=c0, part=part: e.tensor_scalar(ta[:], ta[:], cw[:, c0:c0 + 1], cw[:, 9 + part:10 + part], ALU.mult, ALU.add),
                         reads=["ta", "consts"], writes=["ta"])
                    P.op("dve", lambda e, c0=c0: e.scalar_tensor_tensor(tb[:], tb[:], cw[:, c0 + 1:c0 + 2], ta[:], ALU.mult, ALU.add),
                         reads=["ta", "tb", "consts"], writes=["tb"])
                    P.op("dve", lambda e, c0=c0, xb=xb: e.scalar_tensor_tensor(xs[xb][:], tcx[:], cw[:, c0 + 2:c0 + 3], tb[:], ALU.mult, ALU.add),
                         reads=["tb", "tcx", "consts"], writes=[("xs", xb)])
                    P.dma("sp", scr[part, :, t0:t0 + PP], xs[xb][:], reads=[("xs", xb)], writes=["scr"])
            P.op("pool", lambda e: e.memset(l1[:], 0.0), writes=["l1"])
            P.op("pool", lambda e: e.memset(zc[:], 0.0), writes=["zc"])
            P.dma("sp", circ[:, L:L + 1], zc[:], reads=["zc"], writes=["circ"], allow_slow_non_contiguous=True)
            for pas in range(1):
                for dr in range(2):
                    for p0 in range(0, L, PP):
                        P.dma("sp", ft[:], ftd[dr, :, p0:p0 + PP], writes=["ft"])
                        P.dma("sp", dc[:], dcd[dr, :, p0:p0 + PP], writes=["dc"])
                        for (src, skey, w, K_, dst, dkey, bi) in ((ft, "ft", w1, 33, h1, "h1", 0), (h1, "h1", w2, 64, h2, "h2", 2)):
                            for q0 in range(0, PP, 512):
                                P.op("pe", lambda e, q0=q0, src=src, w=w, K_=K_: e.matmul(pm[:, :512], w[:K_, :], src[:K_, q0:q0 + 512], start=True, stop=True),
                                     reads=[skey, "consts"], writes=["pm"])
                                P.op("dve", lambda e, q0=q0, dst=dst, bi=bi: e.tensor_scalar(dst[:, q0:q0 + 512], pm[:, :512], bf[:, bi:bi + 1], bf[:, bi + 1:bi + 2], ALU.add, ALU.mult),
                                     reads=["pm", "consts"], writes=[dkey])
                            P.op("dve", lambda e, dst=dst: e.tensor_scalar(dst[:], dst[:], 1.0 / (2.0 * PI), 8.5, ALU.mult, ALU.add), reads=[dkey], writes=[dkey])
                            P.op("dve", lambda e, dst=dst: e.tensor_copy(ki[:], dst[:]), reads=[dkey], writes=["ki"])
                            P.op("dve", lambda e, dst=dst: e.tensor_copy(kf[:], ki[:]), reads=["ki"], writes=["kf"])
                            P.op("dve", lambda e, dst=dst: e.tensor_tensor(dst[:], dst[:], kf[:], ALU.subtract), reads=[dkey, "kf"], writes=[dkey])
                            P.op("dve", lambda e, dst=dst: e.tensor_single_scalar(kf[:], dst[:], 0.0, ALU.is_lt), reads=[dkey], writes=["kf"])
                            P.op("dve", lambda e, dst=dst: e.tensor_tensor(dst[:], dst[:], kf[:], ALU.add), reads=[dkey, "kf"], writes=[dkey])
                            P.op("dve", lambda e, dst=dst: e.tensor_scalar(dst[:], dst[:], 2.0 * PI, -PI, ALU.mult, ALU.add), reads=[dkey], writes=[dkey])
                            P.op("act", lambda e, dst=dst: e.activation(out=dst[:], in_=dst[:], func=AF.Sin), reads=[dkey], writes=[dkey])
                        hb = (p0 // PP) % 2
                        for q0 in range(0, PP, 512):
                            b = (q0 // 512) % 2
                            P.op("pe", lambda e, q0=q0, dr=dr, b=b: e.matmul(ph[b][:, :512], w3[:, dr, :], h2[:, q0:q0 + 512], start=True, stop=True),
                                 reads=["h2", "consts"], writes=[("ph", b)])
                            P.op("dve", lambda e, q0=q0, b=b, hb=hb: e.tensor_tensor(hp[hb][:, q0:q0 + 512], ph[b][:, :512], dc[:, q0:q0 + 512], ALU.mult),
                                 reads=[("ph", b), "dc"], writes=[("hp", hb)])
                        last = (dr == 1 and p0 + PP == L)
                        if last:
                            P.op("dve", lambda e, hb=hb: e.memset(hp[hb][:, PP - 1:PP], 0.0), reads=[("hp", hb)], writes=[("hp", hb)])
                        P.op("dve", lambda e, hb=hb: e.tensor_reduce(out=l1[:, 2:3], in_=hp[hb][:], axis=AX.X, op=ALU.add, apply_absolute_value=True),
                             reads=[("hp", hb)], writes=["l1p"])
                        P.op("dve", lambda e: e.tensor_tensor(l1[:, 0:1], l1[:, 0:1], l1[:, 2:3], ALU.add), reads=["l1p", "l1"], writes=["l1"])
                        if dr == 0:
                            P.dma("sp", circ[:, p0:p0 + PP], hp[hb][:], reads=[("hp", hb)], writes=["circ"])
                        else:
                            n = PP - 1 if last else PP
                            P.dma("sp", circ[:, L + 1 + p0:L + 1 + p0 + n], hp[hb][:, :n], reads=[("hp", hb)], writes=["circ"])
            P.op("dve", lambda e: e.reciprocal(l1[:, 1:2], l1[:, 0:1]), reads=["l1"], writes=["l1"])
            for p0 in range(0, NF, PP):
                hb = (p0 // PP) % 2
                P.dma("sp", hp[hb][:], circ[:, p0:p0 + PP], reads=["circ"], writes=[("hp", hb)])
                P.op("dve", lambda e, hb=hb: e.tensor_scalar(hp[hb][:], hp[hb][:], l1[:, 1:2], None, ALU.mult), reads=[("hp", hb), "l1"], writes=[("hp", hb)])
                if p0 == 0:
                    P.op("dve", lambda e, hb=hb: e.tensor_tensor(hp[hb][:, 0:1], hp[hb][:, 0:1], bs[:, 0:1], ALU.add),
                         reads=[("hp", hb), "consts"], writes=[("hp", hb)])
                P.dma("sp", circ[:, p0:p0 + PP], hp[hb][:], reads=[("hp", hb)], writes=["circ"])

        with ExitStack() as s2:
            if stage < 2:
                P.finish("sp")
                return nc
            Xr = sb(nc, s2, "Xr", [128, 2, CG, 128], F32)
            Ar = sb(nc, s2, "Ar", [128, CG, 256], F32)
            Ai = sb(nc, s2, "Ai", [128, CG, 256], F32)
            Hr = sb(nc, s2, "Hr", [128, CG, 256], F32)
            Hi = sb(nc, s2, "Hi", [128, CG, 256], F32)
            Br = sb(nc, s2, "Br", [128, 2, CG, 128], F32)
            Bi = sb(nc, s2, "Bi", [128, 2, CG, 128], F32)
            U = [sb(nc, s2, "U%d" % i, [128, 512], F32) for i in range(2)]
            V = [sb(nc, s2, "V%d" % i, [128, 512], F32) for i in range(2)]
            xg = [sb(nc, s2, "xg%d" % i, [128, 2, 4, 128], F32) for i in range(2)]
            og = [sb(nc, s2, "og%d" % i, [128, 2, 4, 128], F32) for i in range(2)]
            pA = [ps(nc, s2, "pA%d" % i, [128, 512]) for i in range(2)]
            pCr = ps(nc, s2, "pCr", [128, 512])
            pCi = ps(nc, s2, "pCi", [128, 512])
            pI = [ps(nc, s2, "pI%d" % i, [128, 512]) for i in range(2)]
            pOr = ps(nc, s2, "pOr", [128, 512])
            pOi = ps(nc, s2, "pOi", [128, 512])
            cnt = [0]

            def fwd_A_and_twiddle(c, real_only):
                b = cnt[0] % 2
                cnt[0] += 1
                if real_only:
                    P.op("pe", lambda e: e.matmul(pA[b][:], Xr[:, 0, c, :], fa[:, 0, 0, :], start=True, stop=False), reads=["X", "consts"], writes=[("pA", b)], sig=False)
                    P.op("pe", lambda e: e.matmul(pA[b][:], Xr[:, 1, c, :], fa[:, 1, 0, :], start=False, stop=True), reads=["X", "consts"], writes=[("pA", b)])
                else:
                    P.op("pe", lambda e: e.matmul(pA[b][:], Xr[:, 0, c, :], fa[:, 0, 0, :], start=True, stop=False), reads=["X", "consts"], writes=[("pA", b)], sig=False)
                    P.op("pe", lambda e: e.matmul(pA[b][:], Xr[:, 1, c, :], fa[:, 0, 1, :], start=False, stop=True), reads=["X", "consts"], writes=[("pA", b)])
                P.op("dve", lambda e: e.tensor_tensor(U[b][:], pA[b][:], tw[:, 0, :], ALU.mult), reads=[("pA", b), "consts"], writes=[("U", b)])
                P.op("dve", lambda e: e.tensor_tensor(V[b][:, 0:256], pA[b][:, 256:512], tw[:, 1, 0:256], ALU.mult), reads=[("pA", b), "consts"], writes=[("V", b)])
                P.op("dve", lambda e: e.tensor_tensor(V[b][:, 256:512], pA[b][:, 0:256], tw[:, 1, 256:512], ALU.mult), reads=[("pA", b), "consts"], writes=[("V", b)])
                P.op("pool", lambda e: e.tensor_tensor(Ar[:, c, :], U[b][:, 0:256], V[b][:, 0:256], ALU.add), reads=[("U", b), ("V", b)], writes=[("A", c // 2)])
                P.op("pool", lambda e: e.tensor_tensor(Ai[:, c, :], U[b][:, 256:512], V[b][:, 256:512], ALU.add), reads=[("U", b), ("V", b)], writes=[("A", c // 2)])

            def fwd_C(j):
                ar = Ar[:, 2 * j:2 * j + 2, :]
                ai = Ai[:, 2 * j:2 * j + 2, :]
                P.op("pe", lambda e: e.matmul(pCr[:], gg[:, 0, :], ar, start=True, stop=False), reads=[("A", j), "consts"], writes=["pCr"], sig=False)
                P.op("pe", lambda e: e.matmul(pCr[:], gg[:, 2, :], ai, start=False, stop=True), reads=[("A", j), "consts"], writes=["pCr"])
                P.op("pe", lambda e: e.matmul(pCi[:], gg[:, 1, :], ar, start=True, stop=False), reads=[("A", j), "consts"], writes=["pCi"], sig=False)
                P.op("pe", lambda e: e.matmul(pCi[:], gg[:, 0, :], ai, start=False, stop=True), reads=[("A", j), "consts"], writes=["pCi"])

            for g0 in range(0, 128, CG):
                for blk in range(2):
                    src = circ[g0:g0 + CG, blk * L:(blk + 1) * L].rearrange("c (p n) -> p c n", n=128)
                    P.dma("sp", Xr[:, blk, :, :], src, reads=["circ"], writes=["X"])
                for c in range(CG):
                    fwd_A_and_twiddle(c, True)
                for j in range(CG // 2):
                    fwd_C(j)
                    P.op("act", lambda e, j=j: e.copy(out=Hr[:, 2 * j:2 * j + 2, :], in_=pCr[:]), reads=["pCr"], writes=[("H", j)])
                    P.op("act", lambda e, j=j: e.copy(out=Hi[:, 2 * j:2 * j + 2, :], in_=pCi[:]), reads=["pCi"], writes=[("H", j)])
                hk = [("H", j) for j in range(CG // 2)]
                P.dma("sp", Hs[0, :, g0:g0 + CG, :], Hr[:], reads=hk, writes=["Hs"])
                P.dma("sp", Hs[1, :, g0:g0 + CG, :], Hi[:], reads=hk, writes=["Hs"])

            for o in range(2 if stage >= 3 else 0):
                zsrc = scr[0] if o == 0 else z1d
                xsrc = scr[1 + o]
                zdst = z1d if o == 0 else yo
                for g0 in range(0, 64, CG):
                    for b in range(2):
                        src = zsrc[b * 64 + g0:b * 64 + g0 + CG, :].rearrange("c (p n) -> p c n", n=128)
                        P.dma("sp", Xr[:, b, :, :], src, reads=["scr", "z1d"], writes=["X"])
                    for ri in range(2):
                        P.dma("sp", (Hr if ri == 0 else Hi)[:], Hs[ri, :, o * 64 + g0:o * 64 + g0 + CG, :], reads=["Hs"],
                              writes=[("H", j) for j in range(CG // 2)])
                    for c in range(CG):
                        fwd_A_and_twiddle(c, False)
                    for j in range(CG // 2):
                        fwd_C(j)
                        b = j % 2
                        hr = Hr[:, 2 * j:2 * j + 2, :]
                        hi = Hi[:, 2 * j:2 * j + 2, :]
                        yr = Ar[:, 2 * j:2 * j + 2, :]
                        yi = Ai[:, 2 * j:2 * j + 2, :]
                        P.op("dve", lambda e, b=b, hr=hr: e.tensor_tensor(U[b][:], pCr[:], hr, ALU.mult), reads=["pCr", ("H", j)], writes=[("U", b)])
                        P.op("dve", lambda e, b=b, hi=hi: e.tensor_tensor(V[b][:], pCi[:], hi, ALU.mult), reads=["pCi", ("H", j)], writes=[("V", b)])
                        P.op("pool", lambda e, b=b, yr=yr: e.tensor_tensor(yr, U[b][:], V[b][:], ALU.subtract), reads=[("U", b), ("V", b)], writes=[("A", j)])
                        P.op("dve", lambda e, b=b, hi=hi: e.tensor_tensor(U[b][:], pCr[:], hi, ALU.mult), reads=["pCr", ("H", j)], writes=[("U", b)])
                        P.op("dve", lambda e, b=b, hr=hr: e.tensor_tensor(V[b][:], pCi[:], hr, ALU.mult), reads=["pCi", ("H", j)], writes=[("V", b)])
                        P.op("pool", lambda e, b=b, yi=yi: e.tensor_tensor(yi, U[b][:], V[b][:], ALU.add), reads=[("U", b), ("V", b)], writes=[("A", j)])
                    for c in range(CG):
                        b = c % 2
                        for blk in range(2):
                            P.op("pe", lambda e, c=c, blk=blk, b=b: e.matmul(pI[b][:, blk * 256:(blk + 1) * 256], Ar[:, c, blk * 128:(blk + 1) * 128], ry[:, 0, :],
                                                                             start=True, stop=False), reads=[("A", c // 2), "consts"], writes=[("pI", b)], sig=False)
                            P.op("pe", lambda e, c=c, blk=blk, b=b: e.matmul(pI[b][:, blk * 256:(blk + 1) * 256], Ai[:, c, blk * 128:(blk + 1) * 128], ry[:, 1, :],
                                                                             start=False, stop=True), reads=[("A", c // 2), "consts"], writes=[("pI", b)])
                        for blk in range(2):
                            lo, mid, hi_ = blk * 256, blk * 256 + 128, blk * 256 + 256
                            P.op("dve", lambda e, b=b, blk=blk, lo=lo, hi_=hi_: e.tensor_tensor(U[b][:, lo:hi_], pI[b][:, lo:hi_], itw[:, blk, 0, :], ALU.mult),
                                 reads=[("pI", b), "consts"], writes=[("U", b)])
                            P.op("dve", lambda e, b=b, blk=blk, lo=lo, mid=mid, hi_=hi_: e.tensor_tensor(V[b][:, lo:mid], pI[b][:, mid:hi_], itw[:, blk, 1, 0:128], ALU.mult),
                                 reads=[("pI", b), "consts"], writes=[("V", b)])
                            P.op("dve", lambda e, b=b, blk=blk, lo=lo, mid=mid, hi_=hi_: e.tensor_tensor(V[b][:, mid:hi_], pI[b][:, lo:mid], itw[:, blk, 1, 128:256], ALU.mult),
                                 reads=[("pI", b), "consts"], writes=[("V", b)])
                            P.op("pool", lambda e, b=b, blk=blk, c=c, lo=lo, mid=mid: e.tensor_tensor(Br[:, blk, c, :], U[b][:, lo:mid], V[b][:, lo:mid], ALU.add),
                                 reads=[("U", b), ("V", b)], writes=[("B", c // 4)])
                            P.op("pool", lambda e, b=b, blk=blk, c=c, mid=mid, hi_=hi_: e.tensor_tensor(Bi[:, blk, c, :], U[b][:, mid:hi_], V[b][:, mid:hi_], ALU.add),
                                 reads=[("U", b), ("V", b)], writes=[("B", c // 4)])
                    for q in range(CG // 4):
                        ob = q % 2
                        cs = slice(4 * q, 4 * q + 4)
                        for b in range(2):
                            src = xsrc[b * 64 + g0 + 4 * q:b * 64 + g0 + 4 * q + 4, :].rearrange("c (p n) -> p c n", n=128)
                            P.dma("sp", xg[ob][:, b, :, :], src, reads=["scr"], writes=[("xg", ob)])
                        for blk in range(2):
                            P.op("pe", lambda e, blk=blk, cs=cs: e.matmul(pOr[:], fi[:, blk, 0, :], Br[:, blk, cs, :], start=(blk == 0), stop=False),
                                 reads=[("B", q), "consts"], writes=["pOr"], sig=False)
                            P.op("pe", lambda e, blk=blk, cs=cs: e.matmul(pOr[:], fi[:, blk, 2, :], Bi[:, blk, cs, :], start=False, stop=(blk == 1)),
                                 reads=[("B", q), "consts"], writes=["pOr"], sig=(blk == 1))
                        for blk in range(2):
                            P.op("pe", lambda e, blk=blk, cs=cs: e.matmul(pOi[:], fi[:, blk, 1, :], Br[:, blk, cs, :], start=(blk == 0), stop=False),
                                 reads=[("B", q), "consts"], writes=["pOi"], sig=False)
                            P.op("pe", lambda e, blk=blk, cs=cs: e.matmul(pOi[:], fi[:, blk, 0, :], Bi[:, blk, cs, :], start=False, stop=(blk == 1)),
                                 reads=[("B", q), "consts"], writes=["pOi"], sig=(blk == 1))
                        P.op("dve", lambda e, ob=ob: e.tensor_tensor(og[ob][:, 0, :, :], pOr[:], xg[ob][:, 0, :, :], ALU.mult), reads=["pOr", ("xg", ob)], writes=[("og", ob)])
                        P.op("dve", lambda e, ob=ob: e.tensor_tensor(og[ob][:, 1, :, :], pOi[:], xg[ob][:, 1, :, :], ALU.mult), reads=["pOi", ("xg", ob)], writes=[("og", ob)])
                        for b in range(2):
                            dst = zdst[b * 64 + g0 + 4 * q:b * 64 + g0 + 4 * q + 4, :].rearrange("c (p n) -> p c n", n=128)
                            P.dma("sp", dst, og[ob][:, b, :, :], reads=[("og", ob)], writes=["z1d" if o == 0 else "yo"])
        P.finish("sp")
    return nc


def hyfft_consts():
    N = NF
    n1 = np.arange(256, dtype=np.float64)
    k1 = np.arange(256, dtype=np.float64)
    n2 = np.arange(128, dtype=np.float64)
    k2 = np.arange(128, dtype=np.float64)
    a = 2 * np.pi * np.outer(n1, k1) / 256
    Fc, Fs = np.cos(a), np.sin(a)
    fa = np.zeros((256, 2, 512))
    fa[:, 0, :256], fa[:, 0, 256:] = Fc, -Fs
    fa[:, 1, :256], fa[:, 1, 256:] = Fs, Fc
    fa = fa.reshape(2, 128, 2, 512).transpose(1, 0, 2, 3)
    t = 2 * np.pi * np.outer(n2, k1) / N
    Tr, Ti = np.cos(t), -np.sin(t)
    tw = np.stack([np.concatenate([Tr, Tr], 1), np.concatenate([-Ti, Ti], 1)], 1)
    g = 2 * np.pi * np.outer(n2, k2) / 128
    Gr, Gi = np.cos(g), -np.sin(g)
    gg = np.stack([Gr, Gi, -Gi], 1)
    ry = np.stack([np.concatenate([Gr, -Gi], 1), np.concatenate([Gi, Gr], 1)], 1)
    tt = 2 * np.pi * np.outer(k1, n2) / N
    cTr, cTi = np.cos(tt), np.sin(tt)
    it = np.stack([np.concatenate([cTr, cTr], 1), np.concatenate([-cTi, cTi], 1)], 1)
    it = it.reshape(2, 128, 2, 256).transpose(1, 0, 2, 3)
    ai = 2 * np.pi * np.outer(k1, n1[:128]) / 256
    fi = np.stack([np.cos(ai), np.sin(ai), -np.sin(ai)], 1) / N
    fi = fi.reshape(2, 128, 3, 128).transpose(1, 0, 2, 3)
    f32 = lambda x: np.ascontiguousarray(x.astype(np.float32))
    return {"fa": f32(fa), "tw": f32(tw), "gg": f32(gg), "ry": f32(ry), "it": f32(it), "fi": f32(fi)}


def run_hyfft(X, cw, cb, w1, b1, f1, w2, b2, f2, w3, bias):
    L = LH
    import os
    nc = _get("hyfft", build_hyfft, int(os.environ.get("HYFFT_STAGE", "3")))
    consts = _get("hyfft_consts", hyfft_consts)
    pos = np.arange(L, dtype=np.float32)
    t = pos / np.float32(L - 1)
    wv = np.float32(2.0 * np.pi) * pos / np.float32(L)
    bands = np.linspace(1e-4, 15, 16, dtype=np.float32)
    feats = np.concatenate([t[:, None], np.cos(wv[:, None] * bands), -np.sin(wv[:, None] * bands)], -1).astype(np.float32).T
    rates = np.abs(np.linspace(np.log(1e-2) / 1.5, np.log(1e-2) / 0.3, 512, dtype=np.float32))
    decay = np.exp(-t[None, :] * rates[:, None]).astype(np.float32)
    Xp = np.pad(X, ((0, 0), (1, 1), (0, 0)))
    feats2 = np.ascontiguousarray(np.stack([feats, feats[:, ::-1]], 0))
    maps = []
    for i in range(NCORE):
        cs = np.arange(i * 64, (i + 1) * 64)

        def rows(a):
            return np.ascontiguousarray(np.stack([a[:, :, p * 512 + cs].transpose(0, 2, 1).reshape(128, L) for p in range(3)], 0))
        cwt = np.concatenate([np.stack([cw[tp, p * 512 + cs] for p in range(3) for tp in range(3)], 1),
                              np.stack([cb[p * 512 + cs] for p in range(3)], 1)], 1)
        w3r = w3.reshape(64, 2, 2, 512)[:, :, :, cs].reshape(64, 2, 128)
        dco = np.concatenate([decay[cs], decay[cs]], 0)
        m = {
            "pb0": rows(Xp[:, 0:L]), "pb1": rows(Xp[:, 1:L + 1]), "pb2": rows(Xp[:, 2:L + 2]),
            "cw": np.ascontiguousarray(np.concatenate([cwt, cwt], 0)).astype(np.float32),
            "bias": np.ascontiguousarray(bias[:, cs].reshape(128, 1)).astype(np.float32),
            "feats": feats2, "decay": np.ascontiguousarray(np.stack([dco, dco[:, ::-1]], 0)),
            "w1": np.ascontiguousarray(w1), "w2": np.ascontiguousarray(w2), "w3": np.ascontiguousarray(w3r).astype(np.float32),
            "bf": np.ascontiguousarray(np.stack([b1, f1, b2, f2], 1)).astype(np.float32),
        }
        m.update(consts)
        maps.append(m)
    res = _run(nc, maps)
    out = np.empty((2, L, 512), np.float32)
    for i in range(NCORE):
        out[:, :, i * 64:(i + 1) * 64] = res[i]["yo"].reshape(2, 64, L).transpose(0, 2, 1)
    return out
```

```python
import numpy as np
from contextlib import ExitStack
import concourse.bass as bass
import concourse.mybir as mybir
from concourse.bass_utils import run_bass_kernel_spmd

F32 = mybir.dt.float32
BF16 = mybir.dt.bfloat16
AF = mybir.ActivationFunctionType
ALU = mybir.AluOpType
AX = mybir.AxisListType

D = 1024
DFF = 2816
SEQ = 16384
CTX = 256
NCORE = 8
TLAT = 4096
TCTX = 64
TT = TLAT + TCTX
DN_ALPHA = 4.0 ** 0.25
LN_EPS = 1e-5
RMS_EPS = 1e-6
NQK = 8576

EPOCH = 12000
NDMASEM = 8


class Prog:
    def __init__(self, nc, stack):
        self.nc = nc
        self.stack = stack
        self.eng = {"pe": nc.tensor, "act": nc.scalar, "dve": nc.vector,
                    "pool": nc.gpsimd, "sp": nc.sync}
        self.cnt = {e: 0 for e in self.eng}
        self.sems = {}
        self.lastw = {}
        self.readers = {}
        self.seen = {e: {} for e in self.eng}
        self.dcnt = {e: 0 for e in self.eng}
        self.dpend = {}
        self.lastw_dma = {}

    def _sem(self, name):
        if name not in self.sems:
            self.sems[name] = self.stack.enter_context(self.nc.semaphore(name))
        return self.sems[name]

    def _wait(self, e, tok):
        name, val = tok
        if self.seen[e].get(name, 0) >= val:
            return
        self.eng[e].wait_ge(self._sem(name), val)
        self.seen[e][name] = val

    def _deps(self, reads, writes):
        deps = []
        for k in list(reads) + list(writes):
            if k in self.lastw:
                deps.append(self.lastw[k])
            if k in self.lastw_dma:
                deps.extend(self.lastw_dma[k].values())
        for k in writes:
            deps.extend(self.readers.get(k, []))
        return deps

    def _commit(self, tok, reads, writes, is_dma=False):
        for k in reads:
            self.readers.setdefault(k, []).append(tok)
        for k in writes:
            if is_dma and k in self.lastw_dma:
                self.lastw_dma[k][tok[0]] = tok
            elif is_dma:
                self.lastw_dma[k] = {tok[0]: tok}
            else:
                self.lastw_dma.pop(k, None)
            self.lastw[k] = tok
            self.readers[k] = []

    def op(self, e, fn, reads=(), writes=(), sig=True):
        ep, v = divmod(self.cnt[e], EPOCH)
        own = "c_%s_%d" % (e, ep)
        for tok in self._deps(reads, writes):
            if tok[0] == own and tok[1] > v:
                continue
            self._wait(e, tok)
        ins = fn(self.eng[e])
        tok = ("c_%s_%d" % (e, ep), v + 1)
        if sig:
            self.cnt[e] += 1
            ins.then_inc(self._sem(tok[0]), 1)
        self._commit(tok, reads, writes)
        return tok

    def dma(self, e, out, in_, reads=(), writes=(), **kw):
        for tok in self._deps(reads, writes):
            self._wait(e, tok)
        i = self.dcnt[e]
        self.dcnt[e] += 1
        name = "d_%s_%d" % (e, i % NDMASEM)
        prev = self.dpend.get(name)
        if prev is not None:
            self._wait(e, prev)
        ins = self.eng[e].dma_start(out=out, in_=in_, **kw)
        ins.then_inc(self._sem(name), 16)
        tok = (name, 16 * (i // NDMASEM + 1))
        self.dpend[name] = tok
        self._commit(tok, reads, writes, is_dma=True)
        return tok

    def finish(self, e="sp"):
        for tok in set(self.lastw.values()):
            self._wait(e, tok)
        for tok in self.dpend.values():
            self._wait(e, tok)


def sb(nc, st, name, shape, dt):
    return st.enter_context(nc.sbuf_tensor(name, shape, dt))


def ps(nc, st, name, shape, dt=F32):
    return st.enter_context(nc.psum_tensor(name, shape, dt))


def load_w(P, nc, st, name, dram, K, N, eng="pool"):
    kc = K // 128
    t = sb(nc, st, name, [128, kc, N], BF16)
    v = dram.rearrange("(k p) n -> p k n", p=128)
    for k in range(kc):
        P.dma(eng, t[:, k, :], v[:, k, :], writes=[(name, k)])
    return t


def wkeys(name, n):
    return [(name, k) for k in range(n)]


def token_tiles(tsz):
    tiles = []
    t = 0
    while t < TLAT:
        n = min(tsz, TLAT - t)
        tiles.append((t, n, 0))
        t += n
    tiles.append((TLAT, TCTX, 1))
    return tiles


class Common:
    def __init__(self, nc, st, P, tsz):
        self.nc, self.st, self.P, self.tsz = nc, st, P, tsz
        self.ones = sb(nc, st, "ones", [128, 128], F32)
        P.op("pool", lambda e: e.memset(self.ones[:], 1.0), writes=["ones"])
        self.r = sb(nc, st, "r", [128, 8, tsz], F32)
        self.sq = sb(nc, st, "sq", [128, 8, tsz], F32)
        self.t1 = sb(nc, st, "t1", [128, 2, tsz], F32)
        self.t2 = sb(nc, st, "t2", [128, 2, tsz], F32)
        self.stt = sb(nc, st, "stt", [128, 4, tsz], F32)
        self.ps1 = ps(nc, st, "ps1", [128, 512])
        self.ps2 = ps(nc, st, "ps2", [128, 512])

    def layer_norm(self, tn, gam, bet, out, outkey):
        P, r, sq = self.P, self.r, self.sq
        for d in range(8):
            P.op("act", lambda e, d=d: e.activation(out=sq[:, d, :tn], in_=r[:, d, :tn], func=AF.Square),
                 reads=[("r", d)], writes=[("sq", d)])
        for d in range(8):
            P.op("pe", lambda e, d=d: e.matmul(self.ps1[:, :tn], self.ones[:], r[:, d, :tn], start=(d == 0), stop=(d == 7)),
                 reads=[("r", d), "ones"], writes=["ps1"], sig=(d == 7))
        for d in range(8):
            P.op("pe", lambda e, d=d: e.matmul(self.ps2[:, :tn], self.ones[:], sq[:, d, :tn], start=(d == 0), stop=(d == 7)),
                 reads=[("sq", d), "ones"], writes=["ps2"], sig=(d == 7))
        mean, msq, var, rstd = (self.stt[:, i, :tn] for i in range(4))
        P.op("dve", lambda e: e.tensor_single_scalar(mean, self.ps1[:, :tn], 1.0 / D, ALU.mult), reads=["ps1"], writes=["mean"])
        P.op("dve", lambda e: e.tensor_tensor(msq, mean, mean, ALU.mult), reads=["mean"], writes=["msq"])
        P.op("dve", lambda e: e.scalar_tensor_tensor(var, self.ps2[:, :tn], 1.0 / D, msq, ALU.mult, ALU.subtract),
             reads=["ps2", "msq"], writes=["var"])
        P.op("dve", lambda e: e.tensor_single_scalar(var, var, LN_EPS, ALU.add), reads=["var"], writes=["var"])
        P.op("act", lambda e: e.sqrt(out=msq, in_=var), reads=["var"], writes=["msq"])
        P.op("dve", lambda e: e.reciprocal(rstd, msq), reads=["msq"], writes=["rstd"])
        for d in range(8):
            b = d % 2
            P.op("dve", lambda e, d=d, b=b: e.tensor_tensor(self.t1[:, b, :tn], r[:, d, :tn], mean, ALU.subtract),
                 reads=[("r", d), "mean"], writes=[("t1", b)])
            P.op("pool", lambda e, d=d, b=b: e.tensor_tensor(self.t2[:, b, :tn], self.t1[:, b, :tn], rstd, ALU.mult),
                 reads=[("t1", b), "rstd"], writes=[("t2", b)])
            P.op("act", lambda e, d=d, b=b: e.activation(out=out[:, d, :tn], in_=self.t2[:, b, :tn], func=AF.Identity,
                                                         scale=gam[:, d:d + 1], bias=bet[:, d:d + 1]),
                 reads=[("t2", b), "tab"], writes=[(outkey, d)])


def emit_modulate(P, tn, src, srckey, dst, dstkey, sc1p, sh):
    for c in range(8):
        P.op("pool", lambda e, c=c: e.tensor_scalar(dst[:, c, :tn], src[:, c, :tn], sc1p[:, c:c + 1], sh[:, c:c + 1],
                                                    ALU.mult, ALU.add),
             reads=[(srckey, c), "tab"], writes=[(dstkey, c)])


NC0 = 2 * 9216 // NCORE


def build_k0():
    nc = bass.Bass("TRN2", target_bir_lowering=False)
    cT = nc.dram_tensor("cT", [D, 3], F32, kind="ExternalInput").ap()
    aw = nc.dram_tensor("aw", [D, NC0], F32, kind="ExternalInput").ap()
    ab = nc.dram_tensor("ab", [128, NC0 // 128], F32, kind="ExternalInput").ap()
    out = nc.dram_tensor("out", [NC0, 3], F32, kind="ExternalOutput").ap()
    nj = NC0 // 128
    with ExitStack() as st:
        P = Prog(nc, st)
        ct = sb(nc, st, "ct", [128, 8, 3], F32)
        stt = sb(nc, st, "st", [128, 8, 3], F32)
        abt = sb(nc, st, "abt", [128, nj], F32)
        ot = sb(nc, st, "ot", [128, nj, 3], F32)
        pp = ps(nc, st, "pp", [128, nj, 4])
        P.dma("sp", ct[:], cT.rearrange("(k p) n -> p k n", p=128), writes=["ct"])
        P.dma("sp", abt[:], ab, writes=["abt"])
        P.op("act", lambda e: e.activation(out=stt[:], in_=ct[:], func=AF.Silu), reads=["ct"], writes=["st"])
        awv = aw.rearrange("(k p) n -> p k n", p=128)
        npc = 3
        cw = NC0 // npc
        wt = [sb(nc, st, "wt%d" % i, [128, 8, cw], F32) for i in range(npc)]
        for i in range(npc):
            for k in range(8):
                P.dma("sp", wt[i][:, k, :], awv[:, k, i * cw:(i + 1) * cw], writes=[("wt", i, k)])
        for j in range(nj):
            i, jj = divmod(j * 128, cw)
            for k in range(8):
                P.op("pe", lambda e, i=i, jj=jj, k=k, j=j: e.matmul(pp[:, j, 0:3], wt[i][:, k, jj:jj + 128], stt[:, k, :],
                                                                  start=(k == 0), stop=(k == 7)),
                     reads=[("wt", i, k), "st"], writes=[("pp", j)])
        for j in range(nj):
            P.op("dve", lambda e, j=j: e.tensor_scalar(ot[:, j, :], pp[:, j, 0:3], abt[:, j:j + 1], None, ALU.add),
                 reads=[("pp", jj) for jj in range(nj)] + ["abt"], writes=["ot"])
        P.dma("sp", out.rearrange("(j p) n -> p j n", p=128), ot[:], reads=["ot"], writes=["out"])
        P.finish("sp")
    return nc


def prep_tab(P, nc, st, tab_d, ncols, gmul):
    tab = sb(nc, st, "tabs", [128, ncols * 8], F32)
    P.dma("sp", tab[:], tab_d, writes=["tab0"])
    sc1p, sh, gt = [], [], []
    for ms in range(2):
        o = ms * 24
        P.op("dve", lambda e, o=o: e.tensor_single_scalar(tab[:, o + 8:o + 16], tab[:, o + 8:o + 16], 1.0, ALU.add),
             reads=["tab0"], writes=["tab"])
        P.op("dve", lambda e, o=o: e.tensor_single_scalar(tab[:, o + 16:o + 24], tab[:, o + 16:o + 24], gmul, ALU.mult),
             reads=["tab0"], writes=["tab"])
        sh.append(tab[:, o:o + 8])
        sc1p.append(tab[:, o + 8:o + 16])
        gt.append(tab[:, o + 16:o + 24])
    return tab, sh, sc1p, gt


def build_ffn(tsz=256):
    nc = bass.Bass("TRN2", target_bir_lowering=False)
    hin = nc.dram_tensor("hin", [D, TT], F32, kind="ExternalInput").ap()
    w1d = nc.dram_tensor("w1", [D, 2 * DFF], F32, kind="ExternalInput").ap()
    w2d = nc.dram_tensor("w2", [DFF, D], F32, kind="ExternalInput").ap()
    tabd = nc.dram_tensor("tab", [128, 64], F32, kind="ExternalInput").ap()
    hout = nc.dram_tensor("hout", [D, TT], F32, kind="ExternalOutput").ap()
    hv = hin.rearrange("(c p) t -> p c t", p=128)
    ov = hout.rearrange("(c p) t -> p c t", p=128)
    NJ = DFF // 128
    with ExitStack() as st:
        P = Prog(nc, st)
        tab, sh, sc1p, gt = prep_tab(P, nc, st, tabd, 8, 0.5)
        gam, bet = tab[:, 48:56], tab[:, 56:64]
        w1 = load_w(P, nc, st, "w1s", w1d, D, 2 * DFF)
        w2 = load_w(P, nc, st, "w2s", w2d, DFF, D)
        C = Common(nc, st, P, tsz)
        hT = sb(nc, st, "hT", [128, 8, tsz], F32)
        hA = sb(nc, st, "hA", [128, 8, tsz], F32)
        uT = sb(nc, st, "uT", [128, 8, tsz], BF16)
        gT = sb(nc, st, "gT", [128, NJ, tsz], BF16)
        sa = sb(nc, st, "sa", [128, 2, tsz], F32)
        psA = [ps(nc, st, "psA%d" % i, [128, 512]) for i in range(2)]
        psB = [ps(nc, st, "psB%d" % i, [128, 512]) for i in range(2)]
        psY = [ps(nc, st, "psY%d" % i, [128, 512]) for i in range(2)]
        for (t0, tn, ms) in token_tiles(tsz):
            for c in range(8):
                P.dma("sp", hT[:, c, :tn], hv[:, c, t0:t0 + tn], writes=[("hT", c)])
            emit_modulate(P, tn, hT, "hT", uT, "uT", sc1p[ms], sh[ms])
            for c in range(8):
                P.op("pool", lambda e, c=c: e.tensor_single_scalar(hA[:, c, :tn], hT[:, c, :tn], DN_ALPHA, ALU.mult),
                     reads=[("hT", c)], writes=[("hA", c)])
            for j in range(NJ):
                b = j % 2
                for k in range(8):
                    P.op("pe", lambda e, j=j, k=k, b=b: e.matmul(psA[b][:, :tn], w1[:, k, j * 128:(j + 1) * 128], uT[:, k, :tn],
                                                                start=(k == 0), stop=(k == 7)),
                         reads=[("w1s", k), ("uT", k)], writes=[("psA", b)], sig=(k == 7))
                for k in range(8):
                    P.op("pe", lambda e, j=j, k=k, b=b: e.matmul(psB[b][:, :tn], w1[:, k, DFF + j * 128:DFF + (j + 1) * 128],
                                                                uT[:, k, :tn], start=(k == 0), stop=(k == 7)),
                         reads=[("w1s", k), ("uT", k)], writes=[("psB", b)], sig=(k == 7))
                P.op("act", lambda e, b=b: e.activation(out=sa[:, b, :tn], in_=psA[b][:, :tn], func=AF.Silu),
                     reads=[("psA", b)], writes=[("sa", b)])
                P.op("dve", lambda e, j=j, b=b: e.tensor_tensor(gT[:, j, :tn], sa[:, b, :tn], psB[b][:, :tn], ALU.mult),
                     reads=[("sa", b), ("psB", b)], writes=[("gT", j)])
            for d in range(8):
                b = d % 2
                for j in range(NJ):
                    P.op("pe", lambda e, j=j, d=d, b=b: e.matmul(psY[b][:, :tn], w2[:, j, d * 128:(d + 1) * 128], gT[:, j, :tn],
                                                                start=(j == 0), stop=(j == NJ - 1)),
                         reads=[("w2s", j), ("gT", j)], writes=[("psY", b)], sig=(j == NJ - 1))
                P.op("dve", lambda e, d=d, b=b: e.scalar_tensor_tensor(C.r[:, d, :tn], psY[b][:, :tn], gt[ms][:, d:d + 1],
                                                                      hA[:, d, :tn], ALU.mult, ALU.add),
                     reads=[("psY", b), ("hA", d), "tab"], writes=[("r", d)])
            C.layer_norm(tn, gam, bet, hT, "hT")
            for c in range(8):
                P.dma("sp", ov[:, c, t0:t0 + tn], hT[:, c, :tn], reads=[("hT", c)], writes=["hout"])
        P.finish("sp")
    return nc


def build_inproj(tsz=256):
    nc = bass.Bass("TRN2", target_bir_lowering=False)
    hin = nc.dram_tensor("hin", [D, TT], F32, kind="ExternalInput").ap()
    wd = nc.dram_tensor("w", [D, NQK], F32, kind="ExternalInput").ap()
    tabd = nc.dram_tensor("tab", [128, 48], F32, kind="ExternalInput").ap()
    pout = nc.dram_tensor("pout", [NQK, TT], F32, kind="ExternalOutput").ap()
    hv = hin.rearrange("(c p) t -> p c t", p=128)
    pv = pout.rearrange("(c p) t -> p c t", p=128)
    NO = NQK // 128
    G = 4
    with ExitStack() as st:
        P = Prog(nc, st)
        tab, sh, sc1p, gt = prep_tab(P, nc, st, tabd, 6, 1.0)
        w = load_w(P, nc, st, "ws", wd, D, NQK)
        hT = sb(nc, st, "hT", [128, 8, tsz], F32)
        uT = sb(nc, st, "uT", [128, 8, tsz], BF16)
        og = [sb(nc, st, "og%d" % i, [128, G, tsz], F32) for i in range(2)]
        pp = [ps(nc, st, "pp%d" % i, [128, 512]) for i in range(4)]
        gi = 0
        for (t0, tn, ms) in token_tiles(tsz):
            for c in range(8):
                P.dma("sp", hT[:, c, :tn], hv[:, c, t0:t0 + tn], writes=[("hT", c)])
            emit_modulate(P, tn, hT, "hT", uT, "uT", sc1p[ms], sh[ms])
            for o0 in range(0, NO, G):
                gn = min(G, NO - o0)
                ob = gi % 2
                gi += 1
                for g in range(gn):
                    o = o0 + g
                    b = o % 4
                    for k in range(8):
                        P.op("pe", lambda e, o=o, k=k, b=b: e.matmul(pp[b][:, :tn], w[:, k, o * 128:(o + 1) * 128], uT[:, k, :tn],
                                                                    start=(k == 0), stop=(k == 7)),
                             reads=[("ws", k), ("uT", k)], writes=[("pp", b)], sig=(k == 7))
                    if o % 2 == 0:
                        P.op("act", lambda e, g=g, b=b, ob=ob: e.copy(out=og[ob][:, g, :tn], in_=pp[b][:, :tn]),
                             reads=[("pp", b)], writes=[("og", ob)])
                    else:
                        P.op("dve", lambda e, g=g, b=b, ob=ob: e.tensor_copy(og[ob][:, g, :tn], pp[b][:, :tn]),
                             reads=[("pp", b)], writes=[("og", ob)])
                P.dma("sp", pv[:, o0:o0 + gn, t0:t0 + tn], og[ob][:, :gn, :tn], reads=[("og", ob)], writes=["pout"])
        P.finish("sp")
    return nc


def build_merge(tsz=256):
    nc = bass.Bass("TRN2", target_bir_lowering=False)
    hin = nc.dram_tensor("hin", [D, TT], F32, kind="ExternalInput").ap()
    yin = nc.dram_tensor("yin", [1536, TT], F32, kind="ExternalInput").ap()
    gin = nc.dram_tensor("gin", [3072, TT], F32, kind="ExternalInput").ap()
    bwd = nc.dram_tensor("bw", [1536, D], F32, kind="ExternalInput").ap()
    owd = nc.dram_tensor("ow", [D, D], F32, kind="ExternalInput").ap()
    tabd = nc.dram_tensor("tab", [128, 64], F32, kind="ExternalInput").ap()
    hout = nc.dram_tensor("hout", [D, TT], F32, kind="ExternalOutput").ap()
    hv = hin.rearrange("(c p) t -> p c t", p=128)
    yv = yin.rearrange("(c p) t -> p c t", p=128)
    gv = gin.rearrange("(c p) t -> p c t", p=128)
    ov = hout.rearrange("(c p) t -> p c t", p=128)
    with ExitStack() as st:
        P = Prog(nc, st)
        tab, sh, sc1p, gt = prep_tab(P, nc, st, tabd, 8, 1.0)
        gam, bet = tab[:, 48:56], tab[:, 56:64]
        bw = load_w(P, nc, st, "bws", bwd, 1536, D)
        ow = load_w(P, nc, st, "ows", owd, D, D)
        C = Common(nc, st, P, tsz)
        hT = sb(nc, st, "hT", [128, 8, tsz], F32)
        hA = sb(nc, st, "hA", [128, 8, tsz], F32)
        yb = sb(nc, st, "yb", [128, 12, tsz], BF16)
        sg = sb(nc, st, "sg", [128, 24, tsz], F32)
        macc = sb(nc, st, "macc", [128, 2, tsz], F32)
        mtmp = sb(nc, st, "mtmp", [128, 2, tsz], F32)
        mT = sb(nc, st, "mT", [128, 8, tsz], BF16)
        psM = [ps(nc, st, "psM%d" % i, [128, 512]) for i in range(3)]
        psY = [ps(nc, st, "psY%d" % i, [128, 512]) for i in range(2)]
        for (t0, tn, ms) in token_tiles(tsz):
            for c in range(8):
                P.dma("sp", hT[:, c, :tn], hv[:, c, t0:t0 + tn], writes=[("hT", c)])
            for c in range(12):
                P.dma("pool", yb[:, c, :tn], yv[:, c, t0:t0 + tn], writes=[("yb", c)])
            for c in range(24):
                P.dma("sp", sg[:, c, :tn], gv[:, c, t0:t0 + tn], writes=[("sg", c)])
                P.op("act", lambda e, c=c: e.activation(out=sg[:, c, :tn], in_=sg[:, c, :tn], func=AF.Sigmoid),
                     reads=[("sg", c)], writes=[("sg", c)])
            for c in range(8):
                P.op("pool", lambda e, c=c: e.tensor_single_scalar(hA[:, c, :tn], hT[:, c, :tn], DN_ALPHA, ALU.mult),
                     reads=[("hT", c)], writes=[("hA", c)])
            for d in range(8):
                b = d % 2
                for n in range(3):
                    for kc in range(4):
                        P.op("pe", lambda e, n=n, kc=kc, d=d: e.matmul(psM[n][:, :tn], bw[:, n * 4 + kc, d * 128:(d + 1) * 128],
                                                                      yb[:, n * 4 + kc, :tn], start=(kc == 0), stop=(kc == 3)),
                             reads=[("bws", n * 4 + kc), ("yb", n * 4 + kc)], writes=[("psM", n)], sig=(kc == 3))
                P.op("dve", lambda e, d=d, b=b: e.tensor_tensor(macc[:, b, :tn], sg[:, d, :tn], psM[0][:, :tn], ALU.mult),
                     reads=[("sg", d), ("psM", 0)], writes=[("macc", b)])
                P.op("dve", lambda e, d=d, b=b: e.tensor_tensor(mtmp[:, 0, :tn], sg[:, 8 + d, :tn], psM[1][:, :tn], ALU.mult),
                     reads=[("sg", 8 + d), ("psM", 1)], writes=[("mtmp", 0)])
                P.op("dve", lambda e, d=d, b=b: e.tensor_tensor(mtmp[:, 1, :tn], sg[:, 16 + d, :tn], psM[2][:, :tn], ALU.mult),
                     reads=[("sg", 16 + d), ("psM", 2)], writes=[("mtmp", 1)])
                P.op("pool", lambda e, b=b: e.tensor_tensor(macc[:, b, :tn], macc[:, b, :tn], mtmp[:, 0, :tn], ALU.add),
                     reads=[("macc", b), ("mtmp", 0)], writes=[("macc", b)])
                P.op("pool", lambda e, d=d, b=b: e.tensor_tensor(mT[:, d, :tn], macc[:, b, :tn], mtmp[:, 1, :tn], ALU.add),
                     reads=[("macc", b), ("mtmp", 1)], writes=[("mT", d)])
            for d in range(8):
                b = d % 2
                for k in range(8):
                    P.op("pe", lambda e, k=k, d=d, b=b: e.matmul(psY[b][:, :tn], ow[:, k, d * 128:(d + 1) * 128], mT[:, k, :tn],
                                                                start=(k == 0), stop=(k == 7)),
                         reads=[("ows", k), ("mT", k)], writes=[("psY", b)], sig=(k == 7))
                P.op("dve", lambda e, d=d, b=b: e.scalar_tensor_tensor(C.r[:, d, :tn], psY[b][:, :tn], gt[ms][:, d:d + 1],
                                                                      hA[:, d, :tn], ALU.mult, ALU.add),
                     reads=[("psY", b), ("hA", d), "tab"], writes=[("r", d)])
            C.layer_norm(tn, gam, bet, hT, "hT")
            for c in range(8):
                P.dma("sp", ov[:, c, t0:t0 + tn], hT[:, c, :tn], reads=[("hT", c)], writes=["hout"])
        P.finish("sp")
    return nc


_CACHE = {}


def _get(name, fn, *a):
    key = (name,) + a
    if key not in _CACHE:
        _CACHE[key] = fn(*a)
    return _CACHE[key]


def _run(nc, in_maps):
    res = run_bass_kernel_spmd(nc, in_maps, core_ids=list(range(NCORE)))
    return res.results


def _pc(v):
    return np.ascontiguousarray(v.reshape(-1, 128).T)


def core_bq(i):
    return i // 4, i % 4


def to_cores(lat, cx):
    outs = []
    for i in range(NCORE):
        b, q = core_bq(i)
        a = np.concatenate([lat[b, q * TLAT:(q + 1) * TLAT], cx[b, q * TCTX:(q + 1) * TCTX]], axis=0)
        outs.append(np.ascontiguousarray(a.T))
    return outs


def from_cores(outs):
    C = outs[0].shape[0]
    lat = np.empty((2, SEQ, C), np.float32)
    cx = np.empty((2, CTX, C), np.float32)
    for i in range(NCORE):
        b, q = core_bq(i)
        lat[b, q * TLAT:(q + 1) * TLAT] = outs[i][:, :TLAT].T
        cx[b, q * TCTX:(q + 1) * TCTX] = outs[i][:, TLAT:].T
    return lat, cx


def run_mods(c, c_ctx, ada_w, ada_b):
    cT = np.ascontiguousarray(np.concatenate([c, c_ctx[None]], 0).T)
    aw = np.concatenate([ada_w[0], ada_w[1]], axis=1)
    ab = np.concatenate([ada_b[0], ada_b[1]], axis=0)
    nc = _get("k0", build_k0)
    maps = []
    for i in range(NCORE):
        sl = slice(i * NC0, (i + 1) * NC0)
        maps.append({"cT": cT, "aw": np.ascontiguousarray(aw[:, sl]), "ab": _pc(ab[sl])})
    res = _run(nc, maps)
    allm = np.concatenate([r["out"] for r in res], axis=0)
    return allm.reshape(2, 9, D, 3)


def make_tab(mods_l, b, idx3, extra):
    cols = []
    for v in (b, 2):
        for m in idx3:
            cols.append(_pc(mods_l[m, :, v]) if m is not None else np.zeros((128, 8), np.float32))
    for e in extra:
        cols.append(_pc(e))
    return np.ascontiguousarray(np.concatenate(cols, axis=1).astype(np.float32))


def run_ffn(hc, mods_l, idx3, w1, w2, g, bta):
    nc = _get("ffn", build_ffn)
    maps = []
    for i in range(NCORE):
        b, q = core_bq(i)
        maps.append({"hin": hc[i], "w1": w1, "w2": w2, "tab": make_tab(mods_l, b, idx3, [g, bta])})
    return [r["hout"] for r in _run(nc, maps)]


NQ = TT
NKL = TLAT + 256
NKB = NKL // 128 + 2
NK = NKB * 128


def build_attn():
    nc = bass.Bass("TRN2", target_bir_lowering=False)
    qf = nc.dram_tensor("qf", [64, 8, NQ], F32, kind="ExternalInput").ap()
    qs = nc.dram_tensor("qs", [64, 8, NQ], F32, kind="ExternalInput").ap()
    cq = nc.dram_tensor("cq", [64, NQ], F32, kind="ExternalInput").ap()
    sq = nc.dram_tensor("sq", [64, NQ], F32, kind="ExternalInput").ap()
    kf = nc.dram_tensor("kf", [64, 2, NK], F32, kind="ExternalInput").ap()
    ks = nc.dram_tensor("ks", [64, 2, NK], F32, kind="ExternalInput").ap()
    ck = nc.dram_tensor("ck", [64, NK], F32, kind="ExternalInput").ap()
    sk = nc.dram_tensor("sk", [64, NK], F32, kind="ExternalInput").ap()
    vt = nc.dram_tensor("vt", [128, NKB, 128], F32, kind="ExternalInput").ap()
    mk = nc.dram_tensor("mk", [128, 4, 4, 128], F32, kind="ExternalInput").ap()
    sk8 = nc.dram_tensor("sink", [64, 8], F32, kind="ExternalInput").ap()
    yo = nc.dram_tensor("yo", [64, 8, NQ], F32, kind="ExternalOutput").ap()
    CH = 1152
    with ExitStack() as st:
        P = Prog(nc, st)
        kr = sb(nc, st, "kr", [64, 2, NK], BF16)
        vb = sb(nc, st, "vb", [128, NKB, 128], BF16)
        mb = sb(nc, st, "mb", [128, 4, 4, 128], BF16)
        ones = sb(nc, st, "ones", [128, 64], BF16)
        es = sb(nc, st, "es", [64, 8], F32)
        P.dma("pool", vb[:], vt, writes=["vb"])
        P.dma("pool", mb[:], mk, writes=["mb"])
        P.dma("sp", es[:], sk8, writes=["es"])
        P.op("act", lambda e: e.activation(out=es[:], in_=es[:], func=AF.Exp), reads=["es"], writes=["es"])
        P.op("pool", lambda e: e.memset(ones[:], 1.0), writes=["ones"])
        ta = sb(nc, st, "ta", [64, 2, CH], F32)
        tb = sb(nc, st, "tb", [64, 2, CH], F32)
        tc_ = sb(nc, st, "tc", [64, CH], F32)
        td = sb(nc, st, "td", [64, CH], F32)
        for c0 in range(0, NK, CH):
            P.dma("sp", ta[:], kf[:, :, c0:c0 + CH], writes=["ta"])
            P.dma("sp", tb[:], ks[:, :, c0:c0 + CH], writes=["tb"])
            P.dma("sp", tc_[:], ck[:, c0:c0 + CH], writes=["tc"])
            P.dma("sp", td[:], sk[:, c0:c0 + CH], writes=["td"])
            for h in range(2):
                P.op("dve", lambda e, h=h: e.tensor_tensor(ta[:, h, :], ta[:, h, :], tc_[:], ALU.mult), reads=["ta", "tc"], writes=["ta"])
                P.op("pool", lambda e, h=h: e.tensor_tensor(tb[:, h, :], tb[:, h, :], td[:], ALU.mult), reads=["tb", "td"], writes=["tb"])
                P.op("dve", lambda e, h=h, c0=c0: e.tensor_tensor(kr[:, h, c0:c0 + CH], ta[:, h, :], tb[:, h, :], ALU.add),
                     reads=["ta", "tb"], writes=["kr"])
        qa = sb(nc, st, "qa", [64, 8, 128], F32)
        qb = sb(nc, st, "qb", [64, 8, 128], F32)
        qc = sb(nc, st, "qc", [64, 128], F32)
        qd = sb(nc, st, "qd", [64, 128], F32)
        qr = sb(nc, st, "qr", [64, 8, 128], BF16)
        pT = [sb(nc, st, "pT%d" % i, [128, 4, 128], BF16) for i in range(5)]
        dn = sb(nc, st, "dn", [64, 4, 128], F32)
        ob = sb(nc, st, "ob", [64, 8, 128], F32)
        psS = [ps(nc, st, "psS%d" % i, [128, 512]) for i in range(3)]
        psN = [ps(nc, st, "psN%d" % i, [64, 512]) for i in range(2)]
        psD = [ps(nc, st, "psD%d" % i, [64, 512]) for i in range(2)]
        si = 0
        nblk = TLAT // 128
        for n in range(nblk + 1):
            t0 = n * 128
            nq = 128 if n < nblk else TCTX
            P.dma("sp", qa[:, :, :nq], qf[:, :, t0:t0 + nq], writes=["qa"])
            P.dma("sp", qb[:, :, :nq], qs[:, :, t0:t0 + nq], writes=["qb"])
            P.dma("sp", qc[:, :nq], cq[:, t0:t0 + nq], writes=["qc"])
            P.dma("sp", qd[:, :nq], sq[:, t0:t0 + nq], writes=["qd"])
            for h in range(8):
                P.op("dve", lambda e, h=h: e.tensor_tensor(qa[:, h, :nq], qa[:, h, :nq], qc[:, :nq], ALU.mult), reads=["qa", "qc"], writes=["qa"])
                P.op("pool", lambda e, h=h: e.tensor_tensor(qb[:, h, :nq], qb[:, h, :nq], qd[:, :nq], ALU.mult), reads=["qb", "qd"], writes=["qb"])
                P.op("dve", lambda e, h=h: e.tensor_tensor(qr[:, h, :nq], qa[:, h, :nq], qb[:, h, :nq], ALU.add),
                     reads=["qa", "qb"], writes=[("qr", h // 4)])
            if n < nblk:
                kbs = [(n, 0 if n == 0 else 1), (n + 1, None), (n + 2, 3 if n == nblk - 1 else 2), (NKB - 2, None), (NKB - 1, None)]
            else:
                kbs = [(NKB - 2, None), (NKB - 1, None)]
            for kvh in range(2):
                pb = kvh
                wide = (nq == 128)
                for i, (kb, mi) in enumerate(kbs):
                    sbk = si % 3
                    si += 1
                    if wide:
                        P.op("pe", lambda e, kb=kb, sbk=sbk, kvh=kvh: e.matmul(psS[sbk][:, :512], kr[:, kvh, kb * 128:(kb + 1) * 128],
                                                                              qr[:, kvh * 4:(kvh + 1) * 4, :], start=True, stop=True),
                             reads=["kr", ("qr", kvh)], writes=[("psS", sbk)])
                        P.op("act", lambda e, i=i, sbk=sbk: e.activation(out=pT[i][:], in_=psS[sbk][:, :512], func=AF.Exp, scale=0.125),
                             reads=[("psS", sbk)], writes=[("pT", i)])
                    else:
                        for g in range(4):
                            P.op("pe", lambda e, kb=kb, g=g, sbk=sbk, kvh=kvh: e.matmul(psS[sbk][:, g * nq:(g + 1) * nq],
                                                                                       kr[:, kvh, kb * 128:(kb + 1) * 128],
                                                                                       qr[:, kvh * 4 + g, :nq], start=True, stop=True),
                                 reads=["kr", ("qr", kvh)], writes=[("psS", sbk)], sig=(g == 3))
                        for g in range(4):
                            P.op("act", lambda e, g=g, i=i, sbk=sbk: e.activation(out=pT[i][:, g, :nq], in_=psS[sbk][:, g * nq:(g + 1) * nq],
                                                                                  func=AF.Exp, scale=0.125),
                                 reads=[("psS", sbk)], writes=[("pT", i)])
                    if mi is not None:
                        P.op("pool", lambda e, i=i, mi=mi: e.tensor_tensor(pT[i][:, :, :nq], pT[i][:, :, :nq], mb[:, mi, :, :nq], ALU.mult),
                             reads=[("pT", i), "mb"], writes=[("pT", i)])
                if wide:
                    for i, (kb, mi) in enumerate(kbs):
                        P.op("pe", lambda e, i=i, kb=kb, kvh=kvh, pb=pb: e.matmul(psN[pb][:, :512], vb[:, kb, kvh * 64:(kvh + 1) * 64], pT[i][:],
                                                                                 start=(i == 0), stop=(i == len(kbs) - 1)),
                             reads=[("pT", i), "vb"], writes=[("psN", pb)], sig=(i == len(kbs) - 1))
                    for i, (kb, mi) in enumerate(kbs):
                        P.op("pe", lambda e, i=i, pb=pb: e.matmul(psD[pb][:, :512], ones[:], pT[i][:], start=(i == 0), stop=(i == len(kbs) - 1)),
                             reads=[("pT", i), "ones"], writes=[("psD", pb)], sig=(i == len(kbs) - 1))
                else:
                    for g in range(4):
                        for i, (kb, mi) in enumerate(kbs):
                            P.op("pe", lambda e, i=i, kb=kb, g=g, kvh=kvh, pb=pb: e.matmul(psN[pb][:, g * nq:(g + 1) * nq],
                                                                                          vb[:, kb, kvh * 64:(kvh + 1) * 64], pT[i][:, g, :nq],
                                                                                          start=(i == 0), stop=(i == len(kbs) - 1)),
                                 reads=[("pT", i), "vb"], writes=[("psN", pb)], sig=(g == 3 and i == len(kbs) - 1))
                    for g in range(4):
                        for i, (kb, mi) in enumerate(kbs):
                            P.op("pe", lambda e, i=i, g=g, pb=pb: e.matmul(psD[pb][:, g * nq:(g + 1) * nq], ones[:], pT[i][:, g, :nq],
                                                                          start=(i == 0), stop=(i == len(kbs) - 1)),
                                 reads=[("pT", i), "ones"], writes=[("psD", pb)], sig=(g == 3 and i == len(kbs) - 1))
                for g in range(4):
                    h = kvh * 4 + g
                    P.op("dve", lambda e, g=g, h=h, pb=pb: e.tensor_scalar(dn[:, g, :nq], psD[pb][:, g * nq:(g + 1) * nq], es[:, h:h + 1], None, ALU.add),
                         reads=[("psD", pb), "es"], writes=["dn"])
                    P.op("dve", lambda e, g=g: e.reciprocal(dn[:, g, :nq], dn[:, g, :nq]), reads=["dn"], writes=["dn"])
                    P.op("dve", lambda e, g=g, h=h, pb=pb: e.tensor_tensor(ob[:, h, :nq], psN[pb][:, g * nq:(g + 1) * nq], dn[:, g, :nq], ALU.mult),
                         reads=[("psN", pb), "dn"], writes=["ob"])
            P.dma("sp", yo[:, :, t0:t0 + nq], ob[:, :, :nq], reads=["ob"], writes=["yo"])
        P.finish("sp")
    return nc


def rope_tables(pos):
    pos = np.asarray(pos)
    valid = pos >= 0
    p = np.where(valid, pos, 0)
    row = (p // 64).astype(np.float32)
    col = (p % 64).astype(np.float32)
    inv = (np.float32(10000.0) ** (-np.arange(16, dtype=np.float32) * np.float32(2.0) / np.float32(32))).astype(np.float32)
    cos = np.ones((64, len(pos)), np.float32)
    sin = np.zeros((64, len(pos)), np.float32)
    for a, base in ((0, row), (1, col)):
        ang = (base[None, :] * inv[:, None]).astype(np.float32)
        for s in range(2):
            sl = slice(a * 32 + s * 16, a * 32 + s * 16 + 16)
            cos[sl] = np.cos(ang)
            sin[sl] = np.sin(ang) * (-1.0 if s == 0 else 1.0)
    cos[:, ~valid] = 1.0
    sin[:, ~valid] = 0.0
    return cos, sin


OFF_B, OFF_Q, OFF_K, OFF_V, OFF_G = 2560, 4096, 4608, 4736, 4864
OFF_QS, OFF_KS = 7936, 8448


def swap_cols(w):
    sh = w.shape
    return w.reshape(sh[:-1] + (-1, 2, 2, 16))[..., ::-1, :].reshape(sh)


def run_attn(Pl, Pc, sink):
    nc = _get("attn", build_attn)
    qi = np.arange(128)
    m_prev = (qi[:, None] >= qi[None, :]).astype(np.float32)
    m_next = (qi[:, None] <= qi[None, :]).astype(np.float32)
    maps = []
    for i in range(NCORE):
        b, q = core_bq(i)
        t0 = q * TLAT
        lat = Pl[b, t0:t0 + TLAT]
        cx = Pc[b, q * TCTX:(q + 1) * TCTX]

        def heads(a, n):
            return np.ascontiguousarray(a.reshape(a.shape[0], n, 64).transpose(2, 1, 0))
        qall = np.concatenate([lat[:, OFF_Q:OFF_K], cx[:, OFF_Q:OFF_K]], 0)
        qsall = np.concatenate([lat[:, OFF_QS:OFF_KS], cx[:, OFF_QS:OFF_KS]], 0)
        qpos = np.concatenate([np.arange(t0, t0 + TLAT), -np.ones(TCTX, np.int64)])
        cqt, sqt = rope_tables(qpos)
        kpos = np.arange(t0 - 128, t0 + TLAT + 128)
        kval = (kpos >= 0) & (kpos < SEQ)
        kidx = np.clip(kpos, 0, SEQ - 1)
        kl = Pl[b, kidx] * kval[:, None]
        kall = np.concatenate([kl[:, OFF_K:OFF_V], Pc[b][:, OFF_K:OFF_V]], 0)
        ksall = np.concatenate([kl[:, OFF_KS:NQK], Pc[b][:, OFF_KS:NQK]], 0)
        vall = np.concatenate([kl[:, OFF_V:OFF_G], Pc[b][:, OFF_V:OFF_G]], 0)
        ckt, skt = rope_tables(np.concatenate([np.where(kval, kpos, -1), -np.ones(CTX, np.int64)]))
        m0 = m_prev if q > 0 else np.zeros_like(m_prev)
        m3 = m_next if q < 3 else np.zeros_like(m_next)
        mk = np.stack([m0, m_prev, m_next, m3], 0)
        mk = np.ascontiguousarray(np.broadcast_to(mk.transpose(1, 0, 2)[:, :, None, :], (128, 4, 4, 128))).astype(np.float32)
        maps.append({
            "qf": heads(qall, 8), "qs": heads(qsall, 8), "cq": cqt, "sq": sqt,
            "kf": heads(kall, 2), "ks": heads(ksall, 2), "ck": ckt, "sk": skt,
            "vt": np.ascontiguousarray(vall.reshape(NKB, 128, 128).transpose(1, 0, 2)),
            "mk": mk, "sink": np.ascontiguousarray(np.broadcast_to(sink[None, :], (64, 8))).astype(np.float32),
        })
    res = _run(nc, maps)
    yl = np.empty((2, SEQ, 512), np.float32)
    yc = np.empty((2, CTX, 512), np.float32)
    for i in range(NCORE):
        b, q = core_bq(i)
        y = res[i]["yo"].transpose(2, 1, 0).reshape(NQ, 512)
        yl[b, q * TLAT:(q + 1) * TLAT] = y[:TLAT]
        yc[b, q * TCTX:(q + 1) * TCTX] = y[TLAT:]
    return yl, yc


NTOK = CTX + SEQ
NG = NTOK // 128


def build_hgrn():
    nc = bass.Bass("TRN2", target_bir_lowering=False)
    zd = [nc.dram_tensor("z%d" % d, [128, NG, 128], F32, kind="ExternalInput").ap() for d in range(2)]
    vd = nc.dram_tensor("v", [128, NG, 128], F32, kind="ExternalInput").ap()
    qd = nc.dram_tensor("q", [128, NTOK], F32, kind="ExternalInput").ap()
    gd = nc.dram_tensor("g", [128, NTOK], F32, kind="ExternalInput").ap()
    lbd = nc.dram_tensor("lb", [128, 2, 2, 128], F32, kind="ExternalInput").ap()
    fld = nc.dram_tensor("flag", [128, 1], F32, kind="ExternalInput").ap()
    nwd = nc.dram_tensor("nw", [128, 1], F32, kind="ExternalInput").ap()
    cfd = nc.dram_tensor("cf", [128, 5, 128], F32, kind="ExternalInput").ap()
    cmd = nc.dram_tensor("cm", [128, 4], F32, kind="ExternalInput").ap()
    yo = nc.dram_tensor("yo", [128, NTOK], F32, kind="ExternalOutput").ap()
    GB = 4
    with ExitStack() as st:
        P = Prog(nc, st)
        vb = sb(nc, st, "vb", [128, NG, 128], BF16)
        P.dma("pool", vb[:], vd, writes=["vb"])
        cf = sb(nc, st, "cfs", [128, 5, 128], F32)
        P.dma("sp", cf[:], cfd, writes=["cf"])
        bm = sb(nc, st, "bm", [128, 2, GB, 128], BF16)
        for i in range(GB):
            P.dma("pool", bm[:, :, i, :], cfd[:, 0:2, :], writes=["bm"])
        cm = sb(nc, st, "cms", [128, 4], F32)
        P.dma("sp", cm[:], cmd, writes=["cm"])
        nw = sb(nc, st, "nws", [128, 1], F32)
        P.dma("sp", nw[:], nwd, writes=["nw"])
        fl = sb(nc, st, "fls", [128, 1], F32)
        P.dma("sp", fl[:], fld, writes=["fl"])
        lbt = sb(nc, st, "lbs", [128, 2, 2, 128], F32)
        P.dma("sp", lbt[:], lbd, writes=["lbt"])
        LBb = sb(nc, st, "LBb", [128, 2, GB, 128], F32)
        LBa = sb(nc, st, "LBa", [128, 2, GB, 128], F32)
        for i in range(GB):
            P.op("dve", lambda e, i=i: e.tensor_tensor(LBb[:, :, i, :], lbt[:, 1], lbt[:, 0], ALU.subtract), reads=["lbt"], writes=["LBb"])
        P.op("act", lambda e: e.activation(out=LBb[:], in_=LBb[:], func=AF.Sigmoid), reads=["LBb"], writes=["LBb"])
        P.op("dve", lambda e: e.tensor_scalar(LBb[:], LBb[:], fl[:, 0:1], None, ALU.mult), reads=["LBb", "fl"], writes=["LBb"])
        P.op("dve", lambda e: e.tensor_scalar(LBa[:], LBb[:], -1.0, 1.0, ALU.mult, ALU.add), reads=["LBb"], writes=["LBa"])
        ones = sb(nc, st, "ones", [128, 128], F32)
        P.op("pool", lambda e: e.memset(ones[:], 1.0), writes=["ones"])
        osum = sb(nc, st, "osum", [128, NTOK], F32)
        S = sb(nc, st, "S", [128, 128], F32)
        NSLOT = 8
        Sbr = sb(nc, st, "Sbr", [128, NSLOT, 128], BF16)
        prev_slot = [0]
        W = GB * 128
        zt = [sb(nc, st, "zt%d" % i, [128, W], F32) for i in range(2)]
        qt32 = [sb(nc, st, "qt32%d" % i, [128, W], F32) for i in range(2)]
        ft = [sb(nc, st, "ft%d" % i, [128, W], F32) for i in range(2)]
        lf = [sb(nc, st, "lf%d" % i, [128, W], F32) for i in range(2)]
        kk = [sb(nc, st, "kk%d" % i, [128, W], F32) for i in range(2)]
        Eq = [sb(nc, st, "Eq%d" % i, [128, W], F32) for i in range(2)]
        Ek = [sb(nc, st, "Ek%d" % i, [128, W], F32) for i in range(2)]
        Er = [sb(nc, st, "Er%d" % i, [128, W], F32) for i in range(2)]
        qt = [sb(nc, st, "qt%d" % i, [128, W], BF16) for i in range(2)]
        kt = [sb(nc, st, "kt%d" % i, [128, W], BF16) for i in range(2)]
        k4 = [sb(nc, st, "k4%d" % i, [128, GB, 4, 128], BF16) for i in range(2)]
        am = [sb(nc, st, "am%d" % i, [128, W], BF16) for i in range(2)]
        pB = ps(nc, st, "pB", [128, 512])
        pR = ps(nc, st, "pR", [128, 512])
        pKA = ps(nc, st, "pKA", [128, 512])
        pO = ps(nc, st, "pO", [128, 512])
        pU = ps(nc, st, "pU", [128, GB, 4, 128])

        def phase1(d, batch, b):
            g0 = min(batch)
            nb = len(batch)
            w = nb * 128
            P.dma("sp", zt[b][:, :w], zd[d][:, g0:g0 + nb, :], writes=[("zt", b)])
            P.dma("sp", qt32[b][:, :w], qd[:, g0 * 128:(g0 + nb) * 128], writes=[("qt32", b)])
            P.op("act", lambda e: e.activation(out=zt[b][:, :w], in_=zt[b][:, :w], func=AF.Sigmoid), reads=[("zt", b)], writes=[("zt", b)])
            P.op("dve", lambda e: e.tensor_tensor(ft[b][:, :w], zt[b][:, :w], LBa[:, d, :nb, :], ALU.mult), reads=[("zt", b), "LBa"], writes=[("ft", b)])
            P.op("dve", lambda e: e.tensor_tensor(ft[b][:, :w], ft[b][:, :w], LBb[:, d, :nb, :], ALU.add), reads=[("ft", b), "LBb"], writes=[("ft", b)])
            P.op("act", lambda e: e.activation(out=lf[b][:, :w], in_=ft[b][:, :w], func=AF.Ln), reads=[("ft", b)], writes=[("lf", b)])
            P.op("pool", lambda e: e.tensor_scalar(kk[b][:, :w], ft[b][:, :w], -1.0, 1.0, ALU.mult, ALU.add), reads=[("ft", b)], writes=[("kk", b)])
            for i in range(nb):
                sl = slice(i * 128, (i + 1) * 128)
                P.op("pe", lambda e, sl=sl: e.matmul(pB[:, sl], lf[b][:, sl], cf[:, d, :], start=True, stop=True), reads=[("lf", b), "cf"], writes=["pB"], sig=(i == nb - 1))
            for i in range(nb):
                sl = slice(i * 128, (i + 1) * 128)
                P.op("pe", lambda e, sl=sl: e.matmul(pR[:, sl], cf[:, 2 + d, :], lf[b][:, sl], start=True, stop=True), reads=[("lf", b), "cf"], writes=["pR"], sig=(i == nb - 1))
            for i in range(nb):
                sl = slice(i * 128, (i + 1) * 128)
                P.op("pe", lambda e, sl=sl: e.matmul(pKA[:, sl], kk[b][:, sl], cf[:, 4, :], start=True, stop=True), reads=[("kk", b), "cf"], writes=["pKA"], sig=(i == nb - 1))
            P.op("act", lambda e: e.activation(out=Eq[b][:, :w], in_=pB[:, :w], func=AF.Exp), reads=["pB"], writes=[("Eq", b)])
            P.op("act", lambda e: e.activation(out=Ek[b][:, :w], in_=pB[:, :w], func=AF.Exp, scale=-1.0), reads=["pB"], writes=[("Ek", b)])
            P.op("act", lambda e: e.activation(out=Er[b][:, :w], in_=pR[:, :w], func=AF.Exp), reads=["pR"], writes=[("Er", b)])
            P.op("dve", lambda e: e.tensor_tensor(qt[b][:, :w], qt32[b][:, :w], Eq[b][:, :w], ALU.mult), reads=[("qt32", b), ("Eq", b)], writes=[("qt", b)])
            P.op("dve", lambda e: e.tensor_tensor(kt[b][:, :w], pKA[:, :w], Ek[b][:, :w], ALU.mult), reads=["pKA", ("Ek", b)], writes=[("kt", b)])
            P.op("pool", lambda e: e.tensor_tensor(Er[b][:, :w], kk[b][:, :w], Er[b][:, :w], ALU.mult), reads=[("kk", b), ("Er", b)], writes=[("Er", b)])
            for c in range(4):
                P.op("pool", lambda e, c=c: e.tensor_scalar(k4[b][:, :nb, c, :], Er[b][:, :w], cm[:, c:c + 1], None, ALU.mult),
                     reads=[("Er", b), "cm"], writes=[("k4", b)])
            for i in range(nb):
                sl = slice(i * 128, (i + 1) * 128)
                P.op("pe", lambda e, sl=sl: e.matmul(pKA[:, sl], kt[b][:, sl], qt[b][:, sl], start=True, stop=True), reads=[("kt", b), ("qt", b)], writes=["pKA"], sig=(i == nb - 1))
            P.op("dve", lambda e: e.tensor_tensor(am[b][:, :w], pKA[:, :w], bm[:, d, :nb, :], ALU.mult), reads=["pKA", "bm"], writes=[("am", b)])

        def scan(d, batch, b):
            g0 = min(batch)
            nb = len(batch)
            chunks = [0, 1, 2, 3] if d == 0 else [3, 2, 1, 0]
            for G in batch:
                i = G - g0
                for ci, c in enumerate(chunks):
                    P.op("pe", lambda e, i=i, c=c, G=G: e.matmul(pU[:, i, c, :], k4[b][:, i, c, :], vb[:, G, :], start=True, stop=True),
                         reads=[("k4", b), "vb"], writes=["pU"], sig=(G == batch[-1] and ci == 3))
            for G in batch:
                i = G - g0
                P.op("pe", lambda e, i=i, G=G: e.matmul(pO[:, i * 128:(i + 1) * 128], vb[:, G, :], am[b][:, i * 128:(i + 1) * 128], start=True, stop=False),
                     reads=["vb", ("am", b)], writes=["pO"], sig=False)
                for ci, c in enumerate(chunks):
                    col = i * 128 + c * 32
                    P.op("pe", lambda e, col=col, ci=ci, ps_=prev_slot[0]: e.matmul(pO[:, col:col + 32], Sbr[:, ps_, :], qt[b][:, col:col + 32],
                                                                                 start=False, stop=(ci == 3)),
                         reads=[("Sb", prev_slot[0]), ("qt", b)], writes=["pO"], sig=(ci == 3))
                    ns = (prev_slot[0] + 1) % NSLOT
                    dcol = col + (31 if d == 0 else 0)
                    P.op("dve", lambda e, i=i, c=c, ns=ns, dcol=dcol: e.scalar_tensor_tensor(Sbr[:, ns, :], S[:], Eq[b][:, dcol:dcol + 1], pU[:, i, c, :], ALU.mult, ALU.add),
                         reads=["S", ("Eq", b), "pU"], writes=[("Sb", ns)])
                    P.op("dve", lambda e, i=i, c=c, dcol=dcol: e.scalar_tensor_tensor(S[:], S[:], Eq[b][:, dcol:dcol + 1], pU[:, i, c, :], ALU.mult, ALU.add),
                         reads=["S", ("Eq", b), "pU"], writes=["S"])
                    prev_slot[0] = ns
            w = nb * 128
            ok = [("osum", G) for G in batch]
            if d == 0:
                P.op("act", lambda e: e.copy(out=osum[:, g0 * 128:g0 * 128 + w], in_=pO[:, :w]), reads=["pO"], writes=ok)
            else:
                P.op("dve", lambda e: e.tensor_tensor(osum[:, g0 * 128:g0 * 128 + w], osum[:, g0 * 128:g0 * 128 + w], pO[:, :w], ALU.add),
                     reads=["pO"] + ok, writes=ok)

        it = 0
        for d in range(2):
            P.op("pool", lambda e: e.memset(S[:], 0.0), writes=["S"])
            prev_slot[0] = (prev_slot[0] + 1) % NSLOT
            P.op("pool", lambda e, ps_=prev_slot[0]: e.memset(Sbr[:, ps_, :], 0.0), writes=[("Sb", prev_slot[0])])
            if d == 0:
                order = list(range(NG))
                batches = [order[i:i + GB] for i in range(0, NG, GB)]
            else:
                rest = list(range(NG - 1, 1, -1))
                batches = [[1, 0]] + [rest[i:i + GB] for i in range(0, len(rest), GB)]
            phase1(d, batches[0], it % 2)
            for k, batch in enumerate(batches):
                b = it % 2
                it += 1
                if k + 1 < len(batches):
                    phase1(d, batches[k + 1], it % 2)
                scan(d, batch, b)
        sq = sb(nc, st, "sq", [128, 512], F32)
        gt = sb(nc, st, "gt", [128, 512], F32)
        rs = sb(nc, st, "rs", [128, 512], F32)
        yt = [sb(nc, st, "yt%d" % i, [128, 512], F32) for i in range(2)]
        ri = 0
        for t0 in range(0, NTOK, 512):
            tn = min(512, NTOK - t0)
            b = ri % 2
            ri += 1
            rk = [("osum", G) for G in range(t0 // 128, (t0 + tn) // 128)]
            P.dma("sp", gt[:, :tn], gd[:, t0:t0 + tn], writes=["gt"])
            P.op("act", lambda e, t0=t0, tn=tn: e.activation(out=sq[:, :tn], in_=osum[:, t0:t0 + tn], func=AF.Square), reads=rk, writes=["sq"])
            P.op("pe", lambda e, tn=tn: e.matmul(pB[:, :tn], ones[:], sq[:, :tn], start=True, stop=True), reads=["sq", "ones"], writes=["pB"])
            P.op("dve", lambda e, tn=tn: e.tensor_scalar(rs[:, :tn], pB[:, :tn], 1.0 / 128, RMS_EPS, ALU.mult, ALU.add), reads=["pB"], writes=["rs"])
            P.op("act", lambda e, tn=tn: e.sqrt(out=rs[:, :tn], in_=rs[:, :tn]), reads=["rs"], writes=["rs"])
            P.op("dve", lambda e, tn=tn: e.reciprocal(rs[:, :tn], rs[:, :tn]), reads=["rs"], writes=["rs"])
            P.op("act", lambda e, tn=tn: e.activation(out=gt[:, :tn], in_=gt[:, :tn], func=AF.Silu), reads=["gt"], writes=["gt"])
            P.op("dve", lambda e, b=b, t0=t0, tn=tn: e.tensor_tensor(yt[b][:, :tn], osum[:, t0:t0 + tn], rs[:, :tn], ALU.mult), reads=rk + ["rs"], writes=[("yt", b)])
            P.op("pool", lambda e, b=b, tn=tn: e.tensor_tensor(yt[b][:, :tn], yt[b][:, :tn], gt[:, :tn], ALU.mult), reads=[("yt", b), "gt"], writes=[("yt", b)])
            P.op("act", lambda e, b=b, tn=tn: e.activation(out=yt[b][:, :tn], in_=yt[b][:, :tn], func=AF.Copy, scale=nw[:, 0:1]), reads=[("yt", b), "nw"], writes=[("yt", b)])
            P.dma("sp", yo[:, t0:t0 + tn], yt[b][:, :tn], reads=[("yt", b)], writes=["yo"])
        P.finish("sp")
    return nc


def run_hgrn(Pl, Pc, layer, hgrn_lb, norm_w):
    nc = _get("hgrn", build_hgrn)
    p = np.arange(128)
    same = (p[:, None] // 32) == (p[None, :] // 32)
    incl_f = (same & (p[:, None] <= p[None, :])).astype(np.float32)
    incl_b = (same & (p[:, None] >= p[None, :])).astype(np.float32)
    rem_f = (same & (p[:, None] > p[None, :])).astype(np.float32)
    rem_b = (same & (p[:, None] < p[None, :])).astype(np.float32)
    cf = np.ascontiguousarray(np.stack([incl_f, incl_b, rem_f, rem_b, np.eye(128, dtype=np.float32)], 1))
    cm = (p[:, None] // 32 == np.arange(4)[None, :]).astype(np.float32)
    maps = []
    for i in range(NCORE):
        b, hd = core_bq(i)
        seq = np.concatenate([Pc[b], Pl[b]], 0)
        cs = slice(hd * 128, (hd + 1) * 128)

        def tm(a):
            return np.ascontiguousarray(a.reshape(NG, 128, 128).transpose(1, 0, 2))
        lb = np.broadcast_to(hgrn_lb[:, :, cs][None], (128, 2, 2, 128))
        maps.append({
            "q": np.ascontiguousarray(seq[:, 0:512][:, cs].T), "v": tm(seq[:, 512:1024][:, cs]),
            "g": np.ascontiguousarray(seq[:, 1024:1536][:, cs].T),
            "z0": tm(seq[:, 1536:2048][:, cs]), "z1": tm(seq[:, 2048:2560][:, cs]),
            "lb": np.ascontiguousarray(lb).astype(np.float32),
            "flag": np.full((128, 1), float(layer), np.float32),
            "nw": np.ascontiguousarray(norm_w[cs][:, None]).astype(np.float32), "cf": cf, "cm": cm,
        })
    res = _run(nc, maps)
    yl = np.empty((2, SEQ, 512), np.float32)
    yc = np.empty((2, CTX, 512), np.float32)
    for i in range(NCORE):
        b, hd = core_bq(i)
        y = res[i]["yo"].T
        yc[b, :, hd * 128:(hd + 1) * 128] = y[:CTX]
        yl[b, :, hd * 128:(hd + 1) * 128] = y[CTX:]
    return yl, yc


PI = float(np.pi)


def build_hyena(L):
    nc = bass.Bass("TRN2", target_bir_lowering=False)
    PP = min(L, 1024)
    pbd = [nc.dram_tensor("pb%d" % i, [3, 128, L], F32, kind="ExternalInput").ap() for i in range(3)]
    cwd = nc.dram_tensor("cw", [128, 12], F32, kind="ExternalInput").ap()
    bsd = nc.dram_tensor("bias", [128, 2], F32, kind="ExternalInput").ap()
    ftd = nc.dram_tensor("feats", [33, L], F32, kind="ExternalInput").ap()
    dcd = nc.dram_tensor("decay", [128, L], F32, kind="ExternalInput").ap()
    w1d = nc.dram_tensor("w1", [33, 64], F32, kind="ExternalInput").ap()
    w2d = nc.dram_tensor("w2", [64, 64], F32, kind="ExternalInput").ap()
    w3d = nc.dram_tensor("w3", [64, 4, 128], F32, kind="ExternalInput").ap()
    bfd = nc.dram_tensor("bf", [64, 4], F32, kind="ExternalInput").ap()
    yo = nc.dram_tensor("yo", [128, L], F32, kind="ExternalOutput").ap()
    with ExitStack() as st:
        P = Prog(nc, st)
        cw = sb(nc, st, "cws", [128, 12], F32)
        bs = sb(nc, st, "bss", [128, 2], F32)
        w1 = sb(nc, st, "w1s", [33, 64], F32)
        w2 = sb(nc, st, "w2s", [64, 64], F32)
        w3 = sb(nc, st, "w3s", [64, 4, 128], F32)
        bf = sb(nc, st, "bfs", [64, 4], F32)
        for t, dd in ((cw, cwd), (bs, bsd), (w1, w1d), (w2, w2d), (w3, w3d), (bf, bfd)):
            P.dma("sp", t[:], dd, writes=["consts"])
        z = sb(nc, st, "z", [128, L], F32)
        y = sb(nc, st, "y", [128, L], F32)
        ta = sb(nc, st, "ta", [128, PP], F32)
        tb = sb(nc, st, "tb", [128, PP], F32)
        tcx = sb(nc, st, "tcx", [128, PP], F32)
        xs = sb(nc, st, "xs", [128, PP], F32)
        ft = sb(nc, st, "ft", [33, PP], F32)
        h1 = sb(nc, st, "h1", [64, PP], F32)
        h2 = sb(nc, st, "h2", [64, PP], F32)
        dc = sb(nc, st, "dc", [128, PP], F32)
        hp = [sb(nc, st, "hp%d" % i, [128, PP], F32) for i in range(2)]
        l1 = sb(nc, st, "l1", [128, 4], F32)
        ki = sb(nc, st, "ki", [64, PP], mybir.dt.int32)
        kf = sb(nc, st, "kf", [64, PP], F32)
        pm = ps(nc, st, "pm", [64, 512])
        ph = [ps(nc, st, "ph%d" % i, [128, 512]) for i in range(2)]

        def sconv(part, dst, dkey, t0, tn):
            P.dma("sp", ta[:, :tn], pbd[0][part, :, t0:t0 + tn], writes=["ta"])
            P.dma("sp", tb[:, :tn], pbd[1][part, :, t0:t0 + tn], writes=["tb"])
            P.dma("sp", tcx[:, :tn], pbd[2][part, :, t0:t0 + tn], writes=["tcx"])
            c0 = part * 3
            P.op("dve", lambda e: e.tensor_scalar(ta[:, :tn], ta[:, :tn], cw[:, c0:c0 + 1], cw[:, 9 + part:10 + part], ALU.mult, ALU.add),
                 reads=["ta", "consts"], writes=["ta"])
            P.op("dve", lambda e: e.scalar_tensor_tensor(tb[:, :tn], tb[:, :tn], cw[:, c0 + 1:c0 + 2], ta[:, :tn], ALU.mult, ALU.add),
                 reads=["ta", "tb", "consts"], writes=["tb"])
            P.op("dve", lambda e: e.scalar_tensor_tensor(dst, tcx[:, :tn], cw[:, c0 + 2:c0 + 3], tb[:, :tn], ALU.mult, ALU.add),
                 reads=["tb", "tcx", "consts"], writes=[dkey])

        for t0 in range(0, L, PP):
            sconv(0, z[:, t0:t0 + PP], "z", t0, PP)
        for o in range(2):
            P.op("pool", lambda e: e.memset(y[:], 0.0), writes=["y"])
            P.op("pool", lambda e: e.memset(l1[:], 0.0), writes=["l1"])
            for p0 in range(0, L, PP):
                P.dma("sp", ft[:], ftd[:, p0:p0 + PP], writes=["ft"])
                P.dma("sp", dc[:], dcd[:, p0:p0 + PP], writes=["dc"])
                for (src, skey, w, K_, dst, dkey, bi) in ((ft, "ft", w1, 33, h1, "h1", 0), (h1, "h1", w2, 64, h2, "h2", 2)):
                    for q0 in range(0, PP, 512):
                        qn = min(512, PP - q0)
                        P.op("pe", lambda e, q0=q0, qn=qn, src=src, w=w, K_=K_: e.matmul(pm[:, :qn], w[:K_, :], src[:K_, q0:q0 + qn], start=True, stop=True),
                             reads=[skey, "consts"], writes=["pm"])
                        P.op("dve", lambda e, q0=q0, qn=qn, dst=dst, bi=bi: e.tensor_scalar(dst[:, q0:q0 + qn], pm[:, :qn], bf[:, bi:bi + 1], bf[:, bi + 1:bi + 2], ALU.add, ALU.mult),
                             reads=["pm", "consts"], writes=[dkey])
                    P.op("dve", lambda e, dst=dst: e.tensor_scalar(dst[:], dst[:], 1.0 / (2.0 * PI), 8.5, ALU.mult, ALU.add), reads=[dkey], writes=[dkey])
                    P.op("dve", lambda e, dst=dst: e.tensor_copy(ki[:], dst[:]), reads=[dkey], writes=["ki"])
                    P.op("dve", lambda e, dst=dst: e.tensor_copy(kf[:], ki[:]), reads=["ki"], writes=["kf"])
                    P.op("dve", lambda e, dst=dst: e.tensor_tensor(dst[:], dst[:], kf[:], ALU.subtract), reads=[dkey, "kf"], writes=[dkey])
                    P.op("dve", lambda e, dst=dst: e.tensor_single_scalar(kf[:], dst[:], 0.0, ALU.is_lt), reads=[dkey], writes=["kf"])
                    P.op("dve", lambda e, dst=dst: e.tensor_tensor(dst[:], dst[:], kf[:], ALU.add), reads=[dkey, "kf"], writes=[dkey])
                    P.op("dve", lambda e, dst=dst: e.tensor_scalar(dst[:], dst[:], 2.0 * PI, -PI, ALU.mult, ALU.add), reads=[dkey], writes=[dkey])
                    P.op("act", lambda e, dst=dst: e.activation(out=dst[:], in_=dst[:], func=AF.Sin), reads=[dkey], writes=[dkey])
                for dr in range(2):
                    for q0 in range(0, PP, 512):
                        qn = min(512, PP - q0)
                        b = (q0 // 512) % 2
                        P.op("pe", lambda e, q0=q0, qn=qn, dr=dr, b=b: e.matmul(ph[b][:, :qn], w3[:, dr * 2 + o, :], h2[:, q0:q0 + qn], start=True, stop=True),
                             reads=["h2", "consts"], writes=[("ph", b)])
                        P.op("dve", lambda e, q0=q0, qn=qn, dr=dr, b=b: e.tensor_tensor(hp[dr][:, q0:q0 + qn], ph[b][:, :qn], dc[:, q0:q0 + qn], ALU.mult),
                             reads=[("ph", b), "dc"], writes=[("hp", dr)])
                    if dr == 1 and p0 == 0:
                        P.op("dve", lambda e: e.memset(hp[1][:, 0:1], 0.0), reads=[("hp", 1)], writes=[("hp", 1)])
                    P.op("dve", lambda e, dr=dr: e.tensor_reduce(out=l1[:, 2 + dr:3 + dr], in_=hp[dr][:], axis=AX.X, op=ALU.add, apply_absolute_value=True),
                         reads=[("hp", dr)], writes=["l1p"])
                    P.op("dve", lambda e, dr=dr: e.tensor_tensor(l1[:, 0:1], l1[:, 0:1], l1[:, 2 + dr:3 + dr], ALU.add), reads=["l1p", "l1"], writes=["l1"])
                for j in range(PP):
                    d = p0 + j
                    P.op("dve", lambda e, j=j, d=d: e.scalar_tensor_tensor(y[:, d:L], z[:, 0:L - d], hp[0][:, j:j + 1], y[:, d:L], ALU.mult, ALU.add),
                         reads=[("hp", 0), "z", "y"], writes=["y"])
                    if d >= 1:
                        P.op("dve", lambda e, j=j, d=d: e.scalar_tensor_tensor(y[:, 0:L - d], z[:, d:L], hp[1][:, j:j + 1], y[:, 0:L - d], ALU.mult, ALU.add),
                             reads=[("hp", 1), "z", "y"], writes=["y"])
            P.op("dve", lambda e: e.reciprocal(l1[:, 1:2], l1[:, 0:1]), reads=["l1"], writes=["l1"])
            for t0 in range(0, L, PP):
                sconv(1 + o, xs[:], "xs", t0, PP)
                P.op("dve", lambda e, t0=t0: e.tensor_scalar(y[:, t0:t0 + PP], y[:, t0:t0 + PP], l1[:, 1:2], None, ALU.mult), reads=["y", "l1"], writes=["y"])
                P.op("dve", lambda e, t0=t0: e.scalar_tensor_tensor(y[:, t0:t0 + PP], z[:, t0:t0 + PP], bs[:, o:o + 1], y[:, t0:t0 + PP], ALU.mult, ALU.add),
                     reads=["y", "z", "consts"], writes=["y"])
                P.op("dve", lambda e, t0=t0: e.tensor_tensor(z[:, t0:t0 + PP], y[:, t0:t0 + PP], xs[:], ALU.mult), reads=["y", "xs", "z"], writes=["z"])
        for t0 in range(0, L, PP):
            P.dma("sp", yo[:, t0:t0 + PP], z[:, t0:t0 + PP], reads=["z"], writes=["yo"])
        P.finish("sp")
    return nc


def run_hyena(X, L, cw, cb, w1, b1, f1, w2, b2, f2, w3, bias):
    nc = _get("hyena", build_hyena, L)
    pos = np.arange(L, dtype=np.float32)
    t = pos / np.float32(L - 1)
    wv = np.float32(2.0 * np.pi) * pos / np.float32(L)
    bands = np.linspace(1e-4, 15, 16, dtype=np.float32)
    feats = np.concatenate([t[:, None], np.cos(wv[:, None] * bands), -np.sin(wv[:, None] * bands)], -1).astype(np.float32)
    rates = np.abs(np.linspace(np.log(1e-2) / 1.5, np.log(1e-2) / 0.3, 512, dtype=np.float32))
    decay = np.exp(-t[None, :] * rates[:, None]).astype(np.float32)
    Xp = np.pad(X, ((0, 0), (1, 1), (0, 0)))
    maps = []
    for i in range(NCORE):
        cs = np.arange(i * 64, (i + 1) * 64)

        def rows(a):
            return np.ascontiguousarray(np.stack([a[:, :, p * 512 + cs].transpose(0, 2, 1).reshape(128, L) for p in range(3)], 0))
        cwt = np.concatenate([np.stack([cw[tp, p * 512 + cs] for p in range(3) for tp in range(3)], 1),
                              np.stack([cb[p * 512 + cs] for p in range(3)], 1)], 1)
        w3r = w3.reshape(64, 2, 2, 512)[:, :, :, cs]
        w3r = np.concatenate([w3r, w3r], -1).reshape(64, 4, 128)
        maps.append({
            "pb0": rows(Xp[:, 0:L]), "pb1": rows(Xp[:, 1:L + 1]), "pb2": rows(Xp[:, 2:L + 2]),
            "cw": np.ascontiguousarray(np.concatenate([cwt, cwt], 0)).astype(np.float32),
            "bias": np.ascontiguousarray(np.concatenate([bias[:, cs].T, bias[:, cs].T], 0)).astype(np.float32),
            "feats": np.ascontiguousarray(feats.T), "decay": np.ascontiguousarray(np.concatenate([decay[cs], decay[cs]], 0)),
            "w1": np.ascontiguousarray(w1), "w2": np.ascontiguousarray(w2), "w3": np.ascontiguousarray(w3r).astype(np.float32),
            "bf": np.ascontiguousarray(np.stack([b1, f1, b2, f2], 1)).astype(np.float32),
        })
    res = _run(nc, maps)
    out = np.empty((2, L, 512), np.float32)
    for i in range(NCORE):
        out[:, :, i * 64:(i + 1) * 64] = res[i]["yo"].reshape(2, 64, L).transpose(0, 2, 1)
    return out


def run_inproj(hc, mods_l, w_aug):
    nc = _get("inproj", build_inproj)
    maps = []
    for i in range(NCORE):
        b, q = core_bq(i)
        maps.append({"hin": hc[i], "w": w_aug, "tab": make_tab(mods_l, b, (3, 4, None), [])})
    return [r["pout"] for r in _run(nc, maps)]


def run_merge(hc, yin, gin, mods_l, bw, ow, g, bta):
    nc = _get("merge", build_merge)
    maps = []
    for i in range(NCORE):
        b, q = core_bq(i)
        maps.append({"hin": hc[i], "yin": yin[i], "gin": gin[i], "bw": bw, "ow": ow,
                     "tab": make_tab(mods_l, b, (None, None, 5), [g, bta])})
    return [r["hout"] for r in _run(nc, maps)]


def kernel(x, c, ctx, c_ctx, ada_w, ada_b, ln_g, ln_b, ffn_w_in, ffn_w_out, mix_w_in, hgrn_lb, hgrn_norm_w,
           hyena_conv_w, hyena_conv_b, hyena_w1, hyena_b1, hyena_f1, hyena_w2, hyena_b2, hyena_f2, hyena_w3,
           hyena_bias, attn_sink, branch_w, out_w):
    f = lambda a: np.ascontiguousarray(np.asarray(a, dtype=np.float32))
    (x, c, ctx, c_ctx, ada_w, ada_b, ln_g, ln_b, ffn_w_in, ffn_w_out, mix_w_in, hgrn_lb, hgrn_norm_w, hyena_conv_w,
     hyena_conv_b, hyena_w1, hyena_b1, hyena_f1, hyena_w2, hyena_b2, hyena_f2, hyena_w3, hyena_bias, attn_sink,
     branch_w, out_w) = map(f, (x, c, ctx, c_ctx, ada_w, ada_b, ln_g, ln_b, ffn_w_in, ffn_w_out, mix_w_in, hgrn_lb,
                                hgrn_norm_w, hyena_conv_w, hyena_conv_b, hyena_w1, hyena_b1, hyena_f1, hyena_w2,
                                hyena_b2, hyena_f2, hyena_w3, hyena_bias, attn_sink, branch_w, out_w))
    mods = run_mods(c, c_ctx, ada_w, ada_b)
    hl, hx = x, ctx
    for l in range(2):
        hc = to_cores(hl, hx)
        h1c = run_ffn(hc, mods[l], (0, 1, 2), ffn_w_in[l, 0], ffn_w_out[l, 0], ln_g[l, 0], ln_b[l, 0])
        w = mix_w_in[l]
        w_aug = np.ascontiguousarray(np.concatenate([w, swap_cols(w[:, OFF_Q:OFF_K]), swap_cols(w[:, OFF_K:OFF_V])], 1))
        Pl, Pc = from_cores(run_inproj(h1c, mods[l], w_aug))
        ya, yca = run_hgrn(Pl, Pc, l, hgrn_lb, hgrn_norm_w[l])
        hy = (hyena_conv_w[l], hyena_conv_b[l], hyena_w1[l], hyena_b1[l], hyena_f1[l], hyena_w2[l], hyena_b2[l],
              hyena_f2[l], hyena_w3[l], hyena_bias[l])
        yb = run_hyfft(Pl[..., OFF_B:OFF_Q], *hy)
        ycb = run_hyena(Pc[..., OFF_B:OFF_Q], CTX, *hy) if l == 0 else np.zeros((2, CTX, 512), np.float32)
        yc, ycc = run_attn(Pl, Pc, attn_sink[l])
        yin = to_cores(np.concatenate([ya, yb, yc], -1), np.concatenate([yca, ycb, ycc], -1))
        gin = to_cores(Pl[..., OFF_G:OFF_QS], Pc[..., OFF_G:OFF_QS])
        h2c = run_merge(h1c, yin, gin, mods[l], np.ascontiguousarray(branch_w[l].reshape(1536, D)), out_w[l], ln_g[l, 1], ln_b[l, 1])
        h3c = run_ffn(h2c, mods[l], (6, 7, 8), ffn_w_in[l, 1], ffn_w_out[l, 1], ln_g[l, 2], ln_b[l, 2])
        hl, hx = from_cores(h3c)
    return hl.astype(np.float32)


LH = SEQ
NF = 2 * LH


def build_hyfft(stage=3):
    L = LH
    nc = bass.Bass("TRN2", target_bir_lowering=False)
    PP = 1024
    pbd = [nc.dram_tensor("pb%d" % i, [3, 128, L], F32, kind="ExternalInput").ap() for i in range(3)]
    cwd = nc.dram_tensor("cw", [128, 12], F32, kind="ExternalInput").ap()
    bsd = nc.dram_tensor("bias", [128, 1], F32, kind="ExternalInput").ap()
    ftd = nc.dram_tensor("feats", [2, 33, L], F32, kind="ExternalInput").ap()
    dcd = nc.dram_tensor("decay", [2, 128, L], F32, kind="ExternalInput").ap()
    w1d = nc.dram_tensor("w1", [33, 64], F32, kind="ExternalInput").ap()
    w2d = nc.dram_tensor("w2", [64, 64], F32, kind="ExternalInput").ap()
    w3d = nc.dram_tensor("w3", [64, 2, 128], F32, kind="ExternalInput").ap()
    bfd = nc.dram_tensor("bf", [64, 4], F32, kind="ExternalInput").ap()
    fad = nc.dram_tensor("fa", [128, 2, 2, 512], F32, kind="ExternalInput").ap()
    twd = nc.dram_tensor("tw", [128, 2, 512], F32, kind="ExternalInput").ap()
    ggd = nc.dram_tensor("gg", [128, 3, 128], F32, kind="ExternalInput").ap()
    ryd = nc.dram_tensor("ry", [128, 2, 256], F32, kind="ExternalInput").ap()
    itd = nc.dram_tensor("it", [128, 2, 2, 256], F32, kind="ExternalInput").ap()
    fid = nc.dram_tensor("fi", [128, 2, 3, 128], F32, kind="ExternalInput").ap()
    yo = nc.dram_tensor("yo", [128, L], F32, kind="ExternalOutput").ap()
    scr = nc.dram_tensor("scr", [3, 128, L], F32).ap()
    z1d = nc.dram_tensor("z1d", [128, L], F32).ap()
    circ = nc.dram_tensor("circ", [128, NF], F32).ap()
    Hs = nc.dram_tensor("Hs", [2, 128, 128, 256], F32).ap()
    CG = 16
    with ExitStack() as st:
        P = Prog(nc, st)
        cw = sb(nc, st, "cws", [128, 12], F32)
        bs = sb(nc, st, "bss", [128, 1], F32)
        w1 = sb(nc, st, "w1s", [33, 64], F32)
        w2 = sb(nc, st, "w2s", [64, 64], F32)
        w3 = sb(nc, st, "w3s", [64, 2, 128], F32)
        bf = sb(nc, st, "bfs", [64, 4], F32)
        fa = sb(nc, st, "fas", [128, 2, 2, 512], F32)
        tw = sb(nc, st, "tws", [128, 2, 512], F32)
        gg = sb(nc, st, "ggs", [128, 3, 128], F32)
        ry = sb(nc, st, "rys", [128, 2, 256], F32)
        itw = sb(nc, st, "its", [128, 2, 2, 256], F32)
        fi = sb(nc, st, "fis", [128, 2, 3, 128], F32)
        for t, dd in ((cw, cwd), (bs, bsd), (w1, w1d), (w2, w2d), (w3, w3d), (bf, bfd), (fa, fad), (tw, twd), (gg, ggd),
                      (ry, ryd), (itw, itd), (fi, fid)):
            P.dma("sp", t[:], dd, writes=["consts"])

        with ExitStack() as s1:
            ta = sb(nc, s1, "ta", [128, PP], F32)
            tb = sb(nc, s1, "tb", [128, PP], F32)
            tcx = sb(nc, s1, "tcx", [128, PP], F32)
            xs = [sb(nc, s1, "xs%d" % i, [128, PP], F32) for i in range(2)]
            ft = sb(nc, s1, "ft", [33, PP], F32)
            h1 = sb(nc, s1, "h1", [64, PP], F32)
            h2 = sb(nc, s1, "h2", [64, PP], F32)
            ki = sb(nc, s1, "ki", [64, PP], mybir.dt.int32)
            kf = sb(nc, s1, "kf", [64, PP], F32)
            dc = sb(nc, s1, "dc", [128, PP], F32)
            hp = [sb(nc, s1, "hp%d" % i, [128, PP], F32) for i in range(2)]
            l1 = sb(nc, s1, "l1", [128, 4], F32)
            zc = sb(nc, s1, "zc", [128, 1], F32)
            pm = ps(nc, s1, "pm", [64, 512])
            ph = [ps(nc, s1, "ph%d" % i, [128, 512]) for i in range(2)]
            xi = 0
            for part in range(3):
                for t0 in range(0, L, PP):
                    xb = xi % 2
                    xi += 1
                    P.dma("sp", ta[:], pbd[0][part, :, t0:t0 + PP], writes=["ta"])
                    P.dma("sp", tb[:], pbd[1][part, :, t0:t0 + PP], writes=["tb"])
                    P.dma("sp", tcx[:], pbd[2][part, :, t0:t0 + PP], writes=["tcx"])
                    c0 = part * 3
                    P.op("dve", lambda e, c0=c0, part=part: e.tensor_scalar(ta[:], ta[:], cw[:, c0:c0 + 1], cw[:, 9 + part:10 + part], ALU.mult, ALU.add),
                         reads=["ta", "consts"], writes=["ta"])
                    P.op("dve", lambda e, c0=c0: e.scalar_tensor_tensor(tb[:], tb[:], cw[:, c0 + 1:c0 + 2], ta[:], ALU.mult, ALU.add),
                         reads=["ta", "tb", "consts"], writes=["tb"])
                    P.op("dve", lambda e, c0=c0, xb=xb: e.scalar_tensor_tensor(xs[xb][:], tcx[:], cw[:, c0 + 2:c0 + 3], tb[:], ALU.mult, ALU.add),
                         reads=["tb", "tcx", "consts"], writes=[("xs", xb)])
                    P.dma("sp", scr[part, :, t0:t0 + PP], xs[xb][:], reads=[("xs", xb)], writes=["scr"])
            P.op("pool", lambda e: e.memset(l1[:], 0.0), writes=["l1"])
            P.op("pool", lambda e: e.memset(zc[:], 0.0), writes=["zc"])
            P.dma("sp", circ[:, L:L + 1], zc[:], reads=["zc"], writes=["circ"], allow_slow_non_contiguous=True)
            for pas in range(1):
                for dr in range(2):
                    for p0 in range(0, L, PP):
                        P.dma("sp", ft[:], ftd[dr, :, p0:p0 + PP], writes=["ft"])
                        P.dma("sp", dc[:], dcd[dr, :, p0:p0 + PP], writes=["dc"])
                        for (src, skey, w, K_, dst, dkey, bi) in ((ft, "ft", w1, 33, h1, "h1", 0), (h1, "h1", w2, 64, h2, "h2", 2)):
                            for q0 in range(0, PP, 512):
                                P.op("pe", lambda e, q0=q0, src=src, w=w, K_=K_: e.matmul(pm[:, :512], w[:K_, :], src[:K_, q0:q0 + 512], start=True, stop=True),
                                     reads=[skey, "consts"], writes=["pm"])
                                P.op("dve", lambda e, q0=q0, dst=dst, bi=bi: e.tensor_scalar(dst[:, q0:q0 + 512], pm[:, :512], bf[:, bi:bi + 1], bf[:, bi + 1:bi + 2], ALU.add, ALU.mult),
                                     reads=["pm", "consts"], writes=[dkey])
                            P.op("dve", lambda e, dst=dst: e.tensor_scalar(dst[:], dst[:], 1.0 / (2.0 * PI), 8.5, ALU.mult, ALU.add), reads=[dkey], writes=[dkey])
                            P.op("dve", lambda e, dst=dst: e.tensor_copy(ki[:], dst[:]), reads=[dkey], writes=["ki"])
                            P.op("dve", lambda e, dst=dst: e.tensor_copy(kf[:], ki[:]), reads=["ki"], writes=["kf"])
                            P.op("dve", lambda e, dst=dst: e.tensor_tensor(dst[:], dst[:], kf[:], ALU.subtract), reads=[dkey, "kf"], writes=[dkey])
                            P.op("dve", lambda e, dst=dst: e.tensor_single_scalar(kf[:], dst[:], 0.0, ALU.is_lt), reads=[dkey], writes=["kf"])
                            P.op("dve", lambda e, dst=dst: e.tensor_tensor(dst[:], dst[:], kf[:], ALU.add), reads=[dkey, "kf"], writes=[dkey])
                            P.op("dve", lambda e, dst=dst: e.tensor_scalar(dst[:], dst[:], 2.0 * PI, -PI, ALU.mult, ALU.add), reads=[dkey], writes=[dkey])
                            P.op("act", lambda e, dst=dst: e.activation(out=dst[:], in_=dst[:], func=AF.Sin), reads=[dkey], writes=[dkey])
                        hb = (p0 // PP) % 2
                        for q0 in range(0, PP, 512):
                            b = (q0 // 512) % 2
                            P.op("pe", lambda e, q0=q0, dr=dr, b=b: e.matmul(ph[b][:, :512], w3[:, dr, :], h2[:, q0:q0 + 512], start=True, stop=True),
                                 reads=["h2", "consts"], writes=[("ph", b)])
                            P.op("dve", lambda e, q0=q0, b=b, hb=hb: e.tensor_tensor(hp[hb][:, q0:q0 + 512], ph[b][:, :512], dc[:, q0:q0 + 512], ALU.mult),
                                 reads=[("ph", b), "dc"], writes=[("hp", hb)])
                        last = (dr == 1 and p0 + PP == L)
                        if last:
                            P.op("dve", lambda e, hb=hb: e.memset(hp[hb][:, PP - 1:PP], 0.0), reads=[("hp", hb)], writes=[("hp", hb)])
                        P.op("dve", lambda e, hb=hb: e.tensor_reduce(out=l1[:, 2:3], in_=hp[hb][:], axis=AX.X, op=ALU.add, apply_absolute_value=True),
                             reads=[("hp", hb)], writes=["l1p"])
                        P.op("dve", lambda e: e.tensor_tensor(l1[:, 0:1], l1[:, 0:1], l1[:, 2:3], ALU.add), reads=["l1p", "l1"], writes=["l1"])
                        if dr == 0:
                            P.dma("sp", circ[:, p0:p0 + PP], hp[hb][:], reads=[("hp", hb)], writes=["circ"])
                        else:
                            n = PP - 1 if last else PP
                            P.dma("sp", circ[:, L + 1 + p0:L + 1 + p0 + n], hp[hb][:, :n], reads=[("hp", hb)], writes=["circ"])
            P.op("dve", lambda e: e.reciprocal(l1[:, 1:2], l1[:, 0:1]), reads=["l1"], writes=["l1"])
            for p0 in range(0, NF, PP):
                hb = (p0 // PP) % 2
                P.dma("sp", hp[hb][:], circ[:, p0:p0 + PP], reads=["circ"], writes=[("hp", hb)])
                P.op("dve", lambda e, hb=hb: e.tensor_scalar(hp[hb][:], hp[hb][:], l1[:, 1:2], None, ALU.mult), reads=[("hp", hb), "l1"], writes=[("hp", hb)])
                if p0 == 0:
                    P.op("dve", lambda e, hb=hb: e.tensor_tensor(hp[hb][:, 0:1], hp[hb][:, 0:1], bs[:, 0:1], ALU.add),
                         reads=[("hp", hb), "consts"], writes=[("hp", hb)])
                P.dma("sp", circ[:, p0:p0 + PP], hp[hb][:], reads=[("hp", hb)], writes=["circ"])

        with ExitStack() as s2:
            if stage < 2:
                P.finish("sp")
                return nc
            Xr = sb(nc, s2, "Xr", [128, 2, CG, 128], F32)
            Ar = sb(nc, s2, "Ar", [128, CG, 256], F32)
            Ai = sb(nc, s2, "Ai", [128, CG, 256], F32)
            Hr = sb(nc, s2, "Hr", [128, CG, 256], F32)
            Hi = sb(nc, s2, "Hi", [128, CG, 256], F32)
            Br = sb(nc, s2, "Br", [128, 2, CG, 128], F32)
            Bi = sb(nc, s2, "Bi", [128, 2, CG, 128], F32)
            U = [sb(nc, s2, "U%d" % i, [128, 512], F32) for i in range(2)]
            V = [sb(nc, s2, "V%d" % i, [128, 512], F32) for i in range(2)]
            xg = [sb(nc, s2, "xg%d" % i, [128, 2, 4, 128], F32) for i in range(2)]
            og = [sb(nc, s2, "og%d" % i, [128, 2, 4, 128], F32) for i in range(2)]
            pA = [ps(nc, s2, "pA%d" % i, [128, 512]) for i in range(2)]
            pCr = ps(nc, s2, "pCr", [128, 512])
            pCi = ps(nc, s2, "pCi", [128, 512])
            pI = [ps(nc, s2, "pI%d" % i, [128, 512]) for i in range(2)]
            pOr = ps(nc, s2, "pOr", [128, 512])
            pOi = ps(nc, s2, "pOi", [128, 512])
            cnt = [0]

            def fwd_A_and_twiddle(c, real_only):
                b = cnt[0] % 2
                cnt[0] += 1
                if real_only:
                    P.op("pe", lambda e: e.matmul(pA[b][:], Xr[:, 0, c, :], fa[:, 0, 0, :], start=True, stop=False), reads=["X", "consts"], writes=[("pA", b)], sig=False)
                    P.op("pe", lambda e: e.matmul(pA[b][:], Xr[:, 1, c, :], fa[:, 1, 0, :], start=False, stop=True), reads=["X", "consts"], writes=[("pA", b)])
                else:
                    P.op("pe", lambda e: e.matmul(pA[b][:], Xr[:, 0, c, :], fa[:, 0, 0, :], start=True, stop=False), reads=["X", "consts"], writes=[("pA", b)], sig=False)
                    P.op("pe", lambda e: e.matmul(pA[b][:], Xr[:, 1, c, :], fa[:, 0, 1, :], start=False, stop=True), reads=["X", "consts"], writes=[("pA", b)])
                P.op("dve", lambda e: e.tensor_tensor(U[b][:], pA[b][:], tw[:, 0, :], ALU.mult), reads=[("pA", b), "consts"], writes=[("U", b)])
                P.op("dve", lambda e: e.tensor_tensor(V[b][:, 0:256], pA[b][:, 256:512], tw[:, 1, 0:256], ALU.mult), reads=[("pA", b), "consts"], writes=[("V", b)])
                P.op("dve", lambda e: e.tensor_tensor(V[b][:, 256:512], pA[b][:, 0:256], tw[:, 1, 256:512], ALU.mult), reads=[("pA", b), "consts"], writes=[("V", b)])
                P.op("pool", lambda e: e.tensor_tensor(Ar[:, c, :], U[b][:, 0:256], V[b][:, 0:256], ALU.add), reads=[("U", b), ("V", b)], writes=[("A", c // 2)])
                P.op("pool", lambda e: e.tensor_tensor(Ai[:, c, :], U[b][:, 256:512], V[b][:, 256:512], ALU.add), reads=[("U", b), ("V", b)], writes=[("A", c // 2)])

            def fwd_C(j):
                ar = Ar[:, 2 * j:2 * j + 2, :]
                ai = Ai[:, 2 * j:2 * j + 2, :]
                P.op("pe", lambda e: e.matmul(pCr[:], gg[:, 0, :], ar, start=True, stop=False), reads=[("A", j), "consts"], writes=["pCr"], sig=False)
                P.op("pe", lambda e: e.matmul(pCr[:], gg[:, 2, :], ai, start=False, stop=True), reads=[("A", j), "consts"], writes=["pCr"])
                P.op("pe", lambda e: e.matmul(pCi[:], gg[:, 1, :], ar, start=True, stop=False), reads=[("A", j), "consts"], writes=["pCi"], sig=False)
                P.op("pe", lambda e: e.matmul(pCi[:], gg[:, 0, :], ai, start=False, stop=True), reads=[("A", j), "consts"], writes=["pCi"])

            for g0 in range(0, 128, CG):
                for blk in range(2):
                    src = circ[g0:g0 + CG, blk * L:(blk + 1) * L].rearrange("c (p n) -> p c n", n=128)
                    P.dma("sp", Xr[:, blk, :, :], src, reads=["circ"], writes=["X"])
                for c in range(CG):
                    fwd_A_and_twiddle(c, True)
                for j in range(CG // 2):
                    fwd_C(j)
                    P.op("act", lambda e, j=j: e.copy(out=Hr[:, 2 * j:2 * j + 2, :], in_=pCr[:]), reads=["pCr"], writes=[("H", j)])
                    P.op("act", lambda e, j=j: e.copy(out=Hi[:, 2 * j:2 * j + 2, :], in_=pCi[:]), reads=["pCi"], writes=[("H", j)])
                hk = [("H", j) for j in range(CG // 2)]
                P.dma("sp", Hs[0, :, g0:g0 + CG, :], Hr[:], reads=hk, writes=["Hs"])
                P.dma("sp", Hs[1, :, g0:g0 + CG, :], Hi[:], reads=hk, writes=["Hs"])

            for o in range(2 if stage >= 3 else 0):
                zsrc = scr[0] if o == 0 else z1d
                xsrc = scr[1 + o]
                zdst = z1d if o == 0 else yo
                for g0 in range(0, 64, CG):
                    for b in range(2):
                        src = zsrc[b * 64 + g0:b * 64 + g0 + CG, :].rearrange("c (p n) -> p c n", n=128)
                        P.dma("sp", Xr[:, b, :, :], src, reads=["scr", "z1d"], writes=["X"])
                    for ri in range(2):
                        P.dma("sp", (Hr if ri == 0 else Hi)[:], Hs[ri, :, o * 64 + g0:o * 64 + g0 + CG, :], reads=["Hs"],
                              writes=[("H", j) for j in range(CG // 2)])
                    for c in range(CG):
                        fwd_A_and_twiddle(c, False)
                    for j in range(CG // 2):
                        fwd_C(j)
                        b = j % 2
                        hr = Hr[:, 2 * j:2 * j + 2, :]
                        hi = Hi[:, 2 * j:2 * j + 2, :]
                        yr = Ar[:, 2 * j:2 * j + 2, :]
                        yi = Ai[:, 2 * j:2 * j + 2, :]
                        P.op("dve", lambda e, b=b, hr=hr: e.tensor_tensor(U[b][:], pCr[:], hr, ALU.mult), reads=["pCr", ("H", j)], writes=[("U", b)])
                        P.op("dve", lambda e, b=b, hi=hi: e.tensor_tensor(V[b][:], pCi[:], hi, ALU.mult), reads=["pCi", ("H", j)], writes=[("V", b)])
                        P.op("pool", lambda e, b=b, yr=yr: e.tensor_tensor(yr, U[b][:], V[b][:], ALU.subtract), reads=[("U", b), ("V", b)], writes=[("A", j)])
                        P.op("dve", lambda e, b=b, hi=hi: e.tensor_tensor(U[b][:], pCr[:], hi, ALU.mult), reads=["pCr", ("H", j)], writes=[("U", b)])
                        P.op("dve", lambda e, b=b, hr=hr: e.tensor_tensor(V[b][:], pCi[:], hr, ALU.mult), reads=["pCi", ("H", j)], writes=[("V", b)])
                        P.op("pool", lambda e, b=b, yi=yi: e.tensor_tensor(yi, U[b][:], V[b][:], ALU.add), reads=[("U", b), ("V", b)], writes=[("A", j)])
                    for c in range(CG):
                        b = c % 2
                        for blk in range(2):
                            P.op("pe", lambda e, c=c, blk=blk, b=b: e.matmul(pI[b][:, blk * 256:(blk + 1) * 256], Ar[:, c, blk * 128:(blk + 1) * 128], ry[:, 0, :],
                                                                             start=True, stop=False), reads=[("A", c // 2), "consts"], writes=[("pI", b)], sig=False)
                            P.op("pe", lambda e, c=c, blk=blk, b=b: e.matmul(pI[b][:, blk * 256:(blk + 1) * 256], Ai[:, c, blk * 128:(blk + 1) * 128], ry[:, 1, :],
                                                                             start=False, stop=True), reads=[("A", c // 2), "consts"], writes=[("pI", b)])
                        for blk in range(2):
                            lo, mid, hi_ = blk * 256, blk * 256 + 128, blk * 256 + 256
                            P.op("dve", lambda e, b=b, blk=blk, lo=lo, hi_=hi_: e.tensor_tensor(U[b][:, lo:hi_], pI[b][:, lo:hi_], itw[:, blk, 0, :], ALU.mult),
                                 reads=[("pI", b), "consts"], writes=[("U", b)])
                            P.op("dve", lambda e, b=b, blk=blk, lo=lo, mid=mid, hi_=hi_: e.tensor_tensor(V[b][:, lo:mid], pI[b][:, mid:hi_], itw[:, blk, 1, 0:128], ALU.mult),
                                 reads=[("pI", b), "consts"], writes=[("V", b)])
                            P.op("dve", lambda e, b=b, blk=blk, lo=lo, mid=mid, hi_=hi_: e.tensor_tensor(V[b][:, mid:hi_], pI[b][:, lo:mid], itw[:, blk, 1, 128:256], ALU.mult),
                                 reads=[("pI", b), "consts"], writes=[("V", b)])
                            P.op("pool", lambda e, b=b, blk=blk, c=c, lo=lo, mid=mid: e.tensor_tensor(Br[:, blk, c, :], U[b][:, lo:mid], V[b][:, lo:mid], ALU.add),
                                 reads=[("U", b), ("V", b)], writes=[("B", c // 4)])
                            P.op("pool", lambda e, b=b, blk=blk, c=c, mid=mid, hi_=hi_: e.tensor_tensor(Bi[:, blk, c, :], U[b][:, mid:hi_], V[b][:, mid:hi_], ALU.add),
                                 reads=[("U", b), ("V", b)], writes=[("B", c // 4)])
                    for q in range(CG // 4):
                        ob = q % 2
                        cs = slice(4 * q, 4 * q + 4)
                        for b in range(2):
                            src = xsrc[b * 64 + g0 + 4 * q:b * 64 + g0 + 4 * q + 4, :].rearrange("c (p n) -> p c n", n=128)
                            P.dma("sp", xg[ob][:, b, :, :], src, reads=["scr"], writes=[("xg", ob)])
                        for blk in range(2):
                            P.op("pe", lambda e, blk=blk, cs=cs: e.matmul(pOr[:], fi[:, blk, 0, :], Br[:, blk, cs, :], start=(blk == 0), stop=False),
                                 reads=[("B", q), "consts"], writes=["pOr"], sig=False)
                            P.op("pe", lambda e, blk=blk, cs=cs: e.matmul(pOr[:], fi[:, blk, 2, :], Bi[:, blk, cs, :], start=False, stop=(blk == 1)),
                                 reads=[("B", q), "consts"], writes=["pOr"], sig=(blk == 1))
                        for blk in range(2):
                            P.op("pe", lambda e, blk=blk, cs=cs: e.matmul(pOi[:], fi[:, blk, 1, :], Br[:, blk, cs, :], start=(blk == 0), stop=False),
                                 reads=[("B", q), "consts"], writes=["pOi"], sig=False)
                            P.op("pe", lambda e, blk=blk, cs=cs: e.matmul(pOi[:], fi[:, blk, 0, :], Bi[:, blk, cs, :], start=False, stop=(blk == 1)),
                                 reads=[("B", q), "consts"], writes=["pOi"], sig=(blk == 1))
                        P.op("dve", lambda e, ob=ob: e.tensor_tensor(og[ob][:, 0, :, :], pOr[:], xg[ob][:, 0, :, :], ALU.mult), reads=["pOr", ("xg", ob)], writes=[("og", ob)])
                        P.op("dve", lambda e, ob=ob: e.tensor_tensor(og[ob][:, 1, :, :], pOi[:], xg[ob][:, 1, :, :], ALU.mult), reads=["pOi", ("xg", ob)], writes=[("og", ob)])
                        for b in range(2):
                            dst = zdst[b * 64 + g0 + 4 * q:b * 64 + g0 + 4 * q + 4, :].rearrange("c (p n) -> p c n", n=128)
                            P.dma("sp", dst, og[ob][:, b, :, :], reads=[("og", ob)], writes=["z1d" if o == 0 else "yo"])
        P.finish("sp")
    return nc


def hyfft_consts():
    N = NF
    n1 = np.arange(256, dtype=np.float64)
    k1 = np.arange(256, dtype=np.float64)
    n2 = np.arange(128, dtype=np.float64)
    k2 = np.arange(128, dtype=np.float64)
    a = 2 * np.pi * np.outer(n1, k1) / 256
    Fc, Fs = np.cos(a), np.sin(a)
    fa = np.zeros((256, 2, 512))
    fa[:, 0, :256], fa[:, 0, 256:] = Fc, -Fs
    fa[:, 1, :256], fa[:, 1, 256:] = Fs, Fc
    fa = fa.reshape(2, 128, 2, 512).transpose(1, 0, 2, 3)
    t = 2 * np.pi * np.outer(n2, k1) / N
    Tr, Ti = np.cos(t), -np.sin(t)
    tw = np.stack([np.concatenate([Tr, Tr], 1), np.concatenate([-Ti, Ti], 1)], 1)
    g = 2 * np.pi * np.outer(n2, k2) / 128
    Gr, Gi = np.cos(g), -np.sin(g)
    gg = np.stack([Gr, Gi, -Gi], 1)
    ry = np.stack([np.concatenate([Gr, -Gi], 1), np.concatenate([Gi, Gr], 1)], 1)
    tt = 2 * np.pi * np.outer(k1, n2) / N
    cTr, cTi = np.cos(tt), np.sin(tt)
    it = np.stack([np.concatenate([cTr, cTr], 1), np.concatenate([-cTi, cTi], 1)], 1)
    it = it.reshape(2, 128, 2, 256).transpose(1, 0, 2, 3)
    ai = 2 * np.pi * np.outer(k1, n1[:128]) / 256
    fi = np.stack([np.cos(ai), np.sin(ai), -np.sin(ai)], 1) / N
    fi = fi.reshape(2, 128, 3, 128).transpose(1, 0, 2, 3)
    f32 = lambda x: np.ascontiguousarray(x.astype(np.float32))
    return {"fa": f32(fa), "tw": f32(tw), "gg": f32(gg), "ry": f32(ry), "it": f32(it), "fi": f32(fi)}


def run_hyfft(X, cw, cb, w1, b1, f1, w2, b2, f2, w3, bias):
    L = LH
    import os
    nc = _get("hyfft", build_hyfft, int(os.environ.get("HYFFT_STAGE", "3")))
    consts = _get("hyfft_consts", hyfft_consts)
    pos = np.arange(L, dtype=np.float32)
    t = pos / np.float32(L - 1)
    wv = np.float32(2.0 * np.pi) * pos / np.float32(L)
    bands = np.linspace(1e-4, 15, 16, dtype=np.float32)
    feats = np.concatenate([t[:, None], np.cos(wv[:, None] * bands), -np.sin(wv[:, None] * bands)], -1).astype(np.float32).T
    rates = np.abs(np.linspace(np.log(1e-2) / 1.5, np.log(1e-2) / 0.3, 512, dtype=np.float32))
    decay = np.exp(-t[None, :] * rates[:, None]).astype(np.float32)
    Xp = np.pad(X, ((0, 0), (1, 1), (0, 0)))
    feats2 = np.ascontiguousarray(np.stack([feats, feats[:, ::-1]], 0))
    maps = []
    for i in range(NCORE):
        cs = np.arange(i * 64, (i + 1) * 64)

        def rows(a):
            return np.ascontiguousarray(np.stack([a[:, :, p * 512 + cs].transpose(0, 2, 1).reshape(128, L) for p in range(3)], 0))
        cwt = np.concatenate([np.stack([cw[tp, p * 512 + cs] for p in range(3) for tp in range(3)], 1),
                              np.stack([cb[p * 512 + cs] for p in range(3)], 1)], 1)
        w3r = w3.reshape(64, 2, 2, 512)[:, :, :, cs].reshape(64, 2, 128)
        dco = np.concatenate([decay[cs], decay[cs]], 0)
        m = {
            "pb0": rows(Xp[:, 0:L]), "pb1": rows(Xp[:, 1:L + 1]), "pb2": rows(Xp[:, 2:L + 2]),
            "cw": np.ascontiguousarray(np.concatenate([cwt, cwt], 0)).astype(np.float32),
            "bias": np.ascontiguousarray(bias[:, cs].reshape(128, 1)).astype(np.float32),
            "feats": feats2, "decay": np.ascontiguousarray(np.stack([dco, dco[:, ::-1]], 0)),
            "w1": np.ascontiguousarray(w1), "w2": np.ascontiguousarray(w2), "w3": np.ascontiguousarray(w3r).astype(np.float32),
            "bf": np.ascontiguousarray(np.stack([b1, f1, b2, f2], 1)).astype(np.float32),
        }
        m.update(consts)
        maps.append(m)
    res = _run(nc, maps)
    out = np.empty((2, L, 512), np.float32)
    for i in range(NCORE):
        out[:, :, i * 64:(i + 1) * 64] = res[i]["yo"].reshape(2, 64, L).transpose(0, 2, 1)
    return out
```
